# Optimizing a Trainium2 kernel written in Bass

```python
import math, functools
import jax, jax.numpy as jnp
from jax import lax
import numpy as np

D_MODEL = 1024
BATCH = 8
SEQ = 8192
DEPTH = 1

N_META = 16
CHUNK = 64
GDN_HEADS = 8
GDN_DK = 128
GDN_DV = 128
RET_HEADS = 8
RET_DK = 128
RET_DV = 128
CONV_K = 4
D_FF = 2816
ROPE_BASE = 10000.0
EPS = 1e-6

GDN_QK = GDN_HEADS * GDN_DK
GDN_V = GDN_HEADS * GDN_DV
GDN_CONV = 2 * GDN_QK + GDN_V
RET_QK = RET_HEADS * RET_DK
RET_V = RET_HEADS * RET_DV
PROJ_SIZES = (GDN_CONV, GDN_V, GDN_HEADS, GDN_HEADS, RET_QK, RET_QK, RET_V, RET_V, D_MODEL, D_MODEL)
D_PROJ = sum(PROJ_SIZES)

kernel_name = "hybrid_gdn_retention_macaron_layer"


def rms_norm(x, w):
    xf = x.astype(jnp.float32)
    y = xf * lax.rsqrt(jnp.mean(xf * xf, axis=-1, keepdims=True) + EPS)
    return (y * w.astype(jnp.float32)).astype(x.dtype)


def swiglu(x, w_in, w_out):
    gate, up = jnp.split(x @ w_in, 2, axis=-1)
    return (jax.nn.silu(gate) * up) @ w_out


def causal_depthwise_conv(x, w):
    c = x.shape[-1]
    return lax.conv_general_dilated(
        x, w[:, None, :].astype(x.dtype), window_strides=(1,), padding=[(CONV_K - 1, 0)],
        dimension_numbers=("NWC", "WIO", "NWC"), feature_group_count=c)


def to_heads(t, n_heads):
    b, l, _ = t.shape
    return t.reshape(b, l, n_heads, -1).transpose(0, 2, 1, 3).astype(jnp.float32)


def l2norm(t):
    return t * lax.rsqrt(jnp.sum(t * t, axis=-1, keepdims=True) + EPS)


def rotary(t, pos):
    d = t.shape[-1]
    inv = 1.0 / (ROPE_BASE ** jnp.linspace(0.0, 1.0, d // 2, dtype=jnp.float32))
    ang = pos[:, None] * inv[None, :]
    cos, sin = jnp.cos(ang), jnp.sin(ang)
    tp = t.reshape(*t.shape[:-1], d // 2, 2)
    t0, t1 = tp[..., 0], tp[..., 1]
    return jnp.stack([t0 * cos - t1 * sin, t1 * cos + t0 * sin], axis=-1).reshape(t.shape)


def gdn_chunk_scan(q, k, v, g, beta, state, chunk):
    b, h, l, dk = q.shape
    n = l // chunk
    split = lambda t: t.reshape(b, h, n, chunk, *t.shape[3:])
    q, k, v, g, beta = split(q), split(k), split(v), split(g), split(beta)
    g = jnp.cumsum(g, axis=-1)
    causal = jnp.tril(jnp.ones((chunk, chunk), dtype=bool))
    strict = jnp.tril(jnp.ones((chunk, chunk), dtype=bool), -1)
    diff = g[..., :, None] - g[..., None, :]
    decay = jnp.where(causal, jnp.exp(jnp.where(causal, diff, 0.0)), 0.0)
    k_beta = k * beta[..., None]
    a = jnp.where(strict, jnp.einsum("bhncd,bhnmd->bhncm", k_beta, k) * decay, 0.0) + jnp.eye(chunk, dtype=q.dtype)
    solve = functools.partial(lax.linalg.triangular_solve, left_side=True, lower=True)
    u = solve(a, v * beta[..., None])
    w = solve(a, k_beta * jnp.exp(g)[..., None])
    qk = jnp.einsum("bhncd,bhnmd->bhncm", q, k) * decay
    g_last = g[..., -1]
    q_dec = q * jnp.exp(g)[..., None]
    k_dec = k * jnp.exp(g_last[..., None] - g)[..., None]

    def step(s, xs):
        qk_c, u_c, w_c, qd_c, kd_c, gl_c = xs
        v_new = u_c - jnp.einsum("bhck,bhkv->bhcv", w_c, s)
        o = jnp.einsum("bhck,bhkv->bhcv", qd_c, s) + jnp.einsum("bhcm,bhmv->bhcv", qk_c, v_new)
        s = s * jnp.exp(gl_c)[..., None, None] + jnp.einsum("bhck,bhcv->bhkv", kd_c, v_new)
        return s, o

    xs = tuple(jnp.moveaxis(t, 2, 0) for t in (qk, u, w, q_dec, k_dec, g_last))
    state, o = lax.scan(step, state, xs)
    return jnp.moveaxis(o, 0, 2).reshape(b, h, l, -1), state


def retention_chunk_scan(q, k, v, log_gamma, state, chunk):
    b, h, l, dk = q.shape
    n = l // chunk
    split = lambda t: t.reshape(b, h, n, chunk, t.shape[-1])
    q, k, v = split(q), split(k), split(v)
    pos = jnp.arange(chunk, dtype=jnp.float32)
    lg = log_gamma[:, None]
    causal = jnp.tril(jnp.ones((chunk, chunk), dtype=bool))
    diff = pos[:, None] - pos[None, :]
    decay = jnp.where(causal, jnp.exp(jnp.where(causal, diff, 0.0) * log_gamma[:, None, None]), 0.0)
    scores = jnp.einsum("bhncd,bhnmd->bhncm", q, k) * decay[None, :, None]
    intra = jnp.einsum("bhncm,bhnmv->bhncv", scores, v)
    q_dec = q * jnp.exp((pos + 1.0) * lg)[None, :, None, :, None]
    k_dec = k * jnp.exp((chunk - 1.0 - pos) * lg)[None, :, None, :, None]
    chunk_decay = jnp.exp(chunk * log_gamma)[None, :, None, None]

    def step(s, xs):
        qd, kd, vc = xs
        o = jnp.einsum("bhck,bhkv->bhcv", qd, s)
        s = s * chunk_decay + jnp.einsum("bhck,bhcv->bhkv", kd, vc)
        return s, o

    xs = tuple(jnp.moveaxis(t, 2, 0) for t in (q_dec, k_dec, v))
    state, inter = lax.scan(step, state, xs)
    o = intra + jnp.moveaxis(inter, 0, 2)
    return o.reshape(b, h, l, -1), state


def hybrid_mixer(n, w_in, conv_w, a_log, dt_bias, gdn_norm, ret_norm, w_br_gdn, w_br_ret, w_out):
    b, l, _ = n.shape
    f32 = jnp.float32
    offs = [int(i) for i in np.cumsum(PROJ_SIZES)[:-1]]
    qkv, z, b_raw, a_raw, rq, rk, rv, rg, ga, gb = jnp.split(n @ w_in, offs, axis=-1)

    qkv = jax.nn.silu(causal_depthwise_conv(qkv, conv_w))
    q, k, v = jnp.split(qkv, [GDN_QK, 2 * GDN_QK], axis=-1)
    q = l2norm(to_heads(q, GDN_HEADS)) * (GDN_DK ** -0.5)
    k = l2norm(to_heads(k, GDN_HEADS))
    v = to_heads(v, GDN_HEADS)
    g = (-jnp.exp(a_log.astype(f32)) * jax.nn.softplus(a_raw.astype(f32) + dt_bias.astype(f32))).transpose(0, 2, 1)
    beta = jax.nn.sigmoid(b_raw.astype(f32)).transpose(0, 2, 1)
    s0 = jnp.zeros((b, GDN_HEADS, GDN_DK, GDN_DV), f32)
    o_m, s_m = gdn_chunk_scan(q[:, :, :N_META], k[:, :, :N_META], v[:, :, :N_META],
                              g[:, :, :N_META], beta[:, :, :N_META], s0, N_META)
    o_r, _ = gdn_chunk_scan(q[:, :, N_META:], k[:, :, N_META:], v[:, :, N_META:],
                            g[:, :, N_META:], beta[:, :, N_META:], s_m, CHUNK)
    o_a = jnp.concatenate([o_m, o_r], axis=2).transpose(0, 2, 1, 3)
    o_a = o_a * lax.rsqrt(jnp.mean(o_a * o_a, axis=-1, keepdims=True) + EPS) * gdn_norm.astype(f32)
    y_a = (o_a * jax.nn.silu(z.reshape(b, l, GDN_HEADS, GDN_DV).astype(f32))).reshape(b, l, GDN_V).astype(n.dtype)

    pos = jnp.arange(l, dtype=f32)
    rq = rotary(to_heads(rq, RET_HEADS), pos)
    rk = rotary(to_heads(rk, RET_HEADS), pos) * (RET_DK ** -0.5)
    rv = to_heads(rv, RET_HEADS)
    log_gamma = jnp.log1p(-jnp.exp2(-5.0 - jnp.arange(RET_HEADS, dtype=f32)))
    r0 = jnp.zeros((b, RET_HEADS, RET_DK, RET_DV), f32)
    p_m, r_m = retention_chunk_scan(rq[:, :, :N_META], rk[:, :, :N_META], rv[:, :, :N_META], log_gamma, r0, N_META)
    p_r, _ = retention_chunk_scan(rq[:, :, N_META:], rk[:, :, N_META:], rv[:, :, N_META:], log_gamma, r_m, CHUNK)
    o_b = jnp.concatenate([p_m, p_r], axis=2).transpose(0, 2, 1, 3)
    mu = jnp.mean(o_b, axis=-1, keepdims=True)
    var = jnp.mean(jnp.square(o_b - mu), axis=-1, keepdims=True)
    o_b = ((o_b - mu) * lax.rsqrt(var + EPS)).reshape(b, l, RET_V) * ret_norm.astype(f32)
    y_b = (jax.nn.silu(rg.astype(f32)) * o_b).astype(n.dtype)

    merged = jax.nn.sigmoid(ga) * (y_a @ w_br_gdn) + jax.nn.sigmoid(gb) * (y_b @ w_br_ret)
    return merged @ w_out


def setup_inputs(seed: int = 0) -> dict:
    key = jax.random.key(seed)
    ks = jax.random.split(key, 20)
    f32 = jnp.float32
    nrm = lambda k, shape, scale: jax.random.normal(k, shape, f32) * scale
    gain = lambda k, shape: 1.0 + 0.02 * jax.random.normal(k, shape, f32)
    dt = jnp.exp(jax.random.uniform(ks[8], (DEPTH, GDN_HEADS), f32, math.log(1e-3), math.log(1e-1)))
    return {
        "x": nrm(ks[0], (BATCH, SEQ, D_MODEL), 1.0),
        "meta_tokens": nrm(ks[1], (N_META, D_MODEL), 1.0),
        "ffn1_norm": gain(ks[2], (DEPTH, D_MODEL)),
        "ffn1_w_in": nrm(ks[3], (DEPTH, D_MODEL, 2 * D_FF), D_MODEL ** -0.5),
        "ffn1_w_out": nrm(ks[4], (DEPTH, D_FF, D_MODEL), D_FF ** -0.5),
        "mix_norm": gain(ks[5], (DEPTH, D_MODEL)),
        "w_in": nrm(ks[6], (DEPTH, D_MODEL, D_PROJ), D_MODEL ** -0.5),
        "gdn_conv_w": nrm(ks[7], (DEPTH, CONV_K, GDN_CONV), CONV_K ** -0.5),
        "gdn_a_log": jnp.log(jax.random.uniform(ks[9], (DEPTH, GDN_HEADS), f32, 1.0, 16.0)),
        "gdn_dt_bias": dt + jnp.log(-jnp.expm1(-dt)),
        "gdn_out_norm": gain(ks[10], (DEPTH, GDN_DV)),
        "ret_out_norm": gain(ks[11], (DEPTH, RET_V)),
        "w_branch_gdn": nrm(ks[12], (DEPTH, GDN_V, D_MODEL), GDN_V ** -0.5),
        "w_branch_ret": nrm(ks[13], (DEPTH, RET_V, D_MODEL), RET_V ** -0.5),
        "w_out": nrm(ks[14], (DEPTH, D_MODEL, D_MODEL), D_MODEL ** -0.5),
        "ffn2_norm": gain(ks[15], (DEPTH, D_MODEL)),
        "ffn2_w_in": nrm(ks[16], (DEPTH, D_MODEL, 2 * D_FF), D_MODEL ** -0.5),
        "ffn2_w_out": nrm(ks[17], (DEPTH, D_FF, D_MODEL), D_FF ** -0.5),
        "final_norm": gain(ks[18], (D_MODEL,)),
    }


def reference(x, meta_tokens, ffn1_norm, ffn1_w_in, ffn1_w_out, mix_norm, w_in, gdn_conv_w,
              gdn_a_log, gdn_dt_bias, gdn_out_norm, ret_out_norm, w_branch_gdn, w_branch_ret,
              w_out, ffn2_norm, ffn2_w_in, ffn2_w_out, final_norm):
    b = x.shape[0]
    meta = jnp.broadcast_to(meta_tokens[None].astype(x.dtype), (b, N_META, D_MODEL))
    h = jnp.concatenate([meta, x], axis=1)
    for i in range(DEPTH):
        h = h + 0.5 * swiglu(rms_norm(h, ffn1_norm[i]), ffn1_w_in[i], ffn1_w_out[i])
        h = h + hybrid_mixer(rms_norm(h, mix_norm[i]), w_in[i], gdn_conv_w[i], gdn_a_log[i],
                             gdn_dt_bias[i], gdn_out_norm[i], ret_out_norm[i],
                             w_branch_gdn[i], w_branch_ret[i], w_out[i])
        h = h + 0.5 * swiglu(rms_norm(h, ffn2_norm[i]), ffn2_w_in[i], ffn2_w_out[i])
    return rms_norm(h, final_norm)[:, N_META:]
```

```python
import contextlib
import numpy as np
import ml_dtypes
import concourse.bass as bass
import concourse.mybir as mybir
from concourse.bass_utils import run_bass_kernel_spmd

F32 = mybir.dt.float32
BF16 = mybir.dt.bfloat16
AF = mybir.ActivationFunctionType
ALU = mybir.AluOpType
AX = mybir.AxisListType

D = 1024
NMETA = 16
SEQ = 8192
DFF = 2816
NJ = 22
DPROJ = 10256
EPS = 1e-6
NH = 8
PIECE = 4096
NSLOT = 4
ENGS = ("pe", "act", "dve", "pool", "sp")
N_DMA_SLOTS = 12
SAME_ENGINE_SYNC = True
SB_BASE = 18432
SB_LIMIT = 229376


class Res:
    __slots__ = ("name", "last_w", "readers", "overlaps", "lo", "hi")

    def __init__(self, name, lo=None, hi=None):
        self.name = name
        self.last_w = None
        self.readers = {}
        self.overlaps = []
        self.lo = lo
        self.hi = hi


class Op:
    __slots__ = ("idx", "eng", "fn", "deps", "dma", "needs_inc", "sem", "val", "prev_val")

    def __init__(self, idx, eng, fn, deps, dma):
        self.idx = idx
        self.eng = eng
        self.fn = fn
        self.deps = deps
        self.dma = dma
        self.needs_inc = False
        self.sem = None
        self.val = 0
        self.prev_val = 0


class Prog:
    def __init__(self, nc):
        self.nc = nc
        self.ops = []

    def op(self, eng, fn, reads=(), writes=(), dma=False):
        idx = len(self.ops)
        deps = set()
        wset = []
        for w in writes:
            wset.append(w)
            wset.extend(w.overlaps)
        for r in reads:
            if r.last_w is not None:
                deps.add(r.last_w)
        for w in wset:
            if w.last_w is not None:
                deps.add(w.last_w)
            for ridx in w.readers.values():
                deps.add(ridx)
        o = Op(idx, eng, fn, sorted(deps), dma)
        self.ops.append(o)
        for r in reads:
            r.readers[("dma", idx) if dma else eng] = idx
        for w in wset:
            w.last_w = idx
            w.readers = {}
        return o

    def mm(self, out, lhsT, rhs, start, stop, reads, writes):
        return self.op("pe", lambda e: e.matmul(out, lhsT=lhsT, rhs=rhs, start=start, stop=stop), reads, writes)

    def tr(self, out, in_, ident, reads, writes):
        return self.op("pe", lambda e: e.transpose(out=out, in_=in_, identity=ident), reads, writes)

    def tt(self, eng, out, in0, in1, op, reads, writes):
        return self.op(eng, lambda e: e.tensor_tensor(out=out, in0=in0, in1=in1, op=op), reads, writes)

    def ts(self, eng, out, in0, s1, s2, op0, op1, reads, writes):
        if s2 is None:
            return self.op(eng, lambda e: e.tensor_scalar(out=out, in0=in0, scalar1=s1, scalar2=None, op0=op0), reads, writes)
        return self.op(eng, lambda e: e.tensor_scalar(out=out, in0=in0, scalar1=s1, scalar2=s2, op0=op0, op1=op1), reads, writes)

    def stt(self, eng, out, in0, scalar, in1, op0, op1, reads, writes):
        return self.op(eng, lambda e: e.scalar_tensor_tensor(out=out, in0=in0, scalar=scalar, in1=in1, op0=op0, op1=op1), reads, writes)

    def act(self, out, in_, func, reads, writes, bias=None, scale=None, accum_out=None):
        kw = {}
        if bias is not None:
            kw["bias"] = bias
        if scale is not None:
            kw["scale"] = scale
        if accum_out is not None:
            kw["accum_out"] = accum_out
        return self.op("act", lambda e: e.activation(out=out, in_=in_, func=func, **kw), reads, writes)

    def copy(self, eng, out, in_, reads, writes):
        if eng == "act":
            return self.act(out, in_, AF.Copy, reads, writes)
        return self.op(eng, lambda e: e.tensor_copy(out=out, in_=in_), reads, writes)

    def red(self, eng, out, in_, reads, writes):
        return self.op(eng, lambda e: e.tensor_reduce(out=out, in_=in_, axis=AX.X, op=ALU.add), reads, writes)

    def memset(self, eng, out, val, writes):
        return self.op(eng, lambda e: e.memset(out, val), (), writes)

    def dma(self, out, in_, reads, writes, queue="sp"):
        return self.op(queue, lambda e: e.dma_start(out=out, in_=in_), reads, writes, dma=True)

    def emit(self, final_ops):
        nc = self.nc
        ops = self.ops

        def skip_same(o, dop):
            return (not dop.dma) and (not o.dma) and dop.eng == o.eng and (o.eng == "pe" or not SAME_ENGINE_SYNC)

        for o in ops:
            for d in o.deps:
                dop = ops[d]
                if dop.dma or skip_same(o, dop):
                    continue
                dop.needs_inc = True
        with contextlib.ExitStack() as st:
            esem = {e: st.enter_context(nc.semaphore("prog_" + e)) for e in ENGS}
            dsem = {q: [st.enter_context(nc.semaphore("dma_%s_%d" % (q, i))) for i in range(N_DMA_SLOTS)]
                    for q in ("sp", "pool")}
            cnt = {e: 0 for e in ENGS}
            dcnt = {q: 0 for q in dsem}
            duse = {q: [0] * N_DMA_SLOTS for q in dsem}
            for o in ops:
                if o.dma:
                    q = o.eng
                    s = dcnt[q] % N_DMA_SLOTS
                    dcnt[q] += 1
                    o.prev_val = 16 * duse[q][s]
                    duse[q][s] += 1
                    o.sem = dsem[q][s]
                    o.val = 16 * duse[q][s]
                elif o.needs_inc:
                    cnt[o.eng] += 1
                    o.sem = esem[o.eng]
                    o.val = cnt[o.eng]
            per = {e: [o for o in ops if o.eng == e] for e in ENGS}
            block = st.enter_context(nc.Block())

            def run(ename, eng):
                waited = {}
                for o in per[ename]:
                    need = {}
                    for d in o.deps:
                        dop = ops[d]
                        if skip_same(o, dop):
                            continue
                        k = id(dop.sem)
                        if k not in need or need[k][1] < dop.val:
                            need[k] = (dop.sem, dop.val)
                    if o.dma and o.prev_val > 0:
                        k = id(o.sem)
                        if k not in need or need[k][1] < o.prev_val:
                            need[k] = (o.sem, o.prev_val)
                    for k, (sem, val) in need.items():
                        if waited.get(k, 0) >= val:
                            continue
                        eng.wait_ge(sem, val)
                        waited[k] = val
                    ins = o.fn(eng)
                    if o.dma:
                        ins.then_inc(o.sem, 16)
                    elif o.needs_inc:
                        ins.then_inc(o.sem, 1)
                if ename == "sp":
                    for fo in final_ops:
                        eng.wait_ge(fo.sem, fo.val)

            @block.sync
            def _(e):
                run("sp", e)

            @block.tensor
            def _(e):
                run("pe", e)

            @block.scalar
            def _(e):
                run("act", e)

            @block.vector
            def _(e):
                run("dve", e)

            @block.gpsimd
            def _(e):
                run("pool", e)


class Buf:
    __slots__ = ("t", "r")

    def __init__(self, t, r):
        self.t = t
        self.r = r


class SBAlloc:
    def __init__(self, nc):
        self.nc = nc
        self.off = SB_BASE
        self.peak = SB_BASE
        self.all = []

    def alloc(self, name, shape, dt):
        esz = 4 if dt == F32 else 2
        n = 1
        for s in shape[1:]:
            n *= s
        size = (n * esz + 31) // 32 * 32
        assert self.off + size <= SB_LIMIT, ("SBUF overflow", name, self.off, size)
        t = self.nc.alloc_sbuf_tensor_at(name, list(shape), dt, offset=self.off)
        r = Res(name, self.off, self.off + size)
        self.off += size
        self.peak = max(self.peak, self.off)
        self.all.append(r)
        return Buf(t, r)

    def finalize(self):
        rs = sorted(self.all, key=lambda r: r.lo)
        for i, a in enumerate(rs):
            for b in rs[i + 1:]:
                if b.lo >= a.hi:
                    break
                a.overlaps.append(b)
                b.overlaps.append(a)


def piece_specs():
    sp = []

    def ffn(i):
        win, wout, nrm = "ffn%d_w_in" % i, "ffn%d_w_out" % i, "ffn%d_norm" % i
        for g in range(NJ // 2):
            j0, j1 = 2 * g, 2 * g + 1
            sp.append(dict(kind="FM", w=win, cc=[j0 * 128, DFF + j0 * 128, j1 * 128, DFF + j1 * 128], norm=nrm, tag=("ffn_in", i, g)))
        for ch in range(2):
            for (j0, nj) in ((0, 8), (8, 8), (16, 6)):
                sp.append(dict(kind="TMK", w=wout, j0=j0, nj=nj, c0=ch * 512, norm=None, tag=("ffn_out", i, ch, j0, nj)))

    ffn(1)
    for hg in range(2):
        for nm, base in (("q", 0), ("k", 1024), ("v", 2048), ("z", 3072), ("rg", 7184)):
            sp.append(dict(kind="FM", w="w_in", cc=[base + (4 * hg + i) * 128 for i in range(4)], norm="mix_norm", tag=(nm, hg)))
        for nm, base in (("rq", 4112), ("rk", 5136), ("rv", 6160)):
            sp.append(dict(kind="TM", w="w_in", c0=base + hg * 512, norm="mix_norm", tag=(nm, hg)))
    for p in range(2):
        sp.append(dict(kind="FM", w="w_in", cc=[8208 + (4 * p + i) * 128 for i in range(4)], norm="mix_norm", tag=("ga", p)))
        sp.append(dict(kind="FM", w="w_in", cc=[9232 + (4 * p + i) * 128 for i in range(4)], norm="mix_norm", tag=("gb", p)))
        sp.append(dict(kind="FM", w="w_branch_gdn", cc=[(4 * p + i) * 128 for i in range(4)], norm=None, tag=("brg", p)))
        sp.append(dict(kind="FM", w="w_branch_ret", cc=[(4 * p + i) * 128 for i in range(4)], norm=None, tag=("brr", p)))
    for ch in range(2):
        sp.append(dict(kind="TM", w="w_out", c0=ch * 512, norm=None, tag=("wo", ch)))
    ffn(2)
    return sp


PIECES = piece_specs()
NP = len(PIECES)
NORM_COL = {"ffn1_norm": 0, "mix_norm": 8, "ffn2_norm": 16}
PC_RETN = 24
PC_GDNN = 32
PC_CONV = 33
PC_ALOG = 129
PC_DTB = 137
PC_WBA = 145
NPRM = PC_WBA + 128
CC_U = 0
CC_I = 64
CC_MUS = 128
CC_MUI = 192
CC_MLS = 256
CC_DRT = 320
CC_XI = 832
CC_ZS64 = 1344
CC_ZS16 = 1352
CC_CD64 = 1360
CC_CD16 = 1368
CC_ONES = 1376
NCST = CC_ONES + 128


def host_consts():
    c = np.zeros((128, NCST), np.float32)
    j = np.arange(64)
    c[:64, CC_U:CC_U + 64] = (j[:, None] <= j[None, :])
    c[:64, CC_I:CC_I + 64] = np.eye(64)
    c[:64, CC_MUS:CC_MUS + 64] = (j[:, None] < j[None, :])
    c[:64, CC_MUI:CC_MUI + 64] = (j[:, None] <= j[None, :])
    c[:64, CC_MLS:CC_MLS + 64] = (j[None, :] < j[:, None])
    lg = np.log1p(-np.exp2(-5.0 - np.arange(NH, dtype=np.float64)))
    p = np.arange(128)
    m = (p % 64)[:, None, None]
    cc = j[None, None, :]
    dr = np.where(m <= cc, np.exp((cc - m) * lg[None, :, None]), 0.0)
    c[:, CC_DRT:CC_DRT + 512] = dr.reshape(128, 512)
    xi = np.exp((j[None, None, :] + 1.0) * lg[None, :, None]) * np.ones((128, 1, 1))
    c[:, CC_XI:CC_XI + 512] = xi.reshape(128, 512)
    sc = 128.0 ** -0.5
    c[:, CC_ZS64:CC_ZS64 + 8] = np.exp((63.0 - (p % 64))[:, None] * lg[None, :]) * sc
    c[:16, CC_ZS16:CC_ZS16 + 8] = np.exp((15.0 - p[:16])[:, None] * lg[None, :]) * sc
    c[:, CC_CD64:CC_CD64 + 8] = np.exp(64.0 * lg)[None, :]
    c[:, CC_CD16:CC_CD16 + 8] = np.exp(16.0 * lg)[None, :]
    c[:, CC_ONES:CC_ONES + 128] = 1.0
    return c


def host_rope(L):
    inv = (1.0 / (10000.0 ** np.linspace(0.0, 1.0, 64, dtype=np.float32))).astype(np.float32)
    pos = np.arange(L, dtype=np.float32)
    ang = (pos[:, None] * inv[None, :]).astype(np.float32)
    r = np.zeros((L, 128), np.float32)
    r[:, :64] = np.cos(ang.astype(np.float64))
    r[:, 64:] = np.sin(ang.astype(np.float64))
    return r


def host_pack_weights(W):
    out = np.zeros((NP, 128, PIECE), np.float32)
    for s, sp in enumerate(PIECES):
        w = W[sp["w"]]
        if sp["kind"] == "FM":
            wk = w.reshape(8, 128, -1)
            for i, c0 in enumerate(sp["cc"]):
                blk = wk[:, :, c0:c0 + 128]
                out[s].reshape(128, 8, 4, 128)[:, :, i, :] = blk.transpose(1, 0, 2)
        elif sp["kind"] == "TM":
            wk = w.reshape(8, 128, -1)[:, :, sp["c0"]:sp["c0"] + 512]
            out[s].reshape(128, 8, 512)[:, :, :] = wk.transpose(1, 0, 2)
        else:
            wk = w.reshape(NJ, 128, -1)[sp["j0"]:sp["j0"] + sp["nj"], :, sp["c0"]:sp["c0"] + 512]
            out[s].reshape(128, 8, 512)[:, :sp["nj"], :] = wk.transpose(1, 0, 2)
    return out


def host_pack_params(I):
    prm = np.zeros((128, NPRM), np.float32)
    for nm, c0 in NORM_COL.items():
        prm[:, c0:c0 + 8] = np.asarray(I[nm]).reshape(8, 128).T
    prm[:, PC_RETN:PC_RETN + 8] = np.asarray(I["ret_out_norm"]).reshape(8, 128).T
    prm[:, PC_GDNN] = np.asarray(I["gdn_out_norm"]).reshape(128)
    cw = np.asarray(I["gdn_conv_w"]).reshape(4, 24, 128)
    prm[:, PC_CONV:PC_CONV + 96] = cw.transpose(2, 1, 0).reshape(128, 96)
    prm[:, PC_ALOG:PC_ALOG + 8] = np.asarray(I["gdn_a_log"]).reshape(1, 8)
    prm[:, PC_DTB:PC_DTB + 8] = np.asarray(I["gdn_dt_bias"]).reshape(1, 8)
    wba = np.asarray(I["w_in"]).reshape(8, 128, DPROJ)[:, :, 4096:4112]
    prm[:, PC_WBA:PC_WBA + 128] = wba.transpose(1, 0, 2).reshape(128, 128)
    return prm


def build(nt, debug=False):
    seq = nt * 512
    L = NMETA + seq
    nc = bass.Bass("TRN2", target_bir_lowering=False)
    x_d = nc.dram_tensor("x", [seq, D], F32, kind="ExternalInput").ap()
    meta_d = nc.dram_tensor("meta", [NMETA, D], F32, kind="ExternalInput").ap()
    wst_d = nc.dram_tensor("wst", [NP, 128, PIECE], F32, kind="ExternalInput").ap()
    prm_d = nc.dram_tensor("prm", [128, NPRM], F32, kind="ExternalInput").ap()
    fnw_d = nc.dram_tensor("fnw", [128, D], F32, kind="ExternalInput").ap()
    cst_d = nc.dram_tensor("cst", [128, NCST], F32, kind="ExternalInput").ap()
    rope_d = nc.dram_tensor("rope", [L, 128], F32, kind="ExternalInput").ap()
    out_d = nc.dram_tensor("out", [seq, D], F32, kind="ExternalOutput").ap()
    wsc_d = nc.dram_tensor("wsc", [NP, 128, PIECE], BF16).ap()
    wsc_r = [Res("wsc%d" % s) for s in range(NP)]

    P = Prog(nc)
    sb = SBAlloc(nc)
    A = sb.alloc

    prm = A("prm", [128, NPRM], F32)
    cst = A("cst", [128, NCST], F32)
    fnw = A("fnw", [128, D], F32)
    identb = A("identb", [128, 128], BF16)
    identb64 = A("identb64", [64, 4, 64], BF16)
    wba = A("wba", [128, 8, 16], BF16)
    nA = A("nA", [128, 8], F32)
    H = [A("H%d" % tb, [128, D], F32) for tb in range(4)]
    nb = [A("nb%d" % i, [128, D], BF16) for i in range(2)]
    nT = A("nT", [128, 8, 512], BF16)
    ring = [A("ring%d" % i, [128, PIECE], BF16) for i in range(NSLOT)]
    Sg = [A("Sg%d" % h, [128, 128], F32) for h in range(NH)]
    Sr = [A("Sr%d" % h, [128, 128], F32) for h in range(NH)]
    Sgb = [A("Sgb%d" % h, [128, 128], BF16) for h in range(NH)]
    Srb = [A("Srb%d" % h, [128, 128], BF16) for h in range(NH)]
    ctail = A("ctail", [128, 24, 3], F32)
    yaT = A("yaT", [128, 8, 512], BF16)
    ybT = A("ybT", [128, 8, 512], BF16)
    ss = A("ss", [128, 4], F32)
    rstd = A("rstd", [128, 4], F32)
    junk = A("junk", [128, D], BF16)
    rope = A("rope", [128, 4, 128], F32)
    gtok = A("gtok", [64, 8, 8], F32)
    btok = A("btok", [64, 8, 8], F32)
    braw = A("braw", [64, 8, 16], F32)
    batmp = A("batmp", [64, 8, 8], F32)
    cbias = A("cbias", [128, 2], F32)

    def eps_ap(rows=128):
        return cbias.t[0:rows, 0:1]

    def one_ap(rows=128):
        return cbias.t[0:rows, 1:2]

    arena0 = sb.off

    stage = [A("stage%d" % i, [128, PIECE], F32) for i in range(2)]
    sb.off = arena0
    actT = A("actT", [128, NJ, 512], BF16)
    sil = [A("sil%d" % i, [128, 512], F32) for i in range(2)]
    OUTB = [A("OUTB%d" % tb, [128, D], F32) for tb in range(4)]
    sb.off = arena0
    qT = A("qT", [128, 4, 512], BF16)
    kT = A("kT", [128, 4, 512], BF16)
    vT = A("vT", [128, 4, 512], BF16)
    szT = A("szT", [128, 4, 512], BF16)
    srgT = A("srgT", [128, 4, 512], BF16)
    rkd_tm = A("rkd_tm", [128, 4, 512], BF16)
    rv_tm = A("rv_tm", [128, 4, 512], BF16)
    rqT = A("rqT", [128, 4, 512], BF16)
    rkT = A("rkT", [128, 4, 512], BF16)
    rqdT = A("rqdT", [128, 4, 512], BF16)
    sub0 = sb.off
    rq_tm = A("rq_tm", [128, 4, 512], BF16)
    rk_tm = A("rk_tm", [128, 4, 512], BF16)
    cin = [A("cin%d" % i, [128, 515], F32) for i in range(2)]
    cacc = [A("cacc%d" % i, [128, 512], F32) for i in range(2)]
    cf = [A("cf%d" % i, [128, 512], F32) for i in range(2)]
    csq = [A("csq%d" % i, [128, 512], F32) for i in range(2)]
    crs = [A("crs%d" % i, [128, 512], F32) for i in range(2)]
    rxs = [A("rxs%d" % i, [128, 512], F32) for i in range(2)]
    rt = [A("rt%d" % i, [128, 256], F32) for i in range(4)]
    sb.off = sub0
    ck = []
    for par in range(2):
        d = {}
        for nm, shp, dt in (
            ("GU", [64, 4, 64], F32), ("BI", [64, 4, 64], F32), ("gcs", [128, 8], F32),
            ("eg", [64, 4], F32), ("beg", [64, 4], F32), ("ekd", [64, 4], F32), ("egl", [128, 4], F32),
            ("Dm", [64, 4, 64], F32), ("El", [64, 4, 64], F32),
            ("EB", [64, 4, 64], F32), ("EQ", [64, 4, 64], F32),
            ("EGQ", [128, 4, 64], F32), ("qdT", [128, 4, 64], BF16),
            ("PT", [64, 4, 64], BF16), ("kbg", [64, 4, 128], BF16), ("kd", [64, 4, 128], BF16),
            ("bv", [64, 4, 128], BF16), ("u", [64, 4, 128], F32), ("wT", [128, 4, 64], BF16),
            ("vnew", [64, 4, 128], BF16), ("osb", [64, 4, 128], F32), ("osq", [64, 4, 128], F32),
            ("ost", [64, 8], F32), ("on", [64, 4, 128], BF16),
            ("scT", [128, 4, 64], BF16), ("orb", [64, 4, 128], F32),
            ("ort", [64, 8], F32), ("orn", [64, 4, 128], BF16),
            ("ybt", [128, 4, 64], F32),
        ):
            d[nm] = A("%s_%d" % (nm, par), shp, dt)
        for lv in range(2):
            d["A%d" % lv] = A("A%d_%d" % (lv, par), [64, 4, 64], BF16)
            d["B%d" % lv] = A("B%d_%d" % (lv, par), [64, 4, 64], BF16)
            d["X%d" % lv] = A("X%d_%d" % (lv, par), [64, 4, 64], BF16)
        d["Eu"] = d["Dm"]
        d["EA"] = d["El"]
        d["orc"] = d["orb"]
        d["orq"] = d["osq"]
        ck.append(d)
    hg_end = sb.off
    sb.off = arena0
    sgT = A("sgT", [128, 4, 512], BF16)
    sbT = A("sbT", [128, 4, 512], BF16)
    mtmp = [A("mtmp%d" % i, [128, 512], F32) for i in range(4)]
    mtmp2 = [A("mtmp2_%d" % i, [128, 512], F32) for i in range(2)]
    mergedT = A("mergedT", [128, 8, 512], BF16)
    sb.finalize()
    if debug:
        print("SBUF peak", sb.peak, "of", SB_LIMIT, "hg_end", hg_end, "arena0", arena0)

    psf = [nc.alloc_psum_tensor("psf%d" % i, [128, 512], F32) for i in range(6)]
    psb = [nc.alloc_psum_tensor("psb%d" % i, [128, 1024], BF16) for i in range(2)]
    psf_r = [Res("psf%d" % i) for i in range(6)]
    psb_r = [Res("psb%d" % i) for i in range(2)]
    pctr = {"f": 0, "b": 0}

    def PSF():
        i = pctr["f"] % 6
        pctr["f"] += 1
        return psf[i], psf_r[i]

    def PSB():
        i = pctr["b"] % 2
        pctr["b"] += 1
        return psb[i], psb_r[i]

    def pc(c0, n=1):
        return prm.t[:, c0:c0 + n]

    def cc(c0, n, rows=128):
        return cst.t[0:rows, c0:c0 + n]

    rr = ["act", "dve"]
    rrc = {"i": 0}

    def nxt(choices=("act", "dve")):
        rrc["i"] += 1
        return choices[rrc["i"] % len(choices)]

    P.dma(prm.t[:], prm_d[:, :], [], [prm.r])
    P.dma(cst.t[:], cst_d[:, :], [], [cst.r])
    P.dma(fnw.t[:], fnw_d[:, :], [], [fnw.r])
    P.memset("pool", crs[0].t[:, 0:128], 0.0, [crs[0].r])
    P.op("pool", lambda e: e.affine_select(out=crs[0].t[:, 0:128], in_=crs[0].t[:, 0:128], pattern=[[-1, 128]],
                                           compare_op=ALU.not_equal, fill=1.0, base=0, channel_multiplier=1),
         [crs[0].r], [crs[0].r])
    P.copy("dve", identb.t[:], crs[0].t[:, 0:128], [crs[0].r], [identb.r])
    for i in range(4):
        P.copy("dve", identb64.t[:, i, :], cst.t[0:64, CC_I:CC_I + 64], [cst.r], [identb64.r])
    for kc in range(8):
        P.ts("dve", wba.t[:, kc, :], prm.t[:, PC_WBA + kc * 16:PC_WBA + kc * 16 + 16], pc(NORM_COL["mix_norm"] + kc), None,
             ALU.mult, None, [prm.r], [wba.r])
    P.act(nA.t[:], pc(PC_ALOG, 8), AF.Exp, [prm.r], [nA.r])
    P.ts("dve", nA.t[:], nA.t[:], -1.0, None, ALU.mult, None, [nA.r], [nA.r])
    for h in range(NH):
        P.memset("pool", Sg[h].t[:], 0.0, [Sg[h].r])
        P.memset("pool", Sr[h].t[:], 0.0, [Sr[h].r])
        P.memset("dve", Sgb[h].t[:], 0.0, [Sgb[h].r])
        P.memset("dve", Srb[h].t[:], 0.0, [Srb[h].r])
    P.memset("pool", ctail.t[:], 0.0, [ctail.r])
    P.memset("pool", cbias.t[:, 0:1], EPS, [cbias.r])
    P.memset("pool", cbias.t[:, 1:2], 1.0, [cbias.r])

    for s, spc in enumerate(PIECES):
        stg = stage[s % 2]
        slot = ring[s % NSLOT]
        P.dma(stg.t[:], wst_d[s], [], [stg.r])
        if spc["norm"] is not None:
            c0 = NORM_COL[spc["norm"]]
            for kc in range(8):
                eng = "act" if kc % 2 == 0 else "dve"
                o_ = slot.t[:, kc * 512:(kc + 1) * 512]
                i_ = stg.t[:, kc * 512:(kc + 1) * 512]
                if eng == "act":
                    P.act(o_, i_, AF.Identity, [stg.r, prm.r], [slot.r], scale=pc(c0 + kc))
                else:
                    P.ts("dve", o_, i_, pc(c0 + kc), None, ALU.mult, None, [stg.r, prm.r], [slot.r])
        else:
            P.copy("act", slot.t[:, 0:1536], stg.t[:, 0:1536], [stg.r], [slot.r])
            P.copy("dve", slot.t[:, 1536:3072], stg.t[:, 1536:3072], [stg.r], [slot.r])
            P.copy("pool", slot.t[:, 3072:4096], stg.t[:, 3072:4096], [stg.r], [slot.r])
        P.dma(wsc_d[s], slot.t[:], [slot.r], [wsc_r[s]])

    wstate = {"issued": 0, "cur": 0}
    total_pieces = NP * (nt + 1)

    def w_issue_upto(n):
        while wstate["issued"] < min(n, total_pieces):
            g = wstate["issued"]
            s = g % NP
            slot = ring[g % NSLOT]
            P.dma(slot.t[:], wsc_d[s], [wsc_r[s]], [slot.r])
            wstate["issued"] += 1

    def next_piece(tag_prefix):
        g = wstate["cur"]
        s = g % NP
        assert PIECES[s]["tag"][0] == tag_prefix, (PIECES[s]["tag"], tag_prefix)
        w_issue_upto(g + NSLOT)
        wstate["cur"] += 1
        return ring[g % NSLOT]

    def norm_to_nT(T, tbs):
        ntb = len(tbs)
        for tb, (t0, tn) in enumerate(tbs):
            P.act(junk.t[0:tn, :], H[tb].t[0:tn, :], AF.Square, [H[tb].r], [junk.r, ss.r], accum_out=ss.t[0:tn, tb:tb + 1])
        tn0 = tbs[0][1]
        P.act(rstd.t[0:tn0, 0:ntb], ss.t[0:tn0, 0:ntb], AF.Ln, [ss.r], [rstd.r], scale=1.0 / D, bias=eps_ap(tn0))
        P.act(rstd.t[0:tn0, 0:ntb], rstd.t[0:tn0, 0:ntb], AF.Exp, [rstd.r], [rstd.r], scale=-0.5)
        for tb, (t0, tn) in enumerate(tbs):
            nbb = nb[tb % 2]
            P.ts("dve", nbb.t[0:tn, :], H[tb].t[0:tn, :], rstd.t[0:tn, tb:tb + 1], None, ALU.mult, None, [H[tb].r, rstd.r], [nbb.r])
            pt, pr = PSB()
            for fc in range(8):
                P.tr(pt[:, fc * 128:fc * 128 + tn], nbb.t[0:tn, fc * 128:(fc + 1) * 128], identb.t[0:tn, 0:tn], [nbb.r, identb.r], [pr])
            P.copy(nxt(), nT.t[:, :, t0:t0 + tn], pt[:, :].rearrange("p (f t) -> p f t", f=8)[:, :, 0:tn], [pr], [nT.r])

    def fm_group(slot, i, T, rhsbuf, rhs_r):
        pt, pr = PSF()
        for kc in range(8):
            P.mm(pt[:, 0:T], slot.t[:, kc * 512 + i * 128:kc * 512 + (i + 1) * 128], rhsbuf.t[:, kc, 0:T], kc == 0, kc == 7,
                 [slot.r, rhs_r], [pr])
        return pt, pr

    def tm_group(slot, lbuf, l_r, t0, tn):
        pt, pr = PSF()
        for kc in range(8):
            P.mm(pt[0:tn, :], lbuf.t[:, kc, t0:t0 + tn], slot.t[:, kc * 512:(kc + 1) * 512], kc == 0, kc == 7, [slot.r, l_r], [pr])
        return pt, pr

    def ffn(i, T, tbs):
        norm_to_nT(T, tbs)
        for g in range(NJ // 2):
            slot = next_piece("ffn_in")
            for jj in range(2):
                j = 2 * g + jj
                pg, pgr = fm_group(slot, 2 * jj, T, nT, nT.r)
                pu, pur = fm_group(slot, 2 * jj + 1, T, nT, nT.r)
                sl = sil[j % 2]
                P.act(sl.t[:, 0:T], pg[:, 0:T], AF.Silu, [pgr], [sl.r])
                P.tt("dve", actT.t[:, j, 0:T], sl.t[:, 0:T], pu[:, 0:T], ALU.mult, [sl.r, pur], [actT.r])
        for ch in range(2):
            pts = [PSF() for _ in tbs]
            for (j0, nj) in ((0, 8), (8, 8), (16, 6)):
                slot = next_piece("ffn_out")
                for jj in range(nj):
                    j = j0 + jj
                    for tb, (t0, tn) in enumerate(tbs):
                        P.mm(pts[tb][0][0:tn, :], actT.t[:, j, t0:t0 + tn], slot.t[:, jj * 512:(jj + 1) * 512], j == 0, j == NJ - 1,
                             [slot.r, actT.r], [pts[tb][1]])
            for tb, (t0, tn) in enumerate(tbs):
                hh = H[tb].t[0:tn, ch * 512:(ch + 1) * 512]
                P.stt("dve", hh, pts[tb][0][0:tn, :], 0.5, hh, ALU.mult, ALU.add, [pts[tb][1], H[tb].r], [H[tb].r])

    def mixer(T, tbs, C, tok0):
        NCH = T // C
        nlev = {64: 5, 16: 3}[C]
        zs_c = CC_ZS64 if C == 64 else CC_ZS16
        cd_c = CC_CD64 if C == 64 else CC_CD16
        norm_to_nT(T, tbs)
        for tb, (t0, tn) in enumerate(tbs):
            P.dma(rope.t[0:tn, tb, :], rope_d[tok0 + t0:tok0 + t0 + tn, :], [], [rope.r])
        pt, pr = PSF()
        for n in range(NCH):
            for kc in range(8):
                P.mm(pt[0:C, n * 16:(n + 1) * 16], nT.t[:, kc, n * C:(n + 1) * C], wba.t[:, kc, :], kc == 0, kc == 7, [nT.r, wba.r], [pr])
        P.copy("act", braw.t[0:C, 0:NCH, :], pt[0:C, 0:NCH * 16].rearrange("p (n c) -> p n c", c=16), [pr], [braw.r])
        P.act(btok.t[0:C, 0:NCH, :], braw.t[0:C, 0:NCH, 0:8], AF.Exp, [braw.r], [btok.r], scale=-1.0)
        P.ts("dve", btok.t[0:C, 0:NCH, :], btok.t[0:C, 0:NCH, :], 1.0, None, ALU.add, None, [btok.r], [btok.r])
        P.op("dve", lambda e: e.reciprocal(out=btok.t[0:C, 0:NCH, :], in_=btok.t[0:C, 0:NCH, :]), [btok.r], [btok.r])
        P.tt("dve", batmp.t[0:C, 0:NCH, :], braw.t[0:C, 0:NCH, 8:16], prm.t[0:C, PC_DTB:PC_DTB + 8].unsqueeze(1).broadcast_to([C, NCH, 8]),
             ALU.add, [braw.r, prm.r], [batmp.r])
        P.act(batmp.t[0:C, 0:NCH, :], batmp.t[0:C, 0:NCH, :], AF.Exp, [batmp.r], [batmp.r])
        P.act(batmp.t[0:C, 0:NCH, :], batmp.t[0:C, 0:NCH, :], AF.Ln, [batmp.r], [batmp.r], bias=one_ap(C))
        P.tt("dve", gtok.t[0:C, 0:NCH, :], batmp.t[0:C, 0:NCH, :], nA.t[0:C, :].unsqueeze(1).broadcast_to([C, NCH, 8]), ALU.mult,
             [batmp.r, nA.r], [gtok.r])

        for hg in range(2):
            for qi, (nm, dst) in enumerate((("q", qT), ("k", kT), ("v", vT))):
                slot = next_piece(nm)
                for i in range(4):
                    cch = qi * 8 + hg * 4 + i
                    par = i % 2
                    pt, pr = fm_group(slot, i, T, nT, nT.r)
                    ci = cin[par]
                    P.copy("act", ci.t[:, 3:3 + T], pt[:, 0:T], [pr], [ci.r])
                    P.copy("pool", ci.t[:, 0:3], ctail.t[:, cch, :], [ctail.r], [ci.r])
                    P.copy("pool", ctail.t[:, cch, :], ci.t[:, T:T + 3], [ci.r], [ctail.r])
                    ca = cacc[par]
                    wcol = PC_CONV + cch * 4
                    P.act(ca.t[:, 0:T], ci.t[:, 0:T], AF.Identity, [ci.r, prm.r], [ca.r], scale=pc(wcol))
                    for tap in range(1, 4):
                        P.stt("dve", ca.t[:, 0:T], ci.t[:, tap:tap + T], pc(wcol + tap), ca.t[:, 0:T], ALU.mult, ALU.add,
                              [ci.r, prm.r, ca.r], [ca.r])
                    if nm == "v":
                        P.act(dst.t[:, i, 0:T], ca.t[:, 0:T], AF.Silu, [ca.r], [dst.r])
                        continue
                    f = cf[par]
                    P.act(f.t[:, 0:T], ca.t[:, 0:T], AF.Silu, [ca.r], [f.r])
                    sq = csq[par]
                    P.tt("pool", sq.t[:, 0:T], f.t[:, 0:T], f.t[:, 0:T], ALU.mult, [f.r], [sq.r])
                    p2, p2r = PSF()
                    P.mm(p2[:, 0:T], cst.t[:, CC_ONES:CC_ONES + 128], sq.t[:, 0:T], True, True, [cst.r, sq.r], [p2r])
                    rs_ = crs[par]
                    P.act(rs_.t[:, 0:T], p2[:, 0:T], AF.Ln, [p2r], [rs_.r], bias=eps_ap())
                    P.act(rs_.t[:, 0:T], rs_.t[:, 0:T], AF.Exp, [rs_.r], [rs_.r], scale=-0.5)
                    if nm == "q":
                        P.stt("dve", dst.t[:, i, 0:T], f.t[:, 0:T], 128.0 ** -0.5, rs_.t[:, 0:T], ALU.mult, ALU.mult, [f.r, rs_.r], [dst.r])
                    else:
                        P.tt("dve", dst.t[:, i, 0:T], f.t[:, 0:T], rs_.t[:, 0:T], ALU.mult, [f.r, rs_.r], [dst.r])
            for nm, dst in (("z", szT), ("rg", srgT)):
                slot = next_piece(nm)
                for i in range(4):
                    pt, pr = fm_group(slot, i, T, nT, nT.r)
                    P.act(dst.t[:, i, 0:T], pt[:, 0:T], AF.Silu, [pr], [dst.r])
            for nm, dst in (("rq", rq_tm), ("rk", rk_tm)):
                slot = next_piece(nm)
                for tb, (t0, tn) in enumerate(tbs):
                    pt, pr = tm_group(slot, nT, nT.r, t0, tn)
                    xs = rxs[tb % 2]
                    P.copy("act", xs.t[0:tn, :], pt[0:tn, :], [pr], [xs.r])
                    xv = xs.t[0:tn, :].rearrange("p (h j two) -> p h j two", h=4, two=2)
                    x0 = xv[:, :, :, 0]
                    x1 = xv[:, :, :, 1]
                    cosb = rope.t[0:tn, tb, 0:64].unsqueeze(1).broadcast_to([tn, 4, 64])
                    sinb = rope.t[0:tn, tb, 64:128].unsqueeze(1).broadcast_to([tn, 4, 64])
                    t1, t2, t3, t4 = (rt[k].t[0:tn, :].rearrange("p (h j) -> p h j", h=4) for k in range(4))
                    ov = dst.t[0:tn, tb, :].rearrange("p (h j two) -> p h j two", h=4, two=2)
                    P.tt("dve", t1, x0, cosb, ALU.mult, [xs.r, rope.r], [rt[0].r])
                    P.tt("pool", t2, x1, sinb, ALU.mult, [xs.r, rope.r], [rt[1].r])
                    P.tt("dve", t3, x1, cosb, ALU.mult, [xs.r, rope.r], [rt[2].r])
                    P.tt("pool", t4, x0, sinb, ALU.mult, [xs.r, rope.r], [rt[3].r])
                    P.tt("dve", ov[:, :, :, 0], t1, t2, ALU.subtract, [rt[0].r, rt[1].r], [dst.r])
                    P.tt("pool", ov[:, :, :, 1], t3, t4, ALU.add, [rt[2].r, rt[3].r], [dst.r])
            slot = next_piece("rv")
            for tb, (t0, tn) in enumerate(tbs):
                pt, pr = tm_group(slot, nT, nT.r, t0, tn)
                P.copy("act", rv_tm.t[0:tn, tb, :], pt[0:tn, :], [pr], [rv_tm.r])
            for tb, (t0, tn) in enumerate(tbs):
                P.tt("dve", rkd_tm.t[0:tn, tb, :].rearrange("p (h d) -> p h d", h=4),
                     rk_tm.t[0:tn, tb, :].rearrange("p (h d) -> p h d", h=4),
                     cst.t[0:tn, zs_c + hg * 4:zs_c + hg * 4 + 4].unsqueeze(2).broadcast_to([tn, 4, 128]), ALU.mult,
                     [rk_tm.r, cst.r], [rkd_tm.r])
            for src, dst, scl in ((rq_tm, rqT, None), (rk_tm, rkT, 128.0 ** -0.5)):
                for tb, (t0, tn) in enumerate(tbs):
                    pt, pr = PSB()
                    for i in range(4):
                        P.tr(pt[:, i * 128:i * 128 + tn], src.t[0:tn, tb, i * 128:(i + 1) * 128], identb.t[0:tn, 0:tn], [src.r, identb.r], [pr])
                    pv = pt[:, 0:512].rearrange("p (h t) -> p h t", h=4)[:, :, 0:tn]
                    if scl is None:
                        P.copy("dve", dst.t[:, :, t0:t0 + tn], pv, [pr], [dst.r])
                    else:
                        P.act(dst.t[:, :, t0:t0 + tn], pv, AF.Identity, [pr], [dst.r], scale=scl)
            for i in range(4):
                h = hg * 4 + i
                P.tt("pool", rqdT.t[:, i, 0:T].rearrange("p (n c) -> p n c", c=C), rqT.t[:, i, 0:T].rearrange("p (n c) -> p n c", c=C),
                     cst.t[:, CC_XI + h * 64:CC_XI + h * 64 + C].unsqueeze(1).broadcast_to([128, NCH, C]), ALU.mult, [rqT.r, cst.r], [rqdT.r])

            for n in range(NCH):
                K = ck[n % 2]
                c0 = n * C
                tb = c0 // 128
                po = c0 % 128
                hs = slice(hg * 4, hg * 4 + 4)

                def bc4(ap2):
                    return ap2.unsqueeze(2).broadcast_to([C, 4, C])

                def v3(b, rows=C, w=C):
                    return b.t[0:rows, :, 0:w]

                Ub = cst.t[0:C, CC_U:CC_U + C].unsqueeze(1).broadcast_to([C, 4, C])
                Ib = cst.t[0:C, CC_I:CC_I + C].unsqueeze(1).broadcast_to([C, 4, C])
                P.tt("dve", v3(K["GU"]), Ub, bc4(gtok.t[0:C, n, hs]), ALU.mult, [cst.r, gtok.r], [K["GU"].r])
                P.tt("pool", v3(K["BI"]), Ib, bc4(btok.t[0:C, n, hs]), ALU.mult, [cst.r, btok.r], [K["BI"].r])
                pG, pGr = PSF()
                pB, pBr = PSF()
                pS, pSr = PSF()
                for i in range(4):
                    P.mm(pG[:, i * C:(i + 1) * C], cst.t[0:C, CC_ONES:CC_ONES + 128], K["GU"].t[0:C, i, 0:C], True, True, [cst.r, K["GU"].r], [pGr])
                for i in range(4):
                    P.mm(pB[0:C, i * C:(i + 1) * C], cst.t[0:C, CC_ONES:CC_ONES + C], K["BI"].t[0:C, i, 0:C], True, True, [cst.r, K["BI"].r], [pBr])
                P.mm(pS[0:C, 0:4], cst.t[0:C, CC_U:CC_U + C], gtok.t[0:C, n, hs], True, True, [cst.r, gtok.r], [pSr])
                P.mm(pS[:, 4:8], cst.t[0:C, CC_ONES:CC_ONES + 128], gtok.t[0:C, n, hs], True, True, [cst.r, gtok.r], [pSr])
                gcs = K["gcs"]
                P.copy("act", gcs.t[:, 4:8], pS[:, 4:8], [pSr], [gcs.r])
                P.copy("act", gcs.t[0:C, 0:4], pS[0:C, 0:4], [pSr], [gcs.r])
                P.act(K["eg"].t[0:C, :], gcs.t[0:C, 0:4], AF.Exp, [gcs.r], [K["eg"].r])
                P.tt("dve", K["beg"].t[0:C, :], K["eg"].t[0:C, :], btok.t[0:C, n, hs], ALU.mult, [K["eg"].r, btok.r], [K["beg"].r])
                P.tt("dve", K["ekd"].t[0:C, :], gcs.t[0:C, 4:8], gcs.t[0:C, 0:4], ALU.subtract, [gcs.r], [K["ekd"].r])
                P.act(K["ekd"].t[0:C, :], K["ekd"].t[0:C, :], AF.Exp, [K["ekd"].r], [K["ekd"].r])
                P.act(K["egl"].t[:, :], gcs.t[:, 4:8], AF.Exp, [gcs.r], [K["egl"].r])
                pG3 = pG[0:C, 0:4 * C].rearrange("p (h c) -> p h c", h=4)
                P.tt("dve", v3(K["Dm"]), pG3, bc4(gcs.t[0:C, 0:4]), ALU.subtract, [pGr, gcs.r], [K["Dm"].r])
                P.ts("dve", v3(K["El"]), v3(K["Dm"]), 0.0, None, ALU.max, None, [K["Dm"].r], [K["El"].r])
                P.act(v3(K["El"]), v3(K["El"]), AF.Exp, [K["El"].r], [K["El"].r], scale=-1.0)
                P.ts("dve", v3(K["Eu"]), v3(K["Dm"]), 0.0, None, ALU.min, None, [K["Dm"].r], [K["Eu"].r])
                P.act(v3(K["Eu"]), v3(K["Eu"]), AF.Exp, [K["Eu"].r], [K["Eu"].r])
                MUS = cst.t[0:C, CC_MUS:CC_MUS + C].unsqueeze(1).broadcast_to([C, 4, C])
                MUI = cst.t[0:C, CC_MUI:CC_MUI + C].unsqueeze(1).broadcast_to([C, 4, C])
                MLS = cst.t[0:C, CC_MLS:CC_MLS + C].unsqueeze(1).broadcast_to([C, 4, C])
                P.tt("pool", v3(K["EQ"]), v3(K["Eu"]), MUI, ALU.mult, [K["Eu"].r, cst.r], [K["EQ"].r])
                P.tt("dve", v3(K["EB"]), v3(K["Eu"]), MUS, ALU.mult, [K["Eu"].r, cst.r], [K["EB"].r])
                P.tt("dve", v3(K["EB"]), v3(K["EB"]), pB[0:C, 0:4 * C].rearrange("p (h c) -> p h c", h=4), ALU.mult, [K["EB"].r, pBr], [K["EB"].r])
                P.tt("pool", v3(K["EA"]), v3(K["El"]), MLS, ALU.mult, [K["El"].r, cst.r], [K["EA"].r])
                P.tt("pool", v3(K["EA"]), v3(K["EA"]), bc4(btok.t[0:C, n, hs]), ALU.mult, [K["EA"].r, btok.r], [K["EA"].r])
                P.act(K["EGQ"].t[:, :, 0:C], pG[:, 0:4 * C].rearrange("p (h c) -> p h c", h=4), AF.Exp, [pGr], [K["EGQ"].r])
                P.tt("dve", K["qdT"].t[:, :, 0:C], qT.t[:, :, c0:c0 + C], K["EGQ"].t[:, :, 0:C], ALU.mult, [qT.r, K["EGQ"].r], [K["qdT"].r])
                pK, pKr = PSF()
                pQ, pQr = PSF()
                for i in range(4):
                    P.mm(pK[0:C, i * C:(i + 1) * C], kT.t[:, i, c0:c0 + C], kT.t[:, i, c0:c0 + C], True, True, [kT.r], [pKr])
                for i in range(4):
                    P.mm(pQ[0:C, i * C:(i + 1) * C], kT.t[:, i, c0:c0 + C], qT.t[:, i, c0:c0 + C], True, True, [kT.r, qT.r], [pQr])
                pK3 = pK[0:C, 0:4 * C].rearrange("p (h c) -> p h c", h=4)
                pQ3 = pQ[0:C, 0:4 * C].rearrange("p (h c) -> p h c", h=4)
                P.tt("dve", v3(K["B0"]), pK3, v3(K["EB"]), ALU.mult, [pKr, K["EB"].r], [K["B0"].r])
                P.tt("dve", v3(K["A0"]), pK3, v3(K["EA"]), ALU.mult, [pKr, K["EA"].r], [K["A0"].r])
                P.tt("dve", v3(K["PT"]), pQ3, v3(K["EQ"]), ALU.mult, [pQr, K["EQ"].r], [K["PT"].r])
                P.tt("pool", v3(K["X0"]), identb64.t[0:C, :, 0:C], v3(K["B0"]), ALU.subtract, [identb64.r, K["B0"].r], [K["X0"].r])
                pt, pr = PSB()
                for i in range(4):
                    P.tr(pt[0:C, i * 128:(i + 1) * 128], kT.t[:, i, c0:c0 + C], identb.t[:, :], [kT.r, identb.r], [pr])
                for i in range(4):
                    P.tr(pt[0:C, 512 + i * 128:512 + (i + 1) * 128], vT.t[:, i, c0:c0 + C], identb.t[:, :], [vT.r, identb.r], [pr])
                ktv = pt[0:C, 0:512].rearrange("p (h d) -> p h d", h=4)
                vtv = pt[0:C, 512:1024].rearrange("p (h d) -> p h d", h=4)

                def bcd(ap2):
                    return ap2.unsqueeze(2).broadcast_to([C, 4, 128])

                P.tt("dve", K["kbg"].t[0:C, :, :], ktv, bcd(K["beg"].t[0:C, :]), ALU.mult, [pr, K["beg"].r], [K["kbg"].r])
                P.tt("dve", K["kd"].t[0:C, :, :], ktv, bcd(K["ekd"].t[0:C, :]), ALU.mult, [pr, K["ekd"].r], [K["kd"].r])
                P.tt("dve", K["bv"].t[0:C, :, :], vtv, bcd(btok.t[0:C, n, hs]), ALU.mult, [pr, btok.r], [K["bv"].r])
                for lv in range(nlev):
                    Ap, Bp, Xp = K["A%d" % (lv % 2)], K["B%d" % (lv % 2)], K["X%d" % (lv % 2)]
                    An, Bn, Xn = K["A%d" % ((lv + 1) % 2)], K["B%d" % ((lv + 1) % 2)], K["X%d" % ((lv + 1) % 2)]
                    pA, pAr = PSF()
                    for i in range(4):
                        P.mm(pA[0:C, i * C:(i + 1) * C], Bp.t[0:C, i, 0:C], Ap.t[0:C, i, 0:C], True, True, [Bp.r, Ap.r], [pAr])
                    P.copy("act", v3(An), pA[0:C, 0:4 * C].rearrange("p (h c) -> p h c", h=4), [pAr], [An.r])
                    if lv < nlev - 1:
                        pB2, pB2r = PSF()
                        for i in range(4):
                            P.mm(pB2[0:C, i * C:(i + 1) * C], Ap.t[0:C, i, 0:C], Bp.t[0:C, i, 0:C], True, True, [Bp.r, Ap.r], [pB2r])
                        P.copy("act", v3(Bn), pB2[0:C, 0:4 * C].rearrange("p (h c) -> p h c", h=4), [pB2r], [Bn.r])
                    pX, pXr = PSF()
                    for i in range(4):
                        P.mm(pX[0:C, i * C:(i + 1) * C], An.t[0:C, i, 0:C], Xp.t[0:C, i, 0:C], True, True, [An.r, Xp.r], [pXr])
                    P.tt("dve", v3(Xn), pX[0:C, 0:4 * C].rearrange("p (h c) -> p h c", h=4), v3(Xp), ALU.add, [pXr, Xp.r], [Xn.r])
                TT = K["X%d" % (nlev % 2)]
                pU, pUr = PSF()
                for i in range(4):
                    P.mm(pU[0:C, i * 128:(i + 1) * 128], TT.t[0:C, i, 0:C], K["bv"].t[0:C, i, :], True, True, [TT.r, K["bv"].r], [pUr])
                P.copy("act", K["u"].t[0:C, :, :], pU[0:C, :].rearrange("p (h d) -> p h d", h=4), [pUr], [K["u"].r])
                pW, pWr = PSF()
                for i in range(4):
                    P.mm(pW[:, i * C:(i + 1) * C], K["kbg"].t[0:C, i, :], TT.t[0:C, i, 0:C], True, True, [TT.r, K["kbg"].r], [pWr])
                P.copy("act", K["wT"].t[:, :, 0:C], pW[:, 0:4 * C].rearrange("p (h c) -> p h c", h=4), [pWr], [K["wT"].r])
                p1, p1r = PSF()
                for i in range(4):
                    h = hg * 4 + i
                    P.mm(p1[0:C, i * 128:(i + 1) * 128], K["wT"].t[:, i, 0:C], Sgb[h].t[:, :], True, True, [K["wT"].r, Sgb[h].r], [p1r])
                P.tt("dve", K["vnew"].t[0:C, :, :], K["u"].t[0:C, :, :], p1[0:C, :].rearrange("p (h d) -> p h d", h=4), ALU.subtract,
                     [K["u"].r, p1r], [K["vnew"].r])
                pO, pOr = PSF()
                for i in range(4):
                    h = hg * 4 + i
                    P.mm(pO[0:C, i * 128:(i + 1) * 128], K["qdT"].t[:, i, 0:C], Sgb[h].t[:, :], True, False, [K["qdT"].r, Sgb[h].r], [pOr])
                    P.mm(pO[0:C, i * 128:(i + 1) * 128], K["PT"].t[0:C, i, 0:C], K["vnew"].t[0:C, i, :], False, True, [K["PT"].r, K["vnew"].r], [pOr])
                pSS, pSSr = PSF()
                for i in range(4):
                    P.mm(pSS[:, i * 128:(i + 1) * 128], K["kd"].t[0:C, i, :], K["vnew"].t[0:C, i, :], True, True, [K["kd"].r, K["vnew"].r], [pSSr])
                for i in range(4):
                    h = hg * 4 + i
                    P.stt("dve", Sg[h].t[:, :], Sg[h].t[:, :], K["egl"].t[:, i:i + 1], pSS[:, i * 128:(i + 1) * 128], ALU.mult, ALU.add,
                          [Sg[h].r, K["egl"].r, pSSr], [Sg[h].r])
                    P.copy("act", Sgb[h].t[:, :], Sg[h].t[:, :], [Sg[h].r], [Sgb[h].r])
                P.copy("act", K["osb"].t[0:C, :, :], pO[0:C, :].rearrange("p (h d) -> p h d", h=4), [pOr], [K["osb"].r])
                P.tt("pool", K["osq"].t[0:C, :, :], K["osb"].t[0:C, :, :], K["osb"].t[0:C, :, :], ALU.mult, [K["osb"].r], [K["osq"].r])
                P.red("dve", K["ost"].t[0:C, 0:4], K["osq"].t[0:C, :, :], [K["osq"].r], [K["ost"].r])
                P.act(K["ost"].t[0:C, 4:8], K["ost"].t[0:C, 0:4], AF.Ln, [K["ost"].r], [K["ost"].r], scale=1.0 / 128, bias=eps_ap(C))
                P.act(K["ost"].t[0:C, 4:8], K["ost"].t[0:C, 4:8], AF.Exp, [K["ost"].r], [K["ost"].r], scale=-0.5)
                P.tt("dve", K["on"].t[0:C, :, :], K["osb"].t[0:C, :, :], bcd(K["ost"].t[0:C, 4:8]), ALU.mult, [K["osb"].r, K["ost"].r], [K["on"].r])
                pt, pr = PSB()
                for i in range(4):
                    P.tr(pt[:, i * C:(i + 1) * C], K["on"].t[0:C, i, :], identb.t[0:C, 0:C], [K["on"].r, identb.r], [pr])
                P.stt("dve", yaT.t[:, hs, c0:c0 + C], pt[:, 0:4 * C].rearrange("p (h c) -> p h c", h=4), pc(PC_GDNN), szT.t[:, :, c0:c0 + C],
                      ALU.mult, ALU.mult, [pr, prm.r, szT.r], [yaT.r])
                pSc, pScr = PSF()
                rows = po + C
                for i in range(4):
                    P.mm(pSc[0:rows, i * C:(i + 1) * C], rkT.t[:, i, tb * 128:tb * 128 + rows], rqT.t[:, i, c0:c0 + C], True, True, [rkT.r, rqT.r], [pScr])
                P.tt("dve", K["scT"].t[po:po + C, :, 0:C], pSc[po:po + C, 0:4 * C].rearrange("p (h c) -> p h c", h=4),
                     cst.t[po:po + C, CC_DRT + hg * 256:CC_DRT + hg * 256 + 256].rearrange("p (h c) -> p h c", h=4)[:, :, 0:C], ALU.mult,
                     [pScr, cst.r], [K["scT"].r])
                pOr2, pOr2r = PSF()
                for i in range(4):
                    h = hg * 4 + i
                    P.mm(pOr2[0:C, i * 128:(i + 1) * 128], rqdT.t[:, i, c0:c0 + C], Srb[h].t[:, :], True, False, [rqdT.r, Srb[h].r], [pOr2r])
                    P.mm(pOr2[0:C, i * 128:(i + 1) * 128], K["scT"].t[po:po + C, i, 0:C], rv_tm.t[po:po + C, tb, i * 128:(i + 1) * 128], False, True,
                         [K["scT"].r, rv_tm.r], [pOr2r])
                pSR, pSRr = PSF()
                for i in range(4):
                    P.mm(pSR[:, i * 128:(i + 1) * 128], rkd_tm.t[po:po + C, tb, i * 128:(i + 1) * 128], rv_tm.t[po:po + C, tb, i * 128:(i + 1) * 128], True, True,
                         [rkd_tm.r, rv_tm.r], [pSRr])
                for i in range(4):
                    h = hg * 4 + i
                    P.stt("dve", Sr[h].t[:, :], Sr[h].t[:, :], cst.t[:, cd_c + h:cd_c + h + 1], pSR[:, i * 128:(i + 1) * 128], ALU.mult, ALU.add,
                          [Sr[h].r, cst.r, pSRr], [Sr[h].r])
                    P.copy("act", Srb[h].t[:, :], Sr[h].t[:, :], [Sr[h].r], [Srb[h].r])
                P.copy("act", K["orb"].t[0:C, :, :], pOr2[0:C, :].rearrange("p (h d) -> p h d", h=4), [pOr2r], [K["orb"].r])
                P.red("dve", K["ort"].t[0:C, 0:4], K["orb"].t[0:C, :, :], [K["orb"].r], [K["ort"].r])
                P.ts("dve", K["ort"].t[0:C, 0:4], K["ort"].t[0:C, 0:4], 1.0 / 128, None, ALU.mult, None, [K["ort"].r], [K["ort"].r])
                P.tt("dve", K["orc"].t[0:C, :, :], K["orb"].t[0:C, :, :], bcd(K["ort"].t[0:C, 0:4]), ALU.subtract, [K["orb"].r, K["ort"].r], [K["orc"].r])
                P.tt("pool", K["orq"].t[0:C, :, :], K["orc"].t[0:C, :, :], K["orc"].t[0:C, :, :], ALU.mult, [K["orc"].r], [K["orq"].r])
                P.red("dve", K["ort"].t[0:C, 4:8], K["orq"].t[0:C, :, :], [K["orq"].r], [K["ort"].r])
                P.act(K["ort"].t[0:C, 4:8], K["ort"].t[0:C, 4:8], AF.Ln, [K["ort"].r], [K["ort"].r], scale=1.0 / 128, bias=eps_ap(C))
                P.act(K["ort"].t[0:C, 4:8], K["ort"].t[0:C, 4:8], AF.Exp, [K["ort"].r], [K["ort"].r], scale=-0.5)
                P.tt("dve", K["orn"].t[0:C, :, :], K["orc"].t[0:C, :, :], bcd(K["ort"].t[0:C, 4:8]), ALU.mult, [K["orc"].r, K["ort"].r], [K["orn"].r])
                pt, pr = PSB()
                for i in range(4):
                    P.tr(pt[:, i * C:(i + 1) * C], K["orn"].t[0:C, i, :], identb.t[0:C, 0:C], [K["orn"].r, identb.r], [pr])
                P.tt("dve", K["ybt"].t[:, :, 0:C], pt[:, 0:4 * C].rearrange("p (h c) -> p h c", h=4),
                     prm.t[:, PC_RETN + hg * 4:PC_RETN + hg * 4 + 4].unsqueeze(2).broadcast_to([128, 4, C]), ALU.mult, [pr, prm.r], [K["ybt"].r])
                P.tt("pool", ybT.t[:, hs, c0:c0 + C], K["ybt"].t[:, :, 0:C], srgT.t[:, :, c0:c0 + C], ALU.mult, [K["ybt"].r, srgT.r], [ybT.r])

        for p in range(2):
            slot = next_piece("ga")
            for i in range(4):
                pt, pr = fm_group(slot, i, T, nT, nT.r)
                P.act(sgT.t[:, i, 0:T], pt[:, 0:T], AF.Sigmoid, [pr], [sgT.r])
            slot = next_piece("gb")
            for i in range(4):
                pt, pr = fm_group(slot, i, T, nT, nT.r)
                P.act(sbT.t[:, i, 0:T], pt[:, 0:T], AF.Sigmoid, [pr], [sbT.r])
            slot = next_piece("brg")
            for i in range(4):
                pt, pr = fm_group(slot, i, T, yaT, yaT.r)
                P.tt("dve", mtmp[i].t[:, 0:T], pt[:, 0:T], sgT.t[:, i, 0:T], ALU.mult, [pr, sgT.r], [mtmp[i].r])
            slot = next_piece("brr")
            for i in range(4):
                pt, pr = fm_group(slot, i, T, ybT, ybT.r)
                m2 = mtmp2[i % 2]
                P.tt("dve", m2.t[:, 0:T], pt[:, 0:T], sbT.t[:, i, 0:T], ALU.mult, [pr, sbT.r], [m2.r])
                P.tt("pool", mergedT.t[:, 4 * p + i, 0:T], m2.t[:, 0:T], mtmp[i].t[:, 0:T], ALU.add, [m2.r, mtmp[i].r], [mergedT.r])
        for ch in range(2):
            slot = next_piece("wo")
            for tb, (t0, tn) in enumerate(tbs):
                pt, pr = tm_group(slot, mergedT, mergedT.r, t0, tn)
                hh = H[tb].t[0:tn, ch * 512:(ch + 1) * 512]
                P.tt("dve", hh, pt[0:tn, :], hh, ALU.add, [pr, H[tb].r], [H[tb].r])

    def final_out(t):
        tbs = [(tb * 128, 128) for tb in range(4)]
        fins = []
        for tb in range(4):
            P.act(junk.t[:, :], H[tb].t[:, :], AF.Square, [H[tb].r], [junk.r, ss.r], accum_out=ss.t[:, tb:tb + 1])
        P.act(rstd.t[:, 0:4], ss.t[:, 0:4], AF.Ln, [ss.r], [rstd.r], scale=1.0 / D, bias=eps_ap())
        P.act(rstd.t[:, 0:4], rstd.t[:, 0:4], AF.Exp, [rstd.r], [rstd.r], scale=-0.5)
        for tb in range(4):
            P.stt("dve", OUTB[tb].t[:, :], H[tb].t[:, :], rstd.t[:, tb:tb + 1], fnw.t[:, :], ALU.mult, ALU.mult,
                  [H[tb].r, rstd.r, fnw.r], [OUTB[tb].r])
            r0 = t * 512 + tb * 128
            fins.append(P.dma(out_d[r0:r0 + 128, :], OUTB[tb].t[:, :], [OUTB[tb].r], []))
        return fins

    fin_ops = []
    tbs_m = [(0, NMETA)]
    P.dma(H[0].t[0:NMETA, :], meta_d[:, :], [], [H[0].r])
    ffn(1, NMETA, tbs_m)
    mixer(NMETA, tbs_m, 16, 0)
    ffn(2, NMETA, tbs_m)
    tbs = [(tb * 128, 128) for tb in range(4)]
    for t in range(nt):
        for tb in range(4):
            r0 = t * 512 + tb * 128
            P.dma(H[tb].t[:, :], x_d[r0:r0 + 128, :], [], [H[tb].r])
        ffn(1, 512, tbs)
        mixer(512, tbs, 64, NMETA + t * 512)
        ffn(2, 512, tbs)
        fin_ops += final_out(t)
    P.emit(fin_ops)
    if debug:
        print("ops", len(P.ops), {e: sum(1 for o in P.ops if o.eng == e) for e in ENGS})
    return nc


WNAMES = ("ffn1_w_in", "ffn1_w_out", "w_in", "w_branch_gdn", "w_branch_ret", "w_out", "ffn2_w_in", "ffn2_w_out")


def make_in_maps(inputs, nb, nt):
    W = {k: np.asarray(inputs[k], np.float32)[0] for k in WNAMES}
    wst = host_pack_weights(W)
    prm = host_pack_params({k: np.asarray(inputs[k], np.float32)[0] for k in
                            ("ffn1_norm", "mix_norm", "ffn2_norm", "ret_out_norm", "gdn_out_norm", "gdn_conv_w",
                             "gdn_a_log", "gdn_dt_bias", "w_in")})
    fnw = np.ascontiguousarray(np.broadcast_to(np.asarray(inputs["final_norm"], np.float32)[None, :], (128, D)))
    cst = host_consts()
    rope = host_rope(NMETA + nt * 512)
    meta = np.ascontiguousarray(np.asarray(inputs["meta_tokens"], np.float32))
    x = np.asarray(inputs["x"], np.float32)
    return [{"x": np.ascontiguousarray(x[b, :nt * 512]), "meta": meta, "wst": wst, "prm": prm, "fnw": fnw, "cst": cst, "rope": rope}
            for b in range(nb)]


def kernel(**inputs):
    nt = SEQ // 512
    nc = build(nt)
    in_maps = make_in_maps(inputs, 8, nt)
    res = run_bass_kernel_spmd(nc, in_maps, core_ids=list(range(8)))
    return np.stack([np.asarray(r["out"], np.float32) for r in res.results], axis=0)
```

```python
import contextlib
import numpy as np
import ml_dtypes
import concourse.bass as bass
import concourse.mybir as mybir
from concourse.bass_utils import run_bass_kernel_spmd

F32 = mybir.dt.float32
BF16 = mybir.dt.bfloat16
AF = mybir.ActivationFunctionType
ALU = mybir.AluOpType
AX = mybir.AxisListType

D = 1024
NMETA = 16
SEQ = 8192
DFF = 2816
NJ = 22
DPROJ = 10256
EPS = 1e-6
NH = 8
PIECE = 4096
NSLOT = 4
ENGS = ("pe", "act", "dve", "pool", "sp")
N_DMA_SLOTS = 12
SAME_ENGINE_SYNC = True
SB_BASE = 18432
SB_LIMIT = 229376


class Res:
    __slots__ = ("name", "last_w", "readers", "overlaps", "lo", "hi", "excl")

    def __init__(self, name, lo=None, hi=None, excl=False):
        self.name = name
        self.excl = excl
        self.last_w = None
        self.readers = {}
        self.overlaps = []
        self.lo = lo
        self.hi = hi


class Op:
    __slots__ = ("idx", "eng", "fn", "deps", "dma", "needs_inc", "sem", "val", "prev_val")

    def __init__(self, idx, eng, fn, deps, dma):
        self.idx = idx
        self.eng = eng
        self.fn = fn
        self.deps = deps
        self.dma = dma
        self.needs_inc = False
        self.sem = None
        self.val = 0
        self.prev_val = 0


class Prog:
    def __init__(self, nc):
        self.nc = nc
        self.ops = []

    def op(self, eng, fn, reads=(), writes=(), dma=False):
        idx = len(self.ops)
        deps = set()
        wset = []
        for w in writes:
            wset.append(w)
            wset.extend(w.overlaps)
        rl = []
        for r in reads:
            if r.excl:
                wset.append(r)
            else:
                rl.append(r)
        reads = rl
        for r in reads:
            if r.last_w is not None:
                deps.add(r.last_w)
        for w in wset:
            if w.last_w is not None:
                deps.add(w.last_w)
            for ridx in w.readers.values():
                deps.add(ridx)
        o = Op(idx, eng, fn, sorted(deps), dma)
        self.ops.append(o)
        for r in reads:
            r.readers[("dma", idx) if dma else eng] = idx
        for w in wset:
            w.last_w = idx
            w.readers = {}
        return o

    def mm(self, out, lhsT, rhs, start, stop, reads, writes):
        return self.op("pe", lambda e: e.matmul(out, lhsT=lhsT, rhs=rhs, start=start, stop=stop), reads, writes)

    def tr(self, out, in_, ident, reads, writes):
        return self.op("pe", lambda e: e.transpose(out=out, in_=in_, identity=ident), reads, writes)

    def tt(self, eng, out, in0, in1, op, reads, writes):
        return self.op(eng, lambda e: e.tensor_tensor(out=out, in0=in0, in1=in1, op=op), reads, writes)

    def ts(self, eng, out, in0, s1, s2, op0, op1, reads, writes):
        if s2 is None:
            return self.op(eng, lambda e: e.tensor_scalar(out=out, in0=in0, scalar1=s1, scalar2=None, op0=op0), reads, writes)
        return self.op(eng, lambda e: e.tensor_scalar(out=out, in0=in0, scalar1=s1, scalar2=s2, op0=op0, op1=op1), reads, writes)

    def stt(self, eng, out, in0, scalar, in1, op0, op1, reads, writes):
        return self.op(eng, lambda e: e.scalar_tensor_tensor(out=out, in0=in0, scalar=scalar, in1=in1, op0=op0, op1=op1), reads, writes)

    def act(self, out, in_, func, reads, writes, bias=None, scale=None, accum_out=None):
        kw = {}
        if bias is not None:
            kw["bias"] = bias
        if scale is not None:
            kw["scale"] = scale
        if accum_out is not None:
            kw["accum_out"] = accum_out
        return self.op("act", lambda e: e.activation(out=out, in_=in_, func=func, **kw), reads, writes)

    def copy(self, eng, out, in_, reads, writes):
        if eng == "act":
            return self.act(out, in_, AF.Copy, reads, writes)
        return self.op(eng, lambda e: e.tensor_copy(out=out, in_=in_), reads, writes)

    def red(self, eng, out, in_, reads, writes):
        return self.op(eng, lambda e: e.tensor_reduce(out=out, in_=in_, axis=AX.X, op=ALU.add), reads, writes)

    def memset(self, eng, out, val, writes):
        return self.op(eng, lambda e: e.memset(out, val), (), writes)

    def dma(self, out, in_, reads, writes, queue="sp"):
        return self.op(queue, lambda e: e.dma_start(out=out, in_=in_), reads, writes, dma=True)

    def emit(self, final_ops):
        nc = self.nc
        ops = self.ops

        def skip_same(o, dop):
            return (not dop.dma) and (not o.dma) and dop.eng == o.eng and (o.eng == "pe" or not SAME_ENGINE_SYNC)

        for o in ops:
            for d in o.deps:
                dop = ops[d]
                if dop.dma or skip_same(o, dop):
                    continue
                dop.needs_inc = True
        with contextlib.ExitStack() as st:
            esem = {e: st.enter_context(nc.semaphore("prog_" + e)) for e in ENGS}
            dsem = {q: [st.enter_context(nc.semaphore("dma_%s_%d" % (q, i))) for i in range(N_DMA_SLOTS)]
                    for q in ("sp", "pool")}
            cnt = {e: 0 for e in ENGS}
            dcnt = {q: 0 for q in dsem}
            duse = {q: [0] * N_DMA_SLOTS for q in dsem}
            for o in ops:
                if o.dma:
                    q = o.eng
                    s = dcnt[q] % N_DMA_SLOTS
                    dcnt[q] += 1
                    o.prev_val = 16 * duse[q][s]
                    duse[q][s] += 1
                    o.sem = dsem[q][s]
                    o.val = 16 * duse[q][s]
                elif o.needs_inc:
                    cnt[o.eng] += 1
                    o.sem = esem[o.eng]
                    o.val = cnt[o.eng]
            per = {e: [o for o in ops if o.eng == e] for e in ENGS}
            block = st.enter_context(nc.Block())

            def run(ename, eng):
                waited = {}
                for o in per[ename]:
                    need = {}
                    for d in o.deps:
                        dop = ops[d]
                        if skip_same(o, dop):
                            continue
                        k = id(dop.sem)
                        if k not in need or need[k][1] < dop.val:
                            need[k] = (dop.sem, dop.val)
                    if o.dma and o.prev_val > 0:
                        k = id(o.sem)
                        if k not in need or need[k][1] < o.prev_val:
                            need[k] = (o.sem, o.prev_val)
                    for k, (sem, val) in need.items():
                        if waited.get(k, 0) >= val:
                            continue
                        eng.wait_ge(sem, val)
                        waited[k] = val
                    ins = o.fn(eng)
                    if o.dma:
                        ins.then_inc(o.sem, 16)
                    elif o.needs_inc:
                        ins.then_inc(o.sem, 1)
                if ename == "sp":
                    for fo in final_ops:
                        eng.wait_ge(fo.sem, fo.val)

            @block.sync
            def _(e):
                run("sp", e)

            @block.tensor
            def _(e):
                run("pe", e)

            @block.scalar
            def _(e):
                run("act", e)

            @block.vector
            def _(e):
                run("dve", e)

            @block.gpsimd
            def _(e):
                run("pool", e)


class Buf:
    __slots__ = ("t", "r")

    def __init__(self, t, r):
        self.t = t
        self.r = r


class SBAlloc:
    def __init__(self, nc):
        self.nc = nc
        self.off = SB_BASE
        self.peak = SB_BASE
        self.all = []

    def alloc(self, name, shape, dt):
        esz = 4 if dt == F32 else 2
        n = 1
        for s in shape[1:]:
            n *= s
        size = (n * esz + 31) // 32 * 32
        assert self.off + size <= SB_LIMIT, ("SBUF overflow", name, self.off, size)
        t = self.nc.alloc_sbuf_tensor_at(name, list(shape), dt, offset=self.off)
        r = Res(name, self.off, self.off + size)
        self.off += size
        self.peak = max(self.peak, self.off)
        self.all.append(r)
        return Buf(t, r)

    def finalize(self):
        rs = sorted(self.all, key=lambda r: r.lo)
        for i, a in enumerate(rs):
            for b in rs[i + 1:]:
                if b.lo >= a.hi:
                    break
                a.overlaps.append(b)
                b.overlaps.append(a)


def piece_specs():
    sp = []

    def ffn(i):
        win, wout, nrm = "ffn%d_w_in" % i, "ffn%d_w_out" % i, "ffn%d_norm" % i
        for g in range(NJ // 2):
            j0, j1 = 2 * g, 2 * g + 1
            sp.append(dict(kind="FM", w=win, cc=[j0 * 128, DFF + j0 * 128, j1 * 128, DFF + j1 * 128], norm=nrm, tag=("ffn_in", i, g)))
        for ch in range(2):
            for (j0, nj) in ((0, 8), (8, 8), (16, 6)):
                sp.append(dict(kind="TMK", w=wout, j0=j0, nj=nj, c0=ch * 512, norm=None, tag=("ffn_out", i, ch, j0, nj)))

    ffn(1)
    for hg in range(2):
        for nm, base in (("q", 0), ("k", 1024), ("v", 2048), ("z", 3072), ("rg", 7184)):
            sp.append(dict(kind="FM", w="w_in", cc=[base + (4 * hg + i) * 128 for i in range(4)], norm="mix_norm", tag=(nm, hg)))
        for nm, base in (("rq", 4112), ("rk", 5136), ("rv", 6160)):
            sp.append(dict(kind="TM", w="w_in", c0=base + hg * 512, norm="mix_norm", tag=(nm, hg)))
    for p in range(2):
        sp.append(dict(kind="FM", w="w_in", cc=[8208 + (4 * p + i) * 128 for i in range(4)], norm="mix_norm", tag=("ga", p)))
        sp.append(dict(kind="FM", w="w_in", cc=[9232 + (4 * p + i) * 128 for i in range(4)], norm="mix_norm", tag=("gb", p)))
        sp.append(dict(kind="FM", w="w_branch_gdn", cc=[(4 * p + i) * 128 for i in range(4)], norm=None, tag=("brg", p)))
        sp.append(dict(kind="FM", w="w_branch_ret", cc=[(4 * p + i) * 128 for i in range(4)], norm=None, tag=("brr", p)))
    for ch in range(2):
        sp.append(dict(kind="TM", w="w_out", c0=ch * 512, norm=None, tag=("wo", ch)))
    ffn(2)
    return sp


PIECES = piece_specs()
NP = len(PIECES)
NORM_COL = {"ffn1_norm": 0, "mix_norm": 8, "ffn2_norm": 16}
PC_RETN = 24
PC_GDNN = 32
PC_CONV = 33
PC_ALOG = 129
PC_DTB = 137
PC_WBA = 145
NPRM = PC_WBA + 128
CC_U = 0
CC_MUI = 128
CC_MLS = 256
CC_DRT = 384
CC_XI = 1408
CC_ZS128 = 2432
CC_ZS16 = 2440
CC_CD128 = 2448
CC_CD16 = 2456
CC_ONES = 2464
CC_MLO = 2592
NCST = CC_MLO + 128


def host_consts():
    c = np.zeros((128, NCST), np.float32)
    j = np.arange(128)
    c[:, CC_U:CC_U + 128] = (j[:, None] <= j[None, :])
    c[:, CC_MUI:CC_MUI + 128] = (j[:, None] <= j[None, :])
    c[:, CC_MLS:CC_MLS + 128] = (j[None, :] < j[:, None]) & ((j[None, :] // 64) == (j[:, None] // 64))
    c[:, CC_MLO:CC_MLO + 128] = (j[None, :] < 64) & (j[:, None] >= 64)
    lg = np.log1p(-np.exp2(-5.0 - np.arange(NH, dtype=np.float64)))
    m = j[:, None, None]
    cc = j[None, None, :]
    dr = np.where(m <= cc, np.exp(np.maximum(cc - m, 0) * lg[None, :, None]), 0.0)
    c[:, CC_DRT:CC_DRT + 1024] = dr.reshape(128, 1024)
    xi = np.exp((j[None, None, :] + 1.0) * lg[None, :, None]) * np.ones((128, 1, 1))
    c[:, CC_XI:CC_XI + 1024] = xi.reshape(128, 1024)
    sc = 128.0 ** -0.5
    c[:, CC_ZS128:CC_ZS128 + 8] = np.exp((127.0 - j)[:, None] * lg[None, :]) * sc
    c[:16, CC_ZS16:CC_ZS16 + 8] = np.exp((15.0 - j[:16])[:, None] * lg[None, :]) * sc
    c[:, CC_CD128:CC_CD128 + 8] = np.exp(128.0 * lg)[None, :]
    c[:, CC_CD16:CC_CD16 + 8] = np.exp(16.0 * lg)[None, :]
    c[:, CC_ONES:CC_ONES + 128] = 1.0
    return c


def host_rope(L):
    inv = (1.0 / (10000.0 ** np.linspace(0.0, 1.0, 64, dtype=np.float32))).astype(np.float32)
    pos = np.arange(L, dtype=np.float32)
    ang = (pos[:, None] * inv[None, :]).astype(np.float32)
    r = np.zeros((L, 128), np.float32)
    r[:, :64] = np.cos(ang.astype(np.float64))
    r[:, 64:] = np.sin(ang.astype(np.float64))
    return r


def host_pack_weights(W):
    out = np.zeros((NP, 128, PIECE), np.float32)
    for s, sp in enumerate(PIECES):
        w = W[sp["w"]]
        if sp["kind"] == "FM":
            wk = w.reshape(8, 128, -1)
            for i, c0 in enumerate(sp["cc"]):
                blk = wk[:, :, c0:c0 + 128]
                out[s].reshape(128, 8, 4, 128)[:, :, i, :] = blk.transpose(1, 0, 2)
        elif sp["kind"] == "TM":
            wk = w.reshape(8, 128, -1)[:, :, sp["c0"]:sp["c0"] + 512]
            out[s].reshape(128, 8, 512)[:, :, :] = wk.transpose(1, 0, 2)
        else:
            wk = w.reshape(NJ, 128, -1)[sp["j0"]:sp["j0"] + sp["nj"], :, sp["c0"]:sp["c0"] + 512]
            out[s].reshape(128, 8, 512)[:, :sp["nj"], :] = wk.transpose(1, 0, 2)
    return out


def host_pack_params(I):
    prm = np.zeros((128, NPRM), np.float32)
    for nm, c0 in NORM_COL.items():
        prm[:, c0:c0 + 8] = np.asarray(I[nm]).reshape(8, 128).T
    prm[:, PC_RETN:PC_RETN + 8] = np.asarray(I["ret_out_norm"]).reshape(8, 128).T
    prm[:, PC_GDNN] = np.asarray(I["gdn_out_norm"]).reshape(128)
    cw = np.asarray(I["gdn_conv_w"]).reshape(4, 24, 128)
    prm[:, PC_CONV:PC_CONV + 96] = cw.transpose(2, 1, 0).reshape(128, 96)
    prm[:, PC_ALOG:PC_ALOG + 8] = np.asarray(I["gdn_a_log"]).reshape(1, 8)
    prm[:, PC_DTB:PC_DTB + 8] = np.asarray(I["gdn_dt_bias"]).reshape(1, 8)
    wba = np.asarray(I["w_in"]).reshape(8, 128, DPROJ)[:, :, 4096:4112]
    prm[:, PC_WBA:PC_WBA + 128] = wba.transpose(1, 0, 2).reshape(128, 128)
    return prm


def build(nt, debug=False):
    seq = nt * 512
    L = NMETA + seq
    nc = bass.Bass("TRN2", target_bir_lowering=False)
    x_d = nc.dram_tensor("x", [seq, D], F32, kind="ExternalInput").ap()
    meta_d = nc.dram_tensor("meta", [NMETA, D], F32, kind="ExternalInput").ap()
    wst_d = nc.dram_tensor("wst", [NP, 128, PIECE], F32, kind="ExternalInput").ap()
    prm_d = nc.dram_tensor("prm", [128, NPRM], F32, kind="ExternalInput").ap()
    fnw_d = nc.dram_tensor("fnw", [128, D], F32, kind="ExternalInput").ap()
    cst_d = nc.dram_tensor("cst", [128, NCST], F32, kind="ExternalInput").ap()
    rope_d = nc.dram_tensor("rope", [L, 128], F32, kind="ExternalInput").ap()
    out_d = nc.dram_tensor("out", [seq, D], F32, kind="ExternalOutput").ap()
    wsc_d = nc.dram_tensor("wsc", [NP, 128, PIECE], BF16).ap()
    wsc_r = [Res("wsc%d" % s) for s in range(NP)]

    P = Prog(nc)
    sb = SBAlloc(nc)
    A = sb.alloc

    prm = A("prm", [128, NPRM], F32)
    cst = A("cst", [128, NCST], F32)
    fnw = A("fnw", [128, D], F32)
    identb = A("identb", [128, 128], BF16)
    wba = A("wba", [128, 8, 16], BF16)
    nA = A("nA", [128, 8], F32)
    H = [A("H%d" % tb, [128, D], F32) for tb in range(4)]
    nb = [A("nb%d" % i, [128, D], BF16) for i in range(2)]
    nT = A("nT", [128, 8, 512], BF16)
    ring = [A("ring%d" % i, [128, PIECE], BF16) for i in range(NSLOT)]
    Sg = [A("Sg%d" % h, [128, 128], F32) for h in range(NH)]
    Sr = [A("Sr%d" % h, [128, 128], F32) for h in range(NH)]
    Sgb = [A("Sgb%d" % h, [128, 128], BF16) for h in range(NH)]
    Srb = [A("Srb%d" % h, [128, 128], BF16) for h in range(NH)]
    ctail = A("ctail", [128, 24, 3], F32)
    yaT = A("yaT", [128, 8, 512], BF16)
    ybT = A("ybT", [128, 8, 512], BF16)
    ss = A("ss", [128, 4], F32)
    rstd = A("rstd", [128, 4], F32)
    junk = A("junk", [128, D], BF16)
    rope = A("rope", [128, 4, 128], F32)
    gtok = A("gtok", [128, 4, 8], F32)
    btok = A("btok", [128, 4, 8], F32)
    braw = A("braw", [128, 4, 16], F32)
    batmp = A("batmp", [128, 4, 8], F32)
    cbias = A("cbias", [128, 2], F32)

    def eps_ap(rows=128):
        return cbias.t[0:rows, 0:1]

    def one_ap(rows=128):
        return cbias.t[0:rows, 1:2]

    arena0 = sb.off

    stage = [A("stage%d" % i, [128, PIECE], F32) for i in range(2)]
    sb.off = arena0
    actT = A("actT", [128, NJ, 512], BF16)
    sil = [A("sil%d" % i, [128, 512], F32) for i in range(2)]
    OUTB = [A("OUTB%d" % tb, [128, D], F32) for tb in range(4)]
    sb.off = arena0
    qT = A("qT", [128, 4, 512], BF16)
    kT = A("kT", [128, 4, 512], BF16)
    vT = A("vT", [128, 4, 512], BF16)
    szT = A("szT", [128, 4, 512], BF16)
    srgT = A("srgT", [128, 4, 512], BF16)
    rkd_tm = A("rkd_tm", [128, 4, 512], BF16)
    rv_tm = A("rv_tm", [128, 4, 512], BF16)
    rqT = A("rqT", [128, 4, 512], BF16)
    rkT = A("rkT", [128, 4, 512], BF16)
    rqdT = A("rqdT", [128, 4, 512], BF16)
    sub0 = sb.off
    rq_tm = A("rq_tm", [128, 4, 512], BF16)
    rk_tm = A("rk_tm", [128, 4, 512], BF16)
    cin = [A("cin%d" % i, [128, 515], F32) for i in range(2)]
    cacc = [A("cacc%d" % i, [128, 512], F32) for i in range(2)]
    cf = [A("cf%d" % i, [128, 512], F32) for i in range(2)]
    csq = [A("csq%d" % i, [128, 512], F32) for i in range(2)]
    crs = [A("crs%d" % i, [128, 512], F32) for i in range(2)]
    rxs = [A("rxs%d" % i, [128, 512], F32) for i in range(2)]
    rt = [A("rt%d" % i, [128, 256], F32) for i in range(4)]
    sb.off = sub0
    cksub = []
    for sub in range(2):
        base = {}
        for nm, shp, dt in (
            ("GU", [128, 2, 128], F32), ("gcs", [128, 4], F32), ("eg", [128, 2], F32), ("beg", [128, 2], F32),
            ("ekd", [128, 2], F32), ("Dm", [128, 2, 128], F32), ("El", [128, 2, 128], F32), ("EQ", [128, 2, 128], F32),
            ("EGQ", [128, 2, 128], F32), ("kbg", [128, 2, 128], BF16), ("bv", [128, 2, 128], BF16),
            ("A0", [128, 2, 128], BF16), ("A1", [128, 2, 128], BF16), ("B0", [128, 2, 128], BF16), ("B1", [128, 2, 128], BF16),
            ("X0", [128, 2, 128], BF16), ("X1", [128, 2, 128], BF16), ("Ao", [128, 2, 128], BF16), ("Y", [128, 2, 128], BF16),
            ("sz", [128, 4, 128], BF16),
            ("vnew", [128, 2, 128], BF16), ("osb", [128, 2, 128], F32), ("osq", [128, 2, 128], F32), ("ost", [128, 4], F32),
            ("on", [128, 2, 128], BF16),
            ("scT", [128, 2, 128], BF16), ("orb", [128, 2, 128], F32), ("orq", [128, 2, 128], F32), ("ort", [128, 4], F32),
            ("orn", [128, 2, 128], BF16), ("ybt", [128, 2, 128], F32),
        ):
            base[nm] = A("%s_s%d" % (nm, sub), shp, dt)
        base["Eu"] = base["Dm"]
        base["EA"] = base["El"]
        base["EAo"] = base["GU"]
        base["orc"] = base["orb"]
        pars = []
        for par in range(2):
            d = dict(base)
            for nm, shp, dt in (("wT", [128, 2, 128], BF16), ("u", [128, 2, 128], F32), ("kd", [128, 2, 128], BF16),
                                ("PT", [128, 2, 128], BF16), ("qdT", [128, 2, 128], BF16), ("egl", [128, 2], F32)):
                d[nm] = A("%s_s%d_%d" % (nm, sub, par), shp, dt)
            pars.append(d)
        cksub.append(pars)
    hg_end = sb.off
    sb.off = arena0
    sgT = A("sgT", [128, 4, 512], BF16)
    sbT = A("sbT", [128, 4, 512], BF16)
    mtmp = [A("mtmp%d" % i, [128, 512], F32) for i in range(4)]
    mtmp2 = [A("mtmp2_%d" % i, [128, 512], F32) for i in range(2)]
    mergedT = A("mergedT", [128, 8, 512], BF16)
    sb.finalize()
    if debug:
        print("SBUF peak", sb.peak, "of", SB_LIMIT, "hg_end", hg_end, "arena0", arena0)

    NF, NB = 7, 1
    psf = [nc.alloc_psum_tensor("psf%d" % i, [128, 512], F32) for i in range(NF)]
    psb = [nc.alloc_psum_tensor("psb%d" % i, [128, 1024], BF16) for i in range(NB)]
    psf_r = [Res("psf%d" % i, excl=True) for i in range(NF)]
    psb_r = [Res("psb%d" % i, excl=True) for i in range(NB)]
    from collections import deque
    free_f = deque(range(NF))
    free_b = deque(range(NB))

    def PSF(hold=False):
        i = free_f.popleft()
        if not hold:
            free_f.append(i)
        return psf[i], psf_r[i]

    def PSB(hold=False):
        i = free_b.popleft()
        if not hold:
            free_b.append(i)
        return psb[i], psb_r[i]

    def gPSF(k=1):
        while len(free_f) < k:
            yield
        got = [PSF(hold=True) for _ in range(k)]
        return got[0] if k == 1 else got

    def gPSB():
        while not free_b:
            yield
        return PSB(hold=True)

    def PFREE(r):
        if r in psf_r:
            free_f.append(psf_r.index(r))
        else:
            free_b.append(psb_r.index(r))

    def pc(c0, n=1):
        return prm.t[:, c0:c0 + n]

    def cc(c0, n, rows=128):
        return cst.t[0:rows, c0:c0 + n]

    rr = ["act", "dve"]
    rrc = {"i": 0}

    def nxt(choices=("act", "dve")):
        rrc["i"] += 1
        return choices[rrc["i"] % len(choices)]

    P.dma(prm.t[:], prm_d[:, :], [], [prm.r])
    P.dma(cst.t[:], cst_d[:, :], [], [cst.r])
    P.dma(fnw.t[:], fnw_d[:, :], [], [fnw.r])
    P.memset("pool", crs[0].t[:, 0:128], 0.0, [crs[0].r])
    P.op("pool", lambda e: e.affine_select(out=crs[0].t[:, 0:128], in_=crs[0].t[:, 0:128], pattern=[[-1, 128]],
                                           compare_op=ALU.not_equal, fill=1.0, base=0, channel_multiplier=1),
         [crs[0].r], [crs[0].r])
    P.copy("dve", identb.t[:], crs[0].t[:, 0:128], [crs[0].r], [identb.r])
    for kc in range(8):
        P.ts("dve", wba.t[:, kc, :], prm.t[:, PC_WBA + kc * 16:PC_WBA + kc * 16 + 16], pc(NORM_COL["mix_norm"] + kc), None,
             ALU.mult, None, [prm.r], [wba.r])
    P.act(nA.t[:], pc(PC_ALOG, 8), AF.Exp, [prm.r], [nA.r])
    P.ts("dve", nA.t[:], nA.t[:], -1.0, None, ALU.mult, None, [nA.r], [nA.r])
    for h in range(NH):
        P.memset("pool", Sg[h].t[:], 0.0, [Sg[h].r])
        P.memset("pool", Sr[h].t[:], 0.0, [Sr[h].r])
        P.memset("dve", Sgb[h].t[:], 0.0, [Sgb[h].r])
        P.memset("dve", Srb[h].t[:], 0.0, [Srb[h].r])
    P.memset("pool", ctail.t[:], 0.0, [ctail.r])
    P.memset("pool", cbias.t[:, 0:1], EPS, [cbias.r])
    P.memset("pool", cbias.t[:, 1:2], 1.0, [cbias.r])

    for s, spc in enumerate(PIECES):
        stg = stage[s % 2]
        slot = ring[s % NSLOT]
        P.dma(stg.t[:], wst_d[s], [], [stg.r])
        if spc["norm"] is not None:
            c0 = NORM_COL[spc["norm"]]
            for kc in range(8):
                eng = "act" if kc % 2 == 0 else "dve"
                o_ = slot.t[:, kc * 512:(kc + 1) * 512]
                i_ = stg.t[:, kc * 512:(kc + 1) * 512]
                if eng == "act":
                    P.act(o_, i_, AF.Identity, [stg.r, prm.r], [slot.r], scale=pc(c0 + kc))
                else:
                    P.ts("dve", o_, i_, pc(c0 + kc), None, ALU.mult, None, [stg.r, prm.r], [slot.r])
        else:
            P.copy("act", slot.t[:, 0:1536], stg.t[:, 0:1536], [stg.r], [slot.r])
            P.copy("dve", slot.t[:, 1536:3072], stg.t[:, 1536:3072], [stg.r], [slot.r])
            P.copy("pool", slot.t[:, 3072:4096], stg.t[:, 3072:4096], [stg.r], [slot.r])
        P.dma(wsc_d[s], slot.t[:], [slot.r], [wsc_r[s]])

    wstate = {"issued": 0, "cur": 0}
    total_pieces = NP * (nt + 1)

    def w_issue_upto(n):
        while wstate["issued"] < min(n, total_pieces):
            g = wstate["issued"]
            s = g % NP
            slot = ring[g % NSLOT]
            P.dma(slot.t[:], wsc_d[s], [wsc_r[s]], [slot.r])
            wstate["issued"] += 1

    def next_piece(tag_prefix):
        g = wstate["cur"]
        s = g % NP
        assert PIECES[s]["tag"][0] == tag_prefix, (PIECES[s]["tag"], tag_prefix)
        w_issue_upto(g + NSLOT)
        wstate["cur"] += 1
        return ring[g % NSLOT]

    def run_interleaved(gens):
        gens = list(gens)
        while gens:
            for g in list(gens):
                try:
                    next(g)
                except StopIteration:
                    gens.remove(g)

    def norm_to_nT(T, tbs):
        ntb = len(tbs)
        for tb, (t0, tn) in enumerate(tbs):
            P.act(junk.t[0:tn, :], H[tb].t[0:tn, :], AF.Square, [H[tb].r], [junk.r, ss.r], accum_out=ss.t[0:tn, tb:tb + 1])
        tn0 = tbs[0][1]
        P.act(rstd.t[0:tn0, 0:ntb], ss.t[0:tn0, 0:ntb], AF.Ln, [ss.r], [rstd.r], scale=1.0 / D, bias=eps_ap(tn0))
        P.act(rstd.t[0:tn0, 0:ntb], rstd.t[0:tn0, 0:ntb], AF.Exp, [rstd.r], [rstd.r], scale=-0.5)
        for tb, (t0, tn) in enumerate(tbs):
            nbb = nb[tb % 2]
            P.ts("dve", nbb.t[0:tn, :], H[tb].t[0:tn, :], rstd.t[0:tn, tb:tb + 1], None, ALU.mult, None, [H[tb].r, rstd.r], [nbb.r])
            pt, pr = PSB()
            for fc in range(8):
                P.tr(pt[:, fc * 128:fc * 128 + tn], nbb.t[0:tn, fc * 128:(fc + 1) * 128], identb.t[0:tn, 0:tn], [nbb.r, identb.r], [pr])
            P.copy(nxt(), nT.t[:, :, t0:t0 + tn], pt[:, :].rearrange("p (f t) -> p f t", f=8)[:, :, 0:tn], [pr], [nT.r])

    def fm_group(slot, i, T, rhsbuf, rhs_r):
        pt, pr = PSF()
        for kc in range(8):
            P.mm(pt[:, 0:T], slot.t[:, kc * 512 + i * 128:kc * 512 + (i + 1) * 128], rhsbuf.t[:, kc, 0:T], kc == 0, kc == 7,
                 [slot.r, rhs_r], [pr])
        return pt, pr

    def tm_group(slot, lbuf, l_r, t0, tn):
        pt, pr = PSF()
        for kc in range(8):
            P.mm(pt[0:tn, :], lbuf.t[:, kc, t0:t0 + tn], slot.t[:, kc * 512:(kc + 1) * 512], kc == 0, kc == 7, [slot.r, l_r], [pr])
        return pt, pr

    def ffn(i, T, tbs):
        norm_to_nT(T, tbs)
        for g in range(NJ // 2):
            slot = next_piece("ffn_in")
            for jj in range(2):
                j = 2 * g + jj
                pg, pgr = fm_group(slot, 2 * jj, T, nT, nT.r)
                pu, pur = fm_group(slot, 2 * jj + 1, T, nT, nT.r)
                sl = sil[j % 2]
                P.act(sl.t[:, 0:T], pg[:, 0:T], AF.Silu, [pgr], [sl.r])
                P.tt("dve", actT.t[:, j, 0:T], sl.t[:, 0:T], pu[:, 0:T], ALU.mult, [sl.r, pur], [actT.r])
        for ch in range(2):
            pts = [PSF() for _ in tbs]
            for (j0, nj) in ((0, 8), (8, 8), (16, 6)):
                slot = next_piece("ffn_out")
                for jj in range(nj):
                    j = j0 + jj
                    for tb, (t0, tn) in enumerate(tbs):
                        P.mm(pts[tb][0][0:tn, :], actT.t[:, j, t0:t0 + tn], slot.t[:, jj * 512:(jj + 1) * 512], j == 0, j == NJ - 1,
                             [slot.r, actT.r], [pts[tb][1]])
            for tb, (t0, tn) in enumerate(tbs):
                hh = H[tb].t[0:tn, ch * 512:(ch + 1) * 512]
                P.stt("dve", hh, pts[tb][0][0:tn, :], 0.5, hh, ALU.mult, ALU.add, [pts[tb][1], H[tb].r], [H[tb].r])

    def mixer(T, tbs, C, tok0):
        NCH = T // C
        nlev = {128: 5, 16: 3}[C]
        zs_c = CC_ZS128 if C == 128 else CC_ZS16
        cd_c = CC_CD128 if C == 128 else CC_CD16
        norm_to_nT(T, tbs)
        for tb, (t0, tn) in enumerate(tbs):
            P.dma(rope.t[0:tn, tb, :], rope_d[tok0 + t0:tok0 + t0 + tn, :], [], [rope.r])
        pt, pr = PSF()
        for n in range(NCH):
            for kc in range(8):
                P.mm(pt[0:C, n * 16:(n + 1) * 16], nT.t[:, kc, n * C:(n + 1) * C], wba.t[:, kc, :], kc == 0, kc == 7, [nT.r, wba.r], [pr])
        P.copy("act", braw.t[0:C, 0:NCH, :], pt[0:C, 0:NCH * 16].rearrange("p (n c) -> p n c", c=16), [pr], [braw.r])
        P.act(btok.t[0:C, 0:NCH, :], braw.t[0:C, 0:NCH, 0:8], AF.Exp, [braw.r], [btok.r], scale=-1.0)
        P.ts("dve", btok.t[0:C, 0:NCH, :], btok.t[0:C, 0:NCH, :], 1.0, None, ALU.add, None, [btok.r], [btok.r])
        P.op("dve", lambda e: e.reciprocal(out=btok.t[0:C, 0:NCH, :], in_=btok.t[0:C, 0:NCH, :]), [btok.r], [btok.r])
        P.tt("dve", batmp.t[0:C, 0:NCH, :], braw.t[0:C, 0:NCH, 8:16], prm.t[0:C, PC_DTB:PC_DTB + 8].unsqueeze(1).broadcast_to([C, NCH, 8]),
             ALU.add, [braw.r, prm.r], [batmp.r])
        P.act(batmp.t[0:C, 0:NCH, :], batmp.t[0:C, 0:NCH, :], AF.Exp, [batmp.r], [batmp.r])
        P.act(batmp.t[0:C, 0:NCH, :], batmp.t[0:C, 0:NCH, :], AF.Ln, [batmp.r], [batmp.r], bias=one_ap(C))
        P.tt("dve", gtok.t[0:C, 0:NCH, :], batmp.t[0:C, 0:NCH, :], nA.t[0:C, :].unsqueeze(1).broadcast_to([C, NCH, 8]), ALU.mult,
             [batmp.r, nA.r], [gtok.r])

        for hg in range(2):
            for qi, (nm, dst) in enumerate((("q", qT), ("k", kT), ("v", vT))):
                slot = next_piece(nm)
                for i in range(4):
                    cch = qi * 8 + hg * 4 + i
                    par = i % 2
                    pt, pr = fm_group(slot, i, T, nT, nT.r)
                    ci = cin[par]
                    P.copy("act", ci.t[:, 3:3 + T], pt[:, 0:T], [pr], [ci.r])
                    P.copy("pool", ci.t[:, 0:3], ctail.t[:, cch, :], [ctail.r], [ci.r])
                    P.copy("pool", ctail.t[:, cch, :], ci.t[:, T:T + 3], [ci.r], [ctail.r])
                    ca = cacc[par]
                    wcol = PC_CONV + cch * 4
                    P.act(ca.t[:, 0:T], ci.t[:, 0:T], AF.Identity, [ci.r, prm.r], [ca.r], scale=pc(wcol))
                    for tap in range(1, 4):
                        P.stt("dve", ca.t[:, 0:T], ci.t[:, tap:tap + T], pc(wcol + tap), ca.t[:, 0:T], ALU.mult, ALU.add,
                              [ci.r, prm.r, ca.r], [ca.r])
                    if nm == "v":
                        P.act(dst.t[:, i, 0:T], ca.t[:, 0:T], AF.Silu, [ca.r], [dst.r])
                        continue
                    f = cf[par]
                    P.act(f.t[:, 0:T], ca.t[:, 0:T], AF.Silu, [ca.r], [f.r])
                    sq = csq[par]
                    P.tt("pool", sq.t[:, 0:T], f.t[:, 0:T], f.t[:, 0:T], ALU.mult, [f.r], [sq.r])
                    p2, p2r = PSF()
                    P.mm(p2[:, 0:T], cst.t[:, CC_ONES:CC_ONES + 128], sq.t[:, 0:T], True, True, [cst.r, sq.r], [p2r])
                    rs_ = crs[par]
                    P.act(rs_.t[:, 0:T], p2[:, 0:T], AF.Ln, [p2r], [rs_.r], bias=eps_ap())
                    P.act(rs_.t[:, 0:T], rs_.t[:, 0:T], AF.Exp, [rs_.r], [rs_.r], scale=-0.5)
                    if nm == "q":
                        P.stt("dve", dst.t[:, i, 0:T], f.t[:, 0:T], 128.0 ** -0.5, rs_.t[:, 0:T], ALU.mult, ALU.mult, [f.r, rs_.r], [dst.r])
                    else:
                        P.tt("dve", dst.t[:, i, 0:T], f.t[:, 0:T], rs_.t[:, 0:T], ALU.mult, [f.r, rs_.r], [dst.r])
            for nm, dst in (("z", szT), ("rg", srgT)):
                slot = next_piece(nm)
                for i in range(4):
                    pt, pr = fm_group(slot, i, T, nT, nT.r)
                    P.act(dst.t[:, i, 0:T], pt[:, 0:T], AF.Silu, [pr], [dst.r])
            for nm, dst in (("rq", rq_tm), ("rk", rk_tm)):
                slot = next_piece(nm)
                for tb, (t0, tn) in enumerate(tbs):
                    pt, pr = tm_group(slot, nT, nT.r, t0, tn)
                    xs = rxs[tb % 2]
                    P.copy("act", xs.t[0:tn, :], pt[0:tn, :], [pr], [xs.r])
                    xv = xs.t[0:tn, :].rearrange("p (h j two) -> p h j two", h=4, two=2)
                    x0 = xv[:, :, :, 0]
                    x1 = xv[:, :, :, 1]
                    cosb = rope.t[0:tn, tb, 0:64].unsqueeze(1).broadcast_to([tn, 4, 64])
                    sinb = rope.t[0:tn, tb, 64:128].unsqueeze(1).broadcast_to([tn, 4, 64])
                    t1, t2, t3, t4 = (rt[k].t[0:tn, :].rearrange("p (h j) -> p h j", h=4) for k in range(4))
                    ov = dst.t[0:tn, tb, :].rearrange("p (h j two) -> p h j two", h=4, two=2)
                    P.tt("dve", t1, x0, cosb, ALU.mult, [xs.r, rope.r], [rt[0].r])
                    P.tt("pool", t2, x1, sinb, ALU.mult, [xs.r, rope.r], [rt[1].r])
                    P.tt("dve", t3, x1, cosb, ALU.mult, [xs.r, rope.r], [rt[2].r])
                    P.tt("pool", t4, x0, sinb, ALU.mult, [xs.r, rope.r], [rt[3].r])
                    P.tt("dve", ov[:, :, :, 0], t1, t2, ALU.subtract, [rt[0].r, rt[1].r], [dst.r])
                    P.tt("pool", ov[:, :, :, 1], t3, t4, ALU.add, [rt[2].r, rt[3].r], [dst.r])
            slot = next_piece("rv")
            for tb, (t0, tn) in enumerate(tbs):
                pt, pr = tm_group(slot, nT, nT.r, t0, tn)
                P.copy("act", rv_tm.t[0:tn, tb, :], pt[0:tn, :], [pr], [rv_tm.r])
            for tb, (t0, tn) in enumerate(tbs):
                P.tt("dve", rkd_tm.t[0:tn, tb, :].rearrange("p (h d) -> p h d", h=4),
                     rk_tm.t[0:tn, tb, :].rearrange("p (h d) -> p h d", h=4),
                     cst.t[0:tn, zs_c + hg * 4:zs_c + hg * 4 + 4].unsqueeze(2).broadcast_to([tn, 4, 128]), ALU.mult,
                     [rk_tm.r, cst.r], [rkd_tm.r])
            for src, dst, scl in ((rq_tm, rqT, None), (rk_tm, rkT, 128.0 ** -0.5)):
                for tb, (t0, tn) in enumerate(tbs):
                    pt, pr = PSB()
                    for i in range(4):
                        P.tr(pt[:, i * 128:i * 128 + tn], src.t[0:tn, tb, i * 128:(i + 1) * 128], identb.t[0:tn, 0:tn], [src.r, identb.r], [pr])
                    pv = pt[:, 0:512].rearrange("p (h t) -> p h t", h=4)[:, :, 0:tn]
                    if scl is None:
                        P.copy("dve", dst.t[:, :, t0:t0 + tn], pv, [pr], [dst.r])
                    else:
                        P.act(dst.t[:, :, t0:t0 + tn], pv, AF.Identity, [pr], [dst.r], scale=scl)
            for i in range(4):
                h = hg * 4 + i
                P.tt("pool", rqdT.t[:, i, 0:T].rearrange("p (n c) -> p n c", c=C), rqT.t[:, i, 0:T].rearrange("p (n c) -> p n c", c=C),
                     cst.t[:, CC_XI + h * 128:CC_XI + h * 128 + C].unsqueeze(1).broadcast_to([128, NCH, C]), ALU.mult, [rqT.r, cst.r], [rqdT.r])

            hs = slice(hg * 4, hg * 4 + 4)

            def h3(ap, w):
                return ap.rearrange("p (h c) -> p h c", h=4)

            def gen_prep(n, sub, hg=hg):
                K = cksub[sub][n % 2]
                c0 = n * C
                li = (2 * sub, 2 * sub + 1)
                hcs = slice(hg * 4 + 2 * sub, hg * 4 + 2 * sub + 2)

                def bc2(ap2):
                    return ap2.unsqueeze(2).broadcast_to([C, 2, C])

                def bcd(ap2):
                    return ap2.unsqueeze(2).broadcast_to([C, 2, 128])

                def v3(b):
                    return b.t[0:C, :, 0:C]

                def g3(ap):
                    return ap.rearrange("p (h c) -> p h c", h=2)

                def tab(c0_):
                    return cst.t[0:C, c0_:c0_ + C].unsqueeze(1).broadcast_to([C, 2, C])

                Ub, MUI, MLS, MLO = tab(CC_U), tab(CC_MUI), tab(CC_MLS), tab(CC_MLO)
                Idb = identb.t[0:C, 0:C].unsqueeze(1).broadcast_to([C, 2, C])
                two = (C == 128)
                P.tt("pool", v3(K["GU"]), Ub, bc2(gtok.t[0:C, n, hcs]), ALU.mult, [cst.r, gtok.r], [K["GU"].r])
                (pG, pGr), (pS, pSr) = yield from gPSF(2)
                for ii in range(2):
                    P.mm(pG[:, ii * C:(ii + 1) * C], cst.t[0:C, CC_ONES:CC_ONES + 128], K["GU"].t[0:C, ii, 0:C], True, True, [cst.r, K["GU"].r], [pGr])
                P.mm(pS[0:C, 0:2], cst.t[0:C, CC_U:CC_U + C], gtok.t[0:C, n, hcs], True, True, [cst.r, gtok.r], [pSr])
                P.mm(pS[:, 2:4], cst.t[0:C, CC_ONES:CC_ONES + 128], gtok.t[0:C, n, hcs], True, True, [cst.r, gtok.r], [pSr])
                yield
                gcs = K["gcs"]
                P.copy("act", gcs.t[:, 2:4], pS[:, 2:4], [pSr], [gcs.r])
                P.copy("act", gcs.t[0:C, 0:2], pS[0:C, 0:2], [pSr], [gcs.r])
                PFREE(pSr)
                P.act(K["EGQ"].t[:, :, 0:C], g3(pG[:, 0:2 * C]), AF.Exp, [pGr], [K["EGQ"].r])
                yield
                P.tt("dve", v3(K["Dm"]), g3(pG[0:C, 0:2 * C]), bc2(gcs.t[0:C, 0:2]), ALU.subtract, [pGr, gcs.r], [K["Dm"].r])
                PFREE(pGr)
                P.act(K["eg"].t[0:C, :], gcs.t[0:C, 0:2], AF.Exp, [gcs.r], [K["eg"].r])
                P.tt("pool", K["ekd"].t[0:C, :], gcs.t[0:C, 2:4], gcs.t[0:C, 0:2], ALU.subtract, [gcs.r], [K["ekd"].r])
                P.act(K["egl"].t[:, :], gcs.t[:, 2:4], AF.Exp, [gcs.r], [K["egl"].r])
                P.tt("pool", K["qdT"].t[:, :, 0:C], qT.t[:, li[0]:li[1] + 1, c0:c0 + C], K["EGQ"].t[:, :, 0:C], ALU.mult, [qT.r, K["EGQ"].r], [K["qdT"].r])
                yield
                P.ts("dve", v3(K["El"]), v3(K["Dm"]), 0.0, None, ALU.max, None, [K["Dm"].r], [K["El"].r])
                P.ts("dve", v3(K["Eu"]), v3(K["Dm"]), 0.0, None, ALU.min, None, [K["Dm"].r], [K["Eu"].r])
                P.act(K["ekd"].t[0:C, :], K["ekd"].t[0:C, :], AF.Exp, [K["ekd"].r], [K["ekd"].r])
                P.tt("pool", K["beg"].t[0:C, :], K["eg"].t[0:C, :], btok.t[0:C, n, hcs], ALU.mult, [K["eg"].r, btok.r], [K["beg"].r])
                ptkv, ptkvr = yield from gPSB()
                for ii in range(2):
                    P.tr(ptkv[0:C, ii * 128:(ii + 1) * 128], kT.t[:, li[ii], c0:c0 + C], identb.t[:, :], [kT.r, identb.r], [ptkvr])
                for ii in range(2):
                    P.tr(ptkv[0:C, 256 + ii * 128:256 + (ii + 1) * 128], vT.t[:, li[ii], c0:c0 + C], identb.t[:, :], [vT.r, identb.r], [ptkvr])
                yield
                P.act(v3(K["El"]), v3(K["El"]), AF.Exp, [K["El"].r], [K["El"].r], scale=-1.0)
                P.act(v3(K["Eu"]), v3(K["Eu"]), AF.Exp, [K["Eu"].r], [K["Eu"].r])
                ktv = ptkv[0:C, 0:256].rearrange("p (h d) -> p h d", h=2)
                vtv = ptkv[0:C, 256:512].rearrange("p (h d) -> p h d", h=2)
                P.tt("dve", K["kd"].t[0:C, :, :], ktv, bcd(K["ekd"].t[0:C, :]), ALU.mult, [ptkvr, K["ekd"].r], [K["kd"].r])
                P.tt("dve", K["kbg"].t[0:C, :, :], ktv, bcd(K["beg"].t[0:C, :]), ALU.mult, [ptkvr, K["beg"].r], [K["kbg"].r])
                P.tt("dve", K["bv"].t[0:C, :, :], vtv, bcd(btok.t[0:C, n, hcs]), ALU.mult, [ptkvr, btok.r], [K["bv"].r])
                PFREE(ptkvr)
                (pK, pKr), (pQ, pQr) = yield from gPSF(2)
                for ii in range(2):
                    P.mm(pQ[0:C, ii * C:(ii + 1) * C], kT.t[:, li[ii], c0:c0 + C], qT.t[:, li[ii], c0:c0 + C], True, True, [kT.r, qT.r], [pQr])
                for ii in range(2):
                    P.mm(pK[0:C, ii * C:(ii + 1) * C], kT.t[:, li[ii], c0:c0 + C], kT.t[:, li[ii], c0:c0 + C], True, True, [kT.r], [pKr])
                yield
                P.tt("pool", v3(K["EQ"]), v3(K["Eu"]), MUI, ALU.mult, [K["Eu"].r, cst.r], [K["EQ"].r])
                P.tt("pool", v3(K["El"]), v3(K["El"]), bc2(btok.t[0:C, n, hcs]), ALU.mult, [K["El"].r, btok.r], [K["El"].r])
                yield
                P.tt("dve", v3(K["PT"]), g3(pQ[0:C, 0:2 * C]), v3(K["EQ"]), ALU.mult, [pQr, K["EQ"].r], [K["PT"].r])
                PFREE(pQr)
                if two:
                    P.tt("pool", v3(K["EAo"]), v3(K["El"]), MLO, ALU.mult, [K["El"].r, cst.r], [K["EAo"].r])
                P.tt("pool", v3(K["EA"]), v3(K["El"]), MLS, ALU.mult, [K["El"].r, cst.r], [K["EA"].r])
                yield
                P.tt("dve", v3(K["A0"]), g3(pK[0:C, 0:2 * C]), v3(K["EA"]), ALU.mult, [pKr, K["EA"].r], [K["A0"].r])
                if two:
                    P.tt("dve", v3(K["Ao"]), g3(pK[0:C, 0:2 * C]), v3(K["EAo"]), ALU.mult, [pKr, K["EAo"].r], [K["Ao"].r])
                PFREE(pKr)
                yield
                ptb, ptbr = yield from gPSB()
                for ii in range(2):
                    P.tr(ptb[0:C, ii * C:(ii + 1) * C], K["A0"].t[0:C, ii, 0:C], identb.t[0:C, 0:C], [K["A0"].r, identb.r], [ptbr])
                yield
                P.copy("act", v3(K["B0"]), g3(ptb[0:C, 0:2 * C]), [ptbr], [K["B0"].r])
                P.tt("dve", v3(K["X0"]), Idb, g3(ptb[0:C, 0:2 * C]), ALU.subtract, [identb.r, ptbr], [K["X0"].r])
                PFREE(ptbr)
                yield
                for k in range(nlev + 1):
                    Ak, Bk = K["A%d" % (k % 2)], K["B%d" % (k % 2)]
                    An, Bn = K["A%d" % ((k + 1) % 2)], K["B%d" % ((k + 1) % 2)]
                    Xp, Xn = K["X%d" % ((k + 1) % 2)], K["X%d" % (k % 2)]
                    need = (1 if k < nlev else 0) + (1 if k < nlev - 1 else 0) + (1 if k >= 1 else 0)
                    got = yield from gPSF(need)
                    if need == 1:
                        got = [got]
                    got = list(got)
                    pA = pB2 = pX = None
                    if k < nlev:
                        pA, pAr = got.pop(0)
                        for ii in range(2):
                            P.mm(pA[0:C, ii * C:(ii + 1) * C], Bk.t[0:C, ii, 0:C], Ak.t[0:C, ii, 0:C], True, True, [Bk.r, Ak.r], [pAr])
                    if k < nlev - 1:
                        pB2, pB2r = got.pop(0)
                        for ii in range(2):
                            P.mm(pB2[0:C, ii * C:(ii + 1) * C], Ak.t[0:C, ii, 0:C], Bk.t[0:C, ii, 0:C], True, True, [Bk.r, Ak.r], [pB2r])
                    if k >= 1:
                        pX, pXr = got.pop(0)
                        for ii in range(2):
                            P.mm(pX[0:C, ii * C:(ii + 1) * C], Ak.t[0:C, ii, 0:C], Xp.t[0:C, ii, 0:C], True, True, [Ak.r, Xp.r], [pXr])
                    yield
                    if pA is not None:
                        P.copy("act", v3(An), g3(pA[0:C, 0:2 * C]), [pAr], [An.r])
                        PFREE(pAr)
                    if pB2 is not None:
                        P.copy("act", v3(Bn), g3(pB2[0:C, 0:2 * C]), [pB2r], [Bn.r])
                        PFREE(pB2r)
                    if pX is not None:
                        P.tt("dve", v3(Xn), g3(pX[0:C, 0:2 * C]), v3(Xp), ALU.add, [pXr, Xp.r], [Xn.r])
                        PFREE(pXr)
                    yield
                TT = K["X%d" % (nlev % 2)]
                if two:
                    (pY, pYr), (pZ, pZr) = yield from gPSF(2)
                    for ii in range(2):
                        P.mm(pY[0:C, ii * C:(ii + 1) * C], K["Ao"].t[0:C, ii, 0:C], TT.t[0:C, ii, 0:C], True, True, [K["Ao"].r, TT.r], [pYr])
                    for ii in range(2):
                        P.mm(pZ[0:C, ii * 128:(ii + 1) * 128], TT.t[0:C, ii, 0:C], K["bv"].t[0:C, ii, :], True, True, [TT.r, K["bv"].r], [pZr])
                    for ii in range(2):
                        P.mm(pZ[0:C, 256 + ii * 128:256 + (ii + 1) * 128], TT.t[0:C, ii, 0:C], K["kbg"].t[0:C, ii, :], True, True, [TT.r, K["kbg"].r], [pZr])
                    yield
                    P.tt("dve", v3(K["Y"]), Idb, g3(pY[0:C, 0:2 * C]), ALU.subtract, [identb.r, pYr], [K["Y"].r])
                    PFREE(pYr)
                    P.copy("act", K["sz"].t[0:C, :, :], pZ[0:C, 0:512].rearrange("p (h d) -> p h d", h=4), [pZr], [K["sz"].r])
                    PFREE(pZr)
                    yield
                    (pU, pUr), (pW, pWr) = yield from gPSF(2)
                    for ii in range(2):
                        P.mm(pU[0:C, ii * 128:(ii + 1) * 128], K["Y"].t[0:C, ii, 0:C], K["sz"].t[0:C, ii, :], True, True, [K["Y"].r, K["sz"].r], [pUr])
                    for ii in range(2):
                        P.mm(pW[:, ii * C:(ii + 1) * C], K["sz"].t[0:C, 2 + ii, :], K["Y"].t[0:C, ii, 0:C], True, True, [K["Y"].r, K["sz"].r], [pWr])
                else:
                    (pU, pUr), (pW, pWr) = yield from gPSF(2)
                    for ii in range(2):
                        P.mm(pU[0:C, ii * 128:(ii + 1) * 128], TT.t[0:C, ii, 0:C], K["bv"].t[0:C, ii, :], True, True, [TT.r, K["bv"].r], [pUr])
                    for ii in range(2):
                        P.mm(pW[:, ii * C:(ii + 1) * C], K["kbg"].t[0:C, ii, :], TT.t[0:C, ii, 0:C], True, True, [TT.r, K["kbg"].r], [pWr])
                yield
                P.copy("act", K["u"].t[0:C, :, :], pU[0:C, 0:256].rearrange("p (h d) -> p h d", h=2), [pUr], [K["u"].r])
                PFREE(pUr)
                P.copy("dve", K["wT"].t[:, :, 0:C], g3(pW[:, 0:2 * C]), [pWr], [K["wT"].r])
                PFREE(pWr)
                yield

            def gen_scan(n, sub, hg=hg):
                K = cksub[sub][n % 2]
                c0 = n * C
                li = (2 * sub, 2 * sub + 1)
                hgl = (hg * 4 + 2 * sub, hg * 4 + 2 * sub + 1)

                def bcd(ap2):
                    return ap2.unsqueeze(2).broadcast_to([C, 2, 128])

                def g3(ap):
                    return ap.rearrange("p (h c) -> p h c", h=2)

                p1, p1r = yield from gPSF()
                for ii in range(2):
                    h = hgl[ii]
                    P.mm(p1[0:C, ii * 128:(ii + 1) * 128], K["wT"].t[:, ii, 0:C], Sgb[h].t[:, :], True, True, [K["wT"].r, Sgb[h].r], [p1r])
                yield
                P.tt("dve", K["vnew"].t[0:C, :, :], K["u"].t[0:C, :, :], p1[0:C, 0:256].rearrange("p (h d) -> p h d", h=2), ALU.subtract,
                     [K["u"].r, p1r], [K["vnew"].r])
                PFREE(p1r)
                yield
                (pO, pOr), (pSS, pSSr) = yield from gPSF(2)
                for ii in range(2):
                    P.mm(pSS[:, ii * 128:(ii + 1) * 128], K["kd"].t[0:C, ii, :], K["vnew"].t[0:C, ii, :], True, True, [K["kd"].r, K["vnew"].r], [pSSr])
                for ii in range(2):
                    h = hgl[ii]
                    P.mm(pO[0:C, ii * 128:(ii + 1) * 128], K["qdT"].t[:, ii, 0:C], Sgb[h].t[:, :], True, False, [K["qdT"].r, Sgb[h].r], [pOr])
                    P.mm(pO[0:C, ii * 128:(ii + 1) * 128], K["PT"].t[0:C, ii, 0:C], K["vnew"].t[0:C, ii, :], False, True, [K["PT"].r, K["vnew"].r], [pOr])
                yield
                for ii in range(2):
                    h = hgl[ii]
                    P.stt("dve", Sg[h].t[:, :], Sg[h].t[:, :], K["egl"].t[:, ii:ii + 1], pSS[:, ii * 128:(ii + 1) * 128], ALU.mult, ALU.add,
                          [Sg[h].r, K["egl"].r, pSSr], [Sg[h].r])
                PFREE(pSSr)
                P.copy("act", K["osb"].t[0:C, :, :], pO[0:C, 0:256].rearrange("p (h d) -> p h d", h=2), [pOr], [K["osb"].r])
                PFREE(pOr)
                yield
                for ii in range(2):
                    h = hgl[ii]
                    P.copy("act" if ii == 0 else "pool", Sgb[h].t[:, :], Sg[h].t[:, :], [Sg[h].r], [Sgb[h].r])
                P.tt("pool", K["osq"].t[0:C, :, :], K["osb"].t[0:C, :, :], K["osb"].t[0:C, :, :], ALU.mult, [K["osb"].r], [K["osq"].r])
                yield
                P.red("dve", K["ost"].t[0:C, 0:2], K["osq"].t[0:C, :, :], [K["osq"].r], [K["ost"].r])
                yield
                P.act(K["ost"].t[0:C, 2:4], K["ost"].t[0:C, 0:2], AF.Ln, [K["ost"].r], [K["ost"].r], scale=1.0 / 128, bias=eps_ap(C))
                P.act(K["ost"].t[0:C, 2:4], K["ost"].t[0:C, 2:4], AF.Exp, [K["ost"].r], [K["ost"].r], scale=-0.5)
                yield
                P.tt("pool", K["on"].t[0:C, :, :], K["osb"].t[0:C, :, :], bcd(K["ost"].t[0:C, 2:4]), ALU.mult, [K["osb"].r, K["ost"].r], [K["on"].r])
                yield
                pt, pr = yield from gPSB()
                for ii in range(2):
                    P.tr(pt[:, ii * C:(ii + 1) * C], K["on"].t[0:C, ii, :], identb.t[0:C, 0:C], [K["on"].r, identb.r], [pr])
                yield
                P.stt("dve", yaT.t[:, hgl[0]:hgl[1] + 1, c0:c0 + C], g3(pt[:, 0:2 * C]), pc(PC_GDNN), szT.t[:, li[0]:li[1] + 1, c0:c0 + C],
                      ALU.mult, ALU.mult, [pr, prm.r, szT.r], [yaT.r])
                PFREE(pr)
                yield

            def gen_ret(n, sub, hg=hg):
                K = cksub[sub][n % 2]
                c0 = n * C
                tb = c0 // 128
                li = (2 * sub, 2 * sub + 1)
                hgl = (hg * 4 + 2 * sub, hg * 4 + 2 * sub + 1)

                def bcd(ap2):
                    return ap2.unsqueeze(2).broadcast_to([C, 2, 128])

                def g3(ap):
                    return ap.rearrange("p (h c) -> p h c", h=2)

                pSc, pScr = yield from gPSF()
                for ii in range(2):
                    i = li[ii]
                    P.mm(pSc[0:C, ii * C:(ii + 1) * C], rkT.t[:, i, c0:c0 + C], rqT.t[:, i, c0:c0 + C], True, True, [rkT.r, rqT.r], [pScr])
                yield
                P.tt("dve", K["scT"].t[0:C, :, 0:C], g3(pSc[0:C, 0:2 * C]),
                     cst.t[0:C, CC_DRT + hgl[0] * 128:CC_DRT + hgl[0] * 128 + 256].rearrange("p (h c) -> p h c", h=2)[:, :, 0:C], ALU.mult,
                     [pScr, cst.r], [K["scT"].r])
                PFREE(pScr)
                yield
                (pOr2, pOr2r), (pSR, pSRr) = yield from gPSF(2)
                for ii in range(2):
                    i = li[ii]
                    h = hgl[ii]
                    P.mm(pOr2[0:C, ii * 128:(ii + 1) * 128], rqdT.t[:, i, c0:c0 + C], Srb[h].t[:, :], True, False, [rqdT.r, Srb[h].r], [pOr2r])
                    P.mm(pOr2[0:C, ii * 128:(ii + 1) * 128], K["scT"].t[0:C, ii, 0:C], rv_tm.t[0:C, tb, i * 128:(i + 1) * 128], False, True,
                         [K["scT"].r, rv_tm.r], [pOr2r])
                for ii in range(2):
                    i = li[ii]
                    P.mm(pSR[:, ii * 128:(ii + 1) * 128], rkd_tm.t[0:C, tb, i * 128:(i + 1) * 128], rv_tm.t[0:C, tb, i * 128:(i + 1) * 128], True, True,
                         [rkd_tm.r, rv_tm.r], [pSRr])
                yield
                for ii in range(2):
                    h = hgl[ii]
                    P.stt("dve", Sr[h].t[:, :], Sr[h].t[:, :], cst.t[:, cd_c + h:cd_c + h + 1], pSR[:, ii * 128:(ii + 1) * 128], ALU.mult, ALU.add,
                          [Sr[h].r, cst.r, pSRr], [Sr[h].r])
                PFREE(pSRr)
                P.copy("act", K["orb"].t[0:C, :, :], pOr2[0:C, 0:256].rearrange("p (h d) -> p h d", h=2), [pOr2r], [K["orb"].r])
                PFREE(pOr2r)
                yield
                for ii in range(2):
                    h = hgl[ii]
                    P.copy("act" if ii == 1 else "pool", Srb[h].t[:, :], Sr[h].t[:, :], [Sr[h].r], [Srb[h].r])
                P.red("dve", K["ort"].t[0:C, 0:2], K["orb"].t[0:C, :, :], [K["orb"].r], [K["ort"].r])
                yield
                P.ts("dve", K["ort"].t[0:C, 0:2], K["ort"].t[0:C, 0:2], 1.0 / 128, None, ALU.mult, None, [K["ort"].r], [K["ort"].r])
                yield
                P.tt("pool", K["orc"].t[0:C, :, :], K["orb"].t[0:C, :, :], bcd(K["ort"].t[0:C, 0:2]), ALU.subtract, [K["orb"].r, K["ort"].r], [K["orc"].r])
                yield
                P.tt("pool", K["orq"].t[0:C, :, :], K["orc"].t[0:C, :, :], K["orc"].t[0:C, :, :], ALU.mult, [K["orc"].r], [K["orq"].r])
                yield
                P.red("dve", K["ort"].t[0:C, 2:4], K["orq"].t[0:C, :, :], [K["orq"].r], [K["ort"].r])
                yield
                P.act(K["ort"].t[0:C, 2:4], K["ort"].t[0:C, 2:4], AF.Ln, [K["ort"].r], [K["ort"].r], scale=1.0 / 128, bias=eps_ap(C))
                P.act(K["ort"].t[0:C, 2:4], K["ort"].t[0:C, 2:4], AF.Exp, [K["ort"].r], [K["ort"].r], scale=-0.5)
                yield
                P.tt("pool", K["orn"].t[0:C, :, :], K["orc"].t[0:C, :, :], bcd(K["ort"].t[0:C, 2:4]), ALU.mult, [K["orc"].r, K["ort"].r], [K["orn"].r])
                yield
                pt, pr = yield from gPSB()
                for ii in range(2):
                    P.tr(pt[:, ii * C:(ii + 1) * C], K["orn"].t[0:C, ii, :], identb.t[0:C, 0:C], [K["orn"].r, identb.r], [pr])
                yield
                P.tt("dve", K["ybt"].t[:, :, 0:C], g3(pt[:, 0:2 * C]),
                     prm.t[:, PC_RETN + hgl[0]:PC_RETN + hgl[0] + 2].unsqueeze(2).broadcast_to([128, 2, C]), ALU.mult, [pr, prm.r], [K["ybt"].r])
                PFREE(pr)
                yield
                P.tt("pool", ybT.t[:, hgl[0]:hgl[1] + 1, c0:c0 + C], K["ybt"].t[:, :, 0:C], srgT.t[:, li[0]:li[1] + 1, c0:c0 + C], ALU.mult,
                     [K["ybt"].r, srgT.r], [ybT.r])
                yield

            for n in range(NCH + 1):
                gens = []
                for sub in range(2):
                    if n >= 1:
                        gens.append(gen_scan(n - 1, sub))
                    if n < NCH:
                        gens.append(gen_prep(n, sub))
                    if n >= 1:
                        gens.append(gen_ret(n - 1, sub))
                run_interleaved(gens)

        for p in range(2):
            slot = next_piece("ga")
            for i in range(4):
                pt, pr = fm_group(slot, i, T, nT, nT.r)
                P.act(sgT.t[:, i, 0:T], pt[:, 0:T], AF.Sigmoid, [pr], [sgT.r])
            slot = next_piece("gb")
            for i in range(4):
                pt, pr = fm_group(slot, i, T, nT, nT.r)
                P.act(sbT.t[:, i, 0:T], pt[:, 0:T], AF.Sigmoid, [pr], [sbT.r])
            slot = next_piece("brg")
            for i in range(4):
                pt, pr = fm_group(slot, i, T, yaT, yaT.r)
                P.tt("dve", mtmp[i].t[:, 0:T], pt[:, 0:T], sgT.t[:, i, 0:T], ALU.mult, [pr, sgT.r], [mtmp[i].r])
            slot = next_piece("brr")
            for i in range(4):
                pt, pr = fm_group(slot, i, T, ybT, ybT.r)
                m2 = mtmp2[i % 2]
                P.tt("dve", m2.t[:, 0:T], pt[:, 0:T], sbT.t[:, i, 0:T], ALU.mult, [pr, sbT.r], [m2.r])
                P.tt("pool", mergedT.t[:, 4 * p + i, 0:T], m2.t[:, 0:T], mtmp[i].t[:, 0:T], ALU.add, [m2.r, mtmp[i].r], [mergedT.r])
        for ch in range(2):
            slot = next_piece("wo")
            for tb, (t0, tn) in enumerate(tbs):
                pt, pr = tm_group(slot, mergedT, mergedT.r, t0, tn)
                hh = H[tb].t[0:tn, ch * 512:(ch + 1) * 512]
                P.tt("dve", hh, pt[0:tn, :], hh, ALU.add, [pr, H[tb].r], [H[tb].r])

    def final_out(t):
        tbs = [(tb * 128, 128) for tb in range(4)]
        fins = []
        for tb in range(4):
            P.act(junk.t[:, :], H[tb].t[:, :], AF.Square, [H[tb].r], [junk.r, ss.r], accum_out=ss.t[:, tb:tb + 1])
        P.act(rstd.t[:, 0:4], ss.t[:, 0:4], AF.Ln, [ss.r], [rstd.r], scale=1.0 / D, bias=eps_ap())
        P.act(rstd.t[:, 0:4], rstd.t[:, 0:4], AF.Exp, [rstd.r], [rstd.r], scale=-0.5)
        for tb in range(4):
            P.stt("dve", OUTB[tb].t[:, :], H[tb].t[:, :], rstd.t[:, tb:tb + 1], fnw.t[:, :], ALU.mult, ALU.mult,
                  [H[tb].r, rstd.r, fnw.r], [OUTB[tb].r])
            r0 = t * 512 + tb * 128
            fins.append(P.dma(out_d[r0:r0 + 128, :], OUTB[tb].t[:, :], [OUTB[tb].r], []))
        return fins

    fin_ops = []
    tbs_m = [(0, NMETA)]
    P.dma(H[0].t[0:NMETA, :], meta_d[:, :], [], [H[0].r])
    ffn(1, NMETA, tbs_m)
    mixer(NMETA, tbs_m, 16, 0)
    ffn(2, NMETA, tbs_m)
    tbs = [(tb * 128, 128) for tb in range(4)]
    for t in range(nt):
        for tb in range(4):
            r0 = t * 512 + tb * 128
            P.dma(H[tb].t[:, :], x_d[r0:r0 + 128, :], [], [H[tb].r])
        ffn(1, 512, tbs)
        mixer(512, tbs, 128, NMETA + t * 512)
        ffn(2, 512, tbs)
        fin_ops += final_out(t)
    P.emit(fin_ops)
    if debug:
        print("ops", len(P.ops), {e: sum(1 for o in P.ops if o.eng == e) for e in ENGS})
    return nc


WNAMES = ("ffn1_w_in", "ffn1_w_out", "w_in", "w_branch_gdn", "w_branch_ret", "w_out", "ffn2_w_in", "ffn2_w_out")


def make_in_maps(inputs, nb, nt):
    W = {k: np.asarray(inputs[k], np.float32)[0] for k in WNAMES}
    wst = host_pack_weights(W)
    prm = host_pack_params({k: np.asarray(inputs[k], np.float32)[0] for k in
                            ("ffn1_norm", "mix_norm", "ffn2_norm", "ret_out_norm", "gdn_out_norm", "gdn_conv_w",
                             "gdn_a_log", "gdn_dt_bias", "w_in")})
    fnw = np.ascontiguousarray(np.broadcast_to(np.asarray(inputs["final_norm"], np.float32)[None, :], (128, D)))
    cst = host_consts()
    rope = host_rope(NMETA + nt * 512)
    meta = np.ascontiguousarray(np.asarray(inputs["meta_tokens"], np.float32))
    x = np.asarray(inputs["x"], np.float32)
    return [{"x": np.ascontiguousarray(x[b, :nt * 512]), "meta": meta, "wst": wst, "prm": prm, "fnw": fnw, "cst": cst, "rope": rope}
            for b in range(nb)]


def kernel(**inputs):
    nt = SEQ // 512
    nc = build(nt)
    in_maps = make_in_maps(inputs, 8, nt)
    res = run_bass_kernel_spmd(nc, in_maps, core_ids=list(range(8)))
    return np.stack([np.asarray(r["out"], np.float32) for r in res.results], axis=0)
```

```python
import contextlib
import numpy as np
import ml_dtypes
import concourse.bass as bass
import concourse.mybir as mybir
from concourse.bass_utils import run_bass_kernel_spmd

F32 = mybir.dt.float32
BF16 = mybir.dt.bfloat16
AF = mybir.ActivationFunctionType
ALU = mybir.AluOpType
AX = mybir.AxisListType

D = 1024
NMETA = 16
SEQ = 8192
DFF = 2816
NJ = 22
DPROJ = 10256
EPS = 1e-6
NH = 8
PIECE = 4096
NSLOT = 4
ENGS = ("pe", "act", "dve", "pool", "sp")
N_DMA_SLOTS = 12
SAME_ENGINE_SYNC = True
SB_BASE = 18432
SB_LIMIT = 229376


class Res:
    __slots__ = ("name", "last_w", "readers", "overlaps", "lo", "hi", "excl")

    def __init__(self, name, lo=None, hi=None, excl=False):
        self.name = name
        self.excl = excl
        self.last_w = None
        self.readers = {}
        self.overlaps = []
        self.lo = lo
        self.hi = hi


class Op:
    __slots__ = ("idx", "eng", "fn", "deps", "dma", "needs_inc", "sem", "val", "prev_val")

    def __init__(self, idx, eng, fn, deps, dma):
        self.idx = idx
        self.eng = eng
        self.fn = fn
        self.deps = deps
        self.dma = dma
        self.needs_inc = False
        self.sem = None
        self.val = 0
        self.prev_val = 0


class Prog:
    def __init__(self, nc):
        self.nc = nc
        self.ops = []

    def op(self, eng, fn, reads=(), writes=(), dma=False):
        idx = len(self.ops)
        deps = set()
        wset = []
        for w in writes:
            wset.append(w)
            wset.extend(w.overlaps)
        rl = []
        for r in reads:
            if r.excl:
                wset.append(r)
            else:
                rl.append(r)
        reads = rl
        for r in reads:
            if r.last_w is not None:
                deps.add(r.last_w)
        for w in wset:
            if w.last_w is not None:
                deps.add(w.last_w)
            for ridx in w.readers.values():
                deps.add(ridx)
        o = Op(idx, eng, fn, sorted(deps), dma)
        self.ops.append(o)
        for r in reads:
            r.readers[("dma", idx) if dma else eng] = idx
        for w in wset:
            w.last_w = idx
            w.readers = {}
        return o

    def mm(self, out, lhsT, rhs, start, stop, reads, writes):
        return self.op("pe", lambda e: e.matmul(out, lhsT=lhsT, rhs=rhs, start=start, stop=stop), reads, writes)

    def tr(self, out, in_, ident, reads, writes):
        return self.op("pe", lambda e: e.transpose(out=out, in_=in_, identity=ident), reads, writes)

    def tt(self, eng, out, in0, in1, op, reads, writes):
        return self.op(eng, lambda e: e.tensor_tensor(out=out, in0=in0, in1=in1, op=op), reads, writes)

    def ts(self, eng, out, in0, s1, s2, op0, op1, reads, writes):
        if s2 is None:
            return self.op(eng, lambda e: e.tensor_scalar(out=out, in0=in0, scalar1=s1, scalar2=None, op0=op0), reads, writes)
        return self.op(eng, lambda e: e.tensor_scalar(out=out, in0=in0, scalar1=s1, scalar2=s2, op0=op0, op1=op1), reads, writes)

    def stt(self, eng, out, in0, scalar, in1, op0, op1, reads, writes):
        return self.op(eng, lambda e: e.scalar_tensor_tensor(out=out, in0=in0, scalar=scalar, in1=in1, op0=op0, op1=op1), reads, writes)

    def act(self, out, in_, func, reads, writes, bias=None, scale=None, accum_out=None):
        kw = {}
        if bias is not None:
            kw["bias"] = bias
        if scale is not None:
            kw["scale"] = scale
        if accum_out is not None:
            kw["accum_out"] = accum_out
        return self.op("act", lambda e: e.activation(out=out, in_=in_, func=func, **kw), reads, writes)

    def copy(self, eng, out, in_, reads, writes):
        if eng == "act":
            return self.act(out, in_, AF.Copy, reads, writes)
        return self.op(eng, lambda e: e.tensor_copy(out=out, in_=in_), reads, writes)

    def red(self, eng, out, in_, reads, writes):
        return self.op(eng, lambda e: e.tensor_reduce(out=out, in_=in_, axis=AX.X, op=ALU.add), reads, writes)

    def memset(self, eng, out, val, writes):
        return self.op(eng, lambda e: e.memset(out, val), (), writes)

    def dma(self, out, in_, reads, writes, queue="sp"):
        return self.op(queue, lambda e: e.dma_start(out=out, in_=in_), reads, writes, dma=True)

    def emit(self, final_ops):
        nc = self.nc
        ops = self.ops

        def skip_same(o, dop):
            return (not dop.dma) and (not o.dma) and dop.eng == o.eng and (o.eng == "pe" or not SAME_ENGINE_SYNC)

        for o in ops:
            for d in o.deps:
                dop = ops[d]
                if dop.dma or skip_same(o, dop):
                    continue
                dop.needs_inc = True
        with contextlib.ExitStack() as st:
            esem = {e: st.enter_context(nc.semaphore("prog_" + e)) for e in ENGS}
            dsem = {q: [st.enter_context(nc.semaphore("dma_%s_%d" % (q, i))) for i in range(N_DMA_SLOTS)]
                    for q in ("sp", "pool")}
            cnt = {e: 0 for e in ENGS}
            dcnt = {q: 0 for q in dsem}
            duse = {q: [0] * N_DMA_SLOTS for q in dsem}
            for o in ops:
                if o.dma:
                    q = o.eng
                    s = dcnt[q] % N_DMA_SLOTS
                    dcnt[q] += 1
                    o.prev_val = 16 * duse[q][s]
                    duse[q][s] += 1
                    o.sem = dsem[q][s]
                    o.val = 16 * duse[q][s]
                elif o.needs_inc:
                    cnt[o.eng] += 1
                    o.sem = esem[o.eng]
                    o.val = cnt[o.eng]
            per = {e: [o for o in ops if o.eng == e] for e in ENGS}
            block = st.enter_context(nc.Block())

            def run(ename, eng):
                waited = {}
                for o in per[ename]:
                    need = {}
                    for d in o.deps:
                        dop = ops[d]
                        if skip_same(o, dop):
                            continue
                        k = id(dop.sem)
                        if k not in need or need[k][1] < dop.val:
                            need[k] = (dop.sem, dop.val)
                    if o.dma and o.prev_val > 0:
                        k = id(o.sem)
                        if k not in need or need[k][1] < o.prev_val:
                            need[k] = (o.sem, o.prev_val)
                    for k, (sem, val) in need.items():
                        if waited.get(k, 0) >= val:
                            continue
                        eng.wait_ge(sem, val)
                        waited[k] = val
                    ins = o.fn(eng)
                    if o.dma:
                        ins.then_inc(o.sem, 16)
                    elif o.needs_inc:
                        ins.then_inc(o.sem, 1)
                if ename == "sp":
                    for fo in final_ops:
                        eng.wait_ge(fo.sem, fo.val)

            @block.sync
            def _(e):
                run("sp", e)

            @block.tensor
            def _(e):
                run("pe", e)

            @block.scalar
            def _(e):
                run("act", e)

            @block.vector
            def _(e):
                run("dve", e)

            @block.gpsimd
            def _(e):
                run("pool", e)


class Buf:
    __slots__ = ("t", "r")

    def __init__(self, t, r):
        self.t = t
        self.r = r


class SBAlloc:
    def __init__(self, nc):
        self.nc = nc
        self.off = SB_BASE
        self.peak = SB_BASE
        self.all = []

    def alloc(self, name, shape, dt):
        esz = 4 if dt == F32 else 2
        n = 1
        for s in shape[1:]:
            n *= s
        size = (n * esz + 31) // 32 * 32
        assert self.off + size <= SB_LIMIT, ("SBUF overflow", name, self.off, size)
        t = self.nc.alloc_sbuf_tensor_at(name, list(shape), dt, offset=self.off)
        r = Res(name, self.off, self.off + size)
        self.off += size
        self.peak = max(self.peak, self.off)
        self.all.append(r)
        return Buf(t, r)

    def finalize(self):
        rs = sorted(self.all, key=lambda r: r.lo)
        for i, a in enumerate(rs):
            for b in rs[i + 1:]:
                if b.lo >= a.hi:
                    break
                a.overlaps.append(b)
                b.overlaps.append(a)


def piece_specs():
    sp = []

    def ffn(i):
        win, wout, nrm = "ffn%d_w_in" % i, "ffn%d_w_out" % i, "ffn%d_norm" % i
        for g in range(NJ // 2):
            j0, j1 = 2 * g, 2 * g + 1
            sp.append(dict(kind="FM", w=win, cc=[j0 * 128, DFF + j0 * 128, j1 * 128, DFF + j1 * 128], norm=nrm, tag=("ffn_in", i, g)))
        for ch in range(2):
            for (j0, nj) in ((0, 8), (8, 8), (16, 6)):
                sp.append(dict(kind="TMK", w=wout, j0=j0, nj=nj, c0=ch * 512, norm=None, tag=("ffn_out", i, ch, j0, nj)))

    ffn(1)
    for hg in range(2):
        for nm, base in (("q", 0), ("k", 1024), ("v", 2048), ("z", 3072), ("rg", 7184)):
            sp.append(dict(kind="FM", w="w_in", cc=[base + (4 * hg + i) * 128 for i in range(4)], norm="mix_norm", tag=(nm, hg)))
        for nm, base in (("rq", 4112), ("rk", 5136), ("rv", 6160)):
            sp.append(dict(kind="TM", w="w_in", c0=base + hg * 512, norm="mix_norm", tag=(nm, hg)))
    for p in range(2):
        sp.append(dict(kind="FM", w="w_in", cc=[8208 + (4 * p + i) * 128 for i in range(4)], norm="mix_norm", tag=("ga", p)))
        sp.append(dict(kind="FM", w="w_in", cc=[9232 + (4 * p + i) * 128 for i in range(4)], norm="mix_norm", tag=("gb", p)))
        sp.append(dict(kind="FM", w="w_branch_gdn", cc=[(4 * p + i) * 128 for i in range(4)], norm=None, tag=("brg", p)))
        sp.append(dict(kind="FM", w="w_branch_ret", cc=[(4 * p + i) * 128 for i in range(4)], norm=None, tag=("brr", p)))
    for ch in range(2):
        sp.append(dict(kind="TM", w="w_out", c0=ch * 512, norm=None, tag=("wo", ch)))
    ffn(2)
    return sp


PIECES = piece_specs()
NP = len(PIECES)
NORM_COL = {"ffn1_norm": 0, "mix_norm": 8, "ffn2_norm": 16}
PC_RETN = 24
PC_GDNN = 32
PC_CONV = 33
PC_ALOG = 129
PC_DTB = 137
PC_WBA = 145
NPRM = PC_WBA + 128
CC_U = 0
CC_MUI = 128
CC_MLS = 256
CC_DRT = 384
CC_XI = 1408
CC_ZS128 = 2432
CC_ZS16 = 2440
CC_CD128 = 2448
CC_CD16 = 2456
CC_ONES = 2464
CC_MLO = 2592
NCST = CC_MLO + 128


def host_consts():
    c = np.zeros((128, NCST), np.float32)
    j = np.arange(128)
    c[:, CC_U:CC_U + 128] = (j[:, None] <= j[None, :])
    c[:, CC_MUI:CC_MUI + 128] = (j[:, None] <= j[None, :])
    c[:, CC_MLS:CC_MLS + 128] = (j[None, :] < j[:, None]) & ((j[None, :] // 64) == (j[:, None] // 64))
    c[:, CC_MLO:CC_MLO + 128] = (j[None, :] < 64) & (j[:, None] >= 64)
    lg = np.log1p(-np.exp2(-5.0 - np.arange(NH, dtype=np.float64)))
    m = j[:, None, None]
    cc = j[None, None, :]
    dr = np.where(m <= cc, np.exp(np.maximum(cc - m, 0) * lg[None, :, None]), 0.0)
    c[:, CC_DRT:CC_DRT + 1024] = dr.reshape(128, 1024)
    xi = np.exp((j[None, None, :] + 1.0) * lg[None, :, None]) * np.ones((128, 1, 1))
    c[:, CC_XI:CC_XI + 1024] = xi.reshape(128, 1024)
    sc = 128.0 ** -0.5
    c[:, CC_ZS128:CC_ZS128 + 8] = np.exp((127.0 - j)[:, None] * lg[None, :]) * sc
    c[:16, CC_ZS16:CC_ZS16 + 8] = np.exp((15.0 - j[:16])[:, None] * lg[None, :]) * sc
    c[:, CC_CD128:CC_CD128 + 8] = np.exp(128.0 * lg)[None, :]
    c[:, CC_CD16:CC_CD16 + 8] = np.exp(16.0 * lg)[None, :]
    c[:, CC_ONES:CC_ONES + 128] = 1.0
    return c


def host_rope(L):
    inv = (1.0 / (10000.0 ** np.linspace(0.0, 1.0, 64, dtype=np.float32))).astype(np.float32)
    pos = np.arange(L, dtype=np.float32)
    ang = (pos[:, None] * inv[None, :]).astype(np.float32)
    r = np.zeros((L, 128), np.float32)
    r[:, :64] = np.cos(ang.astype(np.float64))
    r[:, 64:] = np.sin(ang.astype(np.float64))
    return r


def host_pack_weights(W):
    out = np.zeros((NP, 128, PIECE), np.float32)
    for s, sp in enumerate(PIECES):
        w = W[sp["w"]]
        if sp["kind"] == "FM":
            wk = w.reshape(8, 128, -1)
            for i, c0 in enumerate(sp["cc"]):
                blk = wk[:, :, c0:c0 + 128]
                out[s].reshape(128, 8, 4, 128)[:, :, i, :] = blk.transpose(1, 0, 2)
        elif sp["kind"] == "TM":
            wk = w.reshape(8, 128, -1)[:, :, sp["c0"]:sp["c0"] + 512]
            out[s].reshape(128, 8, 512)[:, :, :] = wk.transpose(1, 0, 2)
        else:
            wk = w.reshape(NJ, 128, -1)[sp["j0"]:sp["j0"] + sp["nj"], :, sp["c0"]:sp["c0"] + 512]
            out[s].reshape(128, 8, 512)[:, :sp["nj"], :] = wk.transpose(1, 0, 2)
    return out


def host_pack_params(I):
    prm = np.zeros((128, NPRM), np.float32)
    for nm, c0 in NORM_COL.items():
        prm[:, c0:c0 + 8] = np.asarray(I[nm]).reshape(8, 128).T
    prm[:, PC_RETN:PC_RETN + 8] = np.asarray(I["ret_out_norm"]).reshape(8, 128).T
    prm[:, PC_GDNN] = np.asarray(I["gdn_out_norm"]).reshape(128)
    cw = np.asarray(I["gdn_conv_w"]).reshape(4, 24, 128)
    prm[:, PC_CONV:PC_CONV + 96] = cw.transpose(2, 1, 0).reshape(128, 96)
    prm[:, PC_ALOG:PC_ALOG + 8] = np.asarray(I["gdn_a_log"]).reshape(1, 8)
    prm[:, PC_DTB:PC_DTB + 8] = np.asarray(I["gdn_dt_bias"]).reshape(1, 8)
    wba = np.asarray(I["w_in"]).reshape(8, 128, DPROJ)[:, :, 4096:4112]
    prm[:, PC_WBA:PC_WBA + 128] = wba.transpose(1, 0, 2).reshape(128, 128)
    return prm


def build(nt, debug=False):
    seq = nt * 512
    L = NMETA + seq
    nc = bass.Bass("TRN2", target_bir_lowering=False)
    x_d = nc.dram_tensor("x", [seq, D], F32, kind="ExternalInput").ap()
    meta_d = nc.dram_tensor("meta", [NMETA, D], F32, kind="ExternalInput").ap()
    wst_d = nc.dram_tensor("wst", [NP, 128, PIECE], F32, kind="ExternalInput").ap()
    prm_d = nc.dram_tensor("prm", [128, NPRM], F32, kind="ExternalInput").ap()
    fnw_d = nc.dram_tensor("fnw", [128, D], F32, kind="ExternalInput").ap()
    cst_d = nc.dram_tensor("cst", [128, NCST], F32, kind="ExternalInput").ap()
    rope_d = nc.dram_tensor("rope", [L, 128], F32, kind="ExternalInput").ap()
    out_d = nc.dram_tensor("out", [seq, D], F32, kind="ExternalOutput").ap()
    wsc_d = nc.dram_tensor("wsc", [NP, 128, PIECE], BF16).ap()
    wsc_r = [Res("wsc%d" % s) for s in range(NP)]

    P = Prog(nc)
    sb = SBAlloc(nc)
    A = sb.alloc

    prm = A("prm", [128, NPRM], F32)
    cst = A("cst", [128, NCST], F32)
    fnw = A("fnw", [128, D], F32)
    identb = A("identb", [128, 128], BF16)
    onesb = A("onesb", [128, 128], BF16)
    wba = A("wba", [128, 8, 16], BF16)
    nA = A("nA", [128, 8], F32)
    H = [A("H%d" % tb, [128, D], F32) for tb in range(4)]
    nb = [A("nb%d" % i, [128, D], BF16) for i in range(2)]
    nT = A("nT", [128, 8, 512], BF16)
    ring = [A("ring%d" % i, [128, PIECE], BF16) for i in range(NSLOT)]
    Sg = [A("Sg%d" % h, [128, 128], F32) for h in range(NH)]
    Sr = [A("Sr%d" % h, [128, 128], F32) for h in range(NH)]
    Sgb = [A("Sgb%d" % h, [128, 128], BF16) for h in range(NH)]
    Srb = [A("Srb%d" % h, [128, 128], BF16) for h in range(NH)]
    ctail = A("ctail", [128, 24, 3], F32)
    yaT = A("yaT", [128, 8, 512], BF16)
    ybT = A("ybT", [128, 8, 512], BF16)
    ss = A("ss", [128, 4], F32)
    rstd = A("rstd", [128, 4], F32)
    junk = A("junk", [128, D], BF16)
    rope = A("rope", [128, 4, 128], F32)
    gtok = A("gtok", [128, 4, 8], F32)
    btok = A("btok", [128, 4, 8], F32)
    braw = A("braw", [128, 4, 16], F32)
    batmp = A("batmp", [128, 4, 8], F32)
    cbias = A("cbias", [128, 2], F32)

    def eps_ap(rows=128):
        return cbias.t[0:rows, 0:1]

    def one_ap(rows=128):
        return cbias.t[0:rows, 1:2]

    arena0 = sb.off

    stage = [A("stage%d" % i, [128, PIECE], F32) for i in range(4)]
    sb.off = arena0
    actT = A("actT", [128, NJ, 512], BF16)
    sil = [A("sil%d" % i, [128, 512], F32) for i in range(2)]
    OUTB = [A("OUTB%d" % tb, [128, D], F32) for tb in range(4)]
    sb.off = arena0
    qT = A("qT", [128, 4, 512], BF16)
    kT = A("kT", [128, 4, 512], BF16)
    vT = A("vT", [128, 4, 512], BF16)
    szT = A("szT", [128, 4, 512], BF16)
    srgT = A("srgT", [128, 4, 512], BF16)
    rkd_tm = A("rkd_tm", [128, 4, 512], BF16)
    rv_tm = A("rv_tm", [128, 4, 512], BF16)
    rqT = A("rqT", [128, 4, 512], BF16)
    rkT = A("rkT", [128, 4, 512], BF16)
    rqdT = A("rqdT", [128, 4, 512], BF16)
    sub0 = sb.off
    rq_tm = A("rq_tm", [128, 4, 512], BF16)
    rk_tm = A("rk_tm", [128, 4, 512], BF16)
    NCS = 5
    cin = [A("cin%d" % i, [128, 515], F32) for i in range(NCS)]
    cacc = [A("cacc%d" % i, [128, 512], F32) for i in range(NCS)]
    cf = cacc
    csq = [A("csq%d" % i, [128, 512], BF16) for i in range(NCS)]
    crs = [A("crs%d" % i, [128, 512], F32) for i in range(NCS)]
    rxs = [A("rxs%d" % i, [128, 512], F32) for i in range(2)]
    rt = [[A("rt%d_%d" % (i, k), [128, 256], F32) for k in range(2)] for i in range(2)]
    sb.off = sub0
    cksub = []
    for sub in range(2):
        base = {}
        for nm, shp, dt in (
            ("GU", [128, 2, 128], F32), ("gcs", [128, 4], F32), ("eg", [128, 2], F32), ("beg", [128, 2], F32),
            ("ekd", [128, 2], F32), ("Dm", [128, 2, 128], F32), ("El", [128, 2, 128], F32), ("EQ", [128, 2, 128], F32),
            ("EGQ", [128, 2, 128], F32),
            ("A0", [128, 2, 128], BF16), ("A1", [128, 2, 128], BF16), ("B0", [128, 2, 128], BF16), ("B1", [128, 2, 128], BF16),
            ("X0", [128, 2, 128], BF16), ("X1", [128, 2, 128], BF16), ("Y", [128, 2, 128], BF16),
            ("sz", [128, 4, 128], BF16),
            ("vnew", [128, 2, 128], BF16), ("osb", [128, 2, 128], F32), ("osq", [128, 2, 128], F32), ("ost", [128, 4], F32),
            ("on", [128, 2, 128], BF16),
            ("scT", [128, 2, 128], BF16), ("orb", [128, 2, 128], F32), ("orq", [128, 2, 128], F32), ("ort", [128, 4], F32),
            ("orn", [128, 2, 128], BF16), ("ybt", [128, 2, 128], F32),
        ):
            base[nm] = A("%s_s%d" % (nm, sub), shp, dt)
        base["Eu"] = base["Dm"]
        base["EA"] = base["El"]
        base["EAo"] = base["GU"]
        base["orc"] = base["orb"]
        pars = []
        for par in range(2):
            d = dict(base)
            for nm, shp, dt in (("wT", [128, 2, 128], BF16), ("u", [128, 2, 128], F32), ("kd", [128, 2, 128], BF16),
                                ("PT", [128, 2, 128], BF16), ("qdT", [128, 2, 128], BF16), ("egl", [128, 2], F32),
                                ("Ain", [128, 2, 128], BF16), ("Ao", [128, 2, 128], BF16), ("kbg", [128, 2, 128], BF16),
                                ("bv", [128, 2, 128], BF16)):
                d[nm] = A("%s_s%d_%d" % (nm, sub, par), shp, dt)
            pars.append(d)
        cksub.append(pars)
    hg_end = sb.off
    sb.off = arena0
    sgT = A("sgT", [128, 4, 512], BF16)
    sbT = A("sbT", [128, 4, 512], BF16)
    mtmp = [A("mtmp%d" % i, [128, 512], F32) for i in range(4)]
    mtmp2 = [A("mtmp2_%d" % i, [128, 512], F32) for i in range(2)]
    mergedT = A("mergedT", [128, 8, 512], BF16)
    sb.finalize()
    if debug:
        print("SBUF peak", sb.peak, "of", SB_LIMIT, "hg_end", hg_end, "arena0", arena0)

    NF, NB = 7, 1
    psf = [nc.alloc_psum_tensor("psf%d" % i, [128, 512], F32) for i in range(NF)]
    psb = [nc.alloc_psum_tensor("psb%d" % i, [128, 1024], BF16) for i in range(NB)]
    psf_r = [Res("psf%d" % i, excl=True) for i in range(NF)]
    psb_r = [Res("psb%d" % i, excl=True) for i in range(NB)]
    from collections import deque
    free_f = deque(range(NF))
    free_b = deque(range(NB))

    def PSF(hold=False):
        i = free_f.popleft()
        if not hold:
            free_f.append(i)
        return psf[i], psf_r[i]

    def PSB(hold=False):
        i = free_b.popleft()
        if not hold:
            free_b.append(i)
        return psb[i], psb_r[i]

    def gPSF(k=1):
        while len(free_f) < k:
            yield
        got = [PSF(hold=True) for _ in range(k)]
        return got[0] if k == 1 else got

    def gPSB():
        while not free_b:
            yield
        return PSB(hold=True)

    def PFREE(r):
        if r in psf_r:
            free_f.append(psf_r.index(r))
        else:
            free_b.append(psb_r.index(r))

    def pc(c0, n=1):
        return prm.t[:, c0:c0 + n]

    def cc(c0, n, rows=128):
        return cst.t[0:rows, c0:c0 + n]

    rr = ["act", "dve"]
    rrc = {"i": 0}

    def nxt(choices=("act", "dve")):
        rrc["i"] += 1
        return choices[rrc["i"] % len(choices)]

    P.dma(prm.t[:], prm_d[:, :], [], [prm.r])
    P.dma(cst.t[:], cst_d[:, :], [], [cst.r])
    P.dma(fnw.t[:], fnw_d[:, :], [], [fnw.r])
    P.memset("pool", crs[0].t[:, 0:128], 0.0, [crs[0].r])
    P.op("pool", lambda e: e.affine_select(out=crs[0].t[:, 0:128], in_=crs[0].t[:, 0:128], pattern=[[-1, 128]],
                                           compare_op=ALU.not_equal, fill=1.0, base=0, channel_multiplier=1),
         [crs[0].r], [crs[0].r])
    P.copy("dve", identb.t[:], crs[0].t[:, 0:128], [crs[0].r], [identb.r])
    P.memset("dve", onesb.t[:], 1.0, [onesb.r])
    for kc in range(8):
        P.ts("dve", wba.t[:, kc, :], prm.t[:, PC_WBA + kc * 16:PC_WBA + kc * 16 + 16], pc(NORM_COL["mix_norm"] + kc), None,
             ALU.mult, None, [prm.r], [wba.r])
    P.act(nA.t[:], pc(PC_ALOG, 8), AF.Exp, [prm.r], [nA.r])
    P.ts("dve", nA.t[:], nA.t[:], -1.0, None, ALU.mult, None, [nA.r], [nA.r])
    for h in range(NH):
        P.memset("pool", Sg[h].t[:], 0.0, [Sg[h].r])
        P.memset("pool", Sr[h].t[:], 0.0, [Sr[h].r])
        P.memset("dve", Sgb[h].t[:], 0.0, [Sgb[h].r])
        P.memset("dve", Srb[h].t[:], 0.0, [Srb[h].r])
    P.memset("pool", ctail.t[:], 0.0, [ctail.r])
    P.memset("pool", cbias.t[:, 0:1], EPS, [cbias.r])
    P.memset("pool", cbias.t[:, 1:2], 1.0, [cbias.r])

    for s, spc in enumerate(PIECES):
        stg = stage[s % 4]
        slot = ring[s % NSLOT]
        P.dma(stg.t[:], wst_d[s], [], [stg.r])
        if spc["norm"] is not None:
            c0 = NORM_COL[spc["norm"]]
            for kc in range(8):
                eng = "act" if kc % 2 == 0 else "dve"
                o_ = slot.t[:, kc * 512:(kc + 1) * 512]
                i_ = stg.t[:, kc * 512:(kc + 1) * 512]
                if eng == "act":
                    P.act(o_, i_, AF.Identity, [stg.r, prm.r], [slot.r], scale=pc(c0 + kc))
                else:
                    P.ts("dve", o_, i_, pc(c0 + kc), None, ALU.mult, None, [stg.r, prm.r], [slot.r])
        else:
            P.copy("act", slot.t[:, 0:1536], stg.t[:, 0:1536], [stg.r], [slot.r])
            P.copy("dve", slot.t[:, 1536:3072], stg.t[:, 1536:3072], [stg.r], [slot.r])
            P.copy("pool", slot.t[:, 3072:4096], stg.t[:, 3072:4096], [stg.r], [slot.r])
        P.dma(wsc_d[s], slot.t[:], [slot.r], [wsc_r[s]])

    wstate = {"issued": 0, "cur": 0}
    total_pieces = NP * (nt + 1)

    def w_issue_upto(n):
        while wstate["issued"] < min(n, total_pieces):
            g = wstate["issued"]
            s = g % NP
            slot = ring[g % NSLOT]
            P.dma(slot.t[:], wsc_d[s], [wsc_r[s]], [slot.r])
            wstate["issued"] += 1

    def next_piece(tag_prefix, lag=0):
        g = wstate["cur"]
        s = g % NP
        assert PIECES[s]["tag"][0] == tag_prefix, (PIECES[s]["tag"], tag_prefix)
        w_issue_upto(g + NSLOT - lag)
        wstate["cur"] += 1
        return ring[g % NSLOT]

    def run_jobs(jobs, maxact, fifo):
        done = set()
        pending = list(jobs)
        active = []
        while pending or active:
            for j in list(pending):
                if len(active) >= maxact:
                    break
                if all(d in done for d in j[2]):
                    active.append((j[0], j[1]()))
                    pending.remove(j)
                elif fifo:
                    break
            assert active, ("scheduler deadlock", [j[0] for j in pending][:5])
            for a in list(active):
                try:
                    next(a[1])
                except StopIteration:
                    active.remove(a)
                    done.add(a[0])

    def run_interleaved(gens):
        gens = list(gens)
        while gens:
            for g in list(gens):
                try:
                    next(g)
                except StopIteration:
                    gens.remove(g)

    def norm_to_nT(T, tbs):
        ntb = len(tbs)
        for tb, (t0, tn) in enumerate(tbs):
            P.act(junk.t[0:tn, :], H[tb].t[0:tn, :], AF.Square, [H[tb].r], [junk.r, ss.r], accum_out=ss.t[0:tn, tb:tb + 1])
        tn0 = tbs[0][1]
        P.act(rstd.t[0:tn0, 0:ntb], ss.t[0:tn0, 0:ntb], AF.Ln, [ss.r], [rstd.r], scale=1.0 / D, bias=eps_ap(tn0))
        P.act(rstd.t[0:tn0, 0:ntb], rstd.t[0:tn0, 0:ntb], AF.Exp, [rstd.r], [rstd.r], scale=-0.5)
        for tb, (t0, tn) in enumerate(tbs):
            nbb = nb[tb % 2]
            P.ts("dve", nbb.t[0:tn, :], H[tb].t[0:tn, :], rstd.t[0:tn, tb:tb + 1], None, ALU.mult, None, [H[tb].r, rstd.r], [nbb.r])
            pt, pr = PSB()
            for fc in range(8):
                P.tr(pt[:, fc * 128:fc * 128 + tn], nbb.t[0:tn, fc * 128:(fc + 1) * 128], identb.t[0:tn, 0:tn], [nbb.r, identb.r], [pr])
            P.copy(nxt(), nT.t[:, :, t0:t0 + tn], pt[:, :].rearrange("p (f t) -> p f t", f=8)[:, :, 0:tn], [pr], [nT.r])

    def fm_group(slot, i, T, rhsbuf, rhs_r):
        pt, pr = PSF()
        for kc in range(8):
            P.mm(pt[:, 0:T], slot.t[:, kc * 512 + i * 128:kc * 512 + (i + 1) * 128], rhsbuf.t[:, kc, 0:T], kc == 0, kc == 7,
                 [slot.r, rhs_r], [pr])
        return pt, pr

    def tm_group(slot, lbuf, l_r, t0, tn):
        pt, pr = PSF()
        for kc in range(8):
            P.mm(pt[0:tn, :], lbuf.t[:, kc, t0:t0 + tn], slot.t[:, kc * 512:(kc + 1) * 512], kc == 0, kc == 7, [slot.r, l_r], [pr])
        return pt, pr

    def ffn(i, T, tbs):
        norm_to_nT(T, tbs)
        for g in range(NJ // 2):
            slot = next_piece("ffn_in")
            for jj in range(2):
                j = 2 * g + jj
                pg, pgr = fm_group(slot, 2 * jj, T, nT, nT.r)
                pu, pur = fm_group(slot, 2 * jj + 1, T, nT, nT.r)
                sl = sil[j % 2]
                P.act(sl.t[:, 0:T], pg[:, 0:T], AF.Silu, [pgr], [sl.r])
                P.tt("dve", actT.t[:, j, 0:T], sl.t[:, 0:T], pu[:, 0:T], ALU.mult, [sl.r, pur], [actT.r])
        for ch in range(2):
            pts = [PSF() for _ in tbs]
            for (j0, nj) in ((0, 8), (8, 8), (16, 6)):
                slot = next_piece("ffn_out")
                for jj in range(nj):
                    j = j0 + jj
                    for tb, (t0, tn) in enumerate(tbs):
                        P.mm(pts[tb][0][0:tn, :], actT.t[:, j, t0:t0 + tn], slot.t[:, jj * 512:(jj + 1) * 512], j == 0, j == NJ - 1,
                             [slot.r, actT.r], [pts[tb][1]])
            for tb, (t0, tn) in enumerate(tbs):
                hh = H[tb].t[0:tn, ch * 512:(ch + 1) * 512]
                P.stt("dve", hh, pts[tb][0][0:tn, :], 0.5, hh, ALU.mult, ALU.add, [pts[tb][1], H[tb].r], [H[tb].r])

    def mixer(T, tbs, C, tok0):
        NCH = T // C
        nlev = {128: 5, 16: 3}[C]
        zs_c = CC_ZS128 if C == 128 else CC_ZS16
        cd_c = CC_CD128 if C == 128 else CC_CD16
        norm_to_nT(T, tbs)
        for tb, (t0, tn) in enumerate(tbs):
            P.dma(rope.t[0:tn, tb, :], rope_d[tok0 + t0:tok0 + t0 + tn, :], [], [rope.r])
        pt, pr = PSF()
        for n in range(NCH):
            for kc in range(8):
                P.mm(pt[0:C, n * 16:(n + 1) * 16], nT.t[:, kc, n * C:(n + 1) * C], wba.t[:, kc, :], kc == 0, kc == 7, [nT.r, wba.r], [pr])
        P.copy("act", braw.t[0:C, 0:NCH, :], pt[0:C, 0:NCH * 16].rearrange("p (n c) -> p n c", c=16), [pr], [braw.r])
        P.act(btok.t[0:C, 0:NCH, :], braw.t[0:C, 0:NCH, 0:8], AF.Exp, [braw.r], [btok.r], scale=-1.0)
        P.ts("dve", btok.t[0:C, 0:NCH, :], btok.t[0:C, 0:NCH, :], 1.0, None, ALU.add, None, [btok.r], [btok.r])
        P.op("dve", lambda e: e.reciprocal(out=btok.t[0:C, 0:NCH, :], in_=btok.t[0:C, 0:NCH, :]), [btok.r], [btok.r])
        P.tt("dve", batmp.t[0:C, 0:NCH, :], braw.t[0:C, 0:NCH, 8:16], prm.t[0:C, PC_DTB:PC_DTB + 8].unsqueeze(1).broadcast_to([C, NCH, 8]),
             ALU.add, [braw.r, prm.r], [batmp.r])
        P.act(batmp.t[0:C, 0:NCH, :], batmp.t[0:C, 0:NCH, :], AF.Exp, [batmp.r], [batmp.r])
        P.act(batmp.t[0:C, 0:NCH, :], batmp.t[0:C, 0:NCH, :], AF.Ln, [batmp.r], [batmp.r], bias=one_ap(C))
        P.tt("dve", gtok.t[0:C, 0:NCH, :], batmp.t[0:C, 0:NCH, :], nA.t[0:C, :].unsqueeze(1).broadcast_to([C, NCH, 8]), ALU.mult,
             [batmp.r, nA.r], [gtok.r])

        for hg in range(2):
            pj = []
            lag = 2
            slots = {}

            def get_slot(nm, first):
                if first:
                    slots[nm] = next_piece(nm, lag)
                return slots[nm]

            def job_conv(nm, dst, qi, i, cs, hg=hg):
                slot = get_slot(nm, i == 0)
                cch = qi * 8 + hg * 4 + i
                pt, pr = yield from gPSF()
                for kc in range(8):
                    P.mm(pt[:, 0:T], slot.t[:, kc * 512 + i * 128:kc * 512 + (i + 1) * 128], nT.t[:, kc, 0:T], kc == 0, kc == 7, [slot.r, nT.r], [pr])
                yield
                ci = cin[cs]
                P.copy("act", ci.t[:, 3:3 + T], pt[:, 0:T], [pr], [ci.r])
                PFREE(pr)
                P.copy("pool", ci.t[:, 0:3], ctail.t[:, cch, :], [ctail.r], [ci.r])
                yield
                P.copy("pool", ctail.t[:, cch, :], ci.t[:, T:T + 3], [ci.r], [ctail.r])
                ca = cacc[cs]
                wcol = PC_CONV + cch * 4
                P.act(ca.t[:, 0:T], ci.t[:, 0:T], AF.Identity, [ci.r, prm.r], [ca.r], scale=pc(wcol))
                yield
                for tap in range(1, 4):
                    P.stt("dve", ca.t[:, 0:T], ci.t[:, tap:tap + T], pc(wcol + tap), ca.t[:, 0:T], ALU.mult, ALU.add,
                          [ci.r, prm.r, ca.r], [ca.r])
                yield
                if nm == "v":
                    P.act(dst.t[:, i, 0:T], ca.t[:, 0:T], AF.Silu, [ca.r], [dst.r])
                    yield
                    return
                f = cf[cs]
                P.act(f.t[:, 0:T], ca.t[:, 0:T], AF.Silu, [ca.r], [f.r])
                yield
                sq = csq[cs]
                P.tt("pool", sq.t[:, 0:T], f.t[:, 0:T], f.t[:, 0:T], ALU.mult, [f.r], [sq.r])
                yield
                p2, p2r = yield from gPSF()
                P.mm(p2[:, 0:T], onesb.t[:, :], sq.t[:, 0:T], True, True, [onesb.r, sq.r], [p2r])
                yield
                rs_ = crs[cs]
                P.act(rs_.t[:, 0:T], p2[:, 0:T], AF.Ln, [p2r], [rs_.r], bias=eps_ap())
                PFREE(p2r)
                P.act(rs_.t[:, 0:T], rs_.t[:, 0:T], AF.Exp, [rs_.r], [rs_.r], scale=-0.5)
                yield
                if nm == "q":
                    P.stt("dve", dst.t[:, i, 0:T], f.t[:, 0:T], 128.0 ** -0.5, rs_.t[:, 0:T], ALU.mult, ALU.mult, [f.r, rs_.r], [dst.r])
                else:
                    P.tt("dve", dst.t[:, i, 0:T], f.t[:, 0:T], rs_.t[:, 0:T], ALU.mult, [f.r, rs_.r], [dst.r])
                yield

            def job_silu(nm, dst, i):
                slot = get_slot(nm, i == 0)
                pt, pr = yield from gPSF()
                for kc in range(8):
                    P.mm(pt[:, 0:T], slot.t[:, kc * 512 + i * 128:kc * 512 + (i + 1) * 128], nT.t[:, kc, 0:T], kc == 0, kc == 7, [slot.r, nT.r], [pr])
                yield
                P.act(dst.t[:, i, 0:T], pt[:, 0:T], AF.Silu, [pr], [dst.r])
                PFREE(pr)
                yield

            def job_rot(nm, dst, tb, t0, tn, rs):
                slot = get_slot(nm, tb == 0)
                pt, pr = yield from gPSF()
                for kc in range(8):
                    P.mm(pt[0:tn, :], nT.t[:, kc, t0:t0 + tn], slot.t[:, kc * 512:(kc + 1) * 512], kc == 0, kc == 7, [slot.r, nT.r], [pr])
                yield
                xs = rxs[rs]
                P.copy("act", xs.t[0:tn, :], pt[0:tn, :], [pr], [xs.r])
                PFREE(pr)
                yield
                xv = xs.t[0:tn, :].rearrange("p (h j two) -> p h j two", h=4, two=2)
                x0 = xv[:, :, :, 0]
                x1 = xv[:, :, :, 1]
                cosb = rope.t[0:tn, tb, 0:64].unsqueeze(1).broadcast_to([tn, 4, 64])
                sinb = rope.t[0:tn, tb, 64:128].unsqueeze(1).broadcast_to([tn, 4, 64])
                ta, tb_ = rt[rs]
                t1 = ta.t[0:tn, :].rearrange("p (h j) -> p h j", h=4)
                t2 = tb_.t[0:tn, :].rearrange("p (h j) -> p h j", h=4)
                ov = dst.t[0:tn, tb, :].rearrange("p (h j two) -> p h j two", h=4, two=2)
                P.tt("dve", t1, x0, cosb, ALU.mult, [xs.r, rope.r], [ta.r])
                P.tt("pool", t2, x1, sinb, ALU.mult, [xs.r, rope.r], [tb_.r])
                yield
                P.tt("dve", ov[:, :, :, 0], t1, t2, ALU.subtract, [ta.r, tb_.r], [dst.r])
                yield
                P.tt("dve", t1, x1, cosb, ALU.mult, [xs.r, rope.r], [ta.r])
                P.tt("pool", t2, x0, sinb, ALU.mult, [xs.r, rope.r], [tb_.r])
                yield
                P.tt("pool", ov[:, :, :, 1], t1, t2, ALU.add, [ta.r, tb_.r], [dst.r])
                yield

            def job_rv(tb, t0, tn):
                slot = get_slot("rv", tb == 0)
                pt, pr = yield from gPSF()
                for kc in range(8):
                    P.mm(pt[0:tn, :], nT.t[:, kc, t0:t0 + tn], slot.t[:, kc * 512:(kc + 1) * 512], kc == 0, kc == 7, [slot.r, nT.r], [pr])
                yield
                P.copy("act", rv_tm.t[0:tn, tb, :], pt[0:tn, :], [pr], [rv_tm.r])
                PFREE(pr)
                yield

            def job_tr(src, dst, scl, tb, t0, tn, hg=hg):
                if src is rk_tm:
                    P.tt("dve", rkd_tm.t[0:tn, tb, :].rearrange("p (h d) -> p h d", h=4),
                         rk_tm.t[0:tn, tb, :].rearrange("p (h d) -> p h d", h=4),
                         cst.t[0:tn, zs_c + hg * 4:zs_c + hg * 4 + 4].unsqueeze(2).broadcast_to([tn, 4, 128]), ALU.mult,
                         [rk_tm.r, cst.r], [rkd_tm.r])
                pt, pr = yield from gPSB()
                for i in range(4):
                    P.tr(pt[:, i * 128:i * 128 + tn], src.t[0:tn, tb, i * 128:(i + 1) * 128], identb.t[0:tn, 0:tn], [src.r, identb.r], [pr])
                yield
                pv = pt[:, 0:512].rearrange("p (h t) -> p h t", h=4)[:, :, 0:tn]
                if scl is None:
                    P.copy("dve", dst.t[:, :, t0:t0 + tn], pv, [pr], [dst.r])
                else:
                    P.act(dst.t[:, :, t0:t0 + tn], pv, AF.Identity, [pr], [dst.r], scale=scl)
                PFREE(pr)
                yield

            cnt = 0
            for qi, (nm, dst) in enumerate((("q", qT), ("k", kT), ("v", vT))):
                for i in range(4):
                    deps = ["conv%d" % (cnt - NCS)] if cnt >= NCS else []
                    pj.append(("conv%d" % cnt, (lambda nm=nm, dst=dst, qi=qi, i=i, cs=cnt % NCS: job_conv(nm, dst, qi, i, cs)), deps))
                    cnt += 1
            for nm, dst in (("z", szT), ("rg", srgT)):
                for i in range(4):
                    pj.append(("silu_%s%d" % (nm, i), (lambda nm=nm, dst=dst, i=i: job_silu(nm, dst, i)), []))
            rcnt = 0
            for nm, dst in (("rq", rq_tm), ("rk", rk_tm)):
                for tb, (t0, tn) in enumerate(tbs):
                    deps = ["rot%d" % (rcnt - 2)] if rcnt >= 2 else []
                    pj.append(("rot%d" % rcnt, (lambda nm=nm, dst=dst, tb=tb, t0=t0, tn=tn, rs=rcnt % 2: job_rot(nm, dst, tb, t0, tn, rs)), deps))
                    rcnt += 1
            for tb, (t0, tn) in enumerate(tbs):
                pj.append(("rv%d" % tb, (lambda tb=tb, t0=t0, tn=tn: job_rv(tb, t0, tn)), []))
            ntb_ = len(tbs)
            for si, (src, dst, scl) in enumerate(((rq_tm, rqT, None), (rk_tm, rkT, 128.0 ** -0.5))):
                for tb, (t0, tn) in enumerate(tbs):
                    pj.append(("tr%d_%d" % (si, tb), (lambda src=src, dst=dst, scl=scl, tb=tb, t0=t0, tn=tn: job_tr(src, dst, scl, tb, t0, tn)),
                               ["rot%d" % (si * ntb_ + tb)]))
            run_jobs(pj, 5 if ntb_ > 1 else 1, True)
            for i in range(4):
                h = hg * 4 + i
                P.tt("pool", rqdT.t[:, i, 0:T].rearrange("p (n c) -> p n c", c=C), rqT.t[:, i, 0:T].rearrange("p (n c) -> p n c", c=C),
                     cst.t[:, CC_XI + h * 128:CC_XI + h * 128 + C].unsqueeze(1).broadcast_to([128, NCH, C]), ALU.mult, [rqT.r, cst.r], [rqdT.r])

            hs = slice(hg * 4, hg * 4 + 4)

            def h3(ap, w):
                return ap.rearrange("p (h c) -> p h c", h=4)

            def gen_prep(n, sub, hg=hg):
                K = cksub[sub][n % 2]
                c0 = n * C
                li = (2 * sub, 2 * sub + 1)
                hcs = slice(hg * 4 + 2 * sub, hg * 4 + 2 * sub + 2)

                def bc2(ap2):
                    return ap2.unsqueeze(2).broadcast_to([C, 2, C])

                def bcd(ap2):
                    return ap2.unsqueeze(2).broadcast_to([C, 2, 128])

                def v3(b):
                    return b.t[0:C, :, 0:C]

                def g3(ap):
                    return ap.rearrange("p (h c) -> p h c", h=2)

                def tab(c0_):
                    return cst.t[0:C, c0_:c0_ + C].unsqueeze(1).broadcast_to([C, 2, C])

                Ub, MUI, MLS, MLO = tab(CC_U), tab(CC_MUI), tab(CC_MLS), tab(CC_MLO)
                Idb = identb.t[0:C, 0:C].unsqueeze(1).broadcast_to([C, 2, C])
                two = (C == 128)
                P.tt("pool", v3(K["GU"]), Ub, bc2(gtok.t[0:C, n, hcs]), ALU.mult, [cst.r, gtok.r], [K["GU"].r])
                (pG, pGr), (pS, pSr) = yield from gPSF(2)
                for ii in range(2):
                    P.mm(pG[:, ii * C:(ii + 1) * C], cst.t[0:C, CC_ONES:CC_ONES + 128], K["GU"].t[0:C, ii, 0:C], True, True, [cst.r, K["GU"].r], [pGr])
                P.mm(pS[0:C, 0:2], cst.t[0:C, CC_U:CC_U + C], gtok.t[0:C, n, hcs], True, True, [cst.r, gtok.r], [pSr])
                P.mm(pS[:, 2:4], cst.t[0:C, CC_ONES:CC_ONES + 128], gtok.t[0:C, n, hcs], True, True, [cst.r, gtok.r], [pSr])
                yield
                gcs = K["gcs"]
                P.copy("act", gcs.t[:, 2:4], pS[:, 2:4], [pSr], [gcs.r])
                P.copy("act", gcs.t[0:C, 0:2], pS[0:C, 0:2], [pSr], [gcs.r])
                PFREE(pSr)
                P.act(K["EGQ"].t[:, :, 0:C], g3(pG[:, 0:2 * C]), AF.Exp, [pGr], [K["EGQ"].r])
                yield
                P.tt("dve", v3(K["Dm"]), g3(pG[0:C, 0:2 * C]), bc2(gcs.t[0:C, 0:2]), ALU.subtract, [pGr, gcs.r], [K["Dm"].r])
                PFREE(pGr)
                P.act(K["eg"].t[0:C, :], gcs.t[0:C, 0:2], AF.Exp, [gcs.r], [K["eg"].r])
                P.tt("pool", K["ekd"].t[0:C, :], gcs.t[0:C, 2:4], gcs.t[0:C, 0:2], ALU.subtract, [gcs.r], [K["ekd"].r])
                P.act(K["egl"].t[:, :], gcs.t[:, 2:4], AF.Exp, [gcs.r], [K["egl"].r])
                P.tt("pool", K["qdT"].t[:, :, 0:C], qT.t[:, li[0]:li[1] + 1, c0:c0 + C], K["EGQ"].t[:, :, 0:C], ALU.mult, [qT.r, K["EGQ"].r], [K["qdT"].r])
                yield
                P.ts("dve", v3(K["El"]), v3(K["Dm"]), 0.0, None, ALU.max, None, [K["Dm"].r], [K["El"].r])
                P.ts("dve", v3(K["Eu"]), v3(K["Dm"]), 0.0, None, ALU.min, None, [K["Dm"].r], [K["Eu"].r])
                P.act(K["ekd"].t[0:C, :], K["ekd"].t[0:C, :], AF.Exp, [K["ekd"].r], [K["ekd"].r])
                P.tt("pool", K["beg"].t[0:C, :], K["eg"].t[0:C, :], btok.t[0:C, n, hcs], ALU.mult, [K["eg"].r, btok.r], [K["beg"].r])
                ptkv, ptkvr = yield from gPSB()
                for ii in range(2):
                    P.tr(ptkv[0:C, ii * 128:(ii + 1) * 128], kT.t[:, li[ii], c0:c0 + C], identb.t[:, :], [kT.r, identb.r], [ptkvr])
                for ii in range(2):
                    P.tr(ptkv[0:C, 256 + ii * 128:256 + (ii + 1) * 128], vT.t[:, li[ii], c0:c0 + C], identb.t[:, :], [vT.r, identb.r], [ptkvr])
                yield
                P.act(v3(K["El"]), v3(K["El"]), AF.Exp, [K["El"].r], [K["El"].r], scale=-1.0)
                P.act(v3(K["Eu"]), v3(K["Eu"]), AF.Exp, [K["Eu"].r], [K["Eu"].r])
                ktv = ptkv[0:C, 0:256].rearrange("p (h d) -> p h d", h=2)
                vtv = ptkv[0:C, 256:512].rearrange("p (h d) -> p h d", h=2)
                P.tt("dve", K["kd"].t[0:C, :, :], ktv, bcd(K["ekd"].t[0:C, :]), ALU.mult, [ptkvr, K["ekd"].r], [K["kd"].r])
                P.tt("dve", K["kbg"].t[0:C, :, :], ktv, bcd(K["beg"].t[0:C, :]), ALU.mult, [ptkvr, K["beg"].r], [K["kbg"].r])
                P.tt("dve", K["bv"].t[0:C, :, :], vtv, bcd(btok.t[0:C, n, hcs]), ALU.mult, [ptkvr, btok.r], [K["bv"].r])
                PFREE(ptkvr)
                (pK, pKr), (pQ, pQr) = yield from gPSF(2)
                for ii in range(2):
                    P.mm(pQ[0:C, ii * C:(ii + 1) * C], kT.t[:, li[ii], c0:c0 + C], qT.t[:, li[ii], c0:c0 + C], True, True, [kT.r, qT.r], [pQr])
                for ii in range(2):
                    P.mm(pK[0:C, ii * C:(ii + 1) * C], kT.t[:, li[ii], c0:c0 + C], kT.t[:, li[ii], c0:c0 + C], True, True, [kT.r], [pKr])
                yield
                P.tt("pool", v3(K["EQ"]), v3(K["Eu"]), MUI, ALU.mult, [K["Eu"].r, cst.r], [K["EQ"].r])
                P.tt("pool", v3(K["El"]), v3(K["El"]), bc2(btok.t[0:C, n, hcs]), ALU.mult, [K["El"].r, btok.r], [K["El"].r])
                yield
                P.tt("dve", v3(K["PT"]), g3(pQ[0:C, 0:2 * C]), v3(K["EQ"]), ALU.mult, [pQr, K["EQ"].r], [K["PT"].r])
                PFREE(pQr)
                if two:
                    P.tt("pool", v3(K["EAo"]), v3(K["El"]), MLO, ALU.mult, [K["El"].r, cst.r], [K["EAo"].r])
                P.tt("pool", v3(K["EA"]), v3(K["El"]), MLS, ALU.mult, [K["El"].r, cst.r], [K["EA"].r])
                yield
                P.tt("dve", v3(K["Ain"]), g3(pK[0:C, 0:2 * C]), v3(K["EA"]), ALU.mult, [pKr, K["EA"].r], [K["Ain"].r])
                if two:
                    P.tt("dve", v3(K["Ao"]), g3(pK[0:C, 0:2 * C]), v3(K["EAo"]), ALU.mult, [pKr, K["EAo"].r], [K["Ao"].r])
                PFREE(pKr)
                yield

            def gen_prepB(n, sub, hg=hg):
                K = cksub[sub][n % 2]

                def v3(b):
                    return b.t[0:C, :, 0:C]

                def g3(ap):
                    return ap.rearrange("p (h c) -> p h c", h=2)

                Idb = identb.t[0:C, 0:C].unsqueeze(1).broadcast_to([C, 2, C])
                two = (C == 128)
                ptb, ptbr = yield from gPSB()
                for ii in range(2):
                    P.tr(ptb[0:C, ii * C:(ii + 1) * C], K["Ain"].t[0:C, ii, 0:C], identb.t[0:C, 0:C], [K["Ain"].r, identb.r], [ptbr])
                yield
                P.copy("act", v3(K["B0"]), g3(ptb[0:C, 0:2 * C]), [ptbr], [K["B0"].r])
                P.tt("dve", v3(K["X0"]), Idb, g3(ptb[0:C, 0:2 * C]), ALU.subtract, [identb.r, ptbr], [K["X0"].r])
                PFREE(ptbr)
                yield
                for k in range(nlev + 1):
                    Ak, Bk = (K["Ain"] if k == 0 else K["A%d" % (k % 2)]), K["B%d" % (k % 2)]
                    An, Bn = K["A%d" % ((k + 1) % 2)], K["B%d" % ((k + 1) % 2)]
                    Xp, Xn = K["X%d" % ((k + 1) % 2)], K["X%d" % (k % 2)]
                    need = (1 if k < nlev else 0) + (1 if k < nlev - 1 else 0) + (1 if k >= 1 else 0)
                    got = yield from gPSF(need)
                    if need == 1:
                        got = [got]
                    got = list(got)
                    pA = pB2 = pX = None
                    if k < nlev:
                        pA, pAr = got.pop(0)
                        for ii in range(2):
                            P.mm(pA[0:C, ii * C:(ii + 1) * C], Bk.t[0:C, ii, 0:C], Ak.t[0:C, ii, 0:C], True, True, [Bk.r, Ak.r], [pAr])
                    if k < nlev - 1:
                        pB2, pB2r = got.pop(0)
                        for ii in range(2):
                            P.mm(pB2[0:C, ii * C:(ii + 1) * C], Ak.t[0:C, ii, 0:C], Bk.t[0:C, ii, 0:C], True, True, [Bk.r, Ak.r], [pB2r])
                    if k >= 1:
                        pX, pXr = got.pop(0)
                        for ii in range(2):
                            P.mm(pX[0:C, ii * C:(ii + 1) * C], Ak.t[0:C, ii, 0:C], Xp.t[0:C, ii, 0:C], True, True, [Ak.r, Xp.r], [pXr])
                    yield
                    if pA is not None:
                        P.copy("act", v3(An), g3(pA[0:C, 0:2 * C]), [pAr], [An.r])
                        PFREE(pAr)
                    if pB2 is not None:
                        P.copy("act", v3(Bn), g3(pB2[0:C, 0:2 * C]), [pB2r], [Bn.r])
                        PFREE(pB2r)
                    if pX is not None:
                        P.tt("dve", v3(Xn), g3(pX[0:C, 0:2 * C]), v3(Xp), ALU.add, [pXr, Xp.r], [Xn.r])
                        PFREE(pXr)
                    yield
                TT = K["X%d" % (nlev % 2)]
                if two:
                    (pY, pYr), (pZ, pZr) = yield from gPSF(2)
                    for ii in range(2):
                        P.mm(pY[0:C, ii * C:(ii + 1) * C], K["Ao"].t[0:C, ii, 0:C], TT.t[0:C, ii, 0:C], True, True, [K["Ao"].r, TT.r], [pYr])
                    for ii in range(2):
                        P.mm(pZ[0:C, ii * 128:(ii + 1) * 128], TT.t[0:C, ii, 0:C], K["bv"].t[0:C, ii, :], True, True, [TT.r, K["bv"].r], [pZr])
                    for ii in range(2):
                        P.mm(pZ[0:C, 256 + ii * 128:256 + (ii + 1) * 128], TT.t[0:C, ii, 0:C], K["kbg"].t[0:C, ii, :], True, True, [TT.r, K["kbg"].r], [pZr])
                    yield
                    P.tt("dve", v3(K["Y"]), Idb, g3(pY[0:C, 0:2 * C]), ALU.subtract, [identb.r, pYr], [K["Y"].r])
                    PFREE(pYr)
                    P.copy("act", K["sz"].t[0:C, :, :], pZ[0:C, 0:512].rearrange("p (h d) -> p h d", h=4), [pZr], [K["sz"].r])
                    PFREE(pZr)
                    yield
                    (pU, pUr), (pW, pWr) = yield from gPSF(2)
                    for ii in range(2):
                        P.mm(pU[0:C, ii * 128:(ii + 1) * 128], K["Y"].t[0:C, ii, 0:C], K["sz"].t[0:C, ii, :], True, True, [K["Y"].r, K["sz"].r], [pUr])
                    for ii in range(2):
                        P.mm(pW[:, ii * C:(ii + 1) * C], K["sz"].t[0:C, 2 + ii, :], K["Y"].t[0:C, ii, 0:C], True, True, [K["Y"].r, K["sz"].r], [pWr])
                else:
                    (pU, pUr), (pW, pWr) = yield from gPSF(2)
                    for ii in range(2):
                        P.mm(pU[0:C, ii * 128:(ii + 1) * 128], TT.t[0:C, ii, 0:C], K["bv"].t[0:C, ii, :], True, True, [TT.r, K["bv"].r], [pUr])
                    for ii in range(2):
                        P.mm(pW[:, ii * C:(ii + 1) * C], K["kbg"].t[0:C, ii, :], TT.t[0:C, ii, 0:C], True, True, [TT.r, K["kbg"].r], [pWr])
                yield
                P.copy("act", K["u"].t[0:C, :, :], pU[0:C, 0:256].rearrange("p (h d) -> p h d", h=2), [pUr], [K["u"].r])
                PFREE(pUr)
                P.copy("dve", K["wT"].t[:, :, 0:C], g3(pW[:, 0:2 * C]), [pWr], [K["wT"].r])
                PFREE(pWr)
                yield

            def gen_scan(n, sub, hg=hg):
                K = cksub[sub][n % 2]
                c0 = n * C
                li = (2 * sub, 2 * sub + 1)
                hgl = (hg * 4 + 2 * sub, hg * 4 + 2 * sub + 1)

                def bcd(ap2):
                    return ap2.unsqueeze(2).broadcast_to([C, 2, 128])

                def g3(ap):
                    return ap.rearrange("p (h c) -> p h c", h=2)

                p1, p1r = yield from gPSF()
                for ii in range(2):
                    h = hgl[ii]
                    P.mm(p1[0:C, ii * 128:(ii + 1) * 128], K["wT"].t[:, ii, 0:C], Sgb[h].t[:, :], True, True, [K["wT"].r, Sgb[h].r], [p1r])
                yield
                P.tt("dve", K["vnew"].t[0:C, :, :], K["u"].t[0:C, :, :], p1[0:C, 0:256].rearrange("p (h d) -> p h d", h=2), ALU.subtract,
                     [K["u"].r, p1r], [K["vnew"].r])
                PFREE(p1r)
                yield
                (pO, pOr), (pSS, pSSr) = yield from gPSF(2)
                for ii in range(2):
                    P.mm(pSS[:, ii * 128:(ii + 1) * 128], K["kd"].t[0:C, ii, :], K["vnew"].t[0:C, ii, :], True, True, [K["kd"].r, K["vnew"].r], [pSSr])
                for ii in range(2):
                    h = hgl[ii]
                    P.mm(pO[0:C, ii * 128:(ii + 1) * 128], K["qdT"].t[:, ii, 0:C], Sgb[h].t[:, :], True, False, [K["qdT"].r, Sgb[h].r], [pOr])
                    P.mm(pO[0:C, ii * 128:(ii + 1) * 128], K["PT"].t[0:C, ii, 0:C], K["vnew"].t[0:C, ii, :], False, True, [K["PT"].r, K["vnew"].r], [pOr])
                yield
                for ii in range(2):
                    h = hgl[ii]
                    P.stt("dve", Sg[h].t[:, :], Sg[h].t[:, :], K["egl"].t[:, ii:ii + 1], pSS[:, ii * 128:(ii + 1) * 128], ALU.mult, ALU.add,
                          [Sg[h].r, K["egl"].r, pSSr], [Sg[h].r])
                PFREE(pSSr)
                P.copy("act", K["osb"].t[0:C, :, :], pO[0:C, 0:256].rearrange("p (h d) -> p h d", h=2), [pOr], [K["osb"].r])
                PFREE(pOr)
                yield
                for ii in range(2):
                    h = hgl[ii]
                    P.copy("act" if ii == 0 else "pool", Sgb[h].t[:, :], Sg[h].t[:, :], [Sg[h].r], [Sgb[h].r])
                P.tt("pool", K["osq"].t[0:C, :, :], K["osb"].t[0:C, :, :], K["osb"].t[0:C, :, :], ALU.mult, [K["osb"].r], [K["osq"].r])
                yield
                P.red("dve", K["ost"].t[0:C, 0:2], K["osq"].t[0:C, :, :], [K["osq"].r], [K["ost"].r])
                yield
                P.act(K["ost"].t[0:C, 2:4], K["ost"].t[0:C, 0:2], AF.Ln, [K["ost"].r], [K["ost"].r], scale=1.0 / 128, bias=eps_ap(C))
                P.act(K["ost"].t[0:C, 2:4], K["ost"].t[0:C, 2:4], AF.Exp, [K["ost"].r], [K["ost"].r], scale=-0.5)
                yield
                P.tt("pool", K["on"].t[0:C, :, :], K["osb"].t[0:C, :, :], bcd(K["ost"].t[0:C, 2:4]), ALU.mult, [K["osb"].r, K["ost"].r], [K["on"].r])
                yield
                pt, pr = yield from gPSB()
                for ii in range(2):
                    P.tr(pt[:, ii * C:(ii + 1) * C], K["on"].t[0:C, ii, :], identb.t[0:C, 0:C], [K["on"].r, identb.r], [pr])
                yield
                P.stt("dve", yaT.t[:, hgl[0]:hgl[1] + 1, c0:c0 + C], g3(pt[:, 0:2 * C]), pc(PC_GDNN), szT.t[:, li[0]:li[1] + 1, c0:c0 + C],
                      ALU.mult, ALU.mult, [pr, prm.r, szT.r], [yaT.r])
                PFREE(pr)
                yield

            def gen_ret(n, sub, hg=hg):
                K = cksub[sub][n % 2]
                c0 = n * C
                tb = c0 // 128
                li = (2 * sub, 2 * sub + 1)
                hgl = (hg * 4 + 2 * sub, hg * 4 + 2 * sub + 1)

                def bcd(ap2):
                    return ap2.unsqueeze(2).broadcast_to([C, 2, 128])

                def g3(ap):
                    return ap.rearrange("p (h c) -> p h c", h=2)

                pSc, pScr = yield from gPSF()
                for ii in range(2):
                    i = li[ii]
                    P.mm(pSc[0:C, ii * C:(ii + 1) * C], rkT.t[:, i, c0:c0 + C], rqT.t[:, i, c0:c0 + C], True, True, [rkT.r, rqT.r], [pScr])
                yield
                P.tt("dve", K["scT"].t[0:C, :, 0:C], g3(pSc[0:C, 0:2 * C]),
                     cst.t[0:C, CC_DRT + hgl[0] * 128:CC_DRT + hgl[0] * 128 + 256].rearrange("p (h c) -> p h c", h=2)[:, :, 0:C], ALU.mult,
                     [pScr, cst.r], [K["scT"].r])
                PFREE(pScr)
                yield
                (pOr2, pOr2r), (pSR, pSRr) = yield from gPSF(2)
                for ii in range(2):
                    i = li[ii]
                    h = hgl[ii]
                    P.mm(pOr2[0:C, ii * 128:(ii + 1) * 128], rqdT.t[:, i, c0:c0 + C], Srb[h].t[:, :], True, False, [rqdT.r, Srb[h].r], [pOr2r])
                    P.mm(pOr2[0:C, ii * 128:(ii + 1) * 128], K["scT"].t[0:C, ii, 0:C], rv_tm.t[0:C, tb, i * 128:(i + 1) * 128], False, True,
                         [K["scT"].r, rv_tm.r], [pOr2r])
                for ii in range(2):
                    i = li[ii]
                    P.mm(pSR[:, ii * 128:(ii + 1) * 128], rkd_tm.t[0:C, tb, i * 128:(i + 1) * 128], rv_tm.t[0:C, tb, i * 128:(i + 1) * 128], True, True,
                         [rkd_tm.r, rv_tm.r], [pSRr])
                yield
                for ii in range(2):
                    h = hgl[ii]
                    P.stt("dve", Sr[h].t[:, :], Sr[h].t[:, :], cst.t[:, cd_c + h:cd_c + h + 1], pSR[:, ii * 128:(ii + 1) * 128], ALU.mult, ALU.add,
                          [Sr[h].r, cst.r, pSRr], [Sr[h].r])
                PFREE(pSRr)
                P.copy("act", K["orb"].t[0:C, :, :], pOr2[0:C, 0:256].rearrange("p (h d) -> p h d", h=2), [pOr2r], [K["orb"].r])
                PFREE(pOr2r)
                yield
                for ii in range(2):
                    h = hgl[ii]
                    P.copy("act" if ii == 1 else "pool", Srb[h].t[:, :], Sr[h].t[:, :], [Sr[h].r], [Srb[h].r])
                P.red("dve", K["ort"].t[0:C, 0:2], K["orb"].t[0:C, :, :], [K["orb"].r], [K["ort"].r])
                yield
                P.ts("dve", K["ort"].t[0:C, 0:2], K["ort"].t[0:C, 0:2], 1.0 / 128, None, ALU.mult, None, [K["ort"].r], [K["ort"].r])
                yield
                P.tt("pool", K["orc"].t[0:C, :, :], K["orb"].t[0:C, :, :], bcd(K["ort"].t[0:C, 0:2]), ALU.subtract, [K["orb"].r, K["ort"].r], [K["orc"].r])
                yield
                P.tt("pool", K["orq"].t[0:C, :, :], K["orc"].t[0:C, :, :], K["orc"].t[0:C, :, :], ALU.mult, [K["orc"].r], [K["orq"].r])
                yield
                P.red("dve", K["ort"].t[0:C, 2:4], K["orq"].t[0:C, :, :], [K["orq"].r], [K["ort"].r])
                yield
                P.act(K["ort"].t[0:C, 2:4], K["ort"].t[0:C, 2:4], AF.Ln, [K["ort"].r], [K["ort"].r], scale=1.0 / 128, bias=eps_ap(C))
                P.act(K["ort"].t[0:C, 2:4], K["ort"].t[0:C, 2:4], AF.Exp, [K["ort"].r], [K["ort"].r], scale=-0.5)
                yield
                P.tt("pool", K["orn"].t[0:C, :, :], K["orc"].t[0:C, :, :], bcd(K["ort"].t[0:C, 2:4]), ALU.mult, [K["orc"].r, K["ort"].r], [K["orn"].r])
                yield
                pt, pr = yield from gPSB()
                for ii in range(2):
                    P.tr(pt[:, ii * C:(ii + 1) * C], K["orn"].t[0:C, ii, :], identb.t[0:C, 0:C], [K["orn"].r, identb.r], [pr])
                yield
                P.tt("dve", K["ybt"].t[:, :, 0:C], g3(pt[:, 0:2 * C]),
                     prm.t[:, PC_RETN + hgl[0]:PC_RETN + hgl[0] + 2].unsqueeze(2).broadcast_to([128, 2, C]), ALU.mult, [pr, prm.r], [K["ybt"].r])
                PFREE(pr)
                yield
                P.tt("pool", ybT.t[:, hgl[0]:hgl[1] + 1, c0:c0 + C], K["ybt"].t[:, :, 0:C], srgT.t[:, li[0]:li[1] + 1, c0:c0 + C], ALU.mult,
                     [K["ybt"].r, srgT.r], [ybT.r])
                yield

            cj = []
            for n in range(NCH):
                for sub in range(2):
                    d = []
                    if n >= 1:
                        d.append("prep%d_%d" % (n - 1, sub))
                    if n >= 2:
                        d.append("scan%d_%d" % (n - 2, sub))
                        d.append("prepB%d_%d" % (n - 2, sub))
                    cj.append(("prep%d_%d" % (n, sub), (lambda n=n, sub=sub: gen_prep(n, sub)), d))
                for sub in range(2):
                    d = ["prep%d_%d" % (n, sub)]
                    if n >= 1:
                        d.append("prepB%d_%d" % (n - 1, sub))
                    if n >= 2:
                        d.append("scan%d_%d" % (n - 2, sub))
                    cj.append(("prepB%d_%d" % (n, sub), (lambda n=n, sub=sub: gen_prepB(n, sub)), d))
                for sub in range(2):
                    d = ["ret%d_%d" % (n - 1, sub)] if n >= 1 else []
                    cj.append(("ret%d_%d" % (n, sub), (lambda n=n, sub=sub: gen_ret(n, sub)), d))
                for sub in range(2):
                    d = ["prepB%d_%d" % (n, sub)]
                    if n >= 1:
                        d.append("scan%d_%d" % (n - 1, sub))
                    cj.append(("scan%d_%d" % (n, sub), (lambda n=n, sub=sub: gen_scan(n, sub)), d))
            run_jobs(cj, 10, False)

        for p in range(2):
            slot = next_piece("ga")
            for i in range(4):
                pt, pr = fm_group(slot, i, T, nT, nT.r)
                P.act(sgT.t[:, i, 0:T], pt[:, 0:T], AF.Sigmoid, [pr], [sgT.r])
            slot = next_piece("gb")
            for i in range(4):
                pt, pr = fm_group(slot, i, T, nT, nT.r)
                P.act(sbT.t[:, i, 0:T], pt[:, 0:T], AF.Sigmoid, [pr], [sbT.r])
            slot = next_piece("brg")
            for i in range(4):
                pt, pr = fm_group(slot, i, T, yaT, yaT.r)
                P.tt("dve", mtmp[i].t[:, 0:T], pt[:, 0:T], sgT.t[:, i, 0:T], ALU.mult, [pr, sgT.r], [mtmp[i].r])
            slot = next_piece("brr")
            for i in range(4):
                pt, pr = fm_group(slot, i, T, ybT, ybT.r)
                m2 = mtmp2[i % 2]
                P.tt("dve", m2.t[:, 0:T], pt[:, 0:T], sbT.t[:, i, 0:T], ALU.mult, [pr, sbT.r], [m2.r])
                P.tt("pool", mergedT.t[:, 4 * p + i, 0:T], m2.t[:, 0:T], mtmp[i].t[:, 0:T], ALU.add, [m2.r, mtmp[i].r], [mergedT.r])
        for ch in range(2):
            slot = next_piece("wo")
            for tb, (t0, tn) in enumerate(tbs):
                pt, pr = tm_group(slot, mergedT, mergedT.r, t0, tn)
                hh = H[tb].t[0:tn, ch * 512:(ch + 1) * 512]
                P.tt("dve", hh, pt[0:tn, :], hh, ALU.add, [pr, H[tb].r], [H[tb].r])

    def final_out(t):
        tbs = [(tb * 128, 128) for tb in range(4)]
        fins = []
        for tb in range(4):
            P.act(junk.t[:, :], H[tb].t[:, :], AF.Square, [H[tb].r], [junk.r, ss.r], accum_out=ss.t[:, tb:tb + 1])
        P.act(rstd.t[:, 0:4], ss.t[:, 0:4], AF.Ln, [ss.r], [rstd.r], scale=1.0 / D, bias=eps_ap())
        P.act(rstd.t[:, 0:4], rstd.t[:, 0:4], AF.Exp, [rstd.r], [rstd.r], scale=-0.5)
        for tb in range(4):
            P.stt("dve", OUTB[tb].t[:, :], H[tb].t[:, :], rstd.t[:, tb:tb + 1], fnw.t[:, :], ALU.mult, ALU.mult,
                  [H[tb].r, rstd.r, fnw.r], [OUTB[tb].r])
            r0 = t * 512 + tb * 128
            fins.append(P.dma(out_d[r0:r0 + 128, :], OUTB[tb].t[:, :], [OUTB[tb].r], []))
        return fins

    fin_ops = []
    tbs_m = [(0, NMETA)]
    P.dma(H[0].t[0:NMETA, :], meta_d[:, :], [], [H[0].r])
    ffn(1, NMETA, tbs_m)
    mixer(NMETA, tbs_m, 16, 0)
    ffn(2, NMETA, tbs_m)
    tbs = [(tb * 128, 128) for tb in range(4)]
    for t in range(nt):
        for tb in range(4):
            r0 = t * 512 + tb * 128
            P.dma(H[tb].t[:, :], x_d[r0:r0 + 128, :], [], [H[tb].r])
        ffn(1, 512, tbs)
        mixer(512, tbs, 128, NMETA + t * 512)
        ffn(2, 512, tbs)
        fin_ops += final_out(t)
    P.emit(fin_ops)
    if debug:
        print("ops", len(P.ops), {e: sum(1 for o in P.ops if o.eng == e) for e in ENGS})
    return nc


WNAMES = ("ffn1_w_in", "ffn1_w_out", "w_in", "w_branch_gdn", "w_branch_ret", "w_out", "ffn2_w_in", "ffn2_w_out")


def make_in_maps(inputs, nb, nt):
    W = {k: np.asarray(inputs[k], np.float32)[0] for k in WNAMES}
    wst = host_pack_weights(W)
    prm = host_pack_params({k: np.asarray(inputs[k], np.float32)[0] for k in
                            ("ffn1_norm", "mix_norm", "ffn2_norm", "ret_out_norm", "gdn_out_norm", "gdn_conv_w",
                             "gdn_a_log", "gdn_dt_bias", "w_in")})
    fnw = np.ascontiguousarray(np.broadcast_to(np.asarray(inputs["final_norm"], np.float32)[None, :], (128, D)))
    cst = host_consts()
    rope = host_rope(NMETA + nt * 512)
    meta = np.ascontiguousarray(np.asarray(inputs["meta_tokens"], np.float32))
    x = np.asarray(inputs["x"], np.float32)
    return [{"x": np.ascontiguousarray(x[b, :nt * 512]), "meta": meta, "wst": wst, "prm": prm, "fnw": fnw, "cst": cst, "rope": rope}
            for b in range(nb)]


def kernel(**inputs):
    nt = SEQ // 512
    nc = build(nt)
    in_maps = make_in_maps(inputs, 8, nt)
    res = run_bass_kernel_spmd(nc, in_maps, core_ids=list(range(8)))
    return np.stack([np.asarray(r["out"], np.float32) for r in res.results], axis=0)
```

```python
import contextlib
import numpy as np
import ml_dtypes
import concourse.bass as bass
import concourse.mybir as mybir
from concourse.bass_utils import run_bass_kernel_spmd

F32 = mybir.dt.float32
BF16 = mybir.dt.bfloat16
AF = mybir.ActivationFunctionType
ALU = mybir.AluOpType
AX = mybir.AxisListType

D = 1024
NMETA = 16
SEQ = 8192
DFF = 2816
NJ = 22
DPROJ = 10256
EPS = 1e-6
NH = 8
PIECE = 4096
NSLOT = 4
ENGS = ("pe", "act", "dve", "pool", "sp")
N_DMA_SLOTS = 12
SAME_ENGINE_SYNC = True
SB_BASE = 18432
SB_LIMIT = 229376


class Res:
    __slots__ = ("name", "last_w", "readers", "overlaps", "lo", "hi", "excl")

    def __init__(self, name, lo=None, hi=None, excl=False):
        self.name = name
        self.excl = excl
        self.last_w = None
        self.readers = {}
        self.overlaps = []
        self.lo = lo
        self.hi = hi


class Op:
    __slots__ = ("idx", "eng", "fn", "deps", "dma", "needs_inc", "sem", "val", "prev_val")

    def __init__(self, idx, eng, fn, deps, dma):
        self.idx = idx
        self.eng = eng
        self.fn = fn
        self.deps = deps
        self.dma = dma
        self.needs_inc = False
        self.sem = None
        self.val = 0
        self.prev_val = 0


class Prog:
    def __init__(self, nc):
        self.nc = nc
        self.ops = []

    def op(self, eng, fn, reads=(), writes=(), dma=False):
        idx = len(self.ops)
        deps = set()
        wset = []
        for w in writes:
            wset.append(w)
            wset.extend(w.overlaps)
        rl = []
        for r in reads:
            if r.excl:
                wset.append(r)
            else:
                rl.append(r)
        reads = rl
        for r in reads:
            if r.last_w is not None:
                deps.add(r.last_w)
        for w in wset:
            if w.last_w is not None:
                deps.add(w.last_w)
            for ridx in w.readers.values():
                deps.add(ridx)
        o = Op(idx, eng, fn, sorted(deps), dma)
        self.ops.append(o)
        for r in reads:
            r.readers[("dma", idx) if dma else eng] = idx
        for w in wset:
            w.last_w = idx
            w.readers = {}
        return o

    def mm(self, out, lhsT, rhs, start, stop, reads, writes):
        return self.op("pe", lambda e: e.matmul(out, lhsT=lhsT, rhs=rhs, start=start, stop=stop), reads, writes)

    def tr(self, out, in_, ident, reads, writes):
        return self.op("pe", lambda e: e.transpose(out=out, in_=in_, identity=ident), reads, writes)

    def tt(self, eng, out, in0, in1, op, reads, writes):
        return self.op(eng, lambda e: e.tensor_tensor(out=out, in0=in0, in1=in1, op=op), reads, writes)

    def ts(self, eng, out, in0, s1, s2, op0, op1, reads, writes):
        if s2 is None:
            return self.op(eng, lambda e: e.tensor_scalar(out=out, in0=in0, scalar1=s1, scalar2=None, op0=op0), reads, writes)
        return self.op(eng, lambda e: e.tensor_scalar(out=out, in0=in0, scalar1=s1, scalar2=s2, op0=op0, op1=op1), reads, writes)

    def stt(self, eng, out, in0, scalar, in1, op0, op1, reads, writes):
        return self.op(eng, lambda e: e.scalar_tensor_tensor(out=out, in0=in0, scalar=scalar, in1=in1, op0=op0, op1=op1), reads, writes)

    def act(self, out, in_, func, reads, writes, bias=None, scale=None, accum_out=None):
        kw = {}
        if bias is not None:
            kw["bias"] = bias
        if scale is not None:
            kw["scale"] = scale
        if accum_out is not None:
            kw["accum_out"] = accum_out
        return self.op("act", lambda e: e.activation(out=out, in_=in_, func=func, **kw), reads, writes)

    def copy(self, eng, out, in_, reads, writes):
        if eng == "act":
            return self.act(out, in_, AF.Copy, reads, writes)
        return self.op(eng, lambda e: e.tensor_copy(out=out, in_=in_), reads, writes)

    def red(self, eng, out, in_, reads, writes):
        return self.op(eng, lambda e: e.tensor_reduce(out=out, in_=in_, axis=AX.X, op=ALU.add), reads, writes)

    def memset(self, eng, out, val, writes):
        return self.op(eng, lambda e: e.memset(out, val), (), writes)

    def dma(self, out, in_, reads, writes, queue="sp"):
        return self.op(queue, lambda e: e.dma_start(out=out, in_=in_), reads, writes, dma=True)

    def emit(self, final_ops):
        nc = self.nc
        ops = self.ops

        def skip_same(o, dop):
            return (not dop.dma) and (not o.dma) and dop.eng == o.eng and (o.eng == "pe" or not SAME_ENGINE_SYNC)

        for o in ops:
            for d in o.deps:
                dop = ops[d]
                if dop.dma or skip_same(o, dop):
                    continue
                dop.needs_inc = True
        with contextlib.ExitStack() as st:
            esem = {e: st.enter_context(nc.semaphore("prog_" + e)) for e in ENGS}
            dsem = {q: [st.enter_context(nc.semaphore("dma_%s_%d" % (q, i))) for i in range(N_DMA_SLOTS)]
                    for q in ("sp", "pool")}
            cnt = {e: 0 for e in ENGS}
            dcnt = {q: 0 for q in dsem}
            duse = {q: [0] * N_DMA_SLOTS for q in dsem}
            for o in ops:
                if o.dma:
                    q = o.eng
                    s = dcnt[q] % N_DMA_SLOTS
                    dcnt[q] += 1
                    o.prev_val = 16 * duse[q][s]
                    duse[q][s] += 1
                    o.sem = dsem[q][s]
                    o.val = 16 * duse[q][s]
                elif o.needs_inc:
                    cnt[o.eng] += 1
                    o.sem = esem[o.eng]
                    o.val = cnt[o.eng]
            per = {e: [o for o in ops if o.eng == e] for e in ENGS}
            block = st.enter_context(nc.Block())

            def run(ename, eng):
                waited = {}
                for o in per[ename]:
                    need = {}
                    for d in o.deps:
                        dop = ops[d]
                        if skip_same(o, dop):
                            continue
                        k = id(dop.sem)
                        if k not in need or need[k][1] < dop.val:
                            need[k] = (dop.sem, dop.val)
                    if o.dma and o.prev_val > 0:
                        k = id(o.sem)
                        if k not in need or need[k][1] < o.prev_val:
                            need[k] = (o.sem, o.prev_val)
                    for k, (sem, val) in need.items():
                        if waited.get(k, 0) >= val:
                            continue
                        eng.wait_ge(sem, val)
                        waited[k] = val
                    ins = o.fn(eng)
                    if o.dma:
                        ins.then_inc(o.sem, 16)
                    elif o.needs_inc:
                        ins.then_inc(o.sem, 1)
                if ename == "sp":
                    for fo in final_ops:
                        eng.wait_ge(fo.sem, fo.val)

            @block.sync
            def _(e):
                run("sp", e)

            @block.tensor
            def _(e):
                run("pe", e)

            @block.scalar
            def _(e):
                run("act", e)

            @block.vector
            def _(e):
                run("dve", e)

            @block.gpsimd
            def _(e):
                run("pool", e)


class Buf:
    __slots__ = ("t", "r")

    def __init__(self, t, r):
        self.t = t
        self.r = r


class SBAlloc:
    def __init__(self, nc):
        self.nc = nc
        self.off = SB_BASE
        self.peak = SB_BASE
        self.all = []

    def alloc(self, name, shape, dt):
        esz = 4 if dt == F32 else 2
        n = 1
        for s in shape[1:]:
            n *= s
        size = (n * esz + 31) // 32 * 32
        assert self.off + size <= SB_LIMIT, ("SBUF overflow", name, self.off, size)
        t = self.nc.alloc_sbuf_tensor_at(name, list(shape), dt, offset=self.off)
        r = Res(name, self.off, self.off + size)
        self.off += size
        self.peak = max(self.peak, self.off)
        self.all.append(r)
        return Buf(t, r)

    def finalize(self):
        rs = sorted(self.all, key=lambda r: r.lo)
        for i, a in enumerate(rs):
            for b in rs[i + 1:]:
                if b.lo >= a.hi:
                    break
                a.overlaps.append(b)
                b.overlaps.append(a)


def piece_specs():
    sp = []

    def ffn(i):
        win, wout, nrm = "ffn%d_w_in" % i, "ffn%d_w_out" % i, "ffn%d_norm" % i
        for g in range(NJ // 2):
            j0, j1 = 2 * g, 2 * g + 1
            sp.append(dict(kind="FM", w=win, cc=[j0 * 128, DFF + j0 * 128, j1 * 128, DFF + j1 * 128], norm=nrm, tag=("ffn_in", i, g)))
        for ch in range(2):
            for (j0, nj) in ((0, 8), (8, 8), (16, 6)):
                sp.append(dict(kind="TMK", w=wout, j0=j0, nj=nj, c0=ch * 512, norm=None, tag=("ffn_out", i, ch, j0, nj)))

    ffn(1)
    for hg in range(2):
        for nm, base in (("q", 0), ("k", 1024), ("v", 2048), ("z", 3072), ("rg", 7184)):
            sp.append(dict(kind="FM", w="w_in", cc=[base + (4 * hg + i) * 128 for i in range(4)], norm="mix_norm", tag=(nm, hg)))
        for nm, base in (("rq", 4112), ("rk", 5136), ("rv", 6160)):
            sp.append(dict(kind="TM", w="w_in", c0=base + hg * 512, norm="mix_norm", tag=(nm, hg)))
    for p in range(2):
        sp.append(dict(kind="FM", w="w_in", cc=[8208 + (4 * p + i) * 128 for i in range(4)], norm="mix_norm", tag=("ga", p)))
        sp.append(dict(kind="FM", w="w_in", cc=[9232 + (4 * p + i) * 128 for i in range(4)], norm="mix_norm", tag=("gb", p)))
        sp.append(dict(kind="FM", w="w_branch_gdn", cc=[(4 * p + i) * 128 for i in range(4)], norm=None, tag=("brg", p)))
        sp.append(dict(kind="FM", w="w_branch_ret", cc=[(4 * p + i) * 128 for i in range(4)], norm=None, tag=("brr", p)))
    for ch in range(2):
        sp.append(dict(kind="TM", w="w_out", c0=ch * 512, norm=None, tag=("wo", ch)))
    ffn(2)
    return sp


PIECES = piece_specs()
NP = len(PIECES)
NORM_COL = {"ffn1_norm": 0, "mix_norm": 8, "ffn2_norm": 16}
PC_RETN = 24
PC_GDNN = 32
PC_CONV = 33
PC_ALOG = 129
PC_DTB = 137
PC_WBA = 145
NPRM = PC_WBA + 128
CC_U = 0
CC_MUI = 128
CC_MLS = 256
CC_DRT = 384
CC_XI = 1408
CC_ZS128 = 2432
CC_ZS16 = 2440
CC_CD128 = 2448
CC_CD16 = 2456
CC_ONES = 2464
CC_MLO = 2592
NCST = CC_MLO + 128


def host_consts():
    c = np.zeros((128, NCST), np.float32)
    j = np.arange(128)
    c[:, CC_U:CC_U + 128] = (j[:, None] <= j[None, :])
    c[:, CC_MUI:CC_MUI + 128] = (j[:, None] <= j[None, :])
    c[:, CC_MLS:CC_MLS + 128] = (j[None, :] < j[:, None]) & ((j[None, :] // 64) == (j[:, None] // 64))
    c[:, CC_MLO:CC_MLO + 128] = (j[None, :] < 64) & (j[:, None] >= 64)
    lg = np.log1p(-np.exp2(-5.0 - np.arange(NH, dtype=np.float64)))
    m = j[:, None, None]
    cc = j[None, None, :]
    dr = np.where(m <= cc, np.exp(np.maximum(cc - m, 0) * lg[None, :, None]), 0.0)
    c[:, CC_DRT:CC_DRT + 1024] = dr.reshape(128, 1024)
    xi = np.exp((j[None, None, :] + 1.0) * lg[None, :, None]) * np.ones((128, 1, 1))
    c[:, CC_XI:CC_XI + 1024] = xi.reshape(128, 1024)
    sc = 128.0 ** -0.5
    c[:, CC_ZS128:CC_ZS128 + 8] = np.exp((127.0 - j)[:, None] * lg[None, :]) * sc
    c[:16, CC_ZS16:CC_ZS16 + 8] = np.exp((15.0 - j[:16])[:, None] * lg[None, :]) * sc
    c[:, CC_CD128:CC_CD128 + 8] = np.exp(128.0 * lg)[None, :]
    c[:, CC_CD16:CC_CD16 + 8] = np.exp(16.0 * lg)[None, :]
    c[:, CC_ONES:CC_ONES + 128] = 1.0
    return c


def host_rope(L):
    inv = (1.0 / (10000.0 ** np.linspace(0.0, 1.0, 64, dtype=np.float32))).astype(np.float32)
    pos = np.arange(L, dtype=np.float32)
    ang = (pos[:, None] * inv[None, :]).astype(np.float32)
    r = np.zeros((L, 128), np.float32)
    r[:, :64] = np.cos(ang.astype(np.float64))
    r[:, 64:] = np.sin(ang.astype(np.float64))
    return r


def host_pack_weights(W):
    out = np.zeros((NP, 128, PIECE), np.float32)
    for s, sp in enumerate(PIECES):
        w = W[sp["w"]]
        if sp["kind"] == "FM":
            wk = w.reshape(8, 128, -1)
            for i, c0 in enumerate(sp["cc"]):
                blk = wk[:, :, c0:c0 + 128]
                out[s].reshape(128, 8, 4, 128)[:, :, i, :] = blk.transpose(1, 0, 2)
        elif sp["kind"] == "TM":
            wk = w.reshape(8, 128, -1)[:, :, sp["c0"]:sp["c0"] + 512]
            out[s].reshape(128, 8, 512)[:, :, :] = wk.transpose(1, 0, 2)
        else:
            wk = w.reshape(NJ, 128, -1)[sp["j0"]:sp["j0"] + sp["nj"], :, sp["c0"]:sp["c0"] + 512]
            out[s].reshape(128, 8, 512)[:, :sp["nj"], :] = wk.transpose(1, 0, 2)
    return out


def host_pack_params(I):
    prm = np.zeros((128, NPRM), np.float32)
    for nm, c0 in NORM_COL.items():
        prm[:, c0:c0 + 8] = np.asarray(I[nm]).reshape(8, 128).T
    prm[:, PC_RETN:PC_RETN + 8] = np.asarray(I["ret_out_norm"]).reshape(8, 128).T
    prm[:, PC_GDNN] = np.asarray(I["gdn_out_norm"]).reshape(128)
    cw = np.asarray(I["gdn_conv_w"]).reshape(4, 24, 128)
    prm[:, PC_CONV:PC_CONV + 96] = cw.transpose(2, 1, 0).reshape(128, 96)
    prm[:, PC_ALOG:PC_ALOG + 8] = np.asarray(I["gdn_a_log"]).reshape(1, 8)
    prm[:, PC_DTB:PC_DTB + 8] = np.asarray(I["gdn_dt_bias"]).reshape(1, 8)
    wba = np.asarray(I["w_in"]).reshape(8, 128, DPROJ)[:, :, 4096:4112]
    prm[:, PC_WBA:PC_WBA + 128] = wba.transpose(1, 0, 2).reshape(128, 128)
    return prm


def build(nt, debug=False):
    seq = nt * 512
    L = NMETA + seq
    nc = bass.Bass("TRN2", target_bir_lowering=False)
    x_d = nc.dram_tensor("x", [seq, D], F32, kind="ExternalInput").ap()
    meta_d = nc.dram_tensor("meta", [NMETA, D], F32, kind="ExternalInput").ap()
    wst_d = nc.dram_tensor("wst", [NP, 128, PIECE], F32, kind="ExternalInput").ap()
    prm_d = nc.dram_tensor("prm", [128, NPRM], F32, kind="ExternalInput").ap()
    fnw_d = nc.dram_tensor("fnw", [128, D], F32, kind="ExternalInput").ap()
    cst_d = nc.dram_tensor("cst", [128, NCST], F32, kind="ExternalInput").ap()
    rope_d = nc.dram_tensor("rope", [L, 128], F32, kind="ExternalInput").ap()
    out_d = nc.dram_tensor("out", [seq, D], F32, kind="ExternalOutput").ap()
    wsc_d = nc.dram_tensor("wsc", [NP, 128, PIECE], BF16).ap()
    wsc_r = [Res("wsc%d" % s) for s in range(NP)]

    P = Prog(nc)
    sb = SBAlloc(nc)
    A = sb.alloc

    prm = A("prm", [128, NPRM], F32)
    cst = A("cst", [128, NCST], F32)
    identb = A("identb", [128, 128], BF16)
    onesb = A("onesb", [128, 128], BF16)
    wba = A("wba", [128, 8, 16], BF16)
    nA = A("nA", [128, 8], F32)
    H = [A("H%d" % tb, [128, D], F32) for tb in range(4)]
    nb = [A("nb%d" % i, [128, D], BF16) for i in range(2)]
    nT = A("nT", [128, 8, 512], BF16)
    ring = [A("ring%d" % i, [128, PIECE], BF16) for i in range(NSLOT)]
    Sg = [A("Sg%d" % h, [128, 128], F32) for h in range(NH)]
    Sr = [A("Sr%d" % h, [128, 128], F32) for h in range(NH)]
    Sgb = [A("Sgb%d" % h, [128, 128], BF16) for h in range(NH)]
    Srb = [A("Srb%d" % h, [128, 128], BF16) for h in range(NH)]
    ctail = A("ctail", [128, 24, 3], F32)
    yaT = A("yaT", [128, 8, 512], BF16)
    ybT = A("ybT", [128, 8, 512], BF16)
    ss = A("ss", [128, 4], F32)
    rstd = A("rstd", [128, 4], F32)
    gtok = A("gtok", [128, 4, 8], F32)
    btok = A("btok", [128, 4, 8], F32)
    braw = A("braw", [128, 4, 16], F32)
    batmp = A("batmp", [128, 4, 8], F32)
    cbias = A("cbias", [128, 2], F32)

    def eps_ap(rows=128):
        return cbias.t[0:rows, 0:1]

    def one_ap(rows=128):
        return cbias.t[0:rows, 1:2]

    arena0 = sb.off

    stage = [A("stage%d" % i, [128, PIECE], F32) for i in range(4)]
    sb.off = arena0
    actT = A("actT", [128, NJ, 512], BF16)
    sil = [A("sil%d" % i, [128, 512], F32) for i in range(2)]
    OUTB = [A("OUTB%d" % tb, [128, D], F32) for tb in range(4)]
    fnw = A("fnw", [128, D], F32)
    junk = A("junk", [128, D], BF16)
    sb.off = arena0
    qT = A("qT", [128, 4, 512], BF16)
    kT = A("kT", [128, 4, 512], BF16)
    vT = A("vT", [128, 4, 512], BF16)
    szT = A("szT", [128, 4, 512], BF16)
    srgT = A("srgT", [128, 4, 512], BF16)
    rkd_tm = A("rkd_tm", [128, 4, 512], BF16)
    rv_tm = A("rv_tm", [128, 4, 512], BF16)
    rqT = A("rqT", [128, 4, 512], BF16)
    rkT = A("rkT", [128, 4, 512], BF16)
    rqdT = A("rqdT", [128, 4, 512], BF16)
    sub0 = sb.off
    rq_tm = A("rq_tm", [128, 4, 512], BF16)
    rk_tm = A("rk_tm", [128, 4, 512], BF16)
    NCS = 5
    rope = A("rope", [128, 4, 128], F32)
    cin = [A("cin%d" % i, [128, 515], F32) for i in range(NCS)]
    cacc = [A("cacc%d" % i, [128, 512], F32) for i in range(NCS)]
    cf = cacc
    csq = [A("csq%d" % i, [128, 512], BF16) for i in range(NCS)]
    crs = [A("crs%d" % i, [128, 512], F32) for i in range(NCS)]
    rxs = [A("rxs%d" % i, [128, 512], F32) for i in range(2)]
    rt = [[A("rt%d_%d" % (i, k), [128, 256], F32) for k in range(2)] for i in range(2)]
    sb.off = sub0
    cksub = []
    for sub in range(2):
        base = {}
        for nm, shp, dt in (
            ("A0", [128, 2, 128], BF16), ("A1", [128, 2, 128], BF16), ("B0", [128, 2, 128], BF16), ("B1", [128, 2, 128], BF16),
            ("X0", [128, 2, 128], BF16), ("X1", [128, 2, 128], BF16), ("Y", [128, 2, 128], BF16),
            ("sz", [128, 4, 128], BF16),
            ("vnew", [128, 2, 128], BF16), ("osb", [128, 2, 128], F32), ("osq", [128, 2, 128], F32), ("ost", [128, 4], F32),
            ("on", [128, 2, 128], BF16),
            ("scT", [128, 2, 128], BF16), ("orb", [128, 2, 128], F32), ("orq", [128, 2, 128], F32), ("ort", [128, 4], F32),
            ("orn", [128, 2, 128], BF16), ("ybt", [128, 2, 128], F32),
        ):
            base[nm] = A("%s_s%d" % (nm, sub), shp, dt)
        base["orc"] = base["orb"]
        pars = []
        for par in range(2):
            d = dict(base)
            for nm, shp, dt in (("wT", [128, 2, 128], BF16), ("u", [128, 2, 128], F32), ("kd", [128, 2, 128], BF16),
                                ("PT", [128, 2, 128], BF16), ("qdT", [128, 2, 128], BF16), ("egl", [128, 2], F32),
                                ("Ain", [128, 2, 128], BF16), ("Ao", [128, 2, 128], BF16), ("kbg", [128, 2, 128], BF16),
                                ("bv", [128, 2, 128], BF16),
                                ("GU", [128, 2, 128], F32), ("gcs", [128, 4], F32), ("eg", [128, 2], F32), ("beg", [128, 2], F32),
                                ("ekd", [128, 2], F32), ("Dm", [128, 2, 128], F32), ("El", [128, 2, 128], F32),
                                ("EQ", [128, 2, 128], F32), ("EGQ", [128, 2, 128], F32)):
                d[nm] = A("%s_s%d_%d" % (nm, sub, par), shp, dt)
            d["Eu"] = d["Dm"]
            d["EA"] = d["El"]
            d["EAo"] = d["GU"]
            pars.append(d)
        cksub.append(pars)
    hg_end = sb.off
    sb.off = arena0
    sgT = A("sgT", [128, 4, 512], BF16)
    sbT = A("sbT", [128, 4, 512], BF16)
    mtmp = [A("mtmp%d" % i, [128, 512], F32) for i in range(4)]
    mtmp2 = [A("mtmp2_%d" % i, [128, 512], F32) for i in range(2)]
    mergedT = A("mergedT", [128, 8, 512], BF16)
    sb.finalize()
    if debug:
        print("SBUF peak", sb.peak, "of", SB_LIMIT, "hg_end", hg_end, "arena0", arena0)

    NF, NB = 7, 1
    psf = [nc.alloc_psum_tensor("psf%d" % i, [128, 512], F32) for i in range(NF)]
    psb = [nc.alloc_psum_tensor("psb%d" % i, [128, 1024], BF16) for i in range(NB)]
    psf_r = [Res("psf%d" % i, excl=True) for i in range(NF)]
    psb_r = [Res("psb%d" % i, excl=True) for i in range(NB)]
    from collections import deque
    free_f = deque(range(NF))
    free_b = deque(range(NB))

    def PSF(hold=False):
        i = free_f.popleft()
        if not hold:
            free_f.append(i)
        return psf[i], psf_r[i]

    def PSB(hold=False):
        i = free_b.popleft()
        if not hold:
            free_b.append(i)
        return psb[i], psb_r[i]

    def gPSF(k=1):
        while len(free_f) < k:
            yield
        got = [PSF(hold=True) for _ in range(k)]
        return got[0] if k == 1 else got

    def gPSB():
        while not free_b:
            yield
        return PSB(hold=True)

    def PFREE(r):
        if r in psf_r:
            free_f.append(psf_r.index(r))
        else:
            free_b.append(psb_r.index(r))

    def pc(c0, n=1):
        return prm.t[:, c0:c0 + n]

    def cc(c0, n, rows=128):
        return cst.t[0:rows, c0:c0 + n]

    rr = ["act", "dve"]
    rrc = {"i": 0}

    def nxt(choices=("act", "dve")):
        rrc["i"] += 1
        return choices[rrc["i"] % len(choices)]

    P.dma(prm.t[:], prm_d[:, :], [], [prm.r])
    P.dma(cst.t[:], cst_d[:, :], [], [cst.r])
    P.memset("pool", crs[0].t[:, 0:128], 0.0, [crs[0].r])
    P.op("pool", lambda e: e.affine_select(out=crs[0].t[:, 0:128], in_=crs[0].t[:, 0:128], pattern=[[-1, 128]],
                                           compare_op=ALU.not_equal, fill=1.0, base=0, channel_multiplier=1),
         [crs[0].r], [crs[0].r])
    P.copy("dve", identb.t[:], crs[0].t[:, 0:128], [crs[0].r], [identb.r])
    P.memset("dve", onesb.t[:], 1.0, [onesb.r])
    for kc in range(8):
        P.ts("dve", wba.t[:, kc, :], prm.t[:, PC_WBA + kc * 16:PC_WBA + kc * 16 + 16], pc(NORM_COL["mix_norm"] + kc), None,
             ALU.mult, None, [prm.r], [wba.r])
    P.act(nA.t[:], pc(PC_ALOG, 8), AF.Exp, [prm.r], [nA.r])
    P.ts("dve", nA.t[:], nA.t[:], -1.0, None, ALU.mult, None, [nA.r], [nA.r])
    for h in range(NH):
        P.memset("pool", Sg[h].t[:], 0.0, [Sg[h].r])
        P.memset("pool", Sr[h].t[:], 0.0, [Sr[h].r])
        P.memset("dve", Sgb[h].t[:], 0.0, [Sgb[h].r])
        P.memset("dve", Srb[h].t[:], 0.0, [Srb[h].r])
    P.memset("pool", ctail.t[:], 0.0, [ctail.r])
    P.memset("pool", cbias.t[:, 0:1], EPS, [cbias.r])
    P.memset("pool", cbias.t[:, 1:2], 1.0, [cbias.r])

    def pre_in(s):
        P.dma(stage[s % 4].t[:], wst_d[s], [], [stage[s % 4].r])

    for s in range(min(3, NP)):
        pre_in(s)
    for s, spc in enumerate(PIECES):
        stg = stage[s % 4]
        slot = ring[s % NSLOT]
        if spc["norm"] is not None:
            c0 = NORM_COL[spc["norm"]]
            for kc in range(8):
                eng = "act" if kc % 2 == 0 else "dve"
                o_ = slot.t[:, kc * 512:(kc + 1) * 512]
                i_ = stg.t[:, kc * 512:(kc + 1) * 512]
                if eng == "act":
                    P.act(o_, i_, AF.Identity, [stg.r, prm.r], [slot.r], scale=pc(c0 + kc))
                else:
                    P.ts("dve", o_, i_, pc(c0 + kc), None, ALU.mult, None, [stg.r, prm.r], [slot.r])
        else:
            P.copy("act", slot.t[:, 0:1536], stg.t[:, 0:1536], [stg.r], [slot.r])
            P.copy("dve", slot.t[:, 1536:3072], stg.t[:, 1536:3072], [stg.r], [slot.r])
            P.copy("pool", slot.t[:, 3072:4096], stg.t[:, 3072:4096], [stg.r], [slot.r])
        if s + 3 < NP:
            pre_in(s + 3)
        P.dma(wsc_d[s], slot.t[:], [slot.r], [wsc_r[s]])

    wstate = {"issued": 0, "cur": 0}
    total_pieces = NP * (nt + 1)

    def w_issue_upto(n):
        while wstate["issued"] < min(n, total_pieces):
            g = wstate["issued"]
            s = g % NP
            slot = ring[g % NSLOT]
            P.dma(slot.t[:], wsc_d[s], [wsc_r[s]], [slot.r])
            wstate["issued"] += 1

    def next_piece(tag_prefix, lag=0):
        g = wstate["cur"]
        s = g % NP
        assert PIECES[s]["tag"][0] == tag_prefix, (PIECES[s]["tag"], tag_prefix)
        w_issue_upto(g + NSLOT - lag)
        wstate["cur"] += 1
        return ring[g % NSLOT]

    def run_jobs(jobs, maxact, fifo):
        done = set()
        pending = list(jobs)
        active = []
        while pending or active:
            for j in list(pending):
                if len(active) >= maxact:
                    break
                if all(d in done for d in j[2]):
                    active.append((j[0], j[1]()))
                    pending.remove(j)
                elif fifo:
                    break
            assert active, ("scheduler deadlock", [j[0] for j in pending][:5])
            for a in list(active):
                try:
                    next(a[1])
                except StopIteration:
                    active.remove(a)
                    done.add(a[0])

    def run_interleaved(gens):
        gens = list(gens)
        while gens:
            for g in list(gens):
                try:
                    next(g)
                except StopIteration:
                    gens.remove(g)

    def norm_to_nT(T, tbs):
        ntb = len(tbs)
        for tb, (t0, tn) in enumerate(tbs):
            P.act(junk.t[0:tn, :], H[tb].t[0:tn, :], AF.Square, [H[tb].r], [junk.r, ss.r], accum_out=ss.t[0:tn, tb:tb + 1])
        tn0 = tbs[0][1]
        P.act(rstd.t[0:tn0, 0:ntb], ss.t[0:tn0, 0:ntb], AF.Ln, [ss.r], [rstd.r], scale=1.0 / D, bias=eps_ap(tn0))
        P.act(rstd.t[0:tn0, 0:ntb], rstd.t[0:tn0, 0:ntb], AF.Exp, [rstd.r], [rstd.r], scale=-0.5)
        for tb, (t0, tn) in enumerate(tbs):
            nbb = nb[tb % 2]
            P.ts("dve", nbb.t[0:tn, :], H[tb].t[0:tn, :], rstd.t[0:tn, tb:tb + 1], None, ALU.mult, None, [H[tb].r, rstd.r], [nbb.r])
            pt, pr = PSB()
            for fc in range(8):
                P.tr(pt[:, fc * 128:fc * 128 + tn], nbb.t[0:tn, fc * 128:(fc + 1) * 128], identb.t[0:tn, 0:tn], [nbb.r, identb.r], [pr])
            P.copy(nxt(), nT.t[:, :, t0:t0 + tn], pt[:, :].rearrange("p (f t) -> p f t", f=8)[:, :, 0:tn], [pr], [nT.r])

    def fm_group(slot, i, T, rhsbuf, rhs_r):
        pt, pr = PSF()
        for kc in range(8):
            P.mm(pt[:, 0:T], slot.t[:, kc * 512 + i * 128:kc * 512 + (i + 1) * 128], rhsbuf.t[:, kc, 0:T], kc == 0, kc == 7,
                 [slot.r, rhs_r], [pr])
        return pt, pr

    def tm_group(slot, lbuf, l_r, t0, tn):
        pt, pr = PSF()
        for kc in range(8):
            P.mm(pt[0:tn, :], lbuf.t[:, kc, t0:t0 + tn], slot.t[:, kc * 512:(kc + 1) * 512], kc == 0, kc == 7, [slot.r, l_r], [pr])
        return pt, pr

    def ffn(i, T, tbs):
        norm_to_nT(T, tbs)
        for g in range(NJ // 2):
            slot = next_piece("ffn_in")
            for jj in range(2):
                j = 2 * g + jj
                pg, pgr = fm_group(slot, 2 * jj, T, nT, nT.r)
                pu, pur = fm_group(slot, 2 * jj + 1, T, nT, nT.r)
                sl = sil[j % 2]
                P.act(sl.t[:, 0:T], pg[:, 0:T], AF.Silu, [pgr], [sl.r])
                P.tt("dve", actT.t[:, j, 0:T], sl.t[:, 0:T], pu[:, 0:T], ALU.mult, [sl.r, pur], [actT.r])
        for ch in range(2):
            pts = [PSF() for _ in tbs]
            for (j0, nj) in ((0, 8), (8, 8), (16, 6)):
                slot = next_piece("ffn_out")
                for jj in range(nj):
                    j = j0 + jj
                    for tb, (t0, tn) in enumerate(tbs):
                        P.mm(pts[tb][0][0:tn, :], actT.t[:, j, t0:t0 + tn], slot.t[:, jj * 512:(jj + 1) * 512], j == 0, j == NJ - 1,
                             [slot.r, actT.r], [pts[tb][1]])
            for tb, (t0, tn) in enumerate(tbs):
                hh = H[tb].t[0:tn, ch * 512:(ch + 1) * 512]
                P.stt("dve", hh, pts[tb][0][0:tn, :], 0.5, hh, ALU.mult, ALU.add, [pts[tb][1], H[tb].r], [H[tb].r])

    def mixer(T, tbs, C, tok0):
        NCH = T // C
        nlev = {128: 5, 16: 3}[C]
        zs_c = CC_ZS128 if C == 128 else CC_ZS16
        cd_c = CC_CD128 if C == 128 else CC_CD16
        norm_to_nT(T, tbs)
        def ba_proj():
            pt, pr = yield from gPSF()
            for n in range(NCH):
                for kc in range(8):
                    P.mm(pt[0:C, n * 16:(n + 1) * 16], nT.t[:, kc, n * C:(n + 1) * C], wba.t[:, kc, :], kc == 0, kc == 7, [nT.r, wba.r], [pr])
            P.copy("act", braw.t[0:C, 0:NCH, :], pt[0:C, 0:NCH * 16].rearrange("p (n c) -> p n c", c=16), [pr], [braw.r])
            PFREE(pr)
            yield
            P.act(btok.t[0:C, 0:NCH, :], braw.t[0:C, 0:NCH, 0:8], AF.Exp, [braw.r], [btok.r], scale=-1.0)
            yield
            P.ts("dve", btok.t[0:C, 0:NCH, :], btok.t[0:C, 0:NCH, :], 1.0, None, ALU.add, None, [btok.r], [btok.r])
            yield
            P.op("dve", lambda e: e.reciprocal(out=btok.t[0:C, 0:NCH, :], in_=btok.t[0:C, 0:NCH, :]), [btok.r], [btok.r])
            yield
            P.tt("dve", batmp.t[0:C, 0:NCH, :], braw.t[0:C, 0:NCH, 8:16], prm.t[0:C, PC_DTB:PC_DTB + 8].unsqueeze(1).broadcast_to([C, NCH, 8]),
                 ALU.add, [braw.r, prm.r], [batmp.r])
            yield
            P.act(batmp.t[0:C, 0:NCH, :], batmp.t[0:C, 0:NCH, :], AF.Exp, [batmp.r], [batmp.r])
            yield
            P.act(batmp.t[0:C, 0:NCH, :], batmp.t[0:C, 0:NCH, :], AF.Ln, [batmp.r], [batmp.r], bias=one_ap(C))
            yield
            P.tt("dve", gtok.t[0:C, 0:NCH, :], batmp.t[0:C, 0:NCH, :], nA.t[0:C, :].unsqueeze(1).broadcast_to([C, NCH, 8]), ALU.mult,
                 [batmp.r, nA.r], [gtok.r])
            yield

        for hg in range(2):
            for tb, (t0, tn) in enumerate(tbs):
                P.dma(rope.t[0:tn, tb, :], rope_d[tok0 + t0:tok0 + t0 + tn, :], [], [rope.r])
            pj = []
            lag = 2
            slots = {}

            def get_slot(nm, first):
                if first:
                    slots[nm] = next_piece(nm, lag)
                return slots[nm]

            def job_conv(nm, dst, qi, i, cs, hg=hg):
                slot = get_slot(nm, i == 0)
                cch = qi * 8 + hg * 4 + i
                pt, pr = yield from gPSF()
                for kc in range(8):
                    P.mm(pt[:, 0:T], slot.t[:, kc * 512 + i * 128:kc * 512 + (i + 1) * 128], nT.t[:, kc, 0:T], kc == 0, kc == 7, [slot.r, nT.r], [pr])
                yield
                ci = cin[cs]
                P.copy("act", ci.t[:, 3:3 + T], pt[:, 0:T], [pr], [ci.r])
                PFREE(pr)
                P.copy("pool", ci.t[:, 0:3], ctail.t[:, cch, :], [ctail.r], [ci.r])
                yield
                P.copy("pool", ctail.t[:, cch, :], ci.t[:, T:T + 3], [ci.r], [ctail.r])
                ca = cacc[cs]
                wcol = PC_CONV + cch * 4
                P.act(ca.t[:, 0:T], ci.t[:, 0:T], AF.Identity, [ci.r, prm.r], [ca.r], scale=pc(wcol))
                yield
                for tap in range(1, 4):
                    P.stt("dve", ca.t[:, 0:T], ci.t[:, tap:tap + T], pc(wcol + tap), ca.t[:, 0:T], ALU.mult, ALU.add,
                          [ci.r, prm.r, ca.r], [ca.r])
                yield
                if nm == "v":
                    P.act(dst.t[:, i, 0:T], ca.t[:, 0:T], AF.Silu, [ca.r], [dst.r])
                    yield
                    return
                f = cf[cs]
                P.act(f.t[:, 0:T], ca.t[:, 0:T], AF.Silu, [ca.r], [f.r])
                yield
                sq = csq[cs]
                P.tt("pool", sq.t[:, 0:T], f.t[:, 0:T], f.t[:, 0:T], ALU.mult, [f.r], [sq.r])
                yield
                p2, p2r = yield from gPSF()
                P.mm(p2[:, 0:T], onesb.t[:, :], sq.t[:, 0:T], True, True, [onesb.r, sq.r], [p2r])
                yield
                rs_ = crs[cs]
                P.act(rs_.t[:, 0:T], p2[:, 0:T], AF.Ln, [p2r], [rs_.r], bias=eps_ap())
                PFREE(p2r)
                P.act(rs_.t[:, 0:T], rs_.t[:, 0:T], AF.Exp, [rs_.r], [rs_.r], scale=-0.5)
                yield
                if nm == "q":
                    P.stt("dve", dst.t[:, i, 0:T], f.t[:, 0:T], 128.0 ** -0.5, rs_.t[:, 0:T], ALU.mult, ALU.mult, [f.r, rs_.r], [dst.r])
                else:
                    P.tt("dve", dst.t[:, i, 0:T], f.t[:, 0:T], rs_.t[:, 0:T], ALU.mult, [f.r, rs_.r], [dst.r])
                yield

            def job_silu(nm, dst, i):
                slot = get_slot(nm, i == 0)
                pt, pr = yield from gPSF()
                for kc in range(8):
                    P.mm(pt[:, 0:T], slot.t[:, kc * 512 + i * 128:kc * 512 + (i + 1) * 128], nT.t[:, kc, 0:T], kc == 0, kc == 7, [slot.r, nT.r], [pr])
                yield
                P.act(dst.t[:, i, 0:T], pt[:, 0:T], AF.Silu, [pr], [dst.r])
                PFREE(pr)
                yield

            def job_rot(nm, dst, tb, t0, tn, rs):
                slot = get_slot(nm, tb == 0)
                pt, pr = yield from gPSF()
                for kc in range(8):
                    P.mm(pt[0:tn, :], nT.t[:, kc, t0:t0 + tn], slot.t[:, kc * 512:(kc + 1) * 512], kc == 0, kc == 7, [slot.r, nT.r], [pr])
                yield
                xs = rxs[rs]
                P.copy("act", xs.t[0:tn, :], pt[0:tn, :], [pr], [xs.r])
                PFREE(pr)
                yield
                xv = xs.t[0:tn, :].rearrange("p (h j two) -> p h j two", h=4, two=2)
                x0 = xv[:, :, :, 0]
                x1 = xv[:, :, :, 1]
                cosb = rope.t[0:tn, tb, 0:64].unsqueeze(1).broadcast_to([tn, 4, 64])
                sinb = rope.t[0:tn, tb, 64:128].unsqueeze(1).broadcast_to([tn, 4, 64])
                ta, tb_ = rt[rs]
                t1 = ta.t[0:tn, :].rearrange("p (h j) -> p h j", h=4)
                t2 = tb_.t[0:tn, :].rearrange("p (h j) -> p h j", h=4)
                ov = dst.t[0:tn, tb, :].rearrange("p (h j two) -> p h j two", h=4, two=2)
                P.tt("dve", t1, x0, cosb, ALU.mult, [xs.r, rope.r], [ta.r])
                P.tt("pool", t2, x1, sinb, ALU.mult, [xs.r, rope.r], [tb_.r])
                yield
                P.tt("dve", ov[:, :, :, 0], t1, t2, ALU.subtract, [ta.r, tb_.r], [dst.r])
                yield
                P.tt("dve", t1, x1, cosb, ALU.mult, [xs.r, rope.r], [ta.r])
                P.tt("pool", t2, x0, sinb, ALU.mult, [xs.r, rope.r], [tb_.r])
                yield
                P.tt("pool", ov[:, :, :, 1], t1, t2, ALU.add, [ta.r, tb_.r], [dst.r])
                yield

            def job_rv(tb, t0, tn):
                slot = get_slot("rv", tb == 0)
                pt, pr = yield from gPSF()
                for kc in range(8):
                    P.mm(pt[0:tn, :], nT.t[:, kc, t0:t0 + tn], slot.t[:, kc * 512:(kc + 1) * 512], kc == 0, kc == 7, [slot.r, nT.r], [pr])
                yield
                P.copy("act", rv_tm.t[0:tn, tb, :], pt[0:tn, :], [pr], [rv_tm.r])
                PFREE(pr)
                yield

            def job_tr(src, dst, scl, tb, t0, tn, hg=hg):
                if src is rk_tm:
                    P.tt("dve", rkd_tm.t[0:tn, tb, :].rearrange("p (h d) -> p h d", h=4),
                         rk_tm.t[0:tn, tb, :].rearrange("p (h d) -> p h d", h=4),
                         cst.t[0:tn, zs_c + hg * 4:zs_c + hg * 4 + 4].unsqueeze(2).broadcast_to([tn, 4, 128]), ALU.mult,
                         [rk_tm.r, cst.r], [rkd_tm.r])
                pt, pr = yield from gPSB()
                for i in range(4):
                    P.tr(pt[:, i * 128:i * 128 + tn], src.t[0:tn, tb, i * 128:(i + 1) * 128], identb.t[0:tn, 0:tn], [src.r, identb.r], [pr])
                yield
                pv = pt[:, 0:512].rearrange("p (h t) -> p h t", h=4)[:, :, 0:tn]
                if scl is None:
                    P.copy("dve", dst.t[:, :, t0:t0 + tn], pv, [pr], [dst.r])
                else:
                    P.act(dst.t[:, :, t0:t0 + tn], pv, AF.Identity, [pr], [dst.r], scale=scl)
                PFREE(pr)
                yield

            cnt = 0
            if hg == 0:
                pj.append(("ba", ba_proj, []))
            for qi, (nm, dst) in enumerate((("q", qT), ("k", kT), ("v", vT))):
                for i in range(4):
                    deps = ["conv%d" % (cnt - NCS)] if cnt >= NCS else []
                    pj.append(("conv%d" % cnt, (lambda nm=nm, dst=dst, qi=qi, i=i, cs=cnt % NCS: job_conv(nm, dst, qi, i, cs)), deps))
                    cnt += 1
            for nm, dst in (("z", szT), ("rg", srgT)):
                for i in range(4):
                    pj.append(("silu_%s%d" % (nm, i), (lambda nm=nm, dst=dst, i=i: job_silu(nm, dst, i)), []))
            rcnt = 0
            for nm, dst in (("rq", rq_tm), ("rk", rk_tm)):
                for tb, (t0, tn) in enumerate(tbs):
                    deps = ["rot%d" % (rcnt - 2)] if rcnt >= 2 else []
                    pj.append(("rot%d" % rcnt, (lambda nm=nm, dst=dst, tb=tb, t0=t0, tn=tn, rs=rcnt % 2: job_rot(nm, dst, tb, t0, tn, rs)), deps))
                    rcnt += 1
            for tb, (t0, tn) in enumerate(tbs):
                pj.append(("rv%d" % tb, (lambda tb=tb, t0=t0, tn=tn: job_rv(tb, t0, tn)), []))
            ntb_ = len(tbs)
            for si, (src, dst, scl) in enumerate(((rq_tm, rqT, None), (rk_tm, rkT, 128.0 ** -0.5))):
                for tb, (t0, tn) in enumerate(tbs):
                    pj.append(("tr%d_%d" % (si, tb), (lambda src=src, dst=dst, scl=scl, tb=tb, t0=t0, tn=tn: job_tr(src, dst, scl, tb, t0, tn)),
                               ["rot%d" % (si * ntb_ + tb)]))
            run_jobs(pj, 5 if ntb_ > 1 else 1, True)
            for i in range(4):
                h = hg * 4 + i
                P.tt("pool", rqdT.t[:, i, 0:T].rearrange("p (n c) -> p n c", c=C), rqT.t[:, i, 0:T].rearrange("p (n c) -> p n c", c=C),
                     cst.t[:, CC_XI + h * 128:CC_XI + h * 128 + C].unsqueeze(1).broadcast_to([128, NCH, C]), ALU.mult, [rqT.r, cst.r], [rqdT.r])

            hs = slice(hg * 4, hg * 4 + 4)

            def h3(ap, w):
                return ap.rearrange("p (h c) -> p h c", h=4)

            def gen_prep(n, sub, hg=hg):
                K = cksub[sub][n % 2]
                c0 = n * C
                li = (2 * sub, 2 * sub + 1)
                hcs = slice(hg * 4 + 2 * sub, hg * 4 + 2 * sub + 2)

                def bc2(ap2):
                    return ap2.unsqueeze(2).broadcast_to([C, 2, C])

                def bcd(ap2):
                    return ap2.unsqueeze(2).broadcast_to([C, 2, 128])

                def v3(b):
                    return b.t[0:C, :, 0:C]

                def g3(ap):
                    return ap.rearrange("p (h c) -> p h c", h=2)

                def tab(c0_):
                    return cst.t[0:C, c0_:c0_ + C].unsqueeze(1).broadcast_to([C, 2, C])

                Ub, MUI, MLS, MLO = tab(CC_U), tab(CC_MUI), tab(CC_MLS), tab(CC_MLO)
                Idb = identb.t[0:C, 0:C].unsqueeze(1).broadcast_to([C, 2, C])
                two = (C == 128)
                P.tt("pool", v3(K["GU"]), Ub, bc2(gtok.t[0:C, n, hcs]), ALU.mult, [cst.r, gtok.r], [K["GU"].r])
                (pG, pGr), (pS, pSr) = yield from gPSF(2)
                for ii in range(2):
                    P.mm(pG[:, ii * C:(ii + 1) * C], cst.t[0:C, CC_ONES:CC_ONES + 128], K["GU"].t[0:C, ii, 0:C], True, True, [cst.r, K["GU"].r], [pGr])
                P.mm(pS[0:C, 0:2], cst.t[0:C, CC_U:CC_U + C], gtok.t[0:C, n, hcs], True, True, [cst.r, gtok.r], [pSr])
                P.mm(pS[:, 2:4], cst.t[0:C, CC_ONES:CC_ONES + 128], gtok.t[0:C, n, hcs], True, True, [cst.r, gtok.r], [pSr])
                yield
                gcs = K["gcs"]
                P.copy("act", gcs.t[:, 2:4], pS[:, 2:4], [pSr], [gcs.r])
                P.copy("act", gcs.t[0:C, 0:2], pS[0:C, 0:2], [pSr], [gcs.r])
                PFREE(pSr)
                P.act(K["EGQ"].t[:, :, 0:C], g3(pG[:, 0:2 * C]), AF.Exp, [pGr], [K["EGQ"].r])
                yield
                P.tt("dve", v3(K["Dm"]), g3(pG[0:C, 0:2 * C]), bc2(gcs.t[0:C, 0:2]), ALU.subtract, [pGr, gcs.r], [K["Dm"].r])
                PFREE(pGr)
                P.act(K["eg"].t[0:C, :], gcs.t[0:C, 0:2], AF.Exp, [gcs.r], [K["eg"].r])
                P.tt("pool", K["ekd"].t[0:C, :], gcs.t[0:C, 2:4], gcs.t[0:C, 0:2], ALU.subtract, [gcs.r], [K["ekd"].r])
                P.act(K["egl"].t[:, :], gcs.t[:, 2:4], AF.Exp, [gcs.r], [K["egl"].r])
                P.tt("pool", K["qdT"].t[:, :, 0:C], qT.t[:, li[0]:li[1] + 1, c0:c0 + C], K["EGQ"].t[:, :, 0:C], ALU.mult, [qT.r, K["EGQ"].r], [K["qdT"].r])
                yield
                P.ts("dve", v3(K["El"]), v3(K["Dm"]), 0.0, None, ALU.max, None, [K["Dm"].r], [K["El"].r])
                P.ts("dve", v3(K["Eu"]), v3(K["Dm"]), 0.0, None, ALU.min, None, [K["Dm"].r], [K["Eu"].r])
                P.act(K["ekd"].t[0:C, :], K["ekd"].t[0:C, :], AF.Exp, [K["ekd"].r], [K["ekd"].r])
                P.tt("pool", K["beg"].t[0:C, :], K["eg"].t[0:C, :], btok.t[0:C, n, hcs], ALU.mult, [K["eg"].r, btok.r], [K["beg"].r])
                ptkv, ptkvr = yield from gPSB()
                for ii in range(2):
                    P.tr(ptkv[0:C, ii * 128:(ii + 1) * 128], kT.t[:, li[ii], c0:c0 + C], identb.t[:, :], [kT.r, identb.r], [ptkvr])
                for ii in range(2):
                    P.tr(ptkv[0:C, 256 + ii * 128:256 + (ii + 1) * 128], vT.t[:, li[ii], c0:c0 + C], identb.t[:, :], [vT.r, identb.r], [ptkvr])
                yield
                P.act(v3(K["El"]), v3(K["El"]), AF.Exp, [K["El"].r], [K["El"].r], scale=-1.0)
                P.act(v3(K["Eu"]), v3(K["Eu"]), AF.Exp, [K["Eu"].r], [K["Eu"].r])
                ktv = ptkv[0:C, 0:256].rearrange("p (h d) -> p h d", h=2)
                vtv = ptkv[0:C, 256:512].rearrange("p (h d) -> p h d", h=2)
                P.tt("dve", K["kd"].t[0:C, :, :], ktv, bcd(K["ekd"].t[0:C, :]), ALU.mult, [ptkvr, K["ekd"].r], [K["kd"].r])
                P.tt("dve", K["kbg"].t[0:C, :, :], ktv, bcd(K["beg"].t[0:C, :]), ALU.mult, [ptkvr, K["beg"].r], [K["kbg"].r])
                P.tt("dve", K["bv"].t[0:C, :, :], vtv, bcd(btok.t[0:C, n, hcs]), ALU.mult, [ptkvr, btok.r], [K["bv"].r])
                PFREE(ptkvr)
                (pK, pKr), (pQ, pQr) = yield from gPSF(2)
                for ii in range(2):
                    P.mm(pQ[0:C, ii * C:(ii + 1) * C], kT.t[:, li[ii], c0:c0 + C], qT.t[:, li[ii], c0:c0 + C], True, True, [kT.r, qT.r], [pQr])
                for ii in range(2):
                    P.mm(pK[0:C, ii * C:(ii + 1) * C], kT.t[:, li[ii], c0:c0 + C], kT.t[:, li[ii], c0:c0 + C], True, True, [kT.r], [pKr])
                yield
                P.tt("pool", v3(K["EQ"]), v3(K["Eu"]), MUI, ALU.mult, [K["Eu"].r, cst.r], [K["EQ"].r])
                P.tt("pool", v3(K["El"]), v3(K["El"]), bc2(btok.t[0:C, n, hcs]), ALU.mult, [K["El"].r, btok.r], [K["El"].r])
                yield
                P.tt("dve", v3(K["PT"]), g3(pQ[0:C, 0:2 * C]), v3(K["EQ"]), ALU.mult, [pQr, K["EQ"].r], [K["PT"].r])
                PFREE(pQr)
                if two:
                    P.tt("pool", v3(K["EAo"]), v3(K["El"]), MLO, ALU.mult, [K["El"].r, cst.r], [K["EAo"].r])
                P.tt("pool", v3(K["EA"]), v3(K["El"]), MLS, ALU.mult, [K["El"].r, cst.r], [K["EA"].r])
                yield
                P.tt("dve", v3(K["Ain"]), g3(pK[0:C, 0:2 * C]), v3(K["EA"]), ALU.mult, [pKr, K["EA"].r], [K["Ain"].r])
                if two:
                    P.tt("dve", v3(K["Ao"]), g3(pK[0:C, 0:2 * C]), v3(K["EAo"]), ALU.mult, [pKr, K["EAo"].r], [K["Ao"].r])
                PFREE(pKr)
                yield

            def gen_prepB(n, sub, hg=hg):
                K = cksub[sub][n % 2]

                def v3(b):
                    return b.t[0:C, :, 0:C]

                def g3(ap):
                    return ap.rearrange("p (h c) -> p h c", h=2)

                Idb = identb.t[0:C, 0:C].unsqueeze(1).broadcast_to([C, 2, C])
                two = (C == 128)
                ptb, ptbr = yield from gPSB()
                for ii in range(2):
                    P.tr(ptb[0:C, ii * C:(ii + 1) * C], K["Ain"].t[0:C, ii, 0:C], identb.t[0:C, 0:C], [K["Ain"].r, identb.r], [ptbr])
                yield
                P.copy("act", v3(K["B0"]), g3(ptb[0:C, 0:2 * C]), [ptbr], [K["B0"].r])
                P.tt("dve", v3(K["X0"]), Idb, g3(ptb[0:C, 0:2 * C]), ALU.subtract, [identb.r, ptbr], [K["X0"].r])
                PFREE(ptbr)
                yield
                for k in range(nlev + 1):
                    Ak, Bk = (K["Ain"] if k == 0 else K["A%d" % (k % 2)]), K["B%d" % (k % 2)]
                    An, Bn = K["A%d" % ((k + 1) % 2)], K["B%d" % ((k + 1) % 2)]
                    Xp, Xn = K["X%d" % ((k + 1) % 2)], K["X%d" % (k % 2)]
                    need = (1 if k < nlev else 0) + (1 if k < nlev - 1 else 0) + (1 if k >= 1 else 0)
                    got = yield from gPSF(need)
                    if need == 1:
                        got = [got]
                    got = list(got)
                    pA = pB2 = pX = None
                    if k < nlev:
                        pA, pAr = got.pop(0)
                        for ii in range(2):
                            P.mm(pA[0:C, ii * C:(ii + 1) * C], Bk.t[0:C, ii, 0:C], Ak.t[0:C, ii, 0:C], True, True, [Bk.r, Ak.r], [pAr])
                    if k < nlev - 1:
                        pB2, pB2r = got.pop(0)
                        for ii in range(2):
                            P.mm(pB2[0:C, ii * C:(ii + 1) * C], Ak.t[0:C, ii, 0:C], Bk.t[0:C, ii, 0:C], True, True, [Bk.r, Ak.r], [pB2r])
                    if k >= 1:
                        pX, pXr = got.pop(0)
                        for ii in range(2):
                            P.mm(pX[0:C, ii * C:(ii + 1) * C], Ak.t[0:C, ii, 0:C], Xp.t[0:C, ii, 0:C], True, True, [Ak.r, Xp.r], [pXr])
                    yield
                    if pA is not None:
                        P.copy("act", v3(An), g3(pA[0:C, 0:2 * C]), [pAr], [An.r])
                        PFREE(pAr)
                    if pB2 is not None:
                        P.copy("act", v3(Bn), g3(pB2[0:C, 0:2 * C]), [pB2r], [Bn.r])
                        PFREE(pB2r)
                    if pX is not None:
                        P.tt("dve", v3(Xn), g3(pX[0:C, 0:2 * C]), v3(Xp), ALU.add, [pXr, Xp.r], [Xn.r])
                        PFREE(pXr)
                    yield
                TT = K["X%d" % (nlev % 2)]
                if two:
                    (pY, pYr), (pZ, pZr) = yield from gPSF(2)
                    for ii in range(2):
                        P.mm(pY[0:C, ii * C:(ii + 1) * C], K["Ao"].t[0:C, ii, 0:C], TT.t[0:C, ii, 0:C], True, True, [K["Ao"].r, TT.r], [pYr])
                    for ii in range(2):
                        P.mm(pZ[0:C, ii * 128:(ii + 1) * 128], TT.t[0:C, ii, 0:C], K["bv"].t[0:C, ii, :], True, True, [TT.r, K["bv"].r], [pZr])
                    for ii in range(2):
                        P.mm(pZ[0:C, 256 + ii * 128:256 + (ii + 1) * 128], TT.t[0:C, ii, 0:C], K["kbg"].t[0:C, ii, :], True, True, [TT.r, K["kbg"].r], [pZr])
                    yield
                    P.tt("dve", v3(K["Y"]), Idb, g3(pY[0:C, 0:2 * C]), ALU.subtract, [identb.r, pYr], [K["Y"].r])
                    PFREE(pYr)
                    P.copy("act", K["sz"].t[0:C, :, :], pZ[0:C, 0:512].rearrange("p (h d) -> p h d", h=4), [pZr], [K["sz"].r])
                    PFREE(pZr)
                    yield
                    (pU, pUr), (pW, pWr) = yield from gPSF(2)
                    for ii in range(2):
                        P.mm(pU[0:C, ii * 128:(ii + 1) * 128], K["Y"].t[0:C, ii, 0:C], K["sz"].t[0:C, ii, :], True, True, [K["Y"].r, K["sz"].r], [pUr])
                    for ii in range(2):
                        P.mm(pW[:, ii * C:(ii + 1) * C], K["sz"].t[0:C, 2 + ii, :], K["Y"].t[0:C, ii, 0:C], True, True, [K["Y"].r, K["sz"].r], [pWr])
                else:
                    (pU, pUr), (pW, pWr) = yield from gPSF(2)
                    for ii in range(2):
                        P.mm(pU[0:C, ii * 128:(ii + 1) * 128], TT.t[0:C, ii, 0:C], K["bv"].t[0:C, ii, :], True, True, [TT.r, K["bv"].r], [pUr])
                    for ii in range(2):
                        P.mm(pW[:, ii * C:(ii + 1) * C], K["kbg"].t[0:C, ii, :], TT.t[0:C, ii, 0:C], True, True, [TT.r, K["kbg"].r], [pWr])
                yield
                P.copy("act", K["u"].t[0:C, :, :], pU[0:C, 0:256].rearrange("p (h d) -> p h d", h=2), [pUr], [K["u"].r])
                PFREE(pUr)
                P.copy("dve", K["wT"].t[:, :, 0:C], g3(pW[:, 0:2 * C]), [pWr], [K["wT"].r])
                PFREE(pWr)
                yield

            def gen_scan(n, sub, hg=hg):
                K = cksub[sub][n % 2]
                c0 = n * C
                li = (2 * sub, 2 * sub + 1)
                hgl = (hg * 4 + 2 * sub, hg * 4 + 2 * sub + 1)

                def bcd(ap2):
                    return ap2.unsqueeze(2).broadcast_to([C, 2, 128])

                def g3(ap):
                    return ap.rearrange("p (h c) -> p h c", h=2)

                p1, p1r = yield from gPSF()
                for ii in range(2):
                    h = hgl[ii]
                    P.mm(p1[0:C, ii * 128:(ii + 1) * 128], K["wT"].t[:, ii, 0:C], Sgb[h].t[:, :], True, True, [K["wT"].r, Sgb[h].r], [p1r])
                yield
                P.tt("dve", K["vnew"].t[0:C, :, :], K["u"].t[0:C, :, :], p1[0:C, 0:256].rearrange("p (h d) -> p h d", h=2), ALU.subtract,
                     [K["u"].r, p1r], [K["vnew"].r])
                PFREE(p1r)
                yield
                (pO, pOr), (pSS, pSSr) = yield from gPSF(2)
                for ii in range(2):
                    P.mm(pSS[:, ii * 128:(ii + 1) * 128], K["kd"].t[0:C, ii, :], K["vnew"].t[0:C, ii, :], True, True, [K["kd"].r, K["vnew"].r], [pSSr])
                for ii in range(2):
                    h = hgl[ii]
                    P.mm(pO[0:C, ii * 128:(ii + 1) * 128], K["qdT"].t[:, ii, 0:C], Sgb[h].t[:, :], True, False, [K["qdT"].r, Sgb[h].r], [pOr])
                    P.mm(pO[0:C, ii * 128:(ii + 1) * 128], K["PT"].t[0:C, ii, 0:C], K["vnew"].t[0:C, ii, :], False, True, [K["PT"].r, K["vnew"].r], [pOr])
                yield
                for ii in range(2):
                    h = hgl[ii]
                    P.stt("dve", Sg[h].t[:, :], Sg[h].t[:, :], K["egl"].t[:, ii:ii + 1], pSS[:, ii * 128:(ii + 1) * 128], ALU.mult, ALU.add,
                          [Sg[h].r, K["egl"].r, pSSr], [Sg[h].r])
                PFREE(pSSr)
                P.copy("act", K["osb"].t[0:C, :, :], pO[0:C, 0:256].rearrange("p (h d) -> p h d", h=2), [pOr], [K["osb"].r])
                PFREE(pOr)
                yield
                for ii in range(2):
                    h = hgl[ii]
                    P.copy("act" if ii == 0 else "pool", Sgb[h].t[:, :], Sg[h].t[:, :], [Sg[h].r], [Sgb[h].r])
                P.tt("pool", K["osq"].t[0:C, :, :], K["osb"].t[0:C, :, :], K["osb"].t[0:C, :, :], ALU.mult, [K["osb"].r], [K["osq"].r])
                yield
                P.red("dve", K["ost"].t[0:C, 0:2], K["osq"].t[0:C, :, :], [K["osq"].r], [K["ost"].r])
                yield
                P.act(K["ost"].t[0:C, 2:4], K["ost"].t[0:C, 0:2], AF.Ln, [K["ost"].r], [K["ost"].r], scale=1.0 / 128, bias=eps_ap(C))
                P.act(K["ost"].t[0:C, 2:4], K["ost"].t[0:C, 2:4], AF.Exp, [K["ost"].r], [K["ost"].r], scale=-0.5)
                yield
                P.tt("pool", K["on"].t[0:C, :, :], K["osb"].t[0:C, :, :], bcd(K["ost"].t[0:C, 2:4]), ALU.mult, [K["osb"].r, K["ost"].r], [K["on"].r])
                yield
                pt, pr = yield from gPSB()
                for ii in range(2):
                    P.tr(pt[:, ii * C:(ii + 1) * C], K["on"].t[0:C, ii, :], identb.t[0:C, 0:C], [K["on"].r, identb.r], [pr])
                yield
                P.stt("dve", yaT.t[:, hgl[0]:hgl[1] + 1, c0:c0 + C], g3(pt[:, 0:2 * C]), pc(PC_GDNN), szT.t[:, li[0]:li[1] + 1, c0:c0 + C],
                      ALU.mult, ALU.mult, [pr, prm.r, szT.r], [yaT.r])
                PFREE(pr)
                yield

            def gen_ret(n, sub, hg=hg):
                K = cksub[sub][n % 2]
                c0 = n * C
                tb = c0 // 128
                li = (2 * sub, 2 * sub + 1)
                hgl = (hg * 4 + 2 * sub, hg * 4 + 2 * sub + 1)

                def bcd(ap2):
                    return ap2.unsqueeze(2).broadcast_to([C, 2, 128])

                def g3(ap):
                    return ap.rearrange("p (h c) -> p h c", h=2)

                pSc, pScr = yield from gPSF()
                for ii in range(2):
                    i = li[ii]
                    P.mm(pSc[0:C, ii * C:(ii + 1) * C], rkT.t[:, i, c0:c0 + C], rqT.t[:, i, c0:c0 + C], True, True, [rkT.r, rqT.r], [pScr])
                yield
                P.tt("dve", K["scT"].t[0:C, :, 0:C], g3(pSc[0:C, 0:2 * C]),
                     cst.t[0:C, CC_DRT + hgl[0] * 128:CC_DRT + hgl[0] * 128 + 256].rearrange("p (h c) -> p h c", h=2)[:, :, 0:C], ALU.mult,
                     [pScr, cst.r], [K["scT"].r])
                PFREE(pScr)
                yield
                (pOr2, pOr2r), (pSR, pSRr) = yield from gPSF(2)
                for ii in range(2):
                    i = li[ii]
                    h = hgl[ii]
                    P.mm(pOr2[0:C, ii * 128:(ii + 1) * 128], rqdT.t[:, i, c0:c0 + C], Srb[h].t[:, :], True, False, [rqdT.r, Srb[h].r], [pOr2r])
                    P.mm(pOr2[0:C, ii * 128:(ii + 1) * 128], K["scT"].t[0:C, ii, 0:C], rv_tm.t[0:C, tb, i * 128:(i + 1) * 128], False, True,
                         [K["scT"].r, rv_tm.r], [pOr2r])
                for ii in range(2):
                    i = li[ii]
                    P.mm(pSR[:, ii * 128:(ii + 1) * 128], rkd_tm.t[0:C, tb, i * 128:(i + 1) * 128], rv_tm.t[0:C, tb, i * 128:(i + 1) * 128], True, True,
                         [rkd_tm.r, rv_tm.r], [pSRr])
                yield
                for ii in range(2):
                    h = hgl[ii]
                    P.stt("dve", Sr[h].t[:, :], Sr[h].t[:, :], cst.t[:, cd_c + h:cd_c + h + 1], pSR[:, ii * 128:(ii + 1) * 128], ALU.mult, ALU.add,
                          [Sr[h].r, cst.r, pSRr], [Sr[h].r])
                PFREE(pSRr)
                P.copy("act", K["orb"].t[0:C, :, :], pOr2[0:C, 0:256].rearrange("p (h d) -> p h d", h=2), [pOr2r], [K["orb"].r])
                PFREE(pOr2r)
                yield
                for ii in range(2):
                    h = hgl[ii]
                    P.copy("act" if ii == 1 else "pool", Srb[h].t[:, :], Sr[h].t[:, :], [Sr[h].r], [Srb[h].r])
                P.red("dve", K["ort"].t[0:C, 0:2], K["orb"].t[0:C, :, :], [K["orb"].r], [K["ort"].r])
                yield
                P.ts("dve", K["ort"].t[0:C, 0:2], K["ort"].t[0:C, 0:2], 1.0 / 128, None, ALU.mult, None, [K["ort"].r], [K["ort"].r])
                yield
                P.tt("pool", K["orc"].t[0:C, :, :], K["orb"].t[0:C, :, :], bcd(K["ort"].t[0:C, 0:2]), ALU.subtract, [K["orb"].r, K["ort"].r], [K["orc"].r])
                yield
                P.tt("pool", K["orq"].t[0:C, :, :], K["orc"].t[0:C, :, :], K["orc"].t[0:C, :, :], ALU.mult, [K["orc"].r], [K["orq"].r])
                yield
                P.red("dve", K["ort"].t[0:C, 2:4], K["orq"].t[0:C, :, :], [K["orq"].r], [K["ort"].r])
                yield
                P.act(K["ort"].t[0:C, 2:4], K["ort"].t[0:C, 2:4], AF.Ln, [K["ort"].r], [K["ort"].r], scale=1.0 / 128, bias=eps_ap(C))
                P.act(K["ort"].t[0:C, 2:4], K["ort"].t[0:C, 2:4], AF.Exp, [K["ort"].r], [K["ort"].r], scale=-0.5)
                yield
                P.tt("pool", K["orn"].t[0:C, :, :], K["orc"].t[0:C, :, :], bcd(K["ort"].t[0:C, 2:4]), ALU.mult, [K["orc"].r, K["ort"].r], [K["orn"].r])
                yield
                pt, pr = yield from gPSB()
                for ii in range(2):
                    P.tr(pt[:, ii * C:(ii + 1) * C], K["orn"].t[0:C, ii, :], identb.t[0:C, 0:C], [K["orn"].r, identb.r], [pr])
                yield
                P.tt("dve", K["ybt"].t[:, :, 0:C], g3(pt[:, 0:2 * C]),
                     prm.t[:, PC_RETN + hgl[0]:PC_RETN + hgl[0] + 2].unsqueeze(2).broadcast_to([128, 2, C]), ALU.mult, [pr, prm.r], [K["ybt"].r])
                PFREE(pr)
                yield
                P.tt("pool", ybT.t[:, hgl[0]:hgl[1] + 1, c0:c0 + C], K["ybt"].t[:, :, 0:C], srgT.t[:, li[0]:li[1] + 1, c0:c0 + C], ALU.mult,
                     [K["ybt"].r, srgT.r], [ybT.r])
                yield

            cj = []
            for n in range(NCH):
                for sub in range(2):
                    d = []
                    if n >= 2:
                        d.append("prep%d_%d" % (n - 2, sub))
                        d.append("scan%d_%d" % (n - 2, sub))
                        d.append("prepB%d_%d" % (n - 2, sub))
                    cj.append(("prep%d_%d" % (n, sub), (lambda n=n, sub=sub: gen_prep(n, sub)), d))
                for sub in range(2):
                    d = ["prep%d_%d" % (n, sub)]
                    if n >= 1:
                        d.append("prepB%d_%d" % (n - 1, sub))
                    if n >= 2:
                        d.append("scan%d_%d" % (n - 2, sub))
                    cj.append(("prepB%d_%d" % (n, sub), (lambda n=n, sub=sub: gen_prepB(n, sub)), d))
                for sub in range(2):
                    d = ["ret%d_%d" % (n - 1, sub), "prepB%d_%d" % (n - 1, sub)] if n >= 1 else []
                    cj.append(("ret%d_%d" % (n, sub), (lambda n=n, sub=sub: gen_ret(n, sub)), d))
                for sub in range(2):
                    d = ["prepB%d_%d" % (n, sub)]
                    if n >= 1:
                        d.append("scan%d_%d" % (n - 1, sub))
                    cj.append(("scan%d_%d" % (n, sub), (lambda n=n, sub=sub: gen_scan(n, sub)), d))
            run_jobs(cj, 12, False)

        for p in range(2):
            slot = next_piece("ga")
            for i in range(4):
                pt, pr = fm_group(slot, i, T, nT, nT.r)
                P.act(sgT.t[:, i, 0:T], pt[:, 0:T], AF.Sigmoid, [pr], [sgT.r])
            slot = next_piece("gb")
            for i in range(4):
                pt, pr = fm_group(slot, i, T, nT, nT.r)
                P.act(sbT.t[:, i, 0:T], pt[:, 0:T], AF.Sigmoid, [pr], [sbT.r])
            slot = next_piece("brg")
            for i in range(4):
                pt, pr = fm_group(slot, i, T, yaT, yaT.r)
                P.tt("dve", mtmp[i].t[:, 0:T], pt[:, 0:T], sgT.t[:, i, 0:T], ALU.mult, [pr, sgT.r], [mtmp[i].r])
            slot = next_piece("brr")
            for i in range(4):
                pt, pr = fm_group(slot, i, T, ybT, ybT.r)
                m2 = mtmp2[i % 2]
                P.tt("dve", m2.t[:, 0:T], pt[:, 0:T], sbT.t[:, i, 0:T], ALU.mult, [pr, sbT.r], [m2.r])
                P.tt("pool", mergedT.t[:, 4 * p + i, 0:T], m2.t[:, 0:T], mtmp[i].t[:, 0:T], ALU.add, [m2.r, mtmp[i].r], [mergedT.r])
        for ch in range(2):
            slot = next_piece("wo")
            for tb, (t0, tn) in enumerate(tbs):
                pt, pr = tm_group(slot, mergedT, mergedT.r, t0, tn)
                hh = H[tb].t[0:tn, ch * 512:(ch + 1) * 512]
                P.tt("dve", hh, pt[0:tn, :], hh, ALU.add, [pr, H[tb].r], [H[tb].r])

    def final_out(t):
        tbs = [(tb * 128, 128) for tb in range(4)]
        fins = []
        P.dma(fnw.t[:], fnw_d[:, :], [], [fnw.r])
        for tb in range(4):
            P.act(junk.t[:, :], H[tb].t[:, :], AF.Square, [H[tb].r], [junk.r, ss.r], accum_out=ss.t[:, tb:tb + 1])
        P.act(rstd.t[:, 0:4], ss.t[:, 0:4], AF.Ln, [ss.r], [rstd.r], scale=1.0 / D, bias=eps_ap())
        P.act(rstd.t[:, 0:4], rstd.t[:, 0:4], AF.Exp, [rstd.r], [rstd.r], scale=-0.5)
        for tb in range(4):
            P.stt("dve", OUTB[tb].t[:, :], H[tb].t[:, :], rstd.t[:, tb:tb + 1], fnw.t[:, :], ALU.mult, ALU.mult,
                  [H[tb].r, rstd.r, fnw.r], [OUTB[tb].r])
            r0 = t * 512 + tb * 128
            fins.append(P.dma(out_d[r0:r0 + 128, :], OUTB[tb].t[:, :], [OUTB[tb].r], []))
        return fins

    fin_ops = []
    tbs_m = [(0, NMETA)]
    P.dma(H[0].t[0:NMETA, :], meta_d[:, :], [], [H[0].r])
    ffn(1, NMETA, tbs_m)
    mixer(NMETA, tbs_m, 16, 0)
    ffn(2, NMETA, tbs_m)
    tbs = [(tb * 128, 128) for tb in range(4)]
    for t in range(nt):
        for tb in range(4):
            r0 = t * 512 + tb * 128
            P.dma(H[tb].t[:, :], x_d[r0:r0 + 128, :], [], [H[tb].r])
        ffn(1, 512, tbs)
        mixer(512, tbs, 128, NMETA + t * 512)
        ffn(2, 512, tbs)
        fin_ops += final_out(t)
    P.emit(fin_ops)
    if debug:
        print("ops", len(P.ops), {e: sum(1 for o in P.ops if o.eng == e) for e in ENGS})
    return nc


WNAMES = ("ffn1_w_in", "ffn1_w_out", "w_in", "w_branch_gdn", "w_branch_ret", "w_out", "ffn2_w_in", "ffn2_w_out")


def make_in_maps(inputs, nb, nt):
    W = {k: np.asarray(inputs[k], np.float32)[0] for k in WNAMES}
    wst = host_pack_weights(W)
    prm = host_pack_params({k: np.asarray(inputs[k], np.float32)[0] for k in
                            ("ffn1_norm", "mix_norm", "ffn2_norm", "ret_out_norm", "gdn_out_norm", "gdn_conv_w",
                             "gdn_a_log", "gdn_dt_bias", "w_in")})
    fnw = np.ascontiguousarray(np.broadcast_to(np.asarray(inputs["final_norm"], np.float32)[None, :], (128, D)))
    cst = host_consts()
    rope = host_rope(NMETA + nt * 512)
    meta = np.ascontiguousarray(np.asarray(inputs["meta_tokens"], np.float32))
    x = np.asarray(inputs["x"], np.float32)
    return [{"x": np.ascontiguousarray(x[b, :nt * 512]), "meta": meta, "wst": wst, "prm": prm, "fnw": fnw, "cst": cst, "rope": rope}
            for b in range(nb)]


def kernel(**inputs):
    nt = SEQ // 512
    nc = build(nt)
    in_maps = make_in_maps(inputs, 8, nt)
    res = run_bass_kernel_spmd(nc, in_maps, core_ids=list(range(8)))
    return np.stack([np.asarray(r["out"], np.float32) for r in res.results], axis=0)
```

```python
import contextlib
import numpy as np
import ml_dtypes
import concourse.bass as bass
import concourse.mybir as mybir
from concourse.bass_utils import run_bass_kernel_spmd

F32 = mybir.dt.float32
BF16 = mybir.dt.bfloat16
AF = mybir.ActivationFunctionType
ALU = mybir.AluOpType
AX = mybir.AxisListType

D = 1024
NMETA = 16
SEQ = 8192
DFF = 2816
NJ = 22
DPROJ = 10256
EPS = 1e-6
NH = 8
PIECE = 4096
NSLOT = 4
ENGS = ("pe", "act", "dve", "pool", "sp")
N_DMA_SLOTS = 12
SAME_ENGINE_SYNC = True
SB_BASE = 18432
SB_LIMIT = 229376


class Res:
    __slots__ = ("name", "last_w", "readers", "overlaps", "lo", "hi", "excl")

    def __init__(self, name, lo=None, hi=None, excl=False):
        self.name = name
        self.excl = excl
        self.last_w = None
        self.readers = {}
        self.overlaps = []
        self.lo = lo
        self.hi = hi


class Op:
    __slots__ = ("idx", "eng", "fn", "deps", "dma", "needs_inc", "sem", "val", "prev_val")

    def __init__(self, idx, eng, fn, deps, dma):
        self.idx = idx
        self.eng = eng
        self.fn = fn
        self.deps = deps
        self.dma = dma
        self.needs_inc = False
        self.sem = None
        self.val = 0
        self.prev_val = 0


class Prog:
    def __init__(self, nc):
        self.nc = nc
        self.ops = []

    def op(self, eng, fn, reads=(), writes=(), dma=False):
        idx = len(self.ops)
        deps = set()
        wset = []
        for w in writes:
            wset.append(w)
            wset.extend(w.overlaps)
        rl = []
        for r in reads:
            if r.excl:
                wset.append(r)
            else:
                rl.append(r)
        reads = rl
        for r in reads:
            if r.last_w is not None:
                deps.add(r.last_w)
        for w in wset:
            if w.last_w is not None:
                deps.add(w.last_w)
            for ridx in w.readers.values():
                deps.add(ridx)
        o = Op(idx, eng, fn, sorted(deps), dma)
        self.ops.append(o)
        for r in reads:
            r.readers[("dma", idx) if dma else eng] = idx
        for w in wset:
            w.last_w = idx
            w.readers = {}
        return o

    def mm(self, out, lhsT, rhs, start, stop, reads, writes):
        return self.op("pe", lambda e: e.matmul(out, lhsT=lhsT, rhs=rhs, start=start, stop=stop), reads, writes)

    def tr(self, out, in_, ident, reads, writes):
        return self.op("pe", lambda e: e.transpose(out=out, in_=in_, identity=ident), reads, writes)

    def tt(self, eng, out, in0, in1, op, reads, writes):
        return self.op(eng, lambda e: e.tensor_tensor(out=out, in0=in0, in1=in1, op=op), reads, writes)

    def ts(self, eng, out, in0, s1, s2, op0, op1, reads, writes):
        if s2 is None:
            return self.op(eng, lambda e: e.tensor_scalar(out=out, in0=in0, scalar1=s1, scalar2=None, op0=op0), reads, writes)
        return self.op(eng, lambda e: e.tensor_scalar(out=out, in0=in0, scalar1=s1, scalar2=s2, op0=op0, op1=op1), reads, writes)

    def stt(self, eng, out, in0, scalar, in1, op0, op1, reads, writes):
        return self.op(eng, lambda e: e.scalar_tensor_tensor(out=out, in0=in0, scalar=scalar, in1=in1, op0=op0, op1=op1), reads, writes)

    def act(self, out, in_, func, reads, writes, bias=None, scale=None, accum_out=None):
        kw = {}
        if bias is not None:
            kw["bias"] = bias
        if scale is not None:
            kw["scale"] = scale
        if accum_out is not None:
            kw["accum_out"] = accum_out
        return self.op("act", lambda e: e.activation(out=out, in_=in_, func=func, **kw), reads, writes)

    def copy(self, eng, out, in_, reads, writes):
        if eng == "act":
            return self.act(out, in_, AF.Copy, reads, writes)
        return self.op(eng, lambda e: e.tensor_copy(out=out, in_=in_), reads, writes)

    def red(self, eng, out, in_, reads, writes):
        return self.op(eng, lambda e: e.tensor_reduce(out=out, in_=in_, axis=AX.X, op=ALU.add), reads, writes)

    def memset(self, eng, out, val, writes):
        return self.op(eng, lambda e: e.memset(out, val), (), writes)

    def dma(self, out, in_, reads, writes, queue="sp"):
        return self.op(queue, lambda e: e.dma_start(out=out, in_=in_), reads, writes, dma=True)

    def emit(self, final_ops):
        nc = self.nc
        ops = self.ops

        def skip_same(o, dop):
            return (not dop.dma) and (not o.dma) and dop.eng == o.eng and (o.eng == "pe" or not SAME_ENGINE_SYNC)

        for o in ops:
            for d in o.deps:
                dop = ops[d]
                if dop.dma or skip_same(o, dop):
                    continue
                dop.needs_inc = True
        with contextlib.ExitStack() as st:
            esem = {e: st.enter_context(nc.semaphore("prog_" + e)) for e in ENGS}
            dsem = {q: [st.enter_context(nc.semaphore("dma_%s_%d" % (q, i))) for i in range(N_DMA_SLOTS)]
                    for q in ("sp", "pool")}
            cnt = {e: 0 for e in ENGS}
            dcnt = {q: 0 for q in dsem}
            duse = {q: [0] * N_DMA_SLOTS for q in dsem}
            for o in ops:
                if o.dma:
                    q = o.eng
                    s = dcnt[q] % N_DMA_SLOTS
                    dcnt[q] += 1
                    o.prev_val = 16 * duse[q][s]
                    duse[q][s] += 1
                    o.sem = dsem[q][s]
                    o.val = 16 * duse[q][s]
                elif o.needs_inc:
                    cnt[o.eng] += 1
                    o.sem = esem[o.eng]
                    o.val = cnt[o.eng]
            per = {e: [o for o in ops if o.eng == e] for e in ENGS}
            block = st.enter_context(nc.Block())

            def run(ename, eng):
                waited = {}
                for o in per[ename]:
                    need = {}
                    for d in o.deps:
                        dop = ops[d]
                        if skip_same(o, dop):
                            continue
                        k = id(dop.sem)
                        if k not in need or need[k][1] < dop.val:
                            need[k] = (dop.sem, dop.val)
                    if o.dma and o.prev_val > 0:
                        k = id(o.sem)
                        if k not in need or need[k][1] < o.prev_val:
                            need[k] = (o.sem, o.prev_val)
                    for k, (sem, val) in need.items():
                        if waited.get(k, 0) >= val:
                            continue
                        eng.wait_ge(sem, val)
                        waited[k] = val
                    ins = o.fn(eng)
                    if o.dma:
                        ins.then_inc(o.sem, 16)
                    elif o.needs_inc:
                        ins.then_inc(o.sem, 1)
                if ename == "sp":
                    for fo in final_ops:
                        eng.wait_ge(fo.sem, fo.val)

            @block.sync
            def _(e):
                run("sp", e)

            @block.tensor
            def _(e):
                run("pe", e)

            @block.scalar
            def _(e):
                run("act", e)

            @block.vector
            def _(e):
                run("dve", e)

            @block.gpsimd
            def _(e):
                run("pool", e)


class Buf:
    __slots__ = ("t", "r")

    def __init__(self, t, r):
        self.t = t
        self.r = r


class SBAlloc:
    def __init__(self, nc):
        self.nc = nc
        self.off = SB_BASE
        self.peak = SB_BASE
        self.all = []

    def alloc(self, name, shape, dt):
        esz = 4 if dt == F32 else 2
        n = 1
        for s in shape[1:]:
            n *= s
        size = (n * esz + 31) // 32 * 32
        assert self.off + size <= SB_LIMIT, ("SBUF overflow", name, self.off, size)
        t = self.nc.alloc_sbuf_tensor_at(name, list(shape), dt, offset=self.off)
        r = Res(name, self.off, self.off + size)
        self.off += size
        self.peak = max(self.peak, self.off)
        self.all.append(r)
        return Buf(t, r)

    def finalize(self):
        rs = sorted(self.all, key=lambda r: r.lo)
        for i, a in enumerate(rs):
            for b in rs[i + 1:]:
                if b.lo >= a.hi:
                    break
                a.overlaps.append(b)
                b.overlaps.append(a)


def piece_specs():
    sp = []

    def ffn(i):
        win, wout, nrm = "ffn%d_w_in" % i, "ffn%d_w_out" % i, "ffn%d_norm" % i
        for g in range(NJ // 2):
            j0, j1 = 2 * g, 2 * g + 1
            sp.append(dict(kind="FM", w=win, cc=[j0 * 128, DFF + j0 * 128, j1 * 128, DFF + j1 * 128], norm=nrm, tag=("ffn_in", i, g)))
        for ch in range(2):
            for (j0, nj) in ((0, 8), (8, 8), (16, 6)):
                sp.append(dict(kind="TMK", w=wout, j0=j0, nj=nj, c0=ch * 512, norm=None, tag=("ffn_out", i, ch, j0, nj)))

    ffn(1)
    for hg in range(2):
        for nm, base in (("q", 0), ("k", 1024), ("v", 2048), ("z", 3072), ("rg", 7184)):
            sp.append(dict(kind="FM", w="w_in", cc=[base + (4 * hg + i) * 128 for i in range(4)], norm="mix_norm", tag=(nm, hg)))
        for nm, base in (("rq", 4112), ("rk", 5136), ("rv", 6160)):
            sp.append(dict(kind="TM", w="w_in", c0=base + hg * 512, norm="mix_norm", tag=(nm, hg)))
    for p in range(2):
        sp.append(dict(kind="FM", w="w_in", cc=[8208 + (4 * p + i) * 128 for i in range(4)], norm="mix_norm", tag=("ga", p)))
        sp.append(dict(kind="FM", w="w_in", cc=[9232 + (4 * p + i) * 128 for i in range(4)], norm="mix_norm", tag=("gb", p)))
        sp.append(dict(kind="FM", w="w_branch_gdn", cc=[(4 * p + i) * 128 for i in range(4)], norm=None, tag=("brg", p)))
        sp.append(dict(kind="FM", w="w_branch_ret", cc=[(4 * p + i) * 128 for i in range(4)], norm=None, tag=("brr", p)))
    for ch in range(2):
        sp.append(dict(kind="TM", w="w_out", c0=ch * 512, norm=None, tag=("wo", ch)))
    ffn(2)
    return sp


PIECES = piece_specs()
NP = len(PIECES)
NORM_COL = {"ffn1_norm": 0, "mix_norm": 8, "ffn2_norm": 16}
PC_RETN = 24
PC_GDNN = 32
PC_CONV = 33
PC_ALOG = 129
PC_DTB = 137
PC_WBA = 145
NPRM = PC_WBA + 128
CC_U = 0
CC_MUI = 128
CC_MLS = 256
CC_DRT = 384
CC_XI = 1408
CC_ZS128 = 2432
CC_ZS16 = 2440
CC_CD128 = 2448
CC_CD16 = 2456
CC_ONES = 2464
CC_MLO = 2592
NCST = CC_MLO + 128


def host_consts():
    c = np.zeros((128, NCST), np.float32)
    j = np.arange(128)
    c[:, CC_U:CC_U + 128] = (j[:, None] <= j[None, :])
    c[:, CC_MUI:CC_MUI + 128] = (j[:, None] <= j[None, :])
    c[:, CC_MLS:CC_MLS + 128] = (j[None, :] < j[:, None]) & ((j[None, :] // 64) == (j[:, None] // 64))
    c[:, CC_MLO:CC_MLO + 128] = (j[None, :] < 64) & (j[:, None] >= 64)
    lg = np.log1p(-np.exp2(-5.0 - np.arange(NH, dtype=np.float64)))
    m = j[:, None, None]
    cc = j[None, None, :]
    dr = np.where(m <= cc, np.exp(np.maximum(cc - m, 0) * lg[None, :, None]), 0.0)
    c[:, CC_DRT:CC_DRT + 1024] = dr.reshape(128, 1024)
    xi = np.exp((j[None, None, :] + 1.0) * lg[None, :, None]) * np.ones((128, 1, 1))
    c[:, CC_XI:CC_XI + 1024] = xi.reshape(128, 1024)
    sc = 128.0 ** -0.5
    c[:, CC_ZS128:CC_ZS128 + 8] = np.exp((127.0 - j)[:, None] * lg[None, :]) * sc
    c[:16, CC_ZS16:CC_ZS16 + 8] = np.exp((15.0 - j[:16])[:, None] * lg[None, :]) * sc
    c[:, CC_CD128:CC_CD128 + 8] = np.exp(128.0 * lg)[None, :]
    c[:, CC_CD16:CC_CD16 + 8] = np.exp(16.0 * lg)[None, :]
    c[:, CC_ONES:CC_ONES + 128] = 1.0
    return c


def host_rope(L):
    inv = (1.0 / (10000.0 ** np.linspace(0.0, 1.0, 64, dtype=np.float32))).astype(np.float32)
    pos = np.arange(L, dtype=np.float32)
    ang = (pos[:, None] * inv[None, :]).astype(np.float32)
    r = np.zeros((L, 128), np.float32)
    r[:, :64] = np.cos(ang.astype(np.float64))
    r[:, 64:] = np.sin(ang.astype(np.float64))
    return r


def host_pack_weights(W):
    out = np.zeros((NP, 128, PIECE), np.float32)
    for s, sp in enumerate(PIECES):
        w = W[sp["w"]]
        if sp["kind"] == "FM":
            wk = w.reshape(8, 128, -1)
            for i, c0 in enumerate(sp["cc"]):
                blk = wk[:, :, c0:c0 + 128]
                out[s].reshape(128, 8, 4, 128)[:, :, i, :] = blk.transpose(1, 0, 2)
        elif sp["kind"] == "TM":
            wk = w.reshape(8, 128, -1)[:, :, sp["c0"]:sp["c0"] + 512]
            out[s].reshape(128, 8, 512)[:, :, :] = wk.transpose(1, 0, 2)
        else:
            wk = w.reshape(NJ, 128, -1)[sp["j0"]:sp["j0"] + sp["nj"], :, sp["c0"]:sp["c0"] + 512]
            out[s].reshape(128, 8, 512)[:, :sp["nj"], :] = wk.transpose(1, 0, 2)
    return out


def host_pack_params(I):
    prm = np.zeros((128, NPRM), np.float32)
    for nm, c0 in NORM_COL.items():
        prm[:, c0:c0 + 8] = np.asarray(I[nm]).reshape(8, 128).T
    prm[:, PC_RETN:PC_RETN + 8] = np.asarray(I["ret_out_norm"]).reshape(8, 128).T
    prm[:, PC_GDNN] = np.asarray(I["gdn_out_norm"]).reshape(128)
    cw = np.asarray(I["gdn_conv_w"]).reshape(4, 24, 128)
    prm[:, PC_CONV:PC_CONV + 96] = cw.transpose(2, 1, 0).reshape(128, 96)
    prm[:, PC_ALOG:PC_ALOG + 8] = np.asarray(I["gdn_a_log"]).reshape(1, 8)
    prm[:, PC_DTB:PC_DTB + 8] = np.asarray(I["gdn_dt_bias"]).reshape(1, 8)
    wba = np.asarray(I["w_in"]).reshape(8, 128, DPROJ)[:, :, 4096:4112]
    prm[:, PC_WBA:PC_WBA + 128] = wba.transpose(1, 0, 2).reshape(128, 128)
    return prm


def build(nt, debug=False):
    seq = nt * 512
    L = NMETA + seq
    nc = bass.Bass("TRN2", target_bir_lowering=False)
    x_d = nc.dram_tensor("x", [seq, D], F32, kind="ExternalInput").ap()
    meta_d = nc.dram_tensor("meta", [NMETA, D], F32, kind="ExternalInput").ap()
    wst_d = nc.dram_tensor("wst", [NP, 128, PIECE], F32, kind="ExternalInput").ap()
    prm_d = nc.dram_tensor("prm", [128, NPRM], F32, kind="ExternalInput").ap()
    fnw_d = nc.dram_tensor("fnw", [128, D], F32, kind="ExternalInput").ap()
    cst_d = nc.dram_tensor("cst", [128, NCST], F32, kind="ExternalInput").ap()
    rope_d = nc.dram_tensor("rope", [L, 128], F32, kind="ExternalInput").ap()
    out_d = nc.dram_tensor("out", [seq, D], F32, kind="ExternalOutput").ap()
    wsc_d = nc.dram_tensor("wsc", [NP, 128, PIECE], BF16).ap()
    wsc_r = [Res("wsc%d" % s) for s in range(NP)]

    P = Prog(nc)
    sb = SBAlloc(nc)
    A = sb.alloc

    prm = A("prm", [128, NPRM], F32)
    cst = A("cst", [128, NCST], F32)
    identb = A("identb", [128, 128], BF16)
    onesb = A("onesb", [128, 128], BF16)
    wba = A("wba", [128, 8, 16], BF16)
    nA = A("nA", [128, 8], F32)
    H = [A("H%d" % tb, [128, D], F32) for tb in range(4)]
    nb = [A("nb%d" % i, [128, D], BF16) for i in range(2)]
    nT = A("nT", [128, 8, 512], BF16)
    ring = [A("ring%d" % i, [128, PIECE], BF16) for i in range(NSLOT)]
    Sg = [A("Sg%d" % h, [128, 128], F32) for h in range(NH)]
    Sr = [A("Sr%d" % h, [128, 128], F32) for h in range(NH)]
    Sgb = [A("Sgb%d" % h, [128, 128], BF16) for h in range(NH)]
    Srb = [A("Srb%d" % h, [128, 128], BF16) for h in range(NH)]
    ctail = A("ctail", [128, 24, 3], F32)
    yaT = A("yaT", [128, 8, 512], BF16)
    ybT = A("ybT", [128, 8, 512], BF16)
    ss = A("ss", [128, 4], F32)
    rstd = A("rstd", [128, 4], F32)
    gtok = A("gtok", [128, 4, 8], F32)
    btok = A("btok", [128, 4, 8], F32)
    braw = A("braw", [128, 4, 16], F32)
    batmp = A("batmp", [128, 4, 8], F32)
    cbias = A("cbias", [128, 2], F32)

    def eps_ap(rows=128):
        return cbias.t[0:rows, 0:1]

    def one_ap(rows=128):
        return cbias.t[0:rows, 1:2]

    arena0 = sb.off

    stage = [A("stage%d" % i, [128, PIECE], F32) for i in range(4)]
    sb.off = arena0
    actT = A("actT", [128, NJ, 512], BF16)
    sil = [A("sil%d" % i, [128, 512], F32) for i in range(2)]
    OUTB = [A("OUTB%d" % tb, [128, D], F32) for tb in range(4)]
    fnw = A("fnw", [128, D], F32)
    junk = A("junk", [128, D], BF16)
    XN = [A("XN%d" % tb, [128, D], F32) for tb in range(4)]
    nbx = [A("nbx%d" % tb, [128, D], BF16) for tb in range(4)]
    junk2 = A("junk2", [128, D], BF16)
    ss2 = A("ss2", [128, 4], F32)
    rstd2 = A("rstd2", [128, 4], F32)
    sb.off = arena0
    qT = A("qT", [128, 4, 512], BF16)
    kT = A("kT", [128, 4, 512], BF16)
    vT = A("vT", [128, 4, 512], BF16)
    szT = A("szT", [128, 4, 512], BF16)
    srgT = A("srgT", [128, 4, 512], BF16)
    rkd_tm = A("rkd_tm", [128, 4, 512], BF16)
    rv_tm = A("rv_tm", [128, 4, 512], BF16)
    rqT = A("rqT", [128, 4, 512], BF16)
    rkT = A("rkT", [128, 4, 512], BF16)
    rqdT = A("rqdT", [128, 4, 512], BF16)
    sub0 = sb.off
    rq_tm = A("rq_tm", [128, 4, 512], BF16)
    rk_tm = A("rk_tm", [128, 4, 512], BF16)
    NCS = 5
    rope = A("rope", [128, 4, 128], F32)
    cin = [A("cin%d" % i, [128, 515], F32) for i in range(NCS)]
    cacc = [A("cacc%d" % i, [128, 512], F32) for i in range(NCS)]
    cf = cacc
    csq = [A("csq%d" % i, [128, 512], BF16) for i in range(NCS)]
    crs = [A("crs%d" % i, [128, 512], F32) for i in range(NCS)]
    rxs = [A("rxs%d" % i, [128, 512], F32) for i in range(2)]
    rt = [[A("rt%d_%d" % (i, k), [128, 256], F32) for k in range(2)] for i in range(2)]
    sb.off = sub0
    cksub = []
    for sub in range(2):
        base = {}
        for nm, shp, dt in (
            ("A0", [128, 2, 128], BF16), ("A1", [128, 2, 128], BF16), ("B0", [128, 2, 128], BF16), ("B1", [128, 2, 128], BF16),
            ("X0", [128, 2, 128], BF16), ("X1", [128, 2, 128], BF16), ("Y", [128, 2, 128], BF16),
            ("sz", [128, 4, 128], BF16),
            ("vnew", [128, 2, 128], BF16), ("osb", [128, 2, 128], F32), ("osq", [128, 2, 128], F32), ("ost", [128, 4], F32),
            ("on", [128, 2, 128], BF16),
            ("scT", [128, 2, 128], BF16), ("orb", [128, 2, 128], F32), ("orq", [128, 2, 128], F32), ("ort", [128, 4], F32),
            ("orn", [128, 2, 128], BF16), ("ybt", [128, 2, 128], F32),
        ):
            base[nm] = A("%s_s%d" % (nm, sub), shp, dt)
        base["orc"] = base["orb"]
        pars = []
        for par in range(2):
            d = dict(base)
            for nm, shp, dt in (("wT", [128, 2, 128], BF16), ("u", [128, 2, 128], F32), ("kd", [128, 2, 128], BF16),
                                ("PT", [128, 2, 128], BF16), ("qdT", [128, 2, 128], BF16), ("egl", [128, 2], F32),
                                ("Ain", [128, 2, 128], BF16), ("Ao", [128, 2, 128], BF16), ("kbg", [128, 2, 128], BF16),
                                ("bv", [128, 2, 128], BF16),
                                ("GU", [128, 2, 128], F32), ("gcs", [128, 4], F32), ("eg", [128, 2], F32), ("beg", [128, 2], F32),
                                ("ekd", [128, 2], F32), ("Dm", [128, 2, 128], F32), ("El", [128, 2, 128], F32),
                                ("EQ", [128, 2, 128], F32), ("EGQ", [128, 2, 128], F32)):
                d[nm] = A("%s_s%d_%d" % (nm, sub, par), shp, dt)
            d["Eu"] = d["Dm"]
            d["EA"] = d["El"]
            d["EAo"] = d["GU"]
            pars.append(d)
        cksub.append(pars)
    hg_end = sb.off
    sb.off = arena0
    sgT = A("sgT", [128, 4, 512], BF16)
    sbT = A("sbT", [128, 4, 512], BF16)
    mtmp = [A("mtmp%d" % i, [128, 512], F32) for i in range(4)]
    mtmp2 = [A("mtmp2_%d" % i, [128, 512], F32) for i in range(2)]
    mergedT = A("mergedT", [128, 8, 512], BF16)
    sb.finalize()
    if debug:
        print("SBUF peak", sb.peak, "of", SB_LIMIT, "hg_end", hg_end, "arena0", arena0)

    NF, NB = 7, 1
    psf = [nc.alloc_psum_tensor("psf%d" % i, [128, 512], F32) for i in range(NF)]
    psb = [nc.alloc_psum_tensor("psb%d" % i, [128, 1024], BF16) for i in range(NB)]
    psf_r = [Res("psf%d" % i, excl=True) for i in range(NF)]
    psb_r = [Res("psb%d" % i, excl=True) for i in range(NB)]
    from collections import deque
    free_f = deque(range(NF))
    free_b = deque(range(NB))

    def PSF(hold=False):
        i = free_f.popleft()
        if not hold:
            free_f.append(i)
        return psf[i], psf_r[i]

    def PSB(hold=False):
        i = free_b.popleft()
        if not hold:
            free_b.append(i)
        return psb[i], psb_r[i]

    def gPSF(k=1):
        while len(free_f) < k:
            yield
        got = [PSF(hold=True) for _ in range(k)]
        return got[0] if k == 1 else got

    def gPSB():
        while not free_b:
            yield
        return PSB(hold=True)

    def PFREE(r):
        if r in psf_r:
            free_f.append(psf_r.index(r))
        else:
            free_b.append(psb_r.index(r))

    def pc(c0, n=1):
        return prm.t[:, c0:c0 + n]

    def cc(c0, n, rows=128):
        return cst.t[0:rows, c0:c0 + n]

    rr = ["act", "dve"]
    rrc = {"i": 0}

    def nxt(choices=("act", "dve")):
        rrc["i"] += 1
        return choices[rrc["i"] % len(choices)]

    P.dma(prm.t[:], prm_d[:, :], [], [prm.r])
    P.dma(cst.t[:], cst_d[:, :], [], [cst.r])
    P.memset("pool", crs[0].t[:, 0:128], 0.0, [crs[0].r])
    P.op("pool", lambda e: e.affine_select(out=crs[0].t[:, 0:128], in_=crs[0].t[:, 0:128], pattern=[[-1, 128]],
                                           compare_op=ALU.not_equal, fill=1.0, base=0, channel_multiplier=1),
         [crs[0].r], [crs[0].r])
    P.copy("dve", identb.t[:], crs[0].t[:, 0:128], [crs[0].r], [identb.r])
    P.memset("dve", onesb.t[:], 1.0, [onesb.r])
    for kc in range(8):
        P.ts("dve", wba.t[:, kc, :], prm.t[:, PC_WBA + kc * 16:PC_WBA + kc * 16 + 16], pc(NORM_COL["mix_norm"] + kc), None,
             ALU.mult, None, [prm.r], [wba.r])
    P.act(nA.t[:], pc(PC_ALOG, 8), AF.Exp, [prm.r], [nA.r])
    P.ts("dve", nA.t[:], nA.t[:], -1.0, None, ALU.mult, None, [nA.r], [nA.r])
    for h in range(NH):
        P.memset("pool", Sg[h].t[:], 0.0, [Sg[h].r])
        P.memset("pool", Sr[h].t[:], 0.0, [Sr[h].r])
        P.memset("dve", Sgb[h].t[:], 0.0, [Sgb[h].r])
        P.memset("dve", Srb[h].t[:], 0.0, [Srb[h].r])
    P.memset("pool", ctail.t[:], 0.0, [ctail.r])
    P.memset("pool", cbias.t[:, 0:1], EPS, [cbias.r])
    P.memset("pool", cbias.t[:, 1:2], 1.0, [cbias.r])

    def pre_in(s):
        P.dma(stage[s % 4].t[:], wst_d[s], [], [stage[s % 4].r])

    for s in range(min(3, NP)):
        pre_in(s)
    for s, spc in enumerate(PIECES):
        stg = stage[s % 4]
        slot = ring[s % NSLOT]
        if spc["norm"] is not None:
            c0 = NORM_COL[spc["norm"]]
            for kc in range(8):
                eng = "act" if kc % 2 == 0 else "dve"
                o_ = slot.t[:, kc * 512:(kc + 1) * 512]
                i_ = stg.t[:, kc * 512:(kc + 1) * 512]
                if eng == "act":
                    P.act(o_, i_, AF.Identity, [stg.r, prm.r], [slot.r], scale=pc(c0 + kc))
                else:
                    P.ts("dve", o_, i_, pc(c0 + kc), None, ALU.mult, None, [stg.r, prm.r], [slot.r])
        else:
            P.copy("act", slot.t[:, 0:1536], stg.t[:, 0:1536], [stg.r], [slot.r])
            P.copy("dve", slot.t[:, 1536:3072], stg.t[:, 1536:3072], [stg.r], [slot.r])
            P.copy("pool", slot.t[:, 3072:4096], stg.t[:, 3072:4096], [stg.r], [slot.r])
        if s + 3 < NP:
            pre_in(s + 3)
        P.dma(wsc_d[s], slot.t[:], [slot.r], [wsc_r[s]])

    wstate = {"issued": 0, "cur": 0}
    total_pieces = NP * (nt + 1)

    def w_issue_upto(n):
        while wstate["issued"] < min(n, total_pieces):
            g = wstate["issued"]
            s = g % NP
            slot = ring[g % NSLOT]
            P.dma(slot.t[:], wsc_d[s], [wsc_r[s]], [slot.r])
            wstate["issued"] += 1

    def next_piece(tag_prefix, lag=0):
        g = wstate["cur"]
        s = g % NP
        assert PIECES[s]["tag"][0] == tag_prefix, (PIECES[s]["tag"], tag_prefix)
        w_issue_upto(g + NSLOT - lag)
        wstate["cur"] += 1
        return ring[g % NSLOT]

    def run_jobs(jobs, maxact, fifo):
        done = set()
        pending = list(jobs)
        active = []
        while pending or active:
            for j in list(pending):
                if len(active) >= maxact:
                    break
                if all(d in done for d in j[2]):
                    active.append((j[0], j[1]()))
                    pending.remove(j)
                elif fifo:
                    break
            assert active, ("scheduler deadlock", [j[0] for j in pending][:5])
            for a in list(active):
                try:
                    next(a[1])
                except StopIteration:
                    active.remove(a)
                    done.add(a[0])

    def run_interleaved(gens):
        gens = list(gens)
        while gens:
            for g in list(gens):
                try:
                    next(g)
                except StopIteration:
                    gens.remove(g)

    def norm_A(tbs, src, nbl, jk, ssb, rsb):
        ntb = len(tbs)
        for tb, (t0, tn) in enumerate(tbs):
            P.act(jk.t[0:tn, :], src[tb].t[0:tn, :], AF.Square, [src[tb].r], [jk.r, ssb.r], accum_out=ssb.t[0:tn, tb:tb + 1])
        tn0 = tbs[0][1]
        P.act(rsb.t[0:tn0, 0:ntb], ssb.t[0:tn0, 0:ntb], AF.Ln, [ssb.r], [rsb.r], scale=1.0 / D, bias=eps_ap(tn0))
        P.act(rsb.t[0:tn0, 0:ntb], rsb.t[0:tn0, 0:ntb], AF.Exp, [rsb.r], [rsb.r], scale=-0.5)
        for tb, (t0, tn) in enumerate(tbs):
            nbb = nbl[tb % len(nbl)]
            P.ts("dve", nbb.t[0:tn, :], src[tb].t[0:tn, :], rsb.t[0:tn, tb:tb + 1], None, ALU.mult, None, [src[tb].r, rsb.r], [nbb.r])

    def norm_B(tbs, nbl):
        for tb, (t0, tn) in enumerate(tbs):
            nbb = nbl[tb % len(nbl)]
            pt, pr = PSB()
            for fc in range(8):
                P.tr(pt[:, fc * 128:fc * 128 + tn], nbb.t[0:tn, fc * 128:(fc + 1) * 128], identb.t[0:tn, 0:tn], [nbb.r, identb.r], [pr])
            P.copy(nxt(), nT.t[:, :, t0:t0 + tn], pt[:, :].rearrange("p (f t) -> p f t", f=8)[:, :, 0:tn], [pr], [nT.r])

    def norm_to_nT(T, tbs):
        ntb = len(tbs)
        for tb, (t0, tn) in enumerate(tbs):
            P.act(junk.t[0:tn, :], H[tb].t[0:tn, :], AF.Square, [H[tb].r], [junk.r, ss.r], accum_out=ss.t[0:tn, tb:tb + 1])
        tn0 = tbs[0][1]
        P.act(rstd.t[0:tn0, 0:ntb], ss.t[0:tn0, 0:ntb], AF.Ln, [ss.r], [rstd.r], scale=1.0 / D, bias=eps_ap(tn0))
        P.act(rstd.t[0:tn0, 0:ntb], rstd.t[0:tn0, 0:ntb], AF.Exp, [rstd.r], [rstd.r], scale=-0.5)
        for tb, (t0, tn) in enumerate(tbs):
            nbb = nb[tb % 2]
            P.ts("dve", nbb.t[0:tn, :], H[tb].t[0:tn, :], rstd.t[0:tn, tb:tb + 1], None, ALU.mult, None, [H[tb].r, rstd.r], [nbb.r])
            pt, pr = PSB()
            for fc in range(8):
                P.tr(pt[:, fc * 128:fc * 128 + tn], nbb.t[0:tn, fc * 128:(fc + 1) * 128], identb.t[0:tn, 0:tn], [nbb.r, identb.r], [pr])
            P.copy(nxt(), nT.t[:, :, t0:t0 + tn], pt[:, :].rearrange("p (f t) -> p f t", f=8)[:, :, 0:tn], [pr], [nT.r])

    def fm_group(slot, i, T, rhsbuf, rhs_r):
        pt, pr = PSF()
        for kc in range(8):
            P.mm(pt[:, 0:T], slot.t[:, kc * 512 + i * 128:kc * 512 + (i + 1) * 128], rhsbuf.t[:, kc, 0:T], kc == 0, kc == 7,
                 [slot.r, rhs_r], [pr])
        return pt, pr

    def tm_group(slot, lbuf, l_r, t0, tn):
        pt, pr = PSF()
        for kc in range(8):
            P.mm(pt[0:tn, :], lbuf.t[:, kc, t0:t0 + tn], slot.t[:, kc * 512:(kc + 1) * 512], kc == 0, kc == 7, [slot.r, l_r], [pr])
        return pt, pr

    def ffn(i, T, tbs, skip_norm=False, pre=None, mid=None, post=None):
        if not skip_norm:
            norm_to_nT(T, tbs)
        if pre is not None:
            pre()
        for g in range(NJ // 2):
            slot = next_piece("ffn_in")
            for jj in range(2):
                j = 2 * g + jj
                pg, pgr = fm_group(slot, 2 * jj, T, nT, nT.r)
                pu, pur = fm_group(slot, 2 * jj + 1, T, nT, nT.r)
                sl = sil[j % 2]
                P.act(sl.t[:, 0:T], pg[:, 0:T], AF.Silu, [pgr], [sl.r])
                P.tt("dve", actT.t[:, j, 0:T], sl.t[:, 0:T], pu[:, 0:T], ALU.mult, [sl.r, pur], [actT.r])
        if mid is not None:
            mid()
        for ch in range(2):
            pts = [PSF() for _ in tbs]
            for (j0, nj) in ((0, 8), (8, 8), (16, 6)):
                slot = next_piece("ffn_out")
                for jj in range(nj):
                    j = j0 + jj
                    for tb, (t0, tn) in enumerate(tbs):
                        P.mm(pts[tb][0][0:tn, :], actT.t[:, j, t0:t0 + tn], slot.t[:, jj * 512:(jj + 1) * 512], j == 0, j == NJ - 1,
                             [slot.r, actT.r], [pts[tb][1]])
            if ch == 1 and post is not None:
                post()
            for tb, (t0, tn) in enumerate(tbs):
                hh = H[tb].t[0:tn, ch * 512:(ch + 1) * 512]
                P.stt("dve", hh, pts[tb][0][0:tn, :], 0.5, hh, ALU.mult, ALU.add, [pts[tb][1], H[tb].r], [H[tb].r])

    def mixer(T, tbs, C, tok0):
        NCH = T // C
        nlev = {128: 5, 16: 3}[C]
        zs_c = CC_ZS128 if C == 128 else CC_ZS16
        cd_c = CC_CD128 if C == 128 else CC_CD16
        norm_to_nT(T, tbs)
        def ba_proj():
            pt, pr = yield from gPSF()
            for n in range(NCH):
                for kc in range(8):
                    P.mm(pt[0:C, n * 16:(n + 1) * 16], nT.t[:, kc, n * C:(n + 1) * C], wba.t[:, kc, :], kc == 0, kc == 7, [nT.r, wba.r], [pr])
            P.copy("act", braw.t[0:C, 0:NCH, :], pt[0:C, 0:NCH * 16].rearrange("p (n c) -> p n c", c=16), [pr], [braw.r])
            PFREE(pr)
            yield
            P.act(btok.t[0:C, 0:NCH, :], braw.t[0:C, 0:NCH, 0:8], AF.Exp, [braw.r], [btok.r], scale=-1.0)
            yield
            P.ts("dve", btok.t[0:C, 0:NCH, :], btok.t[0:C, 0:NCH, :], 1.0, None, ALU.add, None, [btok.r], [btok.r])
            yield
            P.op("dve", lambda e: e.reciprocal(out=btok.t[0:C, 0:NCH, :], in_=btok.t[0:C, 0:NCH, :]), [btok.r], [btok.r])
            yield
            P.tt("dve", batmp.t[0:C, 0:NCH, :], braw.t[0:C, 0:NCH, 8:16], prm.t[0:C, PC_DTB:PC_DTB + 8].unsqueeze(1).broadcast_to([C, NCH, 8]),
                 ALU.add, [braw.r, prm.r], [batmp.r])
            yield
            P.act(batmp.t[0:C, 0:NCH, :], batmp.t[0:C, 0:NCH, :], AF.Exp, [batmp.r], [batmp.r])
            yield
            P.act(batmp.t[0:C, 0:NCH, :], batmp.t[0:C, 0:NCH, :], AF.Ln, [batmp.r], [batmp.r], bias=one_ap(C))
            yield
            P.tt("dve", gtok.t[0:C, 0:NCH, :], batmp.t[0:C, 0:NCH, :], nA.t[0:C, :].unsqueeze(1).broadcast_to([C, NCH, 8]), ALU.mult,
                 [batmp.r, nA.r], [gtok.r])
            yield

        for hg in range(2):
            for tb, (t0, tn) in enumerate(tbs):
                P.dma(rope.t[0:tn, tb, :], rope_d[tok0 + t0:tok0 + t0 + tn, :], [], [rope.r])
            pj = []
            lag = 2
            slots = {}

            def get_slot(nm, first):
                if first:
                    slots[nm] = next_piece(nm, lag)
                return slots[nm]

            def job_conv(nm, dst, qi, i, cs, hg=hg):
                slot = get_slot(nm, i == 0)
                cch = qi * 8 + hg * 4 + i
                pt, pr = yield from gPSF()
                for kc in range(8):
                    P.mm(pt[:, 0:T], slot.t[:, kc * 512 + i * 128:kc * 512 + (i + 1) * 128], nT.t[:, kc, 0:T], kc == 0, kc == 7, [slot.r, nT.r], [pr])
                yield
                ci = cin[cs]
                P.copy("act", ci.t[:, 3:3 + T], pt[:, 0:T], [pr], [ci.r])
                PFREE(pr)
                P.copy("pool", ci.t[:, 0:3], ctail.t[:, cch, :], [ctail.r], [ci.r])
                yield
                P.copy("pool", ctail.t[:, cch, :], ci.t[:, T:T + 3], [ci.r], [ctail.r])
                ca = cacc[cs]
                wcol = PC_CONV + cch * 4
                P.act(ca.t[:, 0:T], ci.t[:, 0:T], AF.Identity, [ci.r, prm.r], [ca.r], scale=pc(wcol))
                yield
                for tap in range(1, 4):
                    P.stt("dve", ca.t[:, 0:T], ci.t[:, tap:tap + T], pc(wcol + tap), ca.t[:, 0:T], ALU.mult, ALU.add,
                          [ci.r, prm.r, ca.r], [ca.r])
                yield
                if nm == "v":
                    P.act(dst.t[:, i, 0:T], ca.t[:, 0:T], AF.Silu, [ca.r], [dst.r])
                    yield
                    return
                f = cf[cs]
                P.act(f.t[:, 0:T], ca.t[:, 0:T], AF.Silu, [ca.r], [f.r])
                yield
                sq = csq[cs]
                P.tt("pool", sq.t[:, 0:T], f.t[:, 0:T], f.t[:, 0:T], ALU.mult, [f.r], [sq.r])
                yield
                p2, p2r = yield from gPSF()
                P.mm(p2[:, 0:T], onesb.t[:, :], sq.t[:, 0:T], True, True, [onesb.r, sq.r], [p2r])
                yield
                rs_ = crs[cs]
                P.act(rs_.t[:, 0:T], p2[:, 0:T], AF.Ln, [p2r], [rs_.r], bias=eps_ap())
                PFREE(p2r)
                P.act(rs_.t[:, 0:T], rs_.t[:, 0:T], AF.Exp, [rs_.r], [rs_.r], scale=-0.5)
                yield
                if nm == "q":
                    P.stt("dve", dst.t[:, i, 0:T], f.t[:, 0:T], 128.0 ** -0.5, rs_.t[:, 0:T], ALU.mult, ALU.mult, [f.r, rs_.r], [dst.r])
                else:
                    P.tt("dve", dst.t[:, i, 0:T], f.t[:, 0:T], rs_.t[:, 0:T], ALU.mult, [f.r, rs_.r], [dst.r])
                yield

            def job_silu(nm, dst, i):
                slot = get_slot(nm, i == 0)
                pt, pr = yield from gPSF()
                for kc in range(8):
                    P.mm(pt[:, 0:T], slot.t[:, kc * 512 + i * 128:kc * 512 + (i + 1) * 128], nT.t[:, kc, 0:T], kc == 0, kc == 7, [slot.r, nT.r], [pr])
                yield
                P.act(dst.t[:, i, 0:T], pt[:, 0:T], AF.Silu, [pr], [dst.r])
                PFREE(pr)
                yield

            def job_rot(nm, dst, tb, t0, tn, rs):
                slot = get_slot(nm, tb == 0)
                pt, pr = yield from gPSF()
                for kc in range(8):
                    P.mm(pt[0:tn, :], nT.t[:, kc, t0:t0 + tn], slot.t[:, kc * 512:(kc + 1) * 512], kc == 0, kc == 7, [slot.r, nT.r], [pr])
                yield
                xs = rxs[rs]
                P.copy("act", xs.t[0:tn, :], pt[0:tn, :], [pr], [xs.r])
                PFREE(pr)
                yield
                xv = xs.t[0:tn, :].rearrange("p (h j two) -> p h j two", h=4, two=2)
                x0 = xv[:, :, :, 0]
                x1 = xv[:, :, :, 1]
                cosb = rope.t[0:tn, tb, 0:64].unsqueeze(1).broadcast_to([tn, 4, 64])
                sinb = rope.t[0:tn, tb, 64:128].unsqueeze(1).broadcast_to([tn, 4, 64])
                ta, tb_ = rt[rs]
                t1 = ta.t[0:tn, :].rearrange("p (h j) -> p h j", h=4)
                t2 = tb_.t[0:tn, :].rearrange("p (h j) -> p h j", h=4)
                ov = dst.t[0:tn, tb, :].rearrange("p (h j two) -> p h j two", h=4, two=2)
                P.tt("dve", t1, x0, cosb, ALU.mult, [xs.r, rope.r], [ta.r])
                P.tt("pool", t2, x1, sinb, ALU.mult, [xs.r, rope.r], [tb_.r])
                yield
                P.tt("dve", ov[:, :, :, 0], t1, t2, ALU.subtract, [ta.r, tb_.r], [dst.r])
                yield
                P.tt("dve", t1, x1, cosb, ALU.mult, [xs.r, rope.r], [ta.r])
                P.tt("pool", t2, x0, sinb, ALU.mult, [xs.r, rope.r], [tb_.r])
                yield
                P.tt("pool", ov[:, :, :, 1], t1, t2, ALU.add, [ta.r, tb_.r], [dst.r])
                yield

            def job_rv(tb, t0, tn):
                slot = get_slot("rv", tb == 0)
                pt, pr = yield from gPSF()
                for kc in range(8):
                    P.mm(pt[0:tn, :], nT.t[:, kc, t0:t0 + tn], slot.t[:, kc * 512:(kc + 1) * 512], kc == 0, kc == 7, [slot.r, nT.r], [pr])
                yield
                P.copy("act", rv_tm.t[0:tn, tb, :], pt[0:tn, :], [pr], [rv_tm.r])
                PFREE(pr)
                yield

            def job_tr(src, dst, scl, tb, t0, tn, hg=hg):
                if src is rk_tm:
                    P.tt("dve", rkd_tm.t[0:tn, tb, :].rearrange("p (h d) -> p h d", h=4),
                         rk_tm.t[0:tn, tb, :].rearrange("p (h d) -> p h d", h=4),
                         cst.t[0:tn, zs_c + hg * 4:zs_c + hg * 4 + 4].unsqueeze(2).broadcast_to([tn, 4, 128]), ALU.mult,
                         [rk_tm.r, cst.r], [rkd_tm.r])
                pt, pr = yield from gPSB()
                for i in range(4):
                    P.tr(pt[:, i * 128:i * 128 + tn], src.t[0:tn, tb, i * 128:(i + 1) * 128], identb.t[0:tn, 0:tn], [src.r, identb.r], [pr])
                yield
                pv = pt[:, 0:512].rearrange("p (h t) -> p h t", h=4)[:, :, 0:tn]
                if scl is None:
                    P.copy("dve", dst.t[:, :, t0:t0 + tn], pv, [pr], [dst.r])
                else:
                    P.act(dst.t[:, :, t0:t0 + tn], pv, AF.Identity, [pr], [dst.r], scale=scl)
                PFREE(pr)
                yield

            cnt = 0
            if hg == 0:
                pj.append(("ba", ba_proj, []))
            for qi, (nm, dst) in enumerate((("q", qT), ("k", kT), ("v", vT))):
                for i in range(4):
                    deps = ["conv%d" % (cnt - NCS)] if cnt >= NCS else []
                    pj.append(("conv%d" % cnt, (lambda nm=nm, dst=dst, qi=qi, i=i, cs=cnt % NCS: job_conv(nm, dst, qi, i, cs)), deps))
                    cnt += 1
            for nm, dst in (("z", szT), ("rg", srgT)):
                for i in range(4):
                    pj.append(("silu_%s%d" % (nm, i), (lambda nm=nm, dst=dst, i=i: job_silu(nm, dst, i)), []))
            rcnt = 0
            for nm, dst in (("rq", rq_tm), ("rk", rk_tm)):
                for tb, (t0, tn) in enumerate(tbs):
                    deps = ["rot%d" % (rcnt - 2)] if rcnt >= 2 else []
                    pj.append(("rot%d" % rcnt, (lambda nm=nm, dst=dst, tb=tb, t0=t0, tn=tn, rs=rcnt % 2: job_rot(nm, dst, tb, t0, tn, rs)), deps))
                    rcnt += 1
            for tb, (t0, tn) in enumerate(tbs):
                pj.append(("rv%d" % tb, (lambda tb=tb, t0=t0, tn=tn: job_rv(tb, t0, tn)), []))
            ntb_ = len(tbs)
            for si, (src, dst, scl) in enumerate(((rq_tm, rqT, None), (rk_tm, rkT, 128.0 ** -0.5))):
                for tb, (t0, tn) in enumerate(tbs):
                    pj.append(("tr%d_%d" % (si, tb), (lambda src=src, dst=dst, scl=scl, tb=tb, t0=t0, tn=tn: job_tr(src, dst, scl, tb, t0, tn)),
                               ["rot%d" % (si * ntb_ + tb)]))
            run_jobs(pj, 5 if ntb_ > 1 else 1, True)
            for i in range(4):
                h = hg * 4 + i
                P.tt("pool", rqdT.t[:, i, 0:T].rearrange("p (n c) -> p n c", c=C), rqT.t[:, i, 0:T].rearrange("p (n c) -> p n c", c=C),
                     cst.t[:, CC_XI + h * 128:CC_XI + h * 128 + C].unsqueeze(1).broadcast_to([128, NCH, C]), ALU.mult, [rqT.r, cst.r], [rqdT.r])

            hs = slice(hg * 4, hg * 4 + 4)

            def h3(ap, w):
                return ap.rearrange("p (h c) -> p h c", h=4)

            def gen_prep(n, sub, hg=hg):
                K = cksub[sub][n % 2]
                c0 = n * C
                li = (2 * sub, 2 * sub + 1)
                hcs = slice(hg * 4 + 2 * sub, hg * 4 + 2 * sub + 2)

                def bc2(ap2):
                    return ap2.unsqueeze(2).broadcast_to([C, 2, C])

                def bcd(ap2):
                    return ap2.unsqueeze(2).broadcast_to([C, 2, 128])

                def v3(b):
                    return b.t[0:C, :, 0:C]

                def g3(ap):
                    return ap.rearrange("p (h c) -> p h c", h=2)

                def tab(c0_):
                    return cst.t[0:C, c0_:c0_ + C].unsqueeze(1).broadcast_to([C, 2, C])

                Ub, MUI, MLS, MLO = tab(CC_U), tab(CC_MUI), tab(CC_MLS), tab(CC_MLO)
                Idb = identb.t[0:C, 0:C].unsqueeze(1).broadcast_to([C, 2, C])
                two = (C == 128)
                P.tt("pool", v3(K["GU"]), Ub, bc2(gtok.t[0:C, n, hcs]), ALU.mult, [cst.r, gtok.r], [K["GU"].r])
                (pG, pGr), (pS, pSr) = yield from gPSF(2)
                for ii in range(2):
                    P.mm(pG[:, ii * C:(ii + 1) * C], cst.t[0:C, CC_ONES:CC_ONES + 128], K["GU"].t[0:C, ii, 0:C], True, True, [cst.r, K["GU"].r], [pGr])
                P.mm(pS[0:C, 0:2], cst.t[0:C, CC_U:CC_U + C], gtok.t[0:C, n, hcs], True, True, [cst.r, gtok.r], [pSr])
                P.mm(pS[:, 2:4], cst.t[0:C, CC_ONES:CC_ONES + 128], gtok.t[0:C, n, hcs], True, True, [cst.r, gtok.r], [pSr])
                yield
                gcs = K["gcs"]
                P.copy("act", gcs.t[:, 2:4], pS[:, 2:4], [pSr], [gcs.r])
                P.copy("act", gcs.t[0:C, 0:2], pS[0:C, 0:2], [pSr], [gcs.r])
                PFREE(pSr)
                P.act(K["EGQ"].t[:, :, 0:C], g3(pG[:, 0:2 * C]), AF.Exp, [pGr], [K["EGQ"].r])
                yield
                P.tt("dve", v3(K["Dm"]), g3(pG[0:C, 0:2 * C]), bc2(gcs.t[0:C, 0:2]), ALU.subtract, [pGr, gcs.r], [K["Dm"].r])
                PFREE(pGr)
                P.act(K["eg"].t[0:C, :], gcs.t[0:C, 0:2], AF.Exp, [gcs.r], [K["eg"].r])
                P.tt("pool", K["ekd"].t[0:C, :], gcs.t[0:C, 2:4], gcs.t[0:C, 0:2], ALU.subtract, [gcs.r], [K["ekd"].r])
                P.act(K["egl"].t[:, :], gcs.t[:, 2:4], AF.Exp, [gcs.r], [K["egl"].r])
                P.tt("pool", K["qdT"].t[:, :, 0:C], qT.t[:, li[0]:li[1] + 1, c0:c0 + C], K["EGQ"].t[:, :, 0:C], ALU.mult, [qT.r, K["EGQ"].r], [K["qdT"].r])
                yield
                P.ts("dve", v3(K["El"]), v3(K["Dm"]), 0.0, None, ALU.max, None, [K["Dm"].r], [K["El"].r])
                P.ts("dve", v3(K["Eu"]), v3(K["Dm"]), 0.0, None, ALU.min, None, [K["Dm"].r], [K["Eu"].r])
                P.act(K["ekd"].t[0:C, :], K["ekd"].t[0:C, :], AF.Exp, [K["ekd"].r], [K["ekd"].r])
                P.tt("pool", K["beg"].t[0:C, :], K["eg"].t[0:C, :], btok.t[0:C, n, hcs], ALU.mult, [K["eg"].r, btok.r], [K["beg"].r])
                ptkv, ptkvr = yield from gPSB()
                for ii in range(2):
                    P.tr(ptkv[0:C, ii * 128:(ii + 1) * 128], kT.t[:, li[ii], c0:c0 + C], identb.t[:, :], [kT.r, identb.r], [ptkvr])
                for ii in range(2):
                    P.tr(ptkv[0:C, 256 + ii * 128:256 + (ii + 1) * 128], vT.t[:, li[ii], c0:c0 + C], identb.t[:, :], [vT.r, identb.r], [ptkvr])
                yield
                P.act(v3(K["El"]), v3(K["El"]), AF.Exp, [K["El"].r], [K["El"].r], scale=-1.0)
                P.act(v3(K["Eu"]), v3(K["Eu"]), AF.Exp, [K["Eu"].r], [K["Eu"].r])
                ktv = ptkv[0:C, 0:256].rearrange("p (h d) -> p h d", h=2)
                vtv = ptkv[0:C, 256:512].rearrange("p (h d) -> p h d", h=2)
                P.tt("dve", K["kd"].t[0:C, :, :], ktv, bcd(K["ekd"].t[0:C, :]), ALU.mult, [ptkvr, K["ekd"].r], [K["kd"].r])
                P.tt("dve", K["kbg"].t[0:C, :, :], ktv, bcd(K["beg"].t[0:C, :]), ALU.mult, [ptkvr, K["beg"].r], [K["kbg"].r])
                P.tt("dve", K["bv"].t[0:C, :, :], vtv, bcd(btok.t[0:C, n, hcs]), ALU.mult, [ptkvr, btok.r], [K["bv"].r])
                PFREE(ptkvr)
                (pK, pKr), (pQ, pQr) = yield from gPSF(2)
                for ii in range(2):
                    P.mm(pQ[0:C, ii * C:(ii + 1) * C], kT.t[:, li[ii], c0:c0 + C], qT.t[:, li[ii], c0:c0 + C], True, True, [kT.r, qT.r], [pQr])
                for ii in range(2):
                    P.mm(pK[0:C, ii * C:(ii + 1) * C], kT.t[:, li[ii], c0:c0 + C], kT.t[:, li[ii], c0:c0 + C], True, True, [kT.r], [pKr])
                yield
                P.tt("pool", v3(K["EQ"]), v3(K["Eu"]), MUI, ALU.mult, [K["Eu"].r, cst.r], [K["EQ"].r])
                P.tt("pool", v3(K["El"]), v3(K["El"]), bc2(btok.t[0:C, n, hcs]), ALU.mult, [K["El"].r, btok.r], [K["El"].r])
                yield
                P.tt("dve", v3(K["PT"]), g3(pQ[0:C, 0:2 * C]), v3(K["EQ"]), ALU.mult, [pQr, K["EQ"].r], [K["PT"].r])
                PFREE(pQr)
                if two:
                    P.tt("pool", v3(K["EAo"]), v3(K["El"]), MLO, ALU.mult, [K["El"].r, cst.r], [K["EAo"].r])
                P.tt("pool", v3(K["EA"]), v3(K["El"]), MLS, ALU.mult, [K["El"].r, cst.r], [K["EA"].r])
                yield
                P.tt("dve", v3(K["Ain"]), g3(pK[0:C, 0:2 * C]), v3(K["EA"]), ALU.mult, [pKr, K["EA"].r], [K["Ain"].r])
                if two:
                    P.tt("dve", v3(K["Ao"]), g3(pK[0:C, 0:2 * C]), v3(K["EAo"]), ALU.mult, [pKr, K["EAo"].r], [K["Ao"].r])
                PFREE(pKr)
                yield

            def gen_prepB(n, sub, hg=hg):
                K = cksub[sub][n % 2]

                def v3(b):
                    return b.t[0:C, :, 0:C]

                def g3(ap):
                    return ap.rearrange("p (h c) -> p h c", h=2)

                Idb = identb.t[0:C, 0:C].unsqueeze(1).broadcast_to([C, 2, C])
                two = (C == 128)
                ptb, ptbr = yield from gPSB()
                for ii in range(2):
                    P.tr(ptb[0:C, ii * C:(ii + 1) * C], K["Ain"].t[0:C, ii, 0:C], identb.t[0:C, 0:C], [K["Ain"].r, identb.r], [ptbr])
                yield
                P.copy("act", v3(K["B0"]), g3(ptb[0:C, 0:2 * C]), [ptbr], [K["B0"].r])
                P.tt("dve", v3(K["X0"]), Idb, g3(ptb[0:C, 0:2 * C]), ALU.subtract, [identb.r, ptbr], [K["X0"].r])
                PFREE(ptbr)
                yield
                for k in range(nlev + 1):
                    Ak, Bk = (K["Ain"] if k == 0 else K["A%d" % (k % 2)]), K["B%d" % (k % 2)]
                    An, Bn = K["A%d" % ((k + 1) % 2)], K["B%d" % ((k + 1) % 2)]
                    Xp, Xn = K["X%d" % ((k + 1) % 2)], K["X%d" % (k % 2)]
                    need = (1 if k < nlev else 0) + (1 if k < nlev - 1 else 0) + (1 if k >= 1 else 0)
                    got = yield from gPSF(need)
                    if need == 1:
                        got = [got]
                    got = list(got)
                    pA = pB2 = pX = None
                    if k < nlev:
                        pA, pAr = got.pop(0)
                        for ii in range(2):
                            P.mm(pA[0:C, ii * C:(ii + 1) * C], Bk.t[0:C, ii, 0:C], Ak.t[0:C, ii, 0:C], True, True, [Bk.r, Ak.r], [pAr])
                    if k < nlev - 1:
                        pB2, pB2r = got.pop(0)
                        for ii in range(2):
                            P.mm(pB2[0:C, ii * C:(ii + 1) * C], Ak.t[0:C, ii, 0:C], Bk.t[0:C, ii, 0:C], True, True, [Bk.r, Ak.r], [pB2r])
                    if k >= 1:
                        pX, pXr = got.pop(0)
                        for ii in range(2):
                            P.mm(pX[0:C, ii * C:(ii + 1) * C], Ak.t[0:C, ii, 0:C], Xp.t[0:C, ii, 0:C], True, True, [Ak.r, Xp.r], [pXr])
                    yield
                    if pA is not None:
                        P.copy("act", v3(An), g3(pA[0:C, 0:2 * C]), [pAr], [An.r])
                        PFREE(pAr)
                    if pB2 is not None:
                        P.copy("act", v3(Bn), g3(pB2[0:C, 0:2 * C]), [pB2r], [Bn.r])
                        PFREE(pB2r)
                    if pX is not None:
                        P.tt("dve", v3(Xn), g3(pX[0:C, 0:2 * C]), v3(Xp), ALU.add, [pXr, Xp.r], [Xn.r])
                        PFREE(pXr)
                    yield
                TT = K["X%d" % (nlev % 2)]
                if two:
                    (pY, pYr), (pZ, pZr) = yield from gPSF(2)
                    for ii in range(2):
                        P.mm(pY[0:C, ii * C:(ii + 1) * C], K["Ao"].t[0:C, ii, 0:C], TT.t[0:C, ii, 0:C], True, True, [K["Ao"].r, TT.r], [pYr])
                    for ii in range(2):
                        P.mm(pZ[0:C, ii * 128:(ii + 1) * 128], TT.t[0:C, ii, 0:C], K["bv"].t[0:C, ii, :], True, True, [TT.r, K["bv"].r], [pZr])
                    for ii in range(2):
                        P.mm(pZ[0:C, 256 + ii * 128:256 + (ii + 1) * 128], TT.t[0:C, ii, 0:C], K["kbg"].t[0:C, ii, :], True, True, [TT.r, K["kbg"].r], [pZr])
                    yield
                    P.tt("dve", v3(K["Y"]), Idb, g3(pY[0:C, 0:2 * C]), ALU.subtract, [identb.r, pYr], [K["Y"].r])
                    PFREE(pYr)
                    P.copy("act", K["sz"].t[0:C, :, :], pZ[0:C, 0:512].rearrange("p (h d) -> p h d", h=4), [pZr], [K["sz"].r])
                    PFREE(pZr)
                    yield
                    (pU, pUr), (pW, pWr) = yield from gPSF(2)
                    for ii in range(2):
                        P.mm(pU[0:C, ii * 128:(ii + 1) * 128], K["Y"].t[0:C, ii, 0:C], K["sz"].t[0:C, ii, :], True, True, [K["Y"].r, K["sz"].r], [pUr])
                    for ii in range(2):
                        P.mm(pW[:, ii * C:(ii + 1) * C], K["sz"].t[0:C, 2 + ii, :], K["Y"].t[0:C, ii, 0:C], True, True, [K["Y"].r, K["sz"].r], [pWr])
                else:
                    (pU, pUr), (pW, pWr) = yield from gPSF(2)
                    for ii in range(2):
                        P.mm(pU[0:C, ii * 128:(ii + 1) * 128], TT.t[0:C, ii, 0:C], K["bv"].t[0:C, ii, :], True, True, [TT.r, K["bv"].r], [pUr])
                    for ii in range(2):
                        P.mm(pW[:, ii * C:(ii + 1) * C], K["kbg"].t[0:C, ii, :], TT.t[0:C, ii, 0:C], True, True, [TT.r, K["kbg"].r], [pWr])
                yield
                P.copy("act", K["u"].t[0:C, :, :], pU[0:C, 0:256].rearrange("p (h d) -> p h d", h=2), [pUr], [K["u"].r])
                PFREE(pUr)
                P.copy("dve", K["wT"].t[:, :, 0:C], g3(pW[:, 0:2 * C]), [pWr], [K["wT"].r])
                PFREE(pWr)
                yield

            def gen_scan(n, sub, hg=hg):
                K = cksub[sub][n % 2]
                c0 = n * C
                li = (2 * sub, 2 * sub + 1)
                hgl = (hg * 4 + 2 * sub, hg * 4 + 2 * sub + 1)

                def bcd(ap2):
                    return ap2.unsqueeze(2).broadcast_to([C, 2, 128])

                def g3(ap):
                    return ap.rearrange("p (h c) -> p h c", h=2)

                p1, p1r = yield from gPSF()
                for ii in range(2):
                    h = hgl[ii]
                    P.mm(p1[0:C, ii * 128:(ii + 1) * 128], K["wT"].t[:, ii, 0:C], Sgb[h].t[:, :], True, True, [K["wT"].r, Sgb[h].r], [p1r])
                yield
                P.tt("dve", K["vnew"].t[0:C, :, :], K["u"].t[0:C, :, :], p1[0:C, 0:256].rearrange("p (h d) -> p h d", h=2), ALU.subtract,
                     [K["u"].r, p1r], [K["vnew"].r])
                PFREE(p1r)
                yield
                (pO, pOr), (pSS, pSSr) = yield from gPSF(2)
                for ii in range(2):
                    P.mm(pSS[:, ii * 128:(ii + 1) * 128], K["kd"].t[0:C, ii, :], K["vnew"].t[0:C, ii, :], True, True, [K["kd"].r, K["vnew"].r], [pSSr])
                for ii in range(2):
                    h = hgl[ii]
                    P.mm(pO[0:C, ii * 128:(ii + 1) * 128], K["qdT"].t[:, ii, 0:C], Sgb[h].t[:, :], True, False, [K["qdT"].r, Sgb[h].r], [pOr])
                    P.mm(pO[0:C, ii * 128:(ii + 1) * 128], K["PT"].t[0:C, ii, 0:C], K["vnew"].t[0:C, ii, :], False, True, [K["PT"].r, K["vnew"].r], [pOr])
                yield
                for ii in range(2):
                    h = hgl[ii]
                    P.stt("dve", Sg[h].t[:, :], Sg[h].t[:, :], K["egl"].t[:, ii:ii + 1], pSS[:, ii * 128:(ii + 1) * 128], ALU.mult, ALU.add,
                          [Sg[h].r, K["egl"].r, pSSr], [Sg[h].r])
                PFREE(pSSr)
                P.copy("act", K["osb"].t[0:C, :, :], pO[0:C, 0:256].rearrange("p (h d) -> p h d", h=2), [pOr], [K["osb"].r])
                PFREE(pOr)
                yield
                for ii in range(2):
                    h = hgl[ii]
                    P.copy("act" if ii == 0 else "pool", Sgb[h].t[:, :], Sg[h].t[:, :], [Sg[h].r], [Sgb[h].r])
                P.tt("pool", K["osq"].t[0:C, :, :], K["osb"].t[0:C, :, :], K["osb"].t[0:C, :, :], ALU.mult, [K["osb"].r], [K["osq"].r])
                yield
                P.red("dve", K["ost"].t[0:C, 0:2], K["osq"].t[0:C, :, :], [K["osq"].r], [K["ost"].r])
                yield
                P.act(K["ost"].t[0:C, 2:4], K["ost"].t[0:C, 0:2], AF.Ln, [K["ost"].r], [K["ost"].r], scale=1.0 / 128, bias=eps_ap(C))
                P.act(K["ost"].t[0:C, 2:4], K["ost"].t[0:C, 2:4], AF.Exp, [K["ost"].r], [K["ost"].r], scale=-0.5)
                yield
                P.tt("pool", K["on"].t[0:C, :, :], K["osb"].t[0:C, :, :], bcd(K["ost"].t[0:C, 2:4]), ALU.mult, [K["osb"].r, K["ost"].r], [K["on"].r])
                yield
                pt, pr = yield from gPSB()
                for ii in range(2):
                    P.tr(pt[:, ii * C:(ii + 1) * C], K["on"].t[0:C, ii, :], identb.t[0:C, 0:C], [K["on"].r, identb.r], [pr])
                yield
                P.stt("dve", yaT.t[:, hgl[0]:hgl[1] + 1, c0:c0 + C], g3(pt[:, 0:2 * C]), pc(PC_GDNN), szT.t[:, li[0]:li[1] + 1, c0:c0 + C],
                      ALU.mult, ALU.mult, [pr, prm.r, szT.r], [yaT.r])
                PFREE(pr)
                yield

            def gen_ret(n, sub, hg=hg):
                K = cksub[sub][n % 2]
                c0 = n * C
                tb = c0 // 128
                li = (2 * sub, 2 * sub + 1)
                hgl = (hg * 4 + 2 * sub, hg * 4 + 2 * sub + 1)

                def bcd(ap2):
                    return ap2.unsqueeze(2).broadcast_to([C, 2, 128])

                def g3(ap):
                    return ap.rearrange("p (h c) -> p h c", h=2)

                pSc, pScr = yield from gPSF()
                for ii in range(2):
                    i = li[ii]
                    P.mm(pSc[0:C, ii * C:(ii + 1) * C], rkT.t[:, i, c0:c0 + C], rqT.t[:, i, c0:c0 + C], True, True, [rkT.r, rqT.r], [pScr])
                yield
                P.tt("dve", K["scT"].t[0:C, :, 0:C], g3(pSc[0:C, 0:2 * C]),
                     cst.t[0:C, CC_DRT + hgl[0] * 128:CC_DRT + hgl[0] * 128 + 256].rearrange("p (h c) -> p h c", h=2)[:, :, 0:C], ALU.mult,
                     [pScr, cst.r], [K["scT"].r])
                PFREE(pScr)
                yield
                (pOr2, pOr2r), (pSR, pSRr) = yield from gPSF(2)
                for ii in range(2):
                    i = li[ii]
                    h = hgl[ii]
                    P.mm(pOr2[0:C, ii * 128:(ii + 1) * 128], rqdT.t[:, i, c0:c0 + C], Srb[h].t[:, :], True, False, [rqdT.r, Srb[h].r], [pOr2r])
                    P.mm(pOr2[0:C, ii * 128:(ii + 1) * 128], K["scT"].t[0:C, ii, 0:C], rv_tm.t[0:C, tb, i * 128:(i + 1) * 128], False, True,
                         [K["scT"].r, rv_tm.r], [pOr2r])
                for ii in range(2):
                    i = li[ii]
                    P.mm(pSR[:, ii * 128:(ii + 1) * 128], rkd_tm.t[0:C, tb, i * 128:(i + 1) * 128], rv_tm.t[0:C, tb, i * 128:(i + 1) * 128], True, True,
                         [rkd_tm.r, rv_tm.r], [pSRr])
                yield
                for ii in range(2):
                    h = hgl[ii]
                    P.stt("dve", Sr[h].t[:, :], Sr[h].t[:, :], cst.t[:, cd_c + h:cd_c + h + 1], pSR[:, ii * 128:(ii + 1) * 128], ALU.mult, ALU.add,
                          [Sr[h].r, cst.r, pSRr], [Sr[h].r])
                PFREE(pSRr)
                P.copy("act", K["orb"].t[0:C, :, :], pOr2[0:C, 0:256].rearrange("p (h d) -> p h d", h=2), [pOr2r], [K["orb"].r])
                PFREE(pOr2r)
                yield
                for ii in range(2):
                    h = hgl[ii]
                    P.copy("act" if ii == 1 else "pool", Srb[h].t[:, :], Sr[h].t[:, :], [Sr[h].r], [Srb[h].r])
                P.red("dve", K["ort"].t[0:C, 0:2], K["orb"].t[0:C, :, :], [K["orb"].r], [K["ort"].r])
                yield
                P.ts("dve", K["ort"].t[0:C, 0:2], K["ort"].t[0:C, 0:2], 1.0 / 128, None, ALU.mult, None, [K["ort"].r], [K["ort"].r])
                yield
                P.tt("pool", K["orc"].t[0:C, :, :], K["orb"].t[0:C, :, :], bcd(K["ort"].t[0:C, 0:2]), ALU.subtract, [K["orb"].r, K["ort"].r], [K["orc"].r])
                yield
                P.tt("pool", K["orq"].t[0:C, :, :], K["orc"].t[0:C, :, :], K["orc"].t[0:C, :, :], ALU.mult, [K["orc"].r], [K["orq"].r])
                yield
                P.red("dve", K["ort"].t[0:C, 2:4], K["orq"].t[0:C, :, :], [K["orq"].r], [K["ort"].r])
                yield
                P.act(K["ort"].t[0:C, 2:4], K["ort"].t[0:C, 2:4], AF.Ln, [K["ort"].r], [K["ort"].r], scale=1.0 / 128, bias=eps_ap(C))
                P.act(K["ort"].t[0:C, 2:4], K["ort"].t[0:C, 2:4], AF.Exp, [K["ort"].r], [K["ort"].r], scale=-0.5)
                yield
                P.tt("pool", K["orn"].t[0:C, :, :], K["orc"].t[0:C, :, :], bcd(K["ort"].t[0:C, 2:4]), ALU.mult, [K["orc"].r, K["ort"].r], [K["orn"].r])
                yield
                pt, pr = yield from gPSB()
                for ii in range(2):
                    P.tr(pt[:, ii * C:(ii + 1) * C], K["orn"].t[0:C, ii, :], identb.t[0:C, 0:C], [K["orn"].r, identb.r], [pr])
                yield
                P.tt("dve", K["ybt"].t[:, :, 0:C], g3(pt[:, 0:2 * C]),
                     prm.t[:, PC_RETN + hgl[0]:PC_RETN + hgl[0] + 2].unsqueeze(2).broadcast_to([128, 2, C]), ALU.mult, [pr, prm.r], [K["ybt"].r])
                PFREE(pr)
                yield
                P.tt("pool", ybT.t[:, hgl[0]:hgl[1] + 1, c0:c0 + C], K["ybt"].t[:, :, 0:C], srgT.t[:, li[0]:li[1] + 1, c0:c0 + C], ALU.mult,
                     [K["ybt"].r, srgT.r], [ybT.r])
                yield

            cj = []
            for n in range(NCH):
                for sub in range(2):
                    d = []
                    if n >= 2:
                        d.append("prep%d_%d" % (n - 2, sub))
                        d.append("scan%d_%d" % (n - 2, sub))
                        d.append("prepB%d_%d" % (n - 2, sub))
                    cj.append(("prep%d_%d" % (n, sub), (lambda n=n, sub=sub: gen_prep(n, sub)), d))
                for sub in range(2):
                    d = ["prep%d_%d" % (n, sub)]
                    if n >= 1:
                        d.append("prepB%d_%d" % (n - 1, sub))
                    if n >= 2:
                        d.append("scan%d_%d" % (n - 2, sub))
                    cj.append(("prepB%d_%d" % (n, sub), (lambda n=n, sub=sub: gen_prepB(n, sub)), d))
                for sub in range(2):
                    d = ["ret%d_%d" % (n - 1, sub), "prepB%d_%d" % (n - 1, sub)] if n >= 1 else []
                    cj.append(("ret%d_%d" % (n, sub), (lambda n=n, sub=sub: gen_ret(n, sub)), d))
                for sub in range(2):
                    d = ["prepB%d_%d" % (n, sub)]
                    if n >= 1:
                        d.append("scan%d_%d" % (n - 1, sub))
                    cj.append(("scan%d_%d" % (n, sub), (lambda n=n, sub=sub: gen_scan(n, sub)), d))
            run_jobs(cj, 12, False)

        for p in range(2):
            slot = next_piece("ga")
            for i in range(4):
                pt, pr = fm_group(slot, i, T, nT, nT.r)
                P.act(sgT.t[:, i, 0:T], pt[:, 0:T], AF.Sigmoid, [pr], [sgT.r])
            slot = next_piece("gb")
            for i in range(4):
                pt, pr = fm_group(slot, i, T, nT, nT.r)
                P.act(sbT.t[:, i, 0:T], pt[:, 0:T], AF.Sigmoid, [pr], [sbT.r])
            slot = next_piece("brg")
            for i in range(4):
                pt, pr = fm_group(slot, i, T, yaT, yaT.r)
                P.tt("dve", mtmp[i].t[:, 0:T], pt[:, 0:T], sgT.t[:, i, 0:T], ALU.mult, [pr, sgT.r], [mtmp[i].r])
            slot = next_piece("brr")
            for i in range(4):
                pt, pr = fm_group(slot, i, T, ybT, ybT.r)
                m2 = mtmp2[i % 2]
                P.tt("dve", m2.t[:, 0:T], pt[:, 0:T], sbT.t[:, i, 0:T], ALU.mult, [pr, sbT.r], [m2.r])
                P.tt("pool", mergedT.t[:, 4 * p + i, 0:T], m2.t[:, 0:T], mtmp[i].t[:, 0:T], ALU.add, [m2.r, mtmp[i].r], [mergedT.r])
        for ch in range(2):
            slot = next_piece("wo")
            for tb, (t0, tn) in enumerate(tbs):
                pt, pr = tm_group(slot, mergedT, mergedT.r, t0, tn)
                hh = H[tb].t[0:tn, ch * 512:(ch + 1) * 512]
                P.tt("dve", hh, pt[0:tn, :], hh, ALU.add, [pr, H[tb].r], [H[tb].r])

    def final_out(t):
        tbs = [(tb * 128, 128) for tb in range(4)]
        fins = []
        P.dma(fnw.t[:], fnw_d[:, :], [], [fnw.r])
        for tb in range(4):
            P.act(junk.t[:, :], H[tb].t[:, :], AF.Square, [H[tb].r], [junk.r, ss.r], accum_out=ss.t[:, tb:tb + 1])
        P.act(rstd.t[:, 0:4], ss.t[:, 0:4], AF.Ln, [ss.r], [rstd.r], scale=1.0 / D, bias=eps_ap())
        P.act(rstd.t[:, 0:4], rstd.t[:, 0:4], AF.Exp, [rstd.r], [rstd.r], scale=-0.5)
        for tb in range(4):
            P.stt("dve", OUTB[tb].t[:, :], H[tb].t[:, :], rstd.t[:, tb:tb + 1], fnw.t[:, :], ALU.mult, ALU.mult,
                  [H[tb].r, rstd.r, fnw.r], [OUTB[tb].r])
            r0 = t * 512 + tb * 128
            fins.append(P.dma(out_d[r0:r0 + 128, :], OUTB[tb].t[:, :], [OUTB[tb].r], []))
        return fins

    fin_ops = []
    tbs_m = [(0, NMETA)]
    tbs = [(tb * 128, 128) for tb in range(4)]

    def prefetch_hooks(tn_):
        def pre():
            for tb in range(4):
                r0 = tn_ * 512 + tb * 128
                P.dma(XN[tb].t[:, :], x_d[r0:r0 + 128, :], [], [XN[tb].r])

        def mid():
            norm_A(tbs, XN, nbx, junk2, ss2, rstd2)

        def post():
            norm_B(tbs, nbx)
        return pre, mid, post

    P.dma(H[0].t[0:NMETA, :], meta_d[:, :], [], [H[0].r])
    ffn(1, NMETA, tbs_m)
    mixer(NMETA, tbs_m, 16, 0)
    pre, mid, post = prefetch_hooks(0)
    ffn(2, NMETA, tbs_m, pre=pre, mid=mid, post=post)
    for t in range(nt):
        for tb in range(4):
            P.copy("pool", H[tb].t[:, :], XN[tb].t[:, :], [XN[tb].r], [H[tb].r])
        ffn(1, 512, tbs, skip_norm=True)
        mixer(512, tbs, 128, NMETA + t * 512)
        if t + 1 < nt:
            pre, mid, post = prefetch_hooks(t + 1)
            ffn(2, 512, tbs, pre=pre, mid=mid, post=post)
        else:
            ffn(2, 512, tbs)
        fin_ops += final_out(t)
    P.emit(fin_ops)
    if debug:
        print("ops", len(P.ops), {e: sum(1 for o in P.ops if o.eng == e) for e in ENGS})
    return nc


WNAMES = ("ffn1_w_in", "ffn1_w_out", "w_in", "w_branch_gdn", "w_branch_ret", "w_out", "ffn2_w_in", "ffn2_w_out")


def make_in_maps(inputs, nb, nt):
    W = {k: np.asarray(inputs[k], np.float32)[0] for k in WNAMES}
    wst = host_pack_weights(W)
    prm = host_pack_params({k: np.asarray(inputs[k], np.float32)[0] for k in
                            ("ffn1_norm", "mix_norm", "ffn2_norm", "ret_out_norm", "gdn_out_norm", "gdn_conv_w",
                             "gdn_a_log", "gdn_dt_bias", "w_in")})
    fnw = np.ascontiguousarray(np.broadcast_to(np.asarray(inputs["final_norm"], np.float32)[None, :], (128, D)))
    cst = host_consts()
    rope = host_rope(NMETA + nt * 512)
    meta = np.ascontiguousarray(np.asarray(inputs["meta_tokens"], np.float32))
    x = np.asarray(inputs["x"], np.float32)
    return [{"x": np.ascontiguousarray(x[b, :nt * 512]), "meta": meta, "wst": wst, "prm": prm, "fnw": fnw, "cst": cst, "rope": rope}
            for b in range(nb)]


def kernel(**inputs):
    nt = SEQ // 512
    nc = build(nt)
    in_maps = make_in_maps(inputs, 8, nt)
    res = run_bass_kernel_spmd(nc, in_maps, core_ids=list(range(8)))
    return np.stack([np.asarray(r["out"], np.float32) for r in res.results], axis=0)
```

```python
import contextlib
import numpy as np
import ml_dtypes
import concourse.bass as bass
import concourse.mybir as mybir
from concourse.bass_utils import run_bass_kernel_spmd

F32 = mybir.dt.float32
BF16 = mybir.dt.bfloat16
AF = mybir.ActivationFunctionType
ALU = mybir.AluOpType
AX = mybir.AxisListType

D = 1024
NMETA = 16
SEQ = 8192
DFF = 2816
NJ = 22
DPROJ = 10256
EPS = 1e-6
NH = 8
PIECE = 4096
NSLOT = 4
ENGS = ("pe", "act", "dve", "pool", "sp")
N_DMA_SLOTS = 12
SAME_ENGINE_SYNC = True
SB_BASE = 18432
SB_LIMIT = 229376


class Res:
    __slots__ = ("name", "last_w", "readers", "overlaps", "lo", "hi", "excl")

    def __init__(self, name, lo=None, hi=None, excl=False):
        self.name = name
        self.excl = excl
        self.last_w = None
        self.readers = {}
        self.overlaps = []
        self.lo = lo
        self.hi = hi


class Op:
    __slots__ = ("idx", "eng", "fn", "deps", "dma", "needs_inc", "sem", "val", "prev_val")

    def __init__(self, idx, eng, fn, deps, dma):
        self.idx = idx
        self.eng = eng
        self.fn = fn
        self.deps = deps
        self.dma = dma
        self.needs_inc = False
        self.sem = None
        self.val = 0
        self.prev_val = 0


class Prog:
    def __init__(self, nc):
        self.nc = nc
        self.ops = []

    def op(self, eng, fn, reads=(), writes=(), dma=False):
        idx = len(self.ops)
        deps = set()
        wset = []
        for w in writes:
            wset.append(w)
            wset.extend(w.overlaps)
        rl = []
        for r in reads:
            if r.excl:
                wset.append(r)
            else:
                rl.append(r)
        reads = rl
        for r in reads:
            if r.last_w is not None:
                deps.add(r.last_w)
        for w in wset:
            if w.last_w is not None:
                deps.add(w.last_w)
            for ridx in w.readers.values():
                deps.add(ridx)
        o = Op(idx, eng, fn, sorted(deps), dma)
        self.ops.append(o)
        for r in reads:
            r.readers[("dma", idx) if dma else eng] = idx
        for w in wset:
            w.last_w = idx
            w.readers = {}
        return o

    def mm(self, out, lhsT, rhs, start, stop, reads, writes):
        return self.op("pe", lambda e: e.matmul(out, lhsT=lhsT, rhs=rhs, start=start, stop=stop), reads, writes)

    def tr(self, out, in_, ident, reads, writes):
        return self.op("pe", lambda e: e.transpose(out=out, in_=in_, identity=ident), reads, writes)

    def tt(self, eng, out, in0, in1, op, reads, writes):
        return self.op(eng, lambda e: e.tensor_tensor(out=out, in0=in0, in1=in1, op=op), reads, writes)

    def ts(self, eng, out, in0, s1, s2, op0, op1, reads, writes):
        if s2 is None:
            return self.op(eng, lambda e: e.tensor_scalar(out=out, in0=in0, scalar1=s1, scalar2=None, op0=op0), reads, writes)
        return self.op(eng, lambda e: e.tensor_scalar(out=out, in0=in0, scalar1=s1, scalar2=s2, op0=op0, op1=op1), reads, writes)

    def stt(self, eng, out, in0, scalar, in1, op0, op1, reads, writes):
        return self.op(eng, lambda e: e.scalar_tensor_tensor(out=out, in0=in0, scalar=scalar, in1=in1, op0=op0, op1=op1), reads, writes)

    def act(self, out, in_, func, reads, writes, bias=None, scale=None, accum_out=None):
        kw = {}
        if bias is not None:
            kw["bias"] = bias
        if scale is not None:
            kw["scale"] = scale
        if accum_out is not None:
            kw["accum_out"] = accum_out
        return self.op("act", lambda e: e.activation(out=out, in_=in_, func=func, **kw), reads, writes)

    def copy(self, eng, out, in_, reads, writes):
        if eng == "act":
            return self.act(out, in_, AF.Copy, reads, writes)
        return self.op(eng, lambda e: e.tensor_copy(out=out, in_=in_), reads, writes)

    def red(self, eng, out, in_, reads, writes):
        return self.op(eng, lambda e: e.tensor_reduce(out=out, in_=in_, axis=AX.X, op=ALU.add), reads, writes)

    def memset(self, eng, out, val, writes):
        return self.op(eng, lambda e: e.memset(out, val), (), writes)

    def dma(self, out, in_, reads, writes, queue="sp"):
        return self.op(queue, lambda e: e.dma_start(out=out, in_=in_), reads, writes, dma=True)

    def emit(self, final_ops):
        nc = self.nc
        ops = self.ops

        def skip_same(o, dop):
            return (not dop.dma) and (not o.dma) and dop.eng == o.eng and (o.eng == "pe" or not SAME_ENGINE_SYNC)

        for o in ops:
            for d in o.deps:
                dop = ops[d]
                if dop.dma or skip_same(o, dop):
                    continue
                dop.needs_inc = True
        with contextlib.ExitStack() as st:
            esem = {e: st.enter_context(nc.semaphore("prog_" + e)) for e in ENGS}
            dsem = {q: [st.enter_context(nc.semaphore("dma_%s_%d" % (q, i))) for i in range(N_DMA_SLOTS)]
                    for q in ("sp", "pool")}
            cnt = {e: 0 for e in ENGS}
            dcnt = {q: 0 for q in dsem}
            duse = {q: [0] * N_DMA_SLOTS for q in dsem}
            for o in ops:
                if o.dma:
                    q = o.eng
                    s = dcnt[q] % N_DMA_SLOTS
                    dcnt[q] += 1
                    o.prev_val = 16 * duse[q][s]
                    duse[q][s] += 1
                    o.sem = dsem[q][s]
                    o.val = 16 * duse[q][s]
                elif o.needs_inc:
                    cnt[o.eng] += 1
                    o.sem = esem[o.eng]
                    o.val = cnt[o.eng]
            per = {e: [o for o in ops if o.eng == e] for e in ENGS}
            block = st.enter_context(nc.Block())

            def run(ename, eng):
                waited = {}
                for o in per[ename]:
                    need = {}
                    for d in o.deps:
                        dop = ops[d]
                        if skip_same(o, dop):
                            continue
                        k = id(dop.sem)
                        if k not in need or need[k][1] < dop.val:
                            need[k] = (dop.sem, dop.val)
                    if o.dma and o.prev_val > 0:
                        k = id(o.sem)
                        if k not in need or need[k][1] < o.prev_val:
                            need[k] = (o.sem, o.prev_val)
                    for k, (sem, val) in need.items():
                        if waited.get(k, 0) >= val:
                            continue
                        eng.wait_ge(sem, val)
                        waited[k] = val
                    ins = o.fn(eng)
                    if o.dma:
                        ins.then_inc(o.sem, 16)
                    elif o.needs_inc:
                        ins.then_inc(o.sem, 1)
                if ename == "sp":
                    for fo in final_ops:
                        eng.wait_ge(fo.sem, fo.val)

            @block.sync
            def _(e):
                run("sp", e)

            @block.tensor
            def _(e):
                run("pe", e)

            @block.scalar
            def _(e):
                run("act", e)

            @block.vector
            def _(e):
                run("dve", e)

            @block.gpsimd
            def _(e):
                run("pool", e)


class Buf:
    __slots__ = ("t", "r")

    def __init__(self, t, r):
        self.t = t
        self.r = r


class SBAlloc:
    def __init__(self, nc):
        self.nc = nc
        self.off = SB_BASE
        self.peak = SB_BASE
        self.all = []

    def alloc(self, name, shape, dt):
        esz = 4 if dt == F32 else 2
        n = 1
        for s in shape[1:]:
            n *= s
        size = (n * esz + 31) // 32 * 32
        assert self.off + size <= SB_LIMIT, ("SBUF overflow", name, self.off, size)
        t = self.nc.alloc_sbuf_tensor_at(name, list(shape), dt, offset=self.off)
        r = Res(name, self.off, self.off + size)
        self.off += size
        self.peak = max(self.peak, self.off)
        self.all.append(r)
        return Buf(t, r)

    def finalize(self):
        rs = sorted(self.all, key=lambda r: r.lo)
        for i, a in enumerate(rs):
            for b in rs[i + 1:]:
                if b.lo >= a.hi:
                    break
                a.overlaps.append(b)
                b.overlaps.append(a)


def piece_specs():
    sp = []

    def ffn(i):
        win, wout, nrm = "ffn%d_w_in" % i, "ffn%d_w_out" % i, "ffn%d_norm" % i
        for g in range(NJ // 2):
            j0, j1 = 2 * g, 2 * g + 1
            sp.append(dict(kind="FM", w=win, cc=[j0 * 128, DFF + j0 * 128, j1 * 128, DFF + j1 * 128], norm=nrm, tag=("ffn_in", i, g)))
        for ch in range(2):
            for (j0, nj) in ((0, 8), (8, 8), (16, 6)):
                sp.append(dict(kind="TMK", w=wout, j0=j0, nj=nj, c0=ch * 512, norm=None, tag=("ffn_out", i, ch, j0, nj)))

    ffn(1)
    for hg in range(2):
        for nm, base in (("q", 0), ("k", 1024), ("v", 2048), ("z", 3072), ("rg", 7184)):
            sp.append(dict(kind="FM", w="w_in", cc=[base + (4 * hg + i) * 128 for i in range(4)], norm="mix_norm", tag=(nm, hg)))
        for nm, base in (("rq", 4112), ("rk", 5136), ("rv", 6160)):
            sp.append(dict(kind="TM", w="w_in", c0=base + hg * 512, norm="mix_norm", tag=(nm, hg)))
    for p in range(2):
        sp.append(dict(kind="FM", w="w_in", cc=[8208 + (4 * p + i) * 128 for i in range(4)], norm="mix_norm", tag=("ga", p)))
        sp.append(dict(kind="FM", w="w_in", cc=[9232 + (4 * p + i) * 128 for i in range(4)], norm="mix_norm", tag=("gb", p)))
        sp.append(dict(kind="FM", w="w_branch_gdn", cc=[(4 * p + i) * 128 for i in range(4)], norm=None, tag=("brg", p)))
        sp.append(dict(kind="FM", w="w_branch_ret", cc=[(4 * p + i) * 128 for i in range(4)], norm=None, tag=("brr", p)))
    for ch in range(2):
        sp.append(dict(kind="TM", w="w_out", c0=ch * 512, norm=None, tag=("wo", ch)))
    ffn(2)
    return sp


PIECES = piece_specs()
NP = len(PIECES)
NORM_COL = {"ffn1_norm": 0, "mix_norm": 8, "ffn2_norm": 16}
PC_RETN = 24
PC_GDNN = 32
PC_CONV = 33
PC_ALOG = 129
PC_DTB = 137
PC_WBA = 145
NPRM = PC_WBA + 128
CC_U = 0
CC_MUI = 128
CC_MLS = 256
CC_DRT = 384
CC_XI = 1408
CC_ZS128 = 2432
CC_ZS16 = 2440
CC_CD128 = 2448
CC_CD16 = 2456
CC_ONES = 2464
CC_MLO = 2592
NCST = CC_MLO + 128


def host_consts():
    c = np.zeros((128, NCST), np.float32)
    j = np.arange(128)
    c[:, CC_U:CC_U + 128] = (j[:, None] <= j[None, :])
    c[:, CC_MUI:CC_MUI + 128] = (j[:, None] <= j[None, :])
    c[:, CC_MLS:CC_MLS + 128] = (j[None, :] < j[:, None]) & ((j[None, :] // 64) == (j[:, None] // 64))
    c[:, CC_MLO:CC_MLO + 128] = (j[None, :] < 64) & (j[:, None] >= 64)
    lg = np.log1p(-np.exp2(-5.0 - np.arange(NH, dtype=np.float64)))
    m = j[:, None, None]
    cc = j[None, None, :]
    dr = np.where(m <= cc, np.exp(np.maximum(cc - m, 0) * lg[None, :, None]), 0.0)
    c[:, CC_DRT:CC_DRT + 1024] = dr.reshape(128, 1024)
    xi = np.exp((j[None, None, :] + 1.0) * lg[None, :, None]) * np.ones((128, 1, 1))
    c[:, CC_XI:CC_XI + 1024] = xi.reshape(128, 1024)
    sc = 128.0 ** -0.5
    c[:, CC_ZS128:CC_ZS128 + 8] = np.exp((127.0 - j)[:, None] * lg[None, :]) * sc
    c[:16, CC_ZS16:CC_ZS16 + 8] = np.exp((15.0 - j[:16])[:, None] * lg[None, :]) * sc
    c[:, CC_CD128:CC_CD128 + 8] = np.exp(128.0 * lg)[None, :]
    c[:, CC_CD16:CC_CD16 + 8] = np.exp(16.0 * lg)[None, :]
    c[:, CC_ONES:CC_ONES + 128] = 1.0
    return c


def host_rope(L):
    inv = (1.0 / (10000.0 ** np.linspace(0.0, 1.0, 64, dtype=np.float32))).astype(np.float32)
    pos = np.arange(L, dtype=np.float32)
    ang = (pos[:, None] * inv[None, :]).astype(np.float32)
    r = np.zeros((L, 128), np.float32)
    r[:, :64] = np.cos(ang.astype(np.float64))
    r[:, 64:] = np.sin(ang.astype(np.float64))
    return r


def host_pack_weights(W):
    out = np.zeros((NP, 128, PIECE), np.float32)
    for s, sp in enumerate(PIECES):
        w = W[sp["w"]]
        if sp["kind"] == "FM":
            wk = w.reshape(8, 128, -1)
            for i, c0 in enumerate(sp["cc"]):
                blk = wk[:, :, c0:c0 + 128]
                out[s].reshape(128, 8, 4, 128)[:, :, i, :] = blk.transpose(1, 0, 2)
        elif sp["kind"] == "TM":
            wk = w.reshape(8, 128, -1)[:, :, sp["c0"]:sp["c0"] + 512]
            out[s].reshape(128, 8, 512)[:, :, :] = wk.transpose(1, 0, 2)
        else:
            wk = w.reshape(NJ, 128, -1)[sp["j0"]:sp["j0"] + sp["nj"], :, sp["c0"]:sp["c0"] + 512]
            out[s].reshape(128, 8, 512)[:, :sp["nj"], :] = wk.transpose(1, 0, 2)
    return out


def host_pack_params(I):
    prm = np.zeros((128, NPRM), np.float32)
    for nm, c0 in NORM_COL.items():
        prm[:, c0:c0 + 8] = np.asarray(I[nm]).reshape(8, 128).T
    prm[:, PC_RETN:PC_RETN + 8] = np.asarray(I["ret_out_norm"]).reshape(8, 128).T
    prm[:, PC_GDNN] = np.asarray(I["gdn_out_norm"]).reshape(128)
    cw = np.asarray(I["gdn_conv_w"]).reshape(4, 24, 128)
    prm[:, PC_CONV:PC_CONV + 96] = cw.transpose(2, 1, 0).reshape(128, 96)
    prm[:, PC_ALOG:PC_ALOG + 8] = np.asarray(I["gdn_a_log"]).reshape(1, 8)
    prm[:, PC_DTB:PC_DTB + 8] = np.asarray(I["gdn_dt_bias"]).reshape(1, 8)
    wba = np.asarray(I["w_in"]).reshape(8, 128, DPROJ)[:, :, 4096:4112]
    prm[:, PC_WBA:PC_WBA + 128] = wba.transpose(1, 0, 2).reshape(128, 128)
    return prm


def build(nt, debug=False):
    seq = nt * 512
    L = NMETA + seq
    nc = bass.Bass("TRN2", target_bir_lowering=False)
    x_d = nc.dram_tensor("x", [seq, D], F32, kind="ExternalInput").ap()
    meta_d = nc.dram_tensor("meta", [NMETA, D], F32, kind="ExternalInput").ap()
    wst_d = nc.dram_tensor("wst", [NP, 128, PIECE], F32, kind="ExternalInput").ap()
    prm_d = nc.dram_tensor("prm", [128, NPRM], F32, kind="ExternalInput").ap()
    fnw_d = nc.dram_tensor("fnw", [128, D], F32, kind="ExternalInput").ap()
    cst_d = nc.dram_tensor("cst", [128, NCST], F32, kind="ExternalInput").ap()
    rope_d = nc.dram_tensor("rope", [L, 128], F32, kind="ExternalInput").ap()
    out_d = nc.dram_tensor("out", [seq, D], F32, kind="ExternalOutput").ap()
    wsc_d = nc.dram_tensor("wsc", [NP, 128, PIECE], BF16).ap()
    wsc_r = [Res("wsc%d" % s) for s in range(NP)]

    P = Prog(nc)
    sb = SBAlloc(nc)
    A = sb.alloc

    prm = A("prm", [128, NPRM], F32)
    cst = A("cst", [128, NCST], F32)
    identb = A("identb", [128, 128], BF16)
    onesb = A("onesb", [128, 128], BF16)
    wba = A("wba", [128, 8, 16], BF16)
    nA = A("nA", [128, 8], F32)
    H = [A("H%d" % tb, [128, D], F32) for tb in range(4)]
    nb = [A("nb%d" % i, [128, D], BF16) for i in range(2)]
    nT = A("nT", [128, 8, 512], BF16)
    ring = [A("ring%d" % i, [128, PIECE], BF16) for i in range(NSLOT)]
    Sg = [A("Sg%d" % h, [128, 128], F32) for h in range(NH)]
    Sr = [A("Sr%d" % h, [128, 128], F32) for h in range(NH)]
    Sgb = [A("Sgb%d" % h, [128, 128], BF16) for h in range(NH)]
    Srb = [A("Srb%d" % h, [128, 128], BF16) for h in range(NH)]
    ctail = A("ctail", [128, 24, 3], F32)
    yaT = A("yaT", [128, 8, 512], BF16)
    ybT = A("ybT", [128, 8, 512], BF16)
    ss = A("ss", [128, 4], F32)
    rstd = A("rstd", [128, 4], F32)
    gtok = A("gtok", [128, 4, 8], F32)
    btok = A("btok", [128, 4, 8], F32)
    braw = A("braw", [128, 4, 16], F32)
    batmp = A("batmp", [128, 4, 8], F32)
    cbias = A("cbias", [128, 2], F32)

    def eps_ap(rows=128):
        return cbias.t[0:rows, 0:1]

    def one_ap(rows=128):
        return cbias.t[0:rows, 1:2]

    arena0 = sb.off

    stage = [A("stage%d" % i, [128, PIECE], F32) for i in range(4)]
    sb.off = arena0
    actT = A("actT", [128, NJ, 512], BF16)
    sil = [A("sil%d" % i, [128, 512], F32) for i in range(2)]
    OUTB = [A("OUTB%d" % tb, [128, D], F32) for tb in range(4)]
    fnw = A("fnw", [128, D], F32)
    junk = A("junk", [128, D], BF16)
    XN = [A("XN%d" % tb, [128, D], F32) for tb in range(4)]
    nbx = [A("nbx%d" % tb, [128, D], BF16) for tb in range(4)]
    junk2 = A("junk2", [128, D], BF16)
    ss2 = A("ss2", [128, 4], F32)
    rstd2 = A("rstd2", [128, 4], F32)
    sb.off = arena0
    qT = A("qT", [128, 4, 512], BF16)
    kT = A("kT", [128, 4, 512], BF16)
    vT = A("vT", [128, 4, 512], BF16)
    szT = A("szT", [128, 4, 512], BF16)
    srgT = A("srgT", [128, 4, 512], BF16)
    rkd_tm = A("rkd_tm", [128, 4, 512], BF16)
    rv_tm = A("rv_tm", [128, 4, 512], BF16)
    rqT = A("rqT", [128, 4, 512], BF16)
    rkT = A("rkT", [128, 4, 512], BF16)
    rqdT = A("rqdT", [128, 4, 512], BF16)
    sub0 = sb.off
    rq_tm = A("rq_tm", [128, 4, 512], BF16)
    rk_tm = A("rk_tm", [128, 4, 512], BF16)
    NCS = 5
    rope = A("rope", [128, 4, 128], F32)
    cin = [A("cin%d" % i, [128, 515], F32) for i in range(NCS)]
    cacc = [A("cacc%d" % i, [128, 512], F32) for i in range(NCS)]
    cf = cacc
    csq = [A("csq%d" % i, [128, 512], BF16) for i in range(NCS)]
    crs = [A("crs%d" % i, [128, 512], F32) for i in range(NCS)]
    rxs = [A("rxs%d" % i, [128, 512], F32) for i in range(2)]
    rt = [[A("rt%d_%d" % (i, k), [128, 256], F32) for k in range(2)] for i in range(2)]
    sb.off = sub0
    cksub = []
    for sub in range(2):
        base = {}
        for nm, shp, dt in (
            ("A0", [128, 2, 128], BF16), ("A1", [128, 2, 128], BF16), ("BX0", [128, 2, 256], BF16), ("BX1", [128, 2, 256], BF16),
            ("Y", [128, 2, 128], BF16),
            ("sz", [128, 4, 128], BF16),
            ("vnew", [128, 2, 128], BF16), ("osb", [128, 2, 128], F32), ("osq", [128, 2, 128], F32), ("ost", [128, 4], F32),
            ("on", [128, 2, 128], BF16),
            ("scT", [128, 2, 128], BF16), ("orb", [128, 2, 128], F32), ("orq", [128, 2, 128], F32), ("ort", [128, 4], F32),
            ("orn", [128, 2, 128], BF16), ("ybt", [128, 2, 128], F32),
        ):
            base[nm] = A("%s_s%d" % (nm, sub), shp, dt)
        base["orc"] = base["orb"]
        pars = []
        for par in range(2):
            d = dict(base)
            for nm, shp, dt in (("wT", [128, 2, 128], BF16), ("u", [128, 2, 128], F32), ("kd", [128, 2, 128], BF16),
                                ("PT", [128, 2, 128], BF16), ("qdT", [128, 2, 128], BF16), ("egl", [128, 2], F32),
                                ("Ain", [128, 2, 128], BF16), ("Ao", [128, 2, 128], BF16), ("kbg", [128, 2, 128], BF16),
                                ("bv", [128, 2, 128], BF16),
                                ("GU", [128, 2, 128], F32), ("gcs", [128, 4], F32), ("eg", [128, 2], F32), ("beg", [128, 2], F32),
                                ("ekd", [128, 2], F32), ("Dm", [128, 2, 128], F32), ("El", [128, 2, 128], F32),
                                ("EQ", [128, 2, 128], F32), ("EGQ", [128, 2, 128], F32)):
                d[nm] = A("%s_s%d_%d" % (nm, sub, par), shp, dt)
            d["Eu"] = d["Dm"]
            d["EA"] = d["El"]
            d["EAo"] = d["GU"]
            pars.append(d)
        cksub.append(pars)
    hg_end = sb.off
    sb.off = arena0
    sgT = A("sgT", [128, 4, 512], BF16)
    sbT = A("sbT", [128, 4, 512], BF16)
    mtmp = [A("mtmp%d" % i, [128, 512], F32) for i in range(4)]
    mtmp2 = [A("mtmp2_%d" % i, [128, 512], F32) for i in range(2)]
    mergedT = A("mergedT", [128, 8, 512], BF16)
    sb.finalize()
    if debug:
        print("SBUF peak", sb.peak, "of", SB_LIMIT, "hg_end", hg_end, "arena0", arena0)

    NF, NB = 6, 2
    psf = [nc.alloc_psum_tensor("psf%d" % i, [128, 512], F32) for i in range(NF)]
    psb = [nc.alloc_psum_tensor("psb%d" % i, [128, 1024], BF16) for i in range(NB)]
    psf_r = [Res("psf%d" % i, excl=True) for i in range(NF)]
    psb_r = [Res("psb%d" % i, excl=True) for i in range(NB)]
    from collections import deque
    free_f = deque(range(NF))
    free_b = deque(range(NB))

    def PSF(hold=False):
        i = free_f.popleft()
        if not hold:
            free_f.append(i)
        return psf[i], psf_r[i]

    def PSB(hold=False):
        i = free_b.popleft()
        if not hold:
            free_b.append(i)
        return psb[i], psb_r[i]

    def gPSF(k=1):
        while len(free_f) < k:
            yield
        got = [PSF(hold=True) for _ in range(k)]
        return got[0] if k == 1 else got

    def gPSB():
        while not free_b:
            yield
        return PSB(hold=True)

    def PFREE(r):
        if r in psf_r:
            free_f.append(psf_r.index(r))
        else:
            free_b.append(psb_r.index(r))

    def pc(c0, n=1):
        return prm.t[:, c0:c0 + n]

    def cc(c0, n, rows=128):
        return cst.t[0:rows, c0:c0 + n]

    rr = ["act", "dve"]
    rrc = {"i": 0}

    def nxt(choices=("act", "dve")):
        rrc["i"] += 1
        return choices[rrc["i"] % len(choices)]

    P.dma(prm.t[:], prm_d[:, :], [], [prm.r])
    P.dma(cst.t[:], cst_d[:, :], [], [cst.r])
    P.memset("pool", crs[0].t[:, 0:128], 0.0, [crs[0].r])
    P.op("pool", lambda e: e.affine_select(out=crs[0].t[:, 0:128], in_=crs[0].t[:, 0:128], pattern=[[-1, 128]],
                                           compare_op=ALU.not_equal, fill=1.0, base=0, channel_multiplier=1),
         [crs[0].r], [crs[0].r])
    P.copy("dve", identb.t[:], crs[0].t[:, 0:128], [crs[0].r], [identb.r])
    P.memset("dve", onesb.t[:], 1.0, [onesb.r])
    for kc in range(8):
        P.ts("dve", wba.t[:, kc, :], prm.t[:, PC_WBA + kc * 16:PC_WBA + kc * 16 + 16], pc(NORM_COL["mix_norm"] + kc), None,
             ALU.mult, None, [prm.r], [wba.r])
    P.act(nA.t[:], pc(PC_ALOG, 8), AF.Exp, [prm.r], [nA.r])
    P.ts("dve", nA.t[:], nA.t[:], -1.0, None, ALU.mult, None, [nA.r], [nA.r])
    for h in range(NH):
        P.memset("pool", Sg[h].t[:], 0.0, [Sg[h].r])
        P.memset("pool", Sr[h].t[:], 0.0, [Sr[h].r])
        P.memset("dve", Sgb[h].t[:], 0.0, [Sgb[h].r])
        P.memset("dve", Srb[h].t[:], 0.0, [Srb[h].r])
    P.memset("pool", ctail.t[:], 0.0, [ctail.r])
    P.memset("pool", cbias.t[:, 0:1], EPS, [cbias.r])
    P.memset("pool", cbias.t[:, 1:2], 1.0, [cbias.r])

    def pre_in(s):
        P.dma(stage[s % 4].t[:], wst_d[s], [], [stage[s % 4].r])

    for s in range(min(3, NP)):
        pre_in(s)
    for s, spc in enumerate(PIECES):
        stg = stage[s % 4]
        slot = ring[s % NSLOT]
        if spc["norm"] is not None:
            c0 = NORM_COL[spc["norm"]]
            for kc in range(8):
                eng = "act" if kc % 2 == 0 else "dve"
                o_ = slot.t[:, kc * 512:(kc + 1) * 512]
                i_ = stg.t[:, kc * 512:(kc + 1) * 512]
                if eng == "act":
                    P.act(o_, i_, AF.Identity, [stg.r, prm.r], [slot.r], scale=pc(c0 + kc))
                else:
                    P.ts("dve", o_, i_, pc(c0 + kc), None, ALU.mult, None, [stg.r, prm.r], [slot.r])
        else:
            P.copy("act", slot.t[:, 0:1536], stg.t[:, 0:1536], [stg.r], [slot.r])
            P.copy("dve", slot.t[:, 1536:3072], stg.t[:, 1536:3072], [stg.r], [slot.r])
            P.copy("pool", slot.t[:, 3072:4096], stg.t[:, 3072:4096], [stg.r], [slot.r])
        if s + 3 < NP:
            pre_in(s + 3)
        P.dma(wsc_d[s], slot.t[:], [slot.r], [wsc_r[s]])

    wstate = {"issued": 0, "cur": 0}
    total_pieces = NP * (nt + 1)

    def w_issue_upto(n):
        while wstate["issued"] < min(n, total_pieces):
            g = wstate["issued"]
            s = g % NP
            slot = ring[g % NSLOT]
            P.dma(slot.t[:], wsc_d[s], [wsc_r[s]], [slot.r])
            wstate["issued"] += 1

    def next_piece(tag_prefix, lag=0):
        g = wstate["cur"]
        s = g % NP
        assert PIECES[s]["tag"][0] == tag_prefix, (PIECES[s]["tag"], tag_prefix)
        w_issue_upto(g + NSLOT - lag)
        wstate["cur"] += 1
        return ring[g % NSLOT]

    def run_jobs(jobs, maxact, fifo):
        done = set()
        pending = list(jobs)
        active = []
        while pending or active:
            for j in list(pending):
                if len(active) >= maxact:
                    break
                if all(d in done for d in j[2]):
                    active.append((j[0], j[1]()))
                    pending.remove(j)
                elif fifo:
                    break
            assert active, ("scheduler deadlock", [j[0] for j in pending][:5])
            for a in list(active):
                try:
                    next(a[1])
                except StopIteration:
                    active.remove(a)
                    done.add(a[0])

    def run_interleaved(gens):
        gens = list(gens)
        while gens:
            for g in list(gens):
                try:
                    next(g)
                except StopIteration:
                    gens.remove(g)

    def norm_A(tbs, src, nbl, jk, ssb, rsb):
        ntb = len(tbs)
        for tb, (t0, tn) in enumerate(tbs):
            P.act(jk.t[0:tn, :], src[tb].t[0:tn, :], AF.Square, [src[tb].r], [jk.r, ssb.r], accum_out=ssb.t[0:tn, tb:tb + 1])
        tn0 = tbs[0][1]
        P.act(rsb.t[0:tn0, 0:ntb], ssb.t[0:tn0, 0:ntb], AF.Ln, [ssb.r], [rsb.r], scale=1.0 / D, bias=eps_ap(tn0))
        P.act(rsb.t[0:tn0, 0:ntb], rsb.t[0:tn0, 0:ntb], AF.Exp, [rsb.r], [rsb.r], scale=-0.5)
        for tb, (t0, tn) in enumerate(tbs):
            nbb = nbl[tb % len(nbl)]
            P.ts("dve", nbb.t[0:tn, :], src[tb].t[0:tn, :], rsb.t[0:tn, tb:tb + 1], None, ALU.mult, None, [src[tb].r, rsb.r], [nbb.r])

    def norm_B(tbs, nbl):
        for tb, (t0, tn) in enumerate(tbs):
            nbb = nbl[tb % len(nbl)]
            pt, pr = PSB()
            for fc in range(8):
                P.tr(pt[:, fc * 128:fc * 128 + tn], nbb.t[0:tn, fc * 128:(fc + 1) * 128], identb.t[0:tn, 0:tn], [nbb.r, identb.r], [pr])
            P.copy(nxt(), nT.t[:, :, t0:t0 + tn], pt[:, :].rearrange("p (f t) -> p f t", f=8)[:, :, 0:tn], [pr], [nT.r])

    def norm_to_nT(T, tbs):
        ntb = len(tbs)
        for tb, (t0, tn) in enumerate(tbs):
            P.act(junk.t[0:tn, :], H[tb].t[0:tn, :], AF.Square, [H[tb].r], [junk.r, ss.r], accum_out=ss.t[0:tn, tb:tb + 1])
        tn0 = tbs[0][1]
        P.act(rstd.t[0:tn0, 0:ntb], ss.t[0:tn0, 0:ntb], AF.Ln, [ss.r], [rstd.r], scale=1.0 / D, bias=eps_ap(tn0))
        P.act(rstd.t[0:tn0, 0:ntb], rstd.t[0:tn0, 0:ntb], AF.Exp, [rstd.r], [rstd.r], scale=-0.5)
        for tb, (t0, tn) in enumerate(tbs):
            nbb = nb[tb % 2]
            P.ts("dve", nbb.t[0:tn, :], H[tb].t[0:tn, :], rstd.t[0:tn, tb:tb + 1], None, ALU.mult, None, [H[tb].r, rstd.r], [nbb.r])
            pt, pr = PSB()
            for fc in range(8):
                P.tr(pt[:, fc * 128:fc * 128 + tn], nbb.t[0:tn, fc * 128:(fc + 1) * 128], identb.t[0:tn, 0:tn], [nbb.r, identb.r], [pr])
            P.copy(nxt(), nT.t[:, :, t0:t0 + tn], pt[:, :].rearrange("p (f t) -> p f t", f=8)[:, :, 0:tn], [pr], [nT.r])

    def fm_group(slot, i, T, rhsbuf, rhs_r):
        pt, pr = PSF()
        for kc in range(8):
            P.mm(pt[:, 0:T], slot.t[:, kc * 512 + i * 128:kc * 512 + (i + 1) * 128], rhsbuf.t[:, kc, 0:T], kc == 0, kc == 7,
                 [slot.r, rhs_r], [pr])
        return pt, pr

    def tm_group(slot, lbuf, l_r, t0, tn):
        pt, pr = PSF()
        for kc in range(8):
            P.mm(pt[0:tn, :], lbuf.t[:, kc, t0:t0 + tn], slot.t[:, kc * 512:(kc + 1) * 512], kc == 0, kc == 7, [slot.r, l_r], [pr])
        return pt, pr

    def ffn(i, T, tbs, skip_norm=False, pre=None, mid=None, post=None):
        if not skip_norm:
            norm_to_nT(T, tbs)
        if pre is not None:
            pre()
        for g in range(NJ // 2):
            slot = next_piece("ffn_in")
            for jj in range(2):
                j = 2 * g + jj
                pg, pgr = fm_group(slot, 2 * jj, T, nT, nT.r)
                pu, pur = fm_group(slot, 2 * jj + 1, T, nT, nT.r)
                sl = sil[j % 2]
                P.act(sl.t[:, 0:T], pg[:, 0:T], AF.Silu, [pgr], [sl.r])
                P.tt("dve", actT.t[:, j, 0:T], sl.t[:, 0:T], pu[:, 0:T], ALU.mult, [sl.r, pur], [actT.r])
        if mid is not None:
            mid()
        for ch in range(2):
            pts = [PSF() for _ in tbs]
            for (j0, nj) in ((0, 8), (8, 8), (16, 6)):
                slot = next_piece("ffn_out")
                for jj in range(nj):
                    j = j0 + jj
                    for tb, (t0, tn) in enumerate(tbs):
                        P.mm(pts[tb][0][0:tn, :], actT.t[:, j, t0:t0 + tn], slot.t[:, jj * 512:(jj + 1) * 512], j == 0, j == NJ - 1,
                             [slot.r, actT.r], [pts[tb][1]])
            if ch == 1 and post is not None:
                post()
            for tb, (t0, tn) in enumerate(tbs):
                hh = H[tb].t[0:tn, ch * 512:(ch + 1) * 512]
                P.stt("dve", hh, pts[tb][0][0:tn, :], 0.5, hh, ALU.mult, ALU.add, [pts[tb][1], H[tb].r], [H[tb].r])

    def mixer(T, tbs, C, tok0):
        NCH = T // C
        nlev = {128: 5, 16: 3}[C]
        zs_c = CC_ZS128 if C == 128 else CC_ZS16
        cd_c = CC_CD128 if C == 128 else CC_CD16
        norm_to_nT(T, tbs)
        def ba_proj():
            pt, pr = yield from gPSF()
            for n in range(NCH):
                for kc in range(8):
                    P.mm(pt[0:C, n * 16:(n + 1) * 16], nT.t[:, kc, n * C:(n + 1) * C], wba.t[:, kc, :], kc == 0, kc == 7, [nT.r, wba.r], [pr])
            P.copy("act", braw.t[0:C, 0:NCH, :], pt[0:C, 0:NCH * 16].rearrange("p (n c) -> p n c", c=16), [pr], [braw.r])
            PFREE(pr)
            yield
            P.act(btok.t[0:C, 0:NCH, :], braw.t[0:C, 0:NCH, 0:8], AF.Exp, [braw.r], [btok.r], scale=-1.0)
            yield
            P.ts("dve", btok.t[0:C, 0:NCH, :], btok.t[0:C, 0:NCH, :], 1.0, None, ALU.add, None, [btok.r], [btok.r])
            yield
            P.op("dve", lambda e: e.reciprocal(out=btok.t[0:C, 0:NCH, :], in_=btok.t[0:C, 0:NCH, :]), [btok.r], [btok.r])
            yield
            P.tt("dve", batmp.t[0:C, 0:NCH, :], braw.t[0:C, 0:NCH, 8:16], prm.t[0:C, PC_DTB:PC_DTB + 8].unsqueeze(1).broadcast_to([C, NCH, 8]),
                 ALU.add, [braw.r, prm.r], [batmp.r])
            yield
            P.act(batmp.t[0:C, 0:NCH, :], batmp.t[0:C, 0:NCH, :], AF.Exp, [batmp.r], [batmp.r])
            yield
            P.act(batmp.t[0:C, 0:NCH, :], batmp.t[0:C, 0:NCH, :], AF.Ln, [batmp.r], [batmp.r], bias=one_ap(C))
            yield
            P.tt("dve", gtok.t[0:C, 0:NCH, :], batmp.t[0:C, 0:NCH, :], nA.t[0:C, :].unsqueeze(1).broadcast_to([C, NCH, 8]), ALU.mult,
                 [batmp.r, nA.r], [gtok.r])
            yield

        for hg in range(2):
            for tb, (t0, tn) in enumerate(tbs):
                P.dma(rope.t[0:tn, tb, :], rope_d[tok0 + t0:tok0 + t0 + tn, :], [], [rope.r])
            pj = []
            lag = 2
            slots = {}

            def get_slot(nm, first):
                if first:
                    slots[nm] = next_piece(nm, lag)
                return slots[nm]

            def job_conv(nm, dst, qi, i, cs, hg=hg):
                slot = get_slot(nm, i == 0)
                cch = qi * 8 + hg * 4 + i
                pt, pr = yield from gPSF()
                for kc in range(8):
                    P.mm(pt[:, 0:T], slot.t[:, kc * 512 + i * 128:kc * 512 + (i + 1) * 128], nT.t[:, kc, 0:T], kc == 0, kc == 7, [slot.r, nT.r], [pr])
                yield
                ci = cin[cs]
                P.copy("act", ci.t[:, 3:3 + T], pt[:, 0:T], [pr], [ci.r])
                PFREE(pr)
                P.copy("pool", ci.t[:, 0:3], ctail.t[:, cch, :], [ctail.r], [ci.r])
                yield
                P.copy("pool", ctail.t[:, cch, :], ci.t[:, T:T + 3], [ci.r], [ctail.r])
                ca = cacc[cs]
                wcol = PC_CONV + cch * 4
                P.act(ca.t[:, 0:T], ci.t[:, 0:T], AF.Identity, [ci.r, prm.r], [ca.r], scale=pc(wcol))
                yield
                for tap in range(1, 4):
                    P.stt("dve", ca.t[:, 0:T], ci.t[:, tap:tap + T], pc(wcol + tap), ca.t[:, 0:T], ALU.mult, ALU.add,
                          [ci.r, prm.r, ca.r], [ca.r])
                yield
                if nm == "v":
                    P.act(dst.t[:, i, 0:T], ca.t[:, 0:T], AF.Silu, [ca.r], [dst.r])
                    yield
                    return
                f = cf[cs]
                P.act(f.t[:, 0:T], ca.t[:, 0:T], AF.Silu, [ca.r], [f.r])
                yield
                sq = csq[cs]
                P.tt("pool", sq.t[:, 0:T], f.t[:, 0:T], f.t[:, 0:T], ALU.mult, [f.r], [sq.r])
                yield
                p2, p2r = yield from gPSF()
                P.mm(p2[:, 0:T], onesb.t[:, :], sq.t[:, 0:T], True, True, [onesb.r, sq.r], [p2r])
                yield
                rs_ = crs[cs]
                P.act(rs_.t[:, 0:T], p2[:, 0:T], AF.Ln, [p2r], [rs_.r], bias=eps_ap())
                PFREE(p2r)
                P.act(rs_.t[:, 0:T], rs_.t[:, 0:T], AF.Exp, [rs_.r], [rs_.r], scale=-0.5)
                yield
                if nm == "q":
                    P.stt("dve", dst.t[:, i, 0:T], f.t[:, 0:T], 128.0 ** -0.5, rs_.t[:, 0:T], ALU.mult, ALU.mult, [f.r, rs_.r], [dst.r])
                else:
                    P.tt("dve", dst.t[:, i, 0:T], f.t[:, 0:T], rs_.t[:, 0:T], ALU.mult, [f.r, rs_.r], [dst.r])
                yield

            def job_silu(nm, dst, i):
                slot = get_slot(nm, i == 0)
                pt, pr = yield from gPSF()
                for kc in range(8):
                    P.mm(pt[:, 0:T], slot.t[:, kc * 512 + i * 128:kc * 512 + (i + 1) * 128], nT.t[:, kc, 0:T], kc == 0, kc == 7, [slot.r, nT.r], [pr])
                yield
                P.act(dst.t[:, i, 0:T], pt[:, 0:T], AF.Silu, [pr], [dst.r])
                PFREE(pr)
                yield

            def job_rot(nm, dst, tb, t0, tn, rs):
                slot = get_slot(nm, tb == 0)
                pt, pr = yield from gPSF()
                for kc in range(8):
                    P.mm(pt[0:tn, :], nT.t[:, kc, t0:t0 + tn], slot.t[:, kc * 512:(kc + 1) * 512], kc == 0, kc == 7, [slot.r, nT.r], [pr])
                yield
                xs = rxs[rs]
                P.copy("act", xs.t[0:tn, :], pt[0:tn, :], [pr], [xs.r])
                PFREE(pr)
                yield
                xv = xs.t[0:tn, :].rearrange("p (h j two) -> p h j two", h=4, two=2)
                x0 = xv[:, :, :, 0]
                x1 = xv[:, :, :, 1]
                cosb = rope.t[0:tn, tb, 0:64].unsqueeze(1).broadcast_to([tn, 4, 64])
                sinb = rope.t[0:tn, tb, 64:128].unsqueeze(1).broadcast_to([tn, 4, 64])
                ta, tb_ = rt[rs]
                t1 = ta.t[0:tn, :].rearrange("p (h j) -> p h j", h=4)
                t2 = tb_.t[0:tn, :].rearrange("p (h j) -> p h j", h=4)
                ov = dst.t[0:tn, tb, :].rearrange("p (h j two) -> p h j two", h=4, two=2)
                P.tt("dve", t1, x0, cosb, ALU.mult, [xs.r, rope.r], [ta.r])
                P.tt("pool", t2, x1, sinb, ALU.mult, [xs.r, rope.r], [tb_.r])
                yield
                P.tt("dve", ov[:, :, :, 0], t1, t2, ALU.subtract, [ta.r, tb_.r], [dst.r])
                yield
                P.tt("dve", t1, x1, cosb, ALU.mult, [xs.r, rope.r], [ta.r])
                P.tt("pool", t2, x0, sinb, ALU.mult, [xs.r, rope.r], [tb_.r])
                yield
                P.tt("pool", ov[:, :, :, 1], t1, t2, ALU.add, [ta.r, tb_.r], [dst.r])
                yield

            def job_rv(tb, t0, tn):
                slot = get_slot("rv", tb == 0)
                pt, pr = yield from gPSF()
                for kc in range(8):
                    P.mm(pt[0:tn, :], nT.t[:, kc, t0:t0 + tn], slot.t[:, kc * 512:(kc + 1) * 512], kc == 0, kc == 7, [slot.r, nT.r], [pr])
                yield
                P.copy("act", rv_tm.t[0:tn, tb, :], pt[0:tn, :], [pr], [rv_tm.r])
                PFREE(pr)
                yield

            def job_tr(src, dst, scl, tb, t0, tn, hg=hg):
                if src is rk_tm:
                    P.tt("dve", rkd_tm.t[0:tn, tb, :].rearrange("p (h d) -> p h d", h=4),
                         rk_tm.t[0:tn, tb, :].rearrange("p (h d) -> p h d", h=4),
                         cst.t[0:tn, zs_c + hg * 4:zs_c + hg * 4 + 4].unsqueeze(2).broadcast_to([tn, 4, 128]), ALU.mult,
                         [rk_tm.r, cst.r], [rkd_tm.r])
                pt, pr = yield from gPSB()
                for i in range(4):
                    P.tr(pt[:, i * 128:i * 128 + tn], src.t[0:tn, tb, i * 128:(i + 1) * 128], identb.t[0:tn, 0:tn], [src.r, identb.r], [pr])
                yield
                pv = pt[:, 0:512].rearrange("p (h t) -> p h t", h=4)[:, :, 0:tn]
                if scl is None:
                    P.copy("dve", dst.t[:, :, t0:t0 + tn], pv, [pr], [dst.r])
                else:
                    P.act(dst.t[:, :, t0:t0 + tn], pv, AF.Identity, [pr], [dst.r], scale=scl)
                PFREE(pr)
                yield

            cnt = 0
            if hg == 0:
                pj.append(("ba", ba_proj, []))
            for qi, (nm, dst) in enumerate((("q", qT), ("k", kT), ("v", vT))):
                for i in range(4):
                    deps = ["conv%d" % (cnt - NCS)] if cnt >= NCS else []
                    pj.append(("conv%d" % cnt, (lambda nm=nm, dst=dst, qi=qi, i=i, cs=cnt % NCS: job_conv(nm, dst, qi, i, cs)), deps))
                    cnt += 1
            for nm, dst in (("z", szT), ("rg", srgT)):
                for i in range(4):
                    pj.append(("silu_%s%d" % (nm, i), (lambda nm=nm, dst=dst, i=i: job_silu(nm, dst, i)), []))
            rcnt = 0
            for nm, dst in (("rq", rq_tm), ("rk", rk_tm)):
                for tb, (t0, tn) in enumerate(tbs):
                    deps = ["rot%d" % (rcnt - 2)] if rcnt >= 2 else []
                    pj.append(("rot%d" % rcnt, (lambda nm=nm, dst=dst, tb=tb, t0=t0, tn=tn, rs=rcnt % 2: job_rot(nm, dst, tb, t0, tn, rs)), deps))
                    rcnt += 1
            for tb, (t0, tn) in enumerate(tbs):
                pj.append(("rv%d" % tb, (lambda tb=tb, t0=t0, tn=tn: job_rv(tb, t0, tn)), []))
            ntb_ = len(tbs)
            for si, (src, dst, scl) in enumerate(((rq_tm, rqT, None), (rk_tm, rkT, 128.0 ** -0.5))):
                for tb, (t0, tn) in enumerate(tbs):
                    pj.append(("tr%d_%d" % (si, tb), (lambda src=src, dst=dst, scl=scl, tb=tb, t0=t0, tn=tn: job_tr(src, dst, scl, tb, t0, tn)),
                               ["rot%d" % (si * ntb_ + tb)]))
            run_jobs(pj, 5 if ntb_ > 1 else 1, True)
            for i in range(4):
                h = hg * 4 + i
                P.tt("pool", rqdT.t[:, i, 0:T].rearrange("p (n c) -> p n c", c=C), rqT.t[:, i, 0:T].rearrange("p (n c) -> p n c", c=C),
                     cst.t[:, CC_XI + h * 128:CC_XI + h * 128 + C].unsqueeze(1).broadcast_to([128, NCH, C]), ALU.mult, [rqT.r, cst.r], [rqdT.r])

            hs = slice(hg * 4, hg * 4 + 4)

            def h3(ap, w):
                return ap.rearrange("p (h c) -> p h c", h=4)

            def gen_prep(n, sub, hg=hg):
                K = cksub[sub][n % 2]
                c0 = n * C
                li = (2 * sub, 2 * sub + 1)
                hcs = slice(hg * 4 + 2 * sub, hg * 4 + 2 * sub + 2)

                def bc2(ap2):
                    return ap2.unsqueeze(2).broadcast_to([C, 2, C])

                def bcd(ap2):
                    return ap2.unsqueeze(2).broadcast_to([C, 2, 128])

                def v3(b):
                    return b.t[0:C, :, 0:C]

                def g3(ap):
                    return ap.rearrange("p (h c) -> p h c", h=2)

                def tab(c0_):
                    return cst.t[0:C, c0_:c0_ + C].unsqueeze(1).broadcast_to([C, 2, C])

                Ub, MUI, MLS, MLO = tab(CC_U), tab(CC_MUI), tab(CC_MLS), tab(CC_MLO)
                Idb = identb.t[0:C, 0:C].unsqueeze(1).broadcast_to([C, 2, C])
                two = (C == 128)
                P.tt("pool", v3(K["GU"]), Ub, bc2(gtok.t[0:C, n, hcs]), ALU.mult, [cst.r, gtok.r], [K["GU"].r])
                (pG, pGr), (pS, pSr) = yield from gPSF(2)
                for ii in range(2):
                    P.mm(pG[:, ii * C:(ii + 1) * C], cst.t[0:C, CC_ONES:CC_ONES + 128], K["GU"].t[0:C, ii, 0:C], True, True, [cst.r, K["GU"].r], [pGr])
                P.mm(pS[0:C, 0:2], cst.t[0:C, CC_U:CC_U + C], gtok.t[0:C, n, hcs], True, True, [cst.r, gtok.r], [pSr])
                P.mm(pS[:, 2:4], cst.t[0:C, CC_ONES:CC_ONES + 128], gtok.t[0:C, n, hcs], True, True, [cst.r, gtok.r], [pSr])
                yield
                gcs = K["gcs"]
                P.copy("act", gcs.t[:, 2:4], pS[:, 2:4], [pSr], [gcs.r])
                P.copy("act", gcs.t[0:C, 0:2], pS[0:C, 0:2], [pSr], [gcs.r])
                PFREE(pSr)
                P.act(K["EGQ"].t[:, :, 0:C], g3(pG[:, 0:2 * C]), AF.Exp, [pGr], [K["EGQ"].r])
                yield
                P.tt("dve", v3(K["Dm"]), g3(pG[0:C, 0:2 * C]), bc2(gcs.t[0:C, 0:2]), ALU.subtract, [pGr, gcs.r], [K["Dm"].r])
                PFREE(pGr)
                P.act(K["eg"].t[0:C, :], gcs.t[0:C, 0:2], AF.Exp, [gcs.r], [K["eg"].r])
                P.tt("pool", K["ekd"].t[0:C, :], gcs.t[0:C, 2:4], gcs.t[0:C, 0:2], ALU.subtract, [gcs.r], [K["ekd"].r])
                P.act(K["egl"].t[:, :], gcs.t[:, 2:4], AF.Exp, [gcs.r], [K["egl"].r])
                P.tt("pool", K["qdT"].t[:, :, 0:C], qT.t[:, li[0]:li[1] + 1, c0:c0 + C], K["EGQ"].t[:, :, 0:C], ALU.mult, [qT.r, K["EGQ"].r], [K["qdT"].r])
                yield
                P.ts("dve", v3(K["El"]), v3(K["Dm"]), 0.0, None, ALU.max, None, [K["Dm"].r], [K["El"].r])
                P.ts("dve", v3(K["Eu"]), v3(K["Dm"]), 0.0, None, ALU.min, None, [K["Dm"].r], [K["Eu"].r])
                P.act(K["ekd"].t[0:C, :], K["ekd"].t[0:C, :], AF.Exp, [K["ekd"].r], [K["ekd"].r])
                P.tt("pool", K["beg"].t[0:C, :], K["eg"].t[0:C, :], btok.t[0:C, n, hcs], ALU.mult, [K["eg"].r, btok.r], [K["beg"].r])
                ptkv, ptkvr = yield from gPSB()
                for ii in range(2):
                    P.tr(ptkv[0:C, ii * 128:(ii + 1) * 128], kT.t[:, li[ii], c0:c0 + C], identb.t[:, :], [kT.r, identb.r], [ptkvr])
                for ii in range(2):
                    P.tr(ptkv[0:C, 256 + ii * 128:256 + (ii + 1) * 128], vT.t[:, li[ii], c0:c0 + C], identb.t[:, :], [vT.r, identb.r], [ptkvr])
                yield
                P.act(v3(K["El"]), v3(K["El"]), AF.Exp, [K["El"].r], [K["El"].r], scale=-1.0)
                P.act(v3(K["Eu"]), v3(K["Eu"]), AF.Exp, [K["Eu"].r], [K["Eu"].r])
                ktv = ptkv[0:C, 0:256].rearrange("p (h d) -> p h d", h=2)
                vtv = ptkv[0:C, 256:512].rearrange("p (h d) -> p h d", h=2)
                P.tt("dve", K["kd"].t[0:C, :, :], ktv, bcd(K["ekd"].t[0:C, :]), ALU.mult, [ptkvr, K["ekd"].r], [K["kd"].r])
                P.tt("dve", K["kbg"].t[0:C, :, :], ktv, bcd(K["beg"].t[0:C, :]), ALU.mult, [ptkvr, K["beg"].r], [K["kbg"].r])
                P.tt("dve", K["bv"].t[0:C, :, :], vtv, bcd(btok.t[0:C, n, hcs]), ALU.mult, [ptkvr, btok.r], [K["bv"].r])
                PFREE(ptkvr)
                (pK, pKr), (pQ, pQr) = yield from gPSF(2)
                for ii in range(2):
                    P.mm(pQ[0:C, ii * C:(ii + 1) * C], kT.t[:, li[ii], c0:c0 + C], qT.t[:, li[ii], c0:c0 + C], True, True, [kT.r, qT.r], [pQr])
                for ii in range(2):
                    P.mm(pK[0:C, ii * C:(ii + 1) * C], kT.t[:, li[ii], c0:c0 + C], kT.t[:, li[ii], c0:c0 + C], True, True, [kT.r], [pKr])
                yield
                P.tt("pool", v3(K["EQ"]), v3(K["Eu"]), MUI, ALU.mult, [K["Eu"].r, cst.r], [K["EQ"].r])
                P.tt("pool", v3(K["El"]), v3(K["El"]), bc2(btok.t[0:C, n, hcs]), ALU.mult, [K["El"].r, btok.r], [K["El"].r])
                yield
                P.tt("dve", v3(K["PT"]), g3(pQ[0:C, 0:2 * C]), v3(K["EQ"]), ALU.mult, [pQr, K["EQ"].r], [K["PT"].r])
                PFREE(pQr)
                if two:
                    P.tt("pool", v3(K["EAo"]), v3(K["El"]), MLO, ALU.mult, [K["El"].r, cst.r], [K["EAo"].r])
                P.tt("pool", v3(K["EA"]), v3(K["El"]), MLS, ALU.mult, [K["El"].r, cst.r], [K["EA"].r])
                yield
                P.tt("dve", v3(K["Ain"]), g3(pK[0:C, 0:2 * C]), v3(K["EA"]), ALU.mult, [pKr, K["EA"].r], [K["Ain"].r])
                if two:
                    P.tt("dve", v3(K["Ao"]), g3(pK[0:C, 0:2 * C]), v3(K["EAo"]), ALU.mult, [pKr, K["EAo"].r], [K["Ao"].r])
                PFREE(pKr)
                yield

            def gen_prepB(n, sub, hg=hg):
                K = cksub[sub][n % 2]

                def v3(b):
                    return b.t[0:C, :, 0:C]

                def g3(ap):
                    return ap.rearrange("p (h c) -> p h c", h=2)

                Idb = identb.t[0:C, 0:C].unsqueeze(1).broadcast_to([C, 2, C])
                two = (C == 128)
                ptb, ptbr = yield from gPSB()
                for ii in range(2):
                    P.tr(ptb[0:C, ii * C:(ii + 1) * C], K["Ain"].t[0:C, ii, 0:C], identb.t[0:C, 0:C], [K["Ain"].r, identb.r], [ptbr])
                yield
                def bxv(buf, j):
                    return buf.t[0:C, :, j * 128:j * 128 + C]

                def pbv(ps_, j):
                    return ps_[0:C, 0:512].rearrange("p (h j c) -> p h j c", h=2, j=2)[:, :, j, 0:C]

                P.copy("act", bxv(K["BX0"], 0), g3(ptb[0:C, 0:2 * C]), [ptbr], [K["BX0"].r])
                P.tt("dve", bxv(K["BX1"], 1), Idb, g3(ptb[0:C, 0:2 * C]), ALU.subtract, [identb.r, ptbr], [K["BX1"].r])
                PFREE(ptbr)
                yield
                for k in range(nlev + 1):
                    Ak = K["Ain"] if k == 0 else K["A%d" % (k % 2)]
                    An = K["A%d" % ((k + 1) % 2)]
                    cur, nxb = K["BX%d" % (k % 2)], K["BX%d" % ((k + 1) % 2)]
                    need_A, need_B, need_X = (k < nlev), (k < nlev - 1), (k >= 1)
                    need = (1 if need_A else 0) + 1
                    got = yield from gPSF(need)
                    if need == 1:
                        got = [got]
                    got = list(got)
                    pA = None
                    if need_A:
                        pA, pAr = got.pop(0)
                        for ii in range(2):
                            P.mm(pA[0:C, ii * C:(ii + 1) * C], cur.t[0:C, ii, 0:C], Ak.t[0:C, ii, 0:C], True, True, [cur.r, Ak.r], [pAr])
                    pBX, pBXr = got.pop(0)
                    for ii in range(2):
                        if need_B and need_X and C == 128:
                            P.mm(pBX[0:C, ii * 256:(ii + 1) * 256], Ak.t[0:C, ii, 0:C], cur.t[0:C, ii, 0:256], True, True, [cur.r, Ak.r], [pBXr])
                        else:
                            if need_B:
                                P.mm(pBX[0:C, ii * 256:ii * 256 + C], Ak.t[0:C, ii, 0:C], cur.t[0:C, ii, 0:C], True, True, [cur.r, Ak.r], [pBXr])
                            if need_X:
                                P.mm(pBX[0:C, ii * 256 + 128:ii * 256 + 128 + C], Ak.t[0:C, ii, 0:C], cur.t[0:C, ii, 128:128 + C], True, True,
                                     [cur.r, Ak.r], [pBXr])
                    yield
                    if pA is not None:
                        P.copy("act", v3(An), g3(pA[0:C, 0:2 * C]), [pAr], [An.r])
                        PFREE(pAr)
                    if need_B:
                        P.copy("act", bxv(nxb, 0), pbv(pBX, 0), [pBXr], [nxb.r])
                    if need_X:
                        P.tt("dve", bxv(nxb, 1), pbv(pBX, 1), bxv(cur, 1), ALU.add, [pBXr, cur.r], [nxb.r])
                    PFREE(pBXr)
                    yield
                TTb = K["BX%d" % ((nlev + 1) % 2)]

                class _TT:
                    r = TTb.r

                    class t:
                        pass
                TT = _TT()
                if two:
                    (pY, pYr), (pZ, pZr) = yield from gPSF(2)
                    for ii in range(2):
                        P.mm(pY[0:C, ii * C:(ii + 1) * C], K["Ao"].t[0:C, ii, 0:C], TTb.t[0:C, ii, 128:128 + C], True, True, [K["Ao"].r, TT.r], [pYr])
                    for ii in range(2):
                        P.mm(pZ[0:C, ii * 128:(ii + 1) * 128], TTb.t[0:C, ii, 128:128 + C], K["bv"].t[0:C, ii, :], True, True, [TT.r, K["bv"].r], [pZr])
                    for ii in range(2):
                        P.mm(pZ[0:C, 256 + ii * 128:256 + (ii + 1) * 128], TTb.t[0:C, ii, 128:128 + C], K["kbg"].t[0:C, ii, :], True, True, [TT.r, K["kbg"].r], [pZr])
                    yield
                    P.tt("dve", v3(K["Y"]), Idb, g3(pY[0:C, 0:2 * C]), ALU.subtract, [identb.r, pYr], [K["Y"].r])
                    PFREE(pYr)
                    P.copy("act", K["sz"].t[0:C, :, :], pZ[0:C, 0:512].rearrange("p (h d) -> p h d", h=4), [pZr], [K["sz"].r])
                    PFREE(pZr)
                    yield
                    (pU, pUr), (pW, pWr) = yield from gPSF(2)
                    for ii in range(2):
                        P.mm(pU[0:C, ii * 128:(ii + 1) * 128], K["Y"].t[0:C, ii, 0:C], K["sz"].t[0:C, ii, :], True, True, [K["Y"].r, K["sz"].r], [pUr])
                    for ii in range(2):
                        P.mm(pW[:, ii * C:(ii + 1) * C], K["sz"].t[0:C, 2 + ii, :], K["Y"].t[0:C, ii, 0:C], True, True, [K["Y"].r, K["sz"].r], [pWr])
                else:
                    (pU, pUr), (pW, pWr) = yield from gPSF(2)
                    for ii in range(2):
                        P.mm(pU[0:C, ii * 128:(ii + 1) * 128], TTb.t[0:C, ii, 128:128 + C], K["bv"].t[0:C, ii, :], True, True, [TT.r, K["bv"].r], [pUr])
                    for ii in range(2):
                        P.mm(pW[:, ii * C:(ii + 1) * C], K["kbg"].t[0:C, ii, :], TTb.t[0:C, ii, 128:128 + C], True, True, [TT.r, K["kbg"].r], [pWr])
                yield
                P.copy("act", K["u"].t[0:C, :, :], pU[0:C, 0:256].rearrange("p (h d) -> p h d", h=2), [pUr], [K["u"].r])
                PFREE(pUr)
                P.copy("dve", K["wT"].t[:, :, 0:C], g3(pW[:, 0:2 * C]), [pWr], [K["wT"].r])
                PFREE(pWr)
                yield

            def gen_scan(n, sub, hg=hg):
                K = cksub[sub][n % 2]
                c0 = n * C
                li = (2 * sub, 2 * sub + 1)
                hgl = (hg * 4 + 2 * sub, hg * 4 + 2 * sub + 1)

                def bcd(ap2):
                    return ap2.unsqueeze(2).broadcast_to([C, 2, 128])

                def g3(ap):
                    return ap.rearrange("p (h c) -> p h c", h=2)

                p1, p1r = yield from gPSF()
                for ii in range(2):
                    h = hgl[ii]
                    P.mm(p1[0:C, ii * 128:(ii + 1) * 128], K["wT"].t[:, ii, 0:C], Sgb[h].t[:, :], True, True, [K["wT"].r, Sgb[h].r], [p1r])
                yield
                P.tt("dve", K["vnew"].t[0:C, :, :], K["u"].t[0:C, :, :], p1[0:C, 0:256].rearrange("p (h d) -> p h d", h=2), ALU.subtract,
                     [K["u"].r, p1r], [K["vnew"].r])
                PFREE(p1r)
                yield
                (pO, pOr), (pSS, pSSr) = yield from gPSF(2)
                for ii in range(2):
                    P.mm(pSS[:, ii * 128:(ii + 1) * 128], K["kd"].t[0:C, ii, :], K["vnew"].t[0:C, ii, :], True, True, [K["kd"].r, K["vnew"].r], [pSSr])
                for ii in range(2):
                    h = hgl[ii]
                    P.mm(pO[0:C, ii * 128:(ii + 1) * 128], K["qdT"].t[:, ii, 0:C], Sgb[h].t[:, :], True, False, [K["qdT"].r, Sgb[h].r], [pOr])
                    P.mm(pO[0:C, ii * 128:(ii + 1) * 128], K["PT"].t[0:C, ii, 0:C], K["vnew"].t[0:C, ii, :], False, True, [K["PT"].r, K["vnew"].r], [pOr])
                yield
                for ii in range(2):
                    h = hgl[ii]
                    P.stt("dve", Sg[h].t[:, :], Sg[h].t[:, :], K["egl"].t[:, ii:ii + 1], pSS[:, ii * 128:(ii + 1) * 128], ALU.mult, ALU.add,
                          [Sg[h].r, K["egl"].r, pSSr], [Sg[h].r])
                PFREE(pSSr)
                P.copy("act", K["osb"].t[0:C, :, :], pO[0:C, 0:256].rearrange("p (h d) -> p h d", h=2), [pOr], [K["osb"].r])
                PFREE(pOr)
                yield
                for ii in range(2):
                    h = hgl[ii]
                    P.copy("act" if ii == 0 else "pool", Sgb[h].t[:, :], Sg[h].t[:, :], [Sg[h].r], [Sgb[h].r])
                P.tt("pool", K["osq"].t[0:C, :, :], K["osb"].t[0:C, :, :], K["osb"].t[0:C, :, :], ALU.mult, [K["osb"].r], [K["osq"].r])
                yield
                P.red("dve", K["ost"].t[0:C, 0:2], K["osq"].t[0:C, :, :], [K["osq"].r], [K["ost"].r])
                yield
                P.act(K["ost"].t[0:C, 2:4], K["ost"].t[0:C, 0:2], AF.Ln, [K["ost"].r], [K["ost"].r], scale=1.0 / 128, bias=eps_ap(C))
                P.act(K["ost"].t[0:C, 2:4], K["ost"].t[0:C, 2:4], AF.Exp, [K["ost"].r], [K["ost"].r], scale=-0.5)
                yield
                P.tt("pool", K["on"].t[0:C, :, :], K["osb"].t[0:C, :, :], bcd(K["ost"].t[0:C, 2:4]), ALU.mult, [K["osb"].r, K["ost"].r], [K["on"].r])
                yield
                pt, pr = yield from gPSB()
                for ii in range(2):
                    P.tr(pt[:, ii * C:(ii + 1) * C], K["on"].t[0:C, ii, :], identb.t[0:C, 0:C], [K["on"].r, identb.r], [pr])
                yield
                P.stt("dve", yaT.t[:, hgl[0]:hgl[1] + 1, c0:c0 + C], g3(pt[:, 0:2 * C]), pc(PC_GDNN), szT.t[:, li[0]:li[1] + 1, c0:c0 + C],
                      ALU.mult, ALU.mult, [pr, prm.r, szT.r], [yaT.r])
                PFREE(pr)
                yield

            def gen_ret(n, sub, hg=hg):
                K = cksub[sub][n % 2]
                c0 = n * C
                tb = c0 // 128
                li = (2 * sub, 2 * sub + 1)
                hgl = (hg * 4 + 2 * sub, hg * 4 + 2 * sub + 1)

                def bcd(ap2):
                    return ap2.unsqueeze(2).broadcast_to([C, 2, 128])

                def g3(ap):
                    return ap.rearrange("p (h c) -> p h c", h=2)

                pSc, pScr = yield from gPSF()
                for ii in range(2):
                    i = li[ii]
                    P.mm(pSc[0:C, ii * C:(ii + 1) * C], rkT.t[:, i, c0:c0 + C], rqT.t[:, i, c0:c0 + C], True, True, [rkT.r, rqT.r], [pScr])
                yield
                P.tt("dve", K["scT"].t[0:C, :, 0:C], g3(pSc[0:C, 0:2 * C]),
                     cst.t[0:C, CC_DRT + hgl[0] * 128:CC_DRT + hgl[0] * 128 + 256].rearrange("p (h c) -> p h c", h=2)[:, :, 0:C], ALU.mult,
                     [pScr, cst.r], [K["scT"].r])
                PFREE(pScr)
                yield
                (pOr2, pOr2r), (pSR, pSRr) = yield from gPSF(2)
                for ii in range(2):
                    i = li[ii]
                    h = hgl[ii]
                    P.mm(pOr2[0:C, ii * 128:(ii + 1) * 128], rqdT.t[:, i, c0:c0 + C], Srb[h].t[:, :], True, False, [rqdT.r, Srb[h].r], [pOr2r])
                    P.mm(pOr2[0:C, ii * 128:(ii + 1) * 128], K["scT"].t[0:C, ii, 0:C], rv_tm.t[0:C, tb, i * 128:(i + 1) * 128], False, True,
                         [K["scT"].r, rv_tm.r], [pOr2r])
                for ii in range(2):
                    i = li[ii]
                    P.mm(pSR[:, ii * 128:(ii + 1) * 128], rkd_tm.t[0:C, tb, i * 128:(i + 1) * 128], rv_tm.t[0:C, tb, i * 128:(i + 1) * 128], True, True,
                         [rkd_tm.r, rv_tm.r], [pSRr])
                yield
                for ii in range(2):
                    h = hgl[ii]
                    P.stt("dve", Sr[h].t[:, :], Sr[h].t[:, :], cst.t[:, cd_c + h:cd_c + h + 1], pSR[:, ii * 128:(ii + 1) * 128], ALU.mult, ALU.add,
                          [Sr[h].r, cst.r, pSRr], [Sr[h].r])
                PFREE(pSRr)
                P.copy("act", K["orb"].t[0:C, :, :], pOr2[0:C, 0:256].rearrange("p (h d) -> p h d", h=2), [pOr2r], [K["orb"].r])
                PFREE(pOr2r)
                yield
                for ii in range(2):
                    h = hgl[ii]
                    P.copy("act" if ii == 1 else "pool", Srb[h].t[:, :], Sr[h].t[:, :], [Sr[h].r], [Srb[h].r])
                P.red("dve", K["ort"].t[0:C, 0:2], K["orb"].t[0:C, :, :], [K["orb"].r], [K["ort"].r])
                yield
                P.ts("dve", K["ort"].t[0:C, 0:2], K["ort"].t[0:C, 0:2], 1.0 / 128, None, ALU.mult, None, [K["ort"].r], [K["ort"].r])
                yield
                P.tt("pool", K["orc"].t[0:C, :, :], K["orb"].t[0:C, :, :], bcd(K["ort"].t[0:C, 0:2]), ALU.subtract, [K["orb"].r, K["ort"].r], [K["orc"].r])
                yield
                P.tt("pool", K["orq"].t[0:C, :, :], K["orc"].t[0:C, :, :], K["orc"].t[0:C, :, :], ALU.mult, [K["orc"].r], [K["orq"].r])
                yield
                P.red("dve", K["ort"].t[0:C, 2:4], K["orq"].t[0:C, :, :], [K["orq"].r], [K["ort"].r])
                yield
                P.act(K["ort"].t[0:C, 2:4], K["ort"].t[0:C, 2:4], AF.Ln, [K["ort"].r], [K["ort"].r], scale=1.0 / 128, bias=eps_ap(C))
                P.act(K["ort"].t[0:C, 2:4], K["ort"].t[0:C, 2:4], AF.Exp, [K["ort"].r], [K["ort"].r], scale=-0.5)
                yield
                P.tt("pool", K["orn"].t[0:C, :, :], K["orc"].t[0:C, :, :], bcd(K["ort"].t[0:C, 2:4]), ALU.mult, [K["orc"].r, K["ort"].r], [K["orn"].r])
                yield
                pt, pr = yield from gPSB()
                for ii in range(2):
                    P.tr(pt[:, ii * C:(ii + 1) * C], K["orn"].t[0:C, ii, :], identb.t[0:C, 0:C], [K["orn"].r, identb.r], [pr])
                yield
                P.tt("dve", K["ybt"].t[:, :, 0:C], g3(pt[:, 0:2 * C]),
                     prm.t[:, PC_RETN + hgl[0]:PC_RETN + hgl[0] + 2].unsqueeze(2).broadcast_to([128, 2, C]), ALU.mult, [pr, prm.r], [K["ybt"].r])
                PFREE(pr)
                yield
                P.tt("pool", ybT.t[:, hgl[0]:hgl[1] + 1, c0:c0 + C], K["ybt"].t[:, :, 0:C], srgT.t[:, li[0]:li[1] + 1, c0:c0 + C], ALU.mult,
                     [K["ybt"].r, srgT.r], [ybT.r])
                yield

            cj = []
            for n in range(NCH):
                for sub in range(2):
                    d = []
                    if n >= 2:
                        d.append("prep%d_%d" % (n - 2, sub))
                        d.append("scan%d_%d" % (n - 2, sub))
                        d.append("prepB%d_%d" % (n - 2, sub))
                    cj.append(("prep%d_%d" % (n, sub), (lambda n=n, sub=sub: gen_prep(n, sub)), d))
                for sub in range(2):
                    d = ["prep%d_%d" % (n, sub)]
                    if n >= 1:
                        d.append("prepB%d_%d" % (n - 1, sub))
                    if n >= 2:
                        d.append("scan%d_%d" % (n - 2, sub))
                    cj.append(("prepB%d_%d" % (n, sub), (lambda n=n, sub=sub: gen_prepB(n, sub)), d))
                for sub in range(2):
                    d = ["ret%d_%d" % (n - 1, sub), "prepB%d_%d" % (n - 1, sub)] if n >= 1 else []
                    cj.append(("ret%d_%d" % (n, sub), (lambda n=n, sub=sub: gen_ret(n, sub)), d))
                for sub in range(2):
                    d = ["prepB%d_%d" % (n, sub)]
                    if n >= 1:
                        d.append("scan%d_%d" % (n - 1, sub))
                    cj.append(("scan%d_%d" % (n, sub), (lambda n=n, sub=sub: gen_scan(n, sub)), d))
            run_jobs(cj, 12, False)

        for p in range(2):
            slot = next_piece("ga")
            for i in range(4):
                pt, pr = fm_group(slot, i, T, nT, nT.r)
                P.act(sgT.t[:, i, 0:T], pt[:, 0:T], AF.Sigmoid, [pr], [sgT.r])
            slot = next_piece("gb")
            for i in range(4):
                pt, pr = fm_group(slot, i, T, nT, nT.r)
                P.act(sbT.t[:, i, 0:T], pt[:, 0:T], AF.Sigmoid, [pr], [sbT.r])
            slot = next_piece("brg")
            for i in range(4):
                pt, pr = fm_group(slot, i, T, yaT, yaT.r)
                P.tt("dve", mtmp[i].t[:, 0:T], pt[:, 0:T], sgT.t[:, i, 0:T], ALU.mult, [pr, sgT.r], [mtmp[i].r])
            slot = next_piece("brr")
            for i in range(4):
                pt, pr = fm_group(slot, i, T, ybT, ybT.r)
                m2 = mtmp2[i % 2]
                P.tt("dve", m2.t[:, 0:T], pt[:, 0:T], sbT.t[:, i, 0:T], ALU.mult, [pr, sbT.r], [m2.r])
                P.tt("pool", mergedT.t[:, 4 * p + i, 0:T], m2.t[:, 0:T], mtmp[i].t[:, 0:T], ALU.add, [m2.r, mtmp[i].r], [mergedT.r])
        for ch in range(2):
            slot = next_piece("wo")
            for tb, (t0, tn) in enumerate(tbs):
                pt, pr = tm_group(slot, mergedT, mergedT.r, t0, tn)
                hh = H[tb].t[0:tn, ch * 512:(ch + 1) * 512]
                P.tt("dve", hh, pt[0:tn, :], hh, ALU.add, [pr, H[tb].r], [H[tb].r])

    def final_out(t):
        tbs = [(tb * 128, 128) for tb in range(4)]
        fins = []
        P.dma(fnw.t[:], fnw_d[:, :], [], [fnw.r])
        for tb in range(4):
            P.act(junk.t[:, :], H[tb].t[:, :], AF.Square, [H[tb].r], [junk.r, ss.r], accum_out=ss.t[:, tb:tb + 1])
        P.act(rstd.t[:, 0:4], ss.t[:, 0:4], AF.Ln, [ss.r], [rstd.r], scale=1.0 / D, bias=eps_ap())
        P.act(rstd.t[:, 0:4], rstd.t[:, 0:4], AF.Exp, [rstd.r], [rstd.r], scale=-0.5)
        for tb in range(4):
            P.stt("dve", OUTB[tb].t[:, :], H[tb].t[:, :], rstd.t[:, tb:tb + 1], fnw.t[:, :], ALU.mult, ALU.mult,
                  [H[tb].r, rstd.r, fnw.r], [OUTB[tb].r])
            r0 = t * 512 + tb * 128
            fins.append(P.dma(out_d[r0:r0 + 128, :], OUTB[tb].t[:, :], [OUTB[tb].r], []))
        return fins

    fin_ops = []
    tbs_m = [(0, NMETA)]
    tbs = [(tb * 128, 128) for tb in range(4)]

    def prefetch_hooks(tn_):
        def pre():
            for tb in range(4):
                r0 = tn_ * 512 + tb * 128
                P.dma(XN[tb].t[:, :], x_d[r0:r0 + 128, :], [], [XN[tb].r])

        def mid():
            norm_A(tbs, XN, nbx, junk2, ss2, rstd2)

        def post():
            norm_B(tbs, nbx)
        return pre, mid, post

    P.dma(H[0].t[0:NMETA, :], meta_d[:, :], [], [H[0].r])
    ffn(1, NMETA, tbs_m)
    mixer(NMETA, tbs_m, 16, 0)
    pre, mid, post = prefetch_hooks(0)
    ffn(2, NMETA, tbs_m, pre=pre, mid=mid, post=post)
    for t in range(nt):
        for tb in range(4):
            P.copy("pool", H[tb].t[:, :], XN[tb].t[:, :], [XN[tb].r], [H[tb].r])
        ffn(1, 512, tbs, skip_norm=True)
        mixer(512, tbs, 128, NMETA + t * 512)
        if t + 1 < nt:
            pre, mid, post = prefetch_hooks(t + 1)
            ffn(2, 512, tbs, pre=pre, mid=mid, post=post)
        else:
            ffn(2, 512, tbs)
        fin_ops += final_out(t)
    P.emit(fin_ops)
    if debug:
        print("ops", len(P.ops), {e: sum(1 for o in P.ops if o.eng == e) for e in ENGS})
    return nc


WNAMES = ("ffn1_w_in", "ffn1_w_out", "w_in", "w_branch_gdn", "w_branch_ret", "w_out", "ffn2_w_in", "ffn2_w_out")


def make_in_maps(inputs, nb, nt):
    W = {k: np.asarray(inputs[k], np.float32)[0] for k in WNAMES}
    wst = host_pack_weights(W)
    prm = host_pack_params({k: np.asarray(inputs[k], np.float32)[0] for k in
                            ("ffn1_norm", "mix_norm", "ffn2_norm", "ret_out_norm", "gdn_out_norm", "gdn_conv_w",
                             "gdn_a_log", "gdn_dt_bias", "w_in")})
    fnw = np.ascontiguousarray(np.broadcast_to(np.asarray(inputs["final_norm"], np.float32)[None, :], (128, D)))
    cst = host_consts()
    rope = host_rope(NMETA + nt * 512)
    meta = np.ascontiguousarray(np.asarray(inputs["meta_tokens"], np.float32))
    x = np.asarray(inputs["x"], np.float32)
    return [{"x": np.ascontiguousarray(x[b, :nt * 512]), "meta": meta, "wst": wst, "prm": prm, "fnw": fnw, "cst": cst, "rope": rope}
            for b in range(nb)]


def kernel(**inputs):
    nt = SEQ // 512
    nc = build(nt)
    in_maps = make_in_maps(inputs, 8, nt)
    res = run_bass_kernel_spmd(nc, in_maps, core_ids=list(range(8)))
    return np.stack([np.asarray(r["out"], np.float32) for r in res.results], axis=0)
```

```python
import contextlib
import numpy as np
import ml_dtypes
import concourse.bass as bass
import concourse.mybir as mybir
from concourse.bass_utils import run_bass_kernel_spmd

F32 = mybir.dt.float32
BF16 = mybir.dt.bfloat16
AF = mybir.ActivationFunctionType
ALU = mybir.AluOpType
AX = mybir.AxisListType

D = 1024
NMETA = 16
SEQ = 8192
DFF = 2816
NJ = 22
DPROJ = 10256
EPS = 1e-6
NH = 8
PIECE = 4096
NSLOT = 4
ENGS = ("pe", "act", "dve", "pool", "sp")
N_DMA_SLOTS = 12
SAME_ENGINE_SYNC = True
SB_BASE = 18432
SB_LIMIT = 229376


class Res:
    __slots__ = ("name", "last_w", "readers", "overlaps", "lo", "hi", "excl")

    def __init__(self, name, lo=None, hi=None, excl=False):
        self.name = name
        self.excl = excl
        self.last_w = None
        self.readers = {}
        self.overlaps = []
        self.lo = lo
        self.hi = hi


class Op:
    __slots__ = ("idx", "eng", "fn", "deps", "dma", "needs_inc", "sem", "val", "prev_val")

    def __init__(self, idx, eng, fn, deps, dma):
        self.idx = idx
        self.eng = eng
        self.fn = fn
        self.deps = deps
        self.dma = dma
        self.needs_inc = False
        self.sem = None
        self.val = 0
        self.prev_val = 0


class Prog:
    def __init__(self, nc):
        self.nc = nc
        self.ops = []

    def op(self, eng, fn, reads=(), writes=(), dma=False):
        idx = len(self.ops)
        deps = set()
        wset = []
        for w in writes:
            wset.append(w)
            wset.extend(w.overlaps)
        rl = []
        for r in reads:
            if r.excl:
                wset.append(r)
            else:
                rl.append(r)
        reads = rl
        for r in reads:
            if r.last_w is not None:
                deps.add(r.last_w)
        for w in wset:
            if w.last_w is not None:
                deps.add(w.last_w)
            for ridx in w.readers.values():
                deps.add(ridx)
        o = Op(idx, eng, fn, sorted(deps), dma)
        self.ops.append(o)
        for r in reads:
            r.readers[("dma", idx) if dma else eng] = idx
        for w in wset:
            w.last_w = idx
            w.readers = {}
        return o

    def mm(self, out, lhsT, rhs, start, stop, reads, writes):
        return self.op("pe", lambda e: e.matmul(out, lhsT=lhsT, rhs=rhs, start=start, stop=stop), reads, writes)

    def tr(self, out, in_, ident, reads, writes):
        return self.op("pe", lambda e: e.transpose(out=out, in_=in_, identity=ident), reads, writes)

    def tt(self, eng, out, in0, in1, op, reads, writes):
        return self.op(eng, lambda e: e.tensor_tensor(out=out, in0=in0, in1=in1, op=op), reads, writes)

    def ts(self, eng, out, in0, s1, s2, op0, op1, reads, writes):
        if s2 is None:
            return self.op(eng, lambda e: e.tensor_scalar(out=out, in0=in0, scalar1=s1, scalar2=None, op0=op0), reads, writes)
        return self.op(eng, lambda e: e.tensor_scalar(out=out, in0=in0, scalar1=s1, scalar2=s2, op0=op0, op1=op1), reads, writes)

    def stt(self, eng, out, in0, scalar, in1, op0, op1, reads, writes):
        return self.op(eng, lambda e: e.scalar_tensor_tensor(out=out, in0=in0, scalar=scalar, in1=in1, op0=op0, op1=op1), reads, writes)

    def act(self, out, in_, func, reads, writes, bias=None, scale=None, accum_out=None):
        kw = {}
        if bias is not None:
            kw["bias"] = bias
        if scale is not None:
            kw["scale"] = scale
        if accum_out is not None:
            kw["accum_out"] = accum_out
        return self.op("act", lambda e: e.activation(out=out, in_=in_, func=func, **kw), reads, writes)

    def copy(self, eng, out, in_, reads, writes):
        if eng == "act":
            return self.act(out, in_, AF.Copy, reads, writes)
        return self.op(eng, lambda e: e.tensor_copy(out=out, in_=in_), reads, writes)

    def red(self, eng, out, in_, reads, writes):
        return self.op(eng, lambda e: e.tensor_reduce(out=out, in_=in_, axis=AX.X, op=ALU.add), reads, writes)

    def memset(self, eng, out, val, writes):
        return self.op(eng, lambda e: e.memset(out, val), (), writes)

    def dma(self, out, in_, reads, writes, queue="sp"):
        return self.op(queue, lambda e: e.dma_start(out=out, in_=in_), reads, writes, dma=True)

    def emit(self, final_ops):
        nc = self.nc
        ops = self.ops

        def skip_same(o, dop):
            return (not dop.dma) and (not o.dma) and dop.eng == o.eng and (o.eng == "pe" or not SAME_ENGINE_SYNC)

        for o in ops:
            for d in o.deps:
                dop = ops[d]
                if dop.dma or skip_same(o, dop):
                    continue
                dop.needs_inc = True
        with contextlib.ExitStack() as st:
            esem = {e: st.enter_context(nc.semaphore("prog_" + e)) for e in ENGS}
            dsem = {q: [st.enter_context(nc.semaphore("dma_%s_%d" % (q, i))) for i in range(N_DMA_SLOTS)]
                    for q in ("sp", "pool")}
            cnt = {e: 0 for e in ENGS}
            dcnt = {q: 0 for q in dsem}
            duse = {q: [0] * N_DMA_SLOTS for q in dsem}
            for o in ops:
                if o.dma:
                    q = o.eng
                    s = dcnt[q] % N_DMA_SLOTS
                    dcnt[q] += 1
                    o.prev_val = 16 * duse[q][s]
                    duse[q][s] += 1
                    o.sem = dsem[q][s]
                    o.val = 16 * duse[q][s]
                elif o.needs_inc:
                    cnt[o.eng] += 1
                    o.sem = esem[o.eng]
                    o.val = cnt[o.eng]
            per = {e: [o for o in ops if o.eng == e] for e in ENGS}
            block = st.enter_context(nc.Block())

            def run(ename, eng):
                waited = {}
                for o in per[ename]:
                    need = {}
                    for d in o.deps:
                        dop = ops[d]
                        if skip_same(o, dop):
                            continue
                        k = id(dop.sem)
                        if k not in need or need[k][1] < dop.val:
                            need[k] = (dop.sem, dop.val)
                    if o.dma and o.prev_val > 0:
                        k = id(o.sem)
                        if k not in need or need[k][1] < o.prev_val:
                            need[k] = (o.sem, o.prev_val)
                    for k, (sem, val) in need.items():
                        if waited.get(k, 0) >= val:
                            continue
                        eng.wait_ge(sem, val)
                        waited[k] = val
                    ins = o.fn(eng)
                    if o.dma:
                        ins.then_inc(o.sem, 16)
                    elif o.needs_inc:
                        ins.then_inc(o.sem, 1)
                if ename == "sp":
                    for fo in final_ops:
                        eng.wait_ge(fo.sem, fo.val)

            @block.sync
            def _(e):
                run("sp", e)

            @block.tensor
            def _(e):
                run("pe", e)

            @block.scalar
            def _(e):
                run("act", e)

            @block.vector
            def _(e):
                run("dve", e)

            @block.gpsimd
            def _(e):
                run("pool", e)


class Buf:
    __slots__ = ("t", "r")

    def __init__(self, t, r):
        self.t = t
        self.r = r


class SBAlloc:
    def __init__(self, nc):
        self.nc = nc
        self.off = SB_BASE
        self.peak = SB_BASE
        self.all = []

    def alloc(self, name, shape, dt):
        esz = 4 if dt == F32 else 2
        n = 1
        for s in shape[1:]:
            n *= s
        size = (n * esz + 31) // 32 * 32
        assert self.off + size <= SB_LIMIT, ("SBUF overflow", name, self.off, size)
        t = self.nc.alloc_sbuf_tensor_at(name, list(shape), dt, offset=self.off)
        r = Res(name, self.off, self.off + size)
        self.off += size
        self.peak = max(self.peak, self.off)
        self.all.append(r)
        return Buf(t, r)

    def finalize(self):
        rs = sorted(self.all, key=lambda r: r.lo)
        for i, a in enumerate(rs):
            for b in rs[i + 1:]:
                if b.lo >= a.hi:
                    break
                a.overlaps.append(b)
                b.overlaps.append(a)


def piece_specs():
    sp = []

    def ffn(i):
        win, wout, nrm = "ffn%d_w_in" % i, "ffn%d_w_out" % i, "ffn%d_norm" % i
        for g in range(NJ // 2):
            j0, j1 = 2 * g, 2 * g + 1
            sp.append(dict(kind="FM", w=win, cc=[j0 * 128, DFF + j0 * 128, j1 * 128, DFF + j1 * 128], norm=nrm, tag=("ffn_in", i, g)))
        for ch in range(2):
            for (j0, nj) in ((0, 8), (8, 8), (16, 6)):
                sp.append(dict(kind="TMK", w=wout, j0=j0, nj=nj, c0=ch * 512, norm=None, tag=("ffn_out", i, ch, j0, nj)))

    ffn(1)
    for hg in range(2):
        for nm, base in (("q", 0), ("k", 1024), ("v", 2048), ("z", 3072), ("rg", 7184)):
            sp.append(dict(kind="FM", w="w_in", cc=[base + (4 * hg + i) * 128 for i in range(4)], norm="mix_norm", tag=(nm, hg)))
        for nm, base in (("rq", 4112), ("rk", 5136), ("rv", 6160)):
            sp.append(dict(kind="TM", w="w_in", c0=base + hg * 512, norm="mix_norm", tag=(nm, hg)))
    for p in range(2):
        sp.append(dict(kind="FM", w="w_in", cc=[8208 + (4 * p + i) * 128 for i in range(4)], norm="mix_norm", tag=("ga", p)))
        sp.append(dict(kind="FM", w="w_in", cc=[9232 + (4 * p + i) * 128 for i in range(4)], norm="mix_norm", tag=("gb", p)))
        sp.append(dict(kind="FM", w="w_branch_gdn", cc=[(4 * p + i) * 128 for i in range(4)], norm=None, tag=("brg", p)))
        sp.append(dict(kind="FM", w="w_branch_ret", cc=[(4 * p + i) * 128 for i in range(4)], norm=None, tag=("brr", p)))
    for ch in range(2):
        sp.append(dict(kind="TM", w="w_out", c0=ch * 512, norm=None, tag=("wo", ch)))
    ffn(2)
    return sp


PIECES = piece_specs()
NP = len(PIECES)
NORM_COL = {"ffn1_norm": 0, "mix_norm": 8, "ffn2_norm": 16}
PC_RETN = 24
PC_GDNN = 32
PC_CONV = 33
PC_ALOG = 129
PC_DTB = 137
PC_WBA = 145
NPRM = PC_WBA + 128
CC_U = 0
CC_MUI = 128
CC_MLS = 256
CC_DRT = 384
CC_XI = 1408
CC_ZS128 = 2432
CC_ZS16 = 2440
CC_CD128 = 2448
CC_CD16 = 2456
CC_ONES = 2464
CC_MLO = 2592
NCST = CC_MLO + 128


def host_consts():
    c = np.zeros((128, NCST), np.float32)
    j = np.arange(128)
    c[:, CC_U:CC_U + 128] = (j[:, None] <= j[None, :])
    c[:, CC_MUI:CC_MUI + 128] = (j[:, None] <= j[None, :])
    c[:, CC_MLS:CC_MLS + 128] = (j[None, :] < j[:, None]) & ((j[None, :] // 64) == (j[:, None] // 64))
    c[:, CC_MLO:CC_MLO + 128] = (j[None, :] < 64) & (j[:, None] >= 64)
    lg = np.log1p(-np.exp2(-5.0 - np.arange(NH, dtype=np.float64)))
    m = j[:, None, None]
    cc = j[None, None, :]
    dr = np.where(m <= cc, np.exp(np.maximum(cc - m, 0) * lg[None, :, None]), 0.0)
    c[:, CC_DRT:CC_DRT + 1024] = dr.reshape(128, 1024)
    xi = np.exp((j[None, None, :] + 1.0) * lg[None, :, None]) * np.ones((128, 1, 1))
    c[:, CC_XI:CC_XI + 1024] = xi.reshape(128, 1024)
    sc = 128.0 ** -0.5
    c[:, CC_ZS128:CC_ZS128 + 8] = np.exp((127.0 - j)[:, None] * lg[None, :]) * sc
    c[:16, CC_ZS16:CC_ZS16 + 8] = np.exp((15.0 - j[:16])[:, None] * lg[None, :]) * sc
    c[:, CC_CD128:CC_CD128 + 8] = np.exp(128.0 * lg)[None, :]
    c[:, CC_CD16:CC_CD16 + 8] = np.exp(16.0 * lg)[None, :]
    c[:, CC_ONES:CC_ONES + 128] = 1.0
    return c


def host_rope(L):
    inv = (1.0 / (10000.0 ** np.linspace(0.0, 1.0, 64, dtype=np.float32))).astype(np.float32)
    pos = np.arange(L, dtype=np.float32)
    ang = (pos[:, None] * inv[None, :]).astype(np.float32)
    r = np.zeros((L, 128), np.float32)
    r[:, :64] = np.cos(ang.astype(np.float64))
    r[:, 64:] = np.sin(ang.astype(np.float64))
    return r


def host_pack_weights(W):
    out = np.zeros((NP, 128, PIECE), np.float32)
    for s, sp in enumerate(PIECES):
        w = W[sp["w"]]
        if sp["kind"] == "FM":
            wk = w.reshape(8, 128, -1)
            for i, c0 in enumerate(sp["cc"]):
                blk = wk[:, :, c0:c0 + 128]
                out[s].reshape(128, 8, 4, 128)[:, :, i, :] = blk.transpose(1, 0, 2)
        elif sp["kind"] == "TM":
            wk = w.reshape(8, 128, -1)[:, :, sp["c0"]:sp["c0"] + 512]
            out[s].reshape(128, 8, 512)[:, :, :] = wk.transpose(1, 0, 2)
        else:
            wk = w.reshape(NJ, 128, -1)[sp["j0"]:sp["j0"] + sp["nj"], :, sp["c0"]:sp["c0"] + 512]
            out[s].reshape(128, 8, 512)[:, :sp["nj"], :] = wk.transpose(1, 0, 2)
    return out


def host_pack_params(I):
    prm = np.zeros((128, NPRM), np.float32)
    for nm, c0 in NORM_COL.items():
        prm[:, c0:c0 + 8] = np.asarray(I[nm]).reshape(8, 128).T
    prm[:, PC_RETN:PC_RETN + 8] = np.asarray(I["ret_out_norm"]).reshape(8, 128).T
    prm[:, PC_GDNN] = np.asarray(I["gdn_out_norm"]).reshape(128)
    cw = np.asarray(I["gdn_conv_w"]).reshape(4, 24, 128)
    prm[:, PC_CONV:PC_CONV + 96] = cw.transpose(2, 1, 0).reshape(128, 96)
    prm[:, PC_ALOG:PC_ALOG + 8] = np.asarray(I["gdn_a_log"]).reshape(1, 8)
    prm[:, PC_DTB:PC_DTB + 8] = np.asarray(I["gdn_dt_bias"]).reshape(1, 8)
    wba = np.asarray(I["w_in"]).reshape(8, 128, DPROJ)[:, :, 4096:4112]
    prm[:, PC_WBA:PC_WBA + 128] = wba.transpose(1, 0, 2).reshape(128, 128)
    return prm


def build(nt, debug=False):
    seq = nt * 512
    L = NMETA + seq
    nc = bass.Bass("TRN2", target_bir_lowering=False)
    x_d = nc.dram_tensor("x", [seq, D], F32, kind="ExternalInput").ap()
    meta_d = nc.dram_tensor("meta", [NMETA, D], F32, kind="ExternalInput").ap()
    wst_d = nc.dram_tensor("wst", [NP, 128, PIECE], F32, kind="ExternalInput").ap()
    prm_d = nc.dram_tensor("prm", [128, NPRM], F32, kind="ExternalInput").ap()
    fnw_d = nc.dram_tensor("fnw", [128, D], F32, kind="ExternalInput").ap()
    cst_d = nc.dram_tensor("cst", [128, NCST], F32, kind="ExternalInput").ap()
    rope_d = nc.dram_tensor("rope", [L, 128], F32, kind="ExternalInput").ap()
    out_d = nc.dram_tensor("out", [seq, D], F32, kind="ExternalOutput").ap()
    wsc_d = nc.dram_tensor("wsc", [NP, 128, PIECE], BF16).ap()
    wsc_r = [Res("wsc%d" % s) for s in range(NP)]

    P = Prog(nc)
    sb = SBAlloc(nc)
    A = sb.alloc

    prm = A("prm", [128, NPRM], F32)
    cst = A("cst", [128, NCST], F32)
    identb = A("identb", [128, 128], BF16)
    onesb = A("onesb", [128, 128], BF16)
    wba = A("wba", [128, 8, 16], BF16)
    nA = A("nA", [128, 8], F32)
    H = [A("H%d" % tb, [128, D], F32) for tb in range(4)]
    nb = [A("nb%d" % i, [128, D], BF16) for i in range(2)]
    nT = A("nT", [128, 8, 512], BF16)
    ring = [A("ring%d" % i, [128, PIECE], BF16) for i in range(NSLOT)]
    Sg = [A("Sg%d" % h, [128, 128], F32) for h in range(NH)]
    Sr = [A("Sr%d" % h, [128, 128], F32) for h in range(NH)]
    Sgb = [A("Sgb%d" % h, [128, 128], BF16) for h in range(NH)]
    Srb = [A("Srb%d" % h, [128, 128], BF16) for h in range(NH)]
    ctail = A("ctail", [128, 24, 3], F32)
    yaT = A("yaT", [128, 8, 512], BF16)
    ybT = A("ybT", [128, 8, 512], BF16)
    ss = A("ss", [128, 4], F32)
    rstd = A("rstd", [128, 4], F32)
    gtok = A("gtok", [128, 4, 8], F32)
    btok = A("btok", [128, 4, 8], F32)
    braw = A("braw", [128, 4, 16], F32)
    batmp = A("batmp", [128, 4, 8], F32)
    cbias = A("cbias", [128, 2], F32)

    def eps_ap(rows=128):
        return cbias.t[0:rows, 0:1]

    def one_ap(rows=128):
        return cbias.t[0:rows, 1:2]

    arena0 = sb.off

    stage = [A("stage%d" % i, [128, PIECE], F32) for i in range(4)]
    sb.off = arena0
    actT = A("actT", [128, NJ, 512], BF16)
    sil = [A("sil%d" % i, [128, 512], F32) for i in range(2)]
    OUTB = [A("OUTB%d" % tb, [128, D], F32) for tb in range(4)]
    fnw = A("fnw", [128, D], F32)
    junk = A("junk", [128, D], BF16)
    XN = [A("XN%d" % tb, [128, D], F32) for tb in range(4)]
    nbx = [A("nbx%d" % tb, [128, D], BF16) for tb in range(4)]
    junk2 = A("junk2", [128, D], BF16)
    ss2 = A("ss2", [128, 4], F32)
    rstd2 = A("rstd2", [128, 4], F32)
    sb.off = arena0
    qT = A("qT", [128, 4, 512], BF16)
    kT = A("kT", [128, 4, 512], BF16)
    vT = A("vT", [128, 4, 512], BF16)
    szT = A("szT", [128, 4, 512], BF16)
    srgT = A("srgT", [128, 4, 512], BF16)
    rkd_tm = A("rkd_tm", [128, 4, 512], BF16)
    rv_tm = A("rv_tm", [128, 4, 512], BF16)
    rqT = A("rqT", [128, 4, 512], BF16)
    rkT = A("rkT", [128, 4, 512], BF16)
    rqdT = A("rqdT", [128, 4, 512], BF16)
    sub0 = sb.off
    rq_tm = A("rq_tm", [128, 4, 512], BF16)
    rk_tm = A("rk_tm", [128, 4, 512], BF16)
    NCS = 5
    rope = A("rope", [128, 4, 128], F32)
    cin = [A("cin%d" % i, [128, 515], F32) for i in range(NCS)]
    cacc = [A("cacc%d" % i, [128, 512], F32) for i in range(NCS)]
    cf = cacc
    csq = [A("csq%d" % i, [128, 512], BF16) for i in range(NCS)]
    crs = [A("crs%d" % i, [128, 512], F32) for i in range(NCS)]
    rxs = [A("rxs%d" % i, [128, 512], F32) for i in range(2)]
    rt = [[A("rt%d_%d" % (i, k), [128, 256], F32) for k in range(2)] for i in range(2)]
    sb.off = sub0
    cksub = []
    for sub in range(2):
        base = {}
        for nm, shp, dt in (
            ("A0", [128, 2, 128], BF16), ("A1", [128, 2, 128], BF16), ("BX0", [128, 2, 256], BF16), ("BX1", [128, 2, 256], BF16),
            ("Y", [128, 2, 128], BF16),
            ("sz", [128, 4, 128], BF16),
            ("vnew", [128, 2, 128], BF16), ("osb", [128, 2, 128], F32), ("osq", [128, 2, 128], F32), ("ost", [128, 4], F32),
            ("on", [128, 2, 128], BF16),
            ("scT", [128, 2, 128], BF16), ("orb", [128, 2, 128], F32), ("orq", [128, 2, 128], F32), ("ort", [128, 4], F32),
            ("orn", [128, 2, 128], BF16), ("ybt", [128, 2, 128], F32),
        ):
            base[nm] = A("%s_s%d" % (nm, sub), shp, dt)
        base["orc"] = base["orb"]
        pars = []
        for par in range(2):
            d = dict(base)
            for nm, shp, dt in (("wT", [128, 2, 128], BF16), ("u", [128, 2, 128], F32), ("kd", [128, 2, 128], BF16),
                                ("PT", [128, 2, 128], BF16), ("qdT", [128, 2, 128], BF16), ("egl", [128, 2], F32),
                                ("Ain", [128, 2, 128], BF16), ("Ao", [128, 2, 128], BF16), ("kbg", [128, 2, 128], BF16),
                                ("bv", [128, 2, 128], BF16),
                                ("GU", [128, 2, 128], F32), ("gcs", [128, 4], F32), ("eg", [128, 2], F32), ("beg", [128, 2], F32),
                                ("ekd", [128, 2], F32), ("Dm", [128, 2, 128], F32), ("El", [128, 2, 128], F32),
                                ("EQ", [128, 2, 128], F32), ("EGQ", [128, 2, 128], F32)):
                d[nm] = A("%s_s%d_%d" % (nm, sub, par), shp, dt)
            d["Eu"] = d["Dm"]
            d["EA"] = d["El"]
            d["EAo"] = d["GU"]
            pars.append(d)
        cksub.append(pars)
    hg_end = sb.off
    sb.off = arena0
    sgT = A("sgT", [128, 4, 512], BF16)
    sbT = A("sbT", [128, 4, 512], BF16)
    mtmp = [A("mtmp%d" % i, [128, 512], F32) for i in range(4)]
    mtmp2 = [A("mtmp2_%d" % i, [128, 512], F32) for i in range(2)]
    mergedT = A("mergedT", [128, 8, 512], BF16)
    sb.finalize()
    if debug:
        print("SBUF peak", sb.peak, "of", SB_LIMIT, "hg_end", hg_end, "arena0", arena0)

    NF, NB = 6, 2
    psf = [nc.alloc_psum_tensor("psf%d" % i, [128, 512], F32) for i in range(NF)]
    psb = [nc.alloc_psum_tensor("psb%d" % i, [128, 1024], BF16) for i in range(NB)]
    psf_r = [Res("psf%d" % i, excl=True) for i in range(NF)]
    psb_r = [Res("psb%d" % i, excl=True) for i in range(NB)]
    from collections import deque
    free_f = deque(range(NF))
    free_b = deque(range(NB))

    def PSF(hold=False):
        i = free_f.popleft()
        if not hold:
            free_f.append(i)
        return psf[i], psf_r[i]

    def PSB(hold=False):
        i = free_b.popleft()
        if not hold:
            free_b.append(i)
        return psb[i], psb_r[i]

    def gPSF(k=1):
        while len(free_f) < k:
            yield
        got = [PSF(hold=True) for _ in range(k)]
        return got[0] if k == 1 else got

    def gPSB():
        while not free_b:
            yield
        return PSB(hold=True)

    def PFREE(r):
        if r in psf_r:
            free_f.append(psf_r.index(r))
        else:
            free_b.append(psb_r.index(r))

    def pc(c0, n=1):
        return prm.t[:, c0:c0 + n]

    def cc(c0, n, rows=128):
        return cst.t[0:rows, c0:c0 + n]

    rr = ["act", "dve"]
    rrc = {"i": 0}

    def nxt(choices=("act", "dve")):
        rrc["i"] += 1
        return choices[rrc["i"] % len(choices)]

    P.dma(prm.t[:], prm_d[:, :], [], [prm.r])
    P.dma(cst.t[:], cst_d[:, :], [], [cst.r])
    P.memset("pool", crs[0].t[:, 0:128], 0.0, [crs[0].r])
    P.op("pool", lambda e: e.affine_select(out=crs[0].t[:, 0:128], in_=crs[0].t[:, 0:128], pattern=[[-1, 128]],
                                           compare_op=ALU.not_equal, fill=1.0, base=0, channel_multiplier=1),
         [crs[0].r], [crs[0].r])
    P.copy("dve", identb.t[:], crs[0].t[:, 0:128], [crs[0].r], [identb.r])
    P.memset("dve", onesb.t[:], 1.0, [onesb.r])
    for kc in range(8):
        P.ts("dve", wba.t[:, kc, :], prm.t[:, PC_WBA + kc * 16:PC_WBA + kc * 16 + 16], pc(NORM_COL["mix_norm"] + kc), None,
             ALU.mult, None, [prm.r], [wba.r])
    P.act(nA.t[:], pc(PC_ALOG, 8), AF.Exp, [prm.r], [nA.r])
    P.ts("dve", nA.t[:], nA.t[:], -1.0, None, ALU.mult, None, [nA.r], [nA.r])
    for h in range(NH):
        P.memset("pool", Sg[h].t[:], 0.0, [Sg[h].r])
        P.memset("pool", Sr[h].t[:], 0.0, [Sr[h].r])
        P.memset("dve", Sgb[h].t[:], 0.0, [Sgb[h].r])
        P.memset("dve", Srb[h].t[:], 0.0, [Srb[h].r])
    P.memset("pool", ctail.t[:], 0.0, [ctail.r])
    P.memset("pool", cbias.t[:, 0:1], EPS, [cbias.r])
    P.memset("pool", cbias.t[:, 1:2], 1.0, [cbias.r])

    def pre_in(s):
        P.dma(stage[s % 4].t[:], wst_d[s], [], [stage[s % 4].r])

    for s in range(min(3, NP)):
        pre_in(s)
    for s, spc in enumerate(PIECES):
        stg = stage[s % 4]
        slot = ring[s % NSLOT]
        if spc["norm"] is not None:
            c0 = NORM_COL[spc["norm"]]
            for kc in range(8):
                eng = "act" if kc % 2 == 0 else "dve"
                o_ = slot.t[:, kc * 512:(kc + 1) * 512]
                i_ = stg.t[:, kc * 512:(kc + 1) * 512]
                if eng == "act":
                    P.act(o_, i_, AF.Identity, [stg.r, prm.r], [slot.r], scale=pc(c0 + kc))
                else:
                    P.ts("dve", o_, i_, pc(c0 + kc), None, ALU.mult, None, [stg.r, prm.r], [slot.r])
        else:
            P.copy("act", slot.t[:, 0:1536], stg.t[:, 0:1536], [stg.r], [slot.r])
            P.copy("dve", slot.t[:, 1536:3072], stg.t[:, 1536:3072], [stg.r], [slot.r])
            P.copy("pool", slot.t[:, 3072:4096], stg.t[:, 3072:4096], [stg.r], [slot.r])
        if s + 3 < NP:
            pre_in(s + 3)
        P.dma(wsc_d[s], slot.t[:], [slot.r], [wsc_r[s]])

    wstate = {"issued": 0, "cur": 0}
    total_pieces = NP * (nt + 1)

    def w_issue_upto(n):
        while wstate["issued"] < min(n, total_pieces):
            g = wstate["issued"]
            s = g % NP
            slot = ring[g % NSLOT]
            P.dma(slot.t[:], wsc_d[s], [wsc_r[s]], [slot.r])
            wstate["issued"] += 1

    def next_piece(tag_prefix, lag=0):
        g = wstate["cur"]
        s = g % NP
        assert PIECES[s]["tag"][0] == tag_prefix, (PIECES[s]["tag"], tag_prefix)
        w_issue_upto(g + NSLOT - lag)
        wstate["cur"] += 1
        return ring[g % NSLOT]

    def run_jobs(jobs, maxact, fifo):
        done = set()
        pending = list(jobs)
        active = []
        while pending or active:
            for j in list(pending):
                if len(active) >= maxact:
                    break
                if all(d in done for d in j[2]):
                    active.append((j[0], j[1]()))
                    pending.remove(j)
                elif fifo:
                    break
            assert active, ("scheduler deadlock", [j[0] for j in pending][:5])
            for a in list(active):
                try:
                    next(a[1])
                except StopIteration:
                    active.remove(a)
                    done.add(a[0])

    def run_interleaved(gens):
        gens = list(gens)
        while gens:
            for g in list(gens):
                try:
                    next(g)
                except StopIteration:
                    gens.remove(g)

    def norm_A(tbs, src, nbl, jk, ssb, rsb):
        ntb = len(tbs)
        for tb, (t0, tn) in enumerate(tbs):
            P.act(jk.t[0:tn, :], src[tb].t[0:tn, :], AF.Square, [src[tb].r], [jk.r, ssb.r], accum_out=ssb.t[0:tn, tb:tb + 1])
        tn0 = tbs[0][1]
        P.act(rsb.t[0:tn0, 0:ntb], ssb.t[0:tn0, 0:ntb], AF.Ln, [ssb.r], [rsb.r], scale=1.0 / D, bias=eps_ap(tn0))
        P.act(rsb.t[0:tn0, 0:ntb], rsb.t[0:tn0, 0:ntb], AF.Exp, [rsb.r], [rsb.r], scale=-0.5)
        for tb, (t0, tn) in enumerate(tbs):
            nbb = nbl[tb % len(nbl)]
            P.ts("dve", nbb.t[0:tn, :], src[tb].t[0:tn, :], rsb.t[0:tn, tb:tb + 1], None, ALU.mult, None, [src[tb].r, rsb.r], [nbb.r])

    def norm_B(tbs, nbl):
        for tb, (t0, tn) in enumerate(tbs):
            nbb = nbl[tb % len(nbl)]
            pt, pr = PSB()
            for fc in range(8):
                P.tr(pt[:, fc * 128:fc * 128 + tn], nbb.t[0:tn, fc * 128:(fc + 1) * 128], identb.t[0:tn, 0:tn], [nbb.r, identb.r], [pr])
            P.copy(nxt(), nT.t[:, :, t0:t0 + tn], pt[:, :].rearrange("p (f t) -> p f t", f=8)[:, :, 0:tn], [pr], [nT.r])

    def norm_to_nT(T, tbs):
        ntb = len(tbs)
        for tb, (t0, tn) in enumerate(tbs):
            P.act(junk.t[0:tn, :], H[tb].t[0:tn, :], AF.Square, [H[tb].r], [junk.r, ss.r], accum_out=ss.t[0:tn, tb:tb + 1])
        tn0 = tbs[0][1]
        P.act(rstd.t[0:tn0, 0:ntb], ss.t[0:tn0, 0:ntb], AF.Ln, [ss.r], [rstd.r], scale=1.0 / D, bias=eps_ap(tn0))
        P.act(rstd.t[0:tn0, 0:ntb], rstd.t[0:tn0, 0:ntb], AF.Exp, [rstd.r], [rstd.r], scale=-0.5)
        for tb, (t0, tn) in enumerate(tbs):
            nbb = nb[tb % 2]
            P.ts("dve", nbb.t[0:tn, :], H[tb].t[0:tn, :], rstd.t[0:tn, tb:tb + 1], None, ALU.mult, None, [H[tb].r, rstd.r], [nbb.r])
            pt, pr = PSB()
            for fc in range(8):
                P.tr(pt[:, fc * 128:fc * 128 + tn], nbb.t[0:tn, fc * 128:(fc + 1) * 128], identb.t[0:tn, 0:tn], [nbb.r, identb.r], [pr])
            P.copy(nxt(), nT.t[:, :, t0:t0 + tn], pt[:, :].rearrange("p (f t) -> p f t", f=8)[:, :, 0:tn], [pr], [nT.r])

    def fm_group(slot, i, T, rhsbuf, rhs_r):
        pt, pr = PSF()
        for kc in range(8):
            P.mm(pt[:, 0:T], slot.t[:, kc * 512 + i * 128:kc * 512 + (i + 1) * 128], rhsbuf.t[:, kc, 0:T], kc == 0, kc == 7,
                 [slot.r, rhs_r], [pr])
        return pt, pr

    def tm_group(slot, lbuf, l_r, t0, tn):
        pt, pr = PSF()
        for kc in range(8):
            P.mm(pt[0:tn, :], lbuf.t[:, kc, t0:t0 + tn], slot.t[:, kc * 512:(kc + 1) * 512], kc == 0, kc == 7, [slot.r, l_r], [pr])
        return pt, pr

    def ffn(i, T, tbs, skip_norm=False, pre=None, mid=None, post=None):
        if not skip_norm:
            norm_to_nT(T, tbs)
        if pre is not None:
            pre()
        for g in range(NJ // 2):
            slot = next_piece("ffn_in")
            for jj in range(2):
                j = 2 * g + jj
                pg, pgr = fm_group(slot, 2 * jj, T, nT, nT.r)
                pu, pur = fm_group(slot, 2 * jj + 1, T, nT, nT.r)
                sl = sil[j % 2]
                P.act(sl.t[:, 0:T], pg[:, 0:T], AF.Silu, [pgr], [sl.r])
                P.tt("dve", actT.t[:, j, 0:T], sl.t[:, 0:T], pu[:, 0:T], ALU.mult, [sl.r, pur], [actT.r])
        if mid is not None:
            mid()
        for ch in range(2):
            pts = [PSF() for _ in tbs]
            for (j0, nj) in ((0, 8), (8, 8), (16, 6)):
                slot = next_piece("ffn_out")
                for jj in range(nj):
                    j = j0 + jj
                    for tb, (t0, tn) in enumerate(tbs):
                        P.mm(pts[tb][0][0:tn, :], actT.t[:, j, t0:t0 + tn], slot.t[:, jj * 512:(jj + 1) * 512], j == 0, j == NJ - 1,
                             [slot.r, actT.r], [pts[tb][1]])
            if ch == 1 and post is not None:
                post()
            for tb, (t0, tn) in enumerate(tbs):
                hh = H[tb].t[0:tn, ch * 512:(ch + 1) * 512]
                P.stt("dve", hh, pts[tb][0][0:tn, :], 0.5, hh, ALU.mult, ALU.add, [pts[tb][1], H[tb].r], [H[tb].r])

    def mixer(T, tbs, C, tok0):
        NCH = T // C
        nlev = {128: 5, 16: 3}[C]
        zs_c = CC_ZS128 if C == 128 else CC_ZS16
        cd_c = CC_CD128 if C == 128 else CC_CD16
        norm_to_nT(T, tbs)
        def ba_proj():
            pt, pr = yield from gPSF()
            for n in range(NCH):
                for kc in range(8):
                    P.mm(pt[0:C, n * 16:(n + 1) * 16], nT.t[:, kc, n * C:(n + 1) * C], wba.t[:, kc, :], kc == 0, kc == 7, [nT.r, wba.r], [pr])
            P.copy("act", braw.t[0:C, 0:NCH, :], pt[0:C, 0:NCH * 16].rearrange("p (n c) -> p n c", c=16), [pr], [braw.r])
            PFREE(pr)
            yield
            P.act(btok.t[0:C, 0:NCH, :], braw.t[0:C, 0:NCH, 0:8], AF.Exp, [braw.r], [btok.r], scale=-1.0)
            yield
            P.ts("dve", btok.t[0:C, 0:NCH, :], btok.t[0:C, 0:NCH, :], 1.0, None, ALU.add, None, [btok.r], [btok.r])
            yield
            P.op("dve", lambda e: e.reciprocal(out=btok.t[0:C, 0:NCH, :], in_=btok.t[0:C, 0:NCH, :]), [btok.r], [btok.r])
            yield
            P.tt("dve", batmp.t[0:C, 0:NCH, :], braw.t[0:C, 0:NCH, 8:16], prm.t[0:C, PC_DTB:PC_DTB + 8].unsqueeze(1).broadcast_to([C, NCH, 8]),
                 ALU.add, [braw.r, prm.r], [batmp.r])
            yield
            P.act(batmp.t[0:C, 0:NCH, :], batmp.t[0:C, 0:NCH, :], AF.Exp, [batmp.r], [batmp.r])
            yield
            P.act(batmp.t[0:C, 0:NCH, :], batmp.t[0:C, 0:NCH, :], AF.Ln, [batmp.r], [batmp.r], bias=one_ap(C))
            yield
            P.tt("dve", gtok.t[0:C, 0:NCH, :], batmp.t[0:C, 0:NCH, :], nA.t[0:C, :].unsqueeze(1).broadcast_to([C, NCH, 8]), ALU.mult,
                 [batmp.r, nA.r], [gtok.r])
            yield

        for hg in range(2):
            for tb, (t0, tn) in enumerate(tbs):
                P.dma(rope.t[0:tn, tb, :], rope_d[tok0 + t0:tok0 + t0 + tn, :], [], [rope.r])
            pj = []
            lag = 2
            slots = {}

            def get_slot(nm, first):
                if first:
                    slots[nm] = next_piece(nm, lag)
                return slots[nm]

            def job_conv(nm, dst, qi, i, cs, hg=hg):
                slot = get_slot(nm, i == 0)
                cch = qi * 8 + hg * 4 + i
                pt, pr = yield from gPSF()
                for kc in range(8):
                    P.mm(pt[:, 0:T], slot.t[:, kc * 512 + i * 128:kc * 512 + (i + 1) * 128], nT.t[:, kc, 0:T], kc == 0, kc == 7, [slot.r, nT.r], [pr])
                yield
                ci = cin[cs]
                P.copy("act", ci.t[:, 3:3 + T], pt[:, 0:T], [pr], [ci.r])
                PFREE(pr)
                P.copy("pool", ci.t[:, 0:3], ctail.t[:, cch, :], [ctail.r], [ci.r])
                yield
                P.copy("pool", ctail.t[:, cch, :], ci.t[:, T:T + 3], [ci.r], [ctail.r])
                ca = cacc[cs]
                wcol = PC_CONV + cch * 4
                P.act(ca.t[:, 0:T], ci.t[:, 0:T], AF.Identity, [ci.r, prm.r], [ca.r], scale=pc(wcol))
                yield
                for tap in range(1, 4):
                    P.stt("dve", ca.t[:, 0:T], ci.t[:, tap:tap + T], pc(wcol + tap), ca.t[:, 0:T], ALU.mult, ALU.add,
                          [ci.r, prm.r, ca.r], [ca.r])
                yield
                if nm == "v":
                    P.act(dst.t[:, i, 0:T], ca.t[:, 0:T], AF.Silu, [ca.r], [dst.r])
                    yield
                    return
                f = cf[cs]
                P.act(f.t[:, 0:T], ca.t[:, 0:T], AF.Silu, [ca.r], [f.r])
                yield
                sq = csq[cs]
                P.tt("pool", sq.t[:, 0:T], f.t[:, 0:T], f.t[:, 0:T], ALU.mult, [f.r], [sq.r])
                yield
                p2, p2r = yield from gPSF()
                P.mm(p2[:, 0:T], onesb.t[:, :], sq.t[:, 0:T], True, True, [onesb.r, sq.r], [p2r])
                yield
                rs_ = crs[cs]
                P.act(rs_.t[:, 0:T], p2[:, 0:T], AF.Ln, [p2r], [rs_.r], bias=eps_ap())
                PFREE(p2r)
                P.act(rs_.t[:, 0:T], rs_.t[:, 0:T], AF.Exp, [rs_.r], [rs_.r], scale=-0.5)
                yield
                if nm == "q":
                    P.stt("dve", dst.t[:, i, 0:T], f.t[:, 0:T], 128.0 ** -0.5, rs_.t[:, 0:T], ALU.mult, ALU.mult, [f.r, rs_.r], [dst.r])
                else:
                    P.tt("dve", dst.t[:, i, 0:T], f.t[:, 0:T], rs_.t[:, 0:T], ALU.mult, [f.r, rs_.r], [dst.r])
                yield

            def job_silu(nm, dst, i):
                slot = get_slot(nm, i == 0)
                pt, pr = yield from gPSF()
                for kc in range(8):
                    P.mm(pt[:, 0:T], slot.t[:, kc * 512 + i * 128:kc * 512 + (i + 1) * 128], nT.t[:, kc, 0:T], kc == 0, kc == 7, [slot.r, nT.r], [pr])
                yield
                P.act(dst.t[:, i, 0:T], pt[:, 0:T], AF.Silu, [pr], [dst.r])
                PFREE(pr)
                yield

            def job_rot(nm, dst, tb, t0, tn, rs):
                slot = get_slot(nm, tb == 0)
                pt, pr = yield from gPSF()
                for kc in range(8):
                    P.mm(pt[0:tn, :], nT.t[:, kc, t0:t0 + tn], slot.t[:, kc * 512:(kc + 1) * 512], kc == 0, kc == 7, [slot.r, nT.r], [pr])
                yield
                xs = rxs[rs]
                P.copy("act", xs.t[0:tn, :], pt[0:tn, :], [pr], [xs.r])
                PFREE(pr)
                yield
                xv = xs.t[0:tn, :].rearrange("p (h j two) -> p h j two", h=4, two=2)
                x0 = xv[:, :, :, 0]
                x1 = xv[:, :, :, 1]
                cosb = rope.t[0:tn, tb, 0:64].unsqueeze(1).broadcast_to([tn, 4, 64])
                sinb = rope.t[0:tn, tb, 64:128].unsqueeze(1).broadcast_to([tn, 4, 64])
                ta, tb_ = rt[rs]
                t1 = ta.t[0:tn, :].rearrange("p (h j) -> p h j", h=4)
                t2 = tb_.t[0:tn, :].rearrange("p (h j) -> p h j", h=4)
                ov = dst.t[0:tn, tb, :].rearrange("p (h j two) -> p h j two", h=4, two=2)
                P.tt("dve", t1, x0, cosb, ALU.mult, [xs.r, rope.r], [ta.r])
                P.tt("pool", t2, x1, sinb, ALU.mult, [xs.r, rope.r], [tb_.r])
                yield
                P.tt("dve", ov[:, :, :, 0], t1, t2, ALU.subtract, [ta.r, tb_.r], [dst.r])
                yield
                P.tt("dve", t1, x1, cosb, ALU.mult, [xs.r, rope.r], [ta.r])
                P.tt("pool", t2, x0, sinb, ALU.mult, [xs.r, rope.r], [tb_.r])
                yield
                P.tt("pool", ov[:, :, :, 1], t1, t2, ALU.add, [ta.r, tb_.r], [dst.r])
                yield

            def job_rv(tb, t0, tn):
                slot = get_slot("rv", tb == 0)
                pt, pr = yield from gPSF()
                for kc in range(8):
                    P.mm(pt[0:tn, :], nT.t[:, kc, t0:t0 + tn], slot.t[:, kc * 512:(kc + 1) * 512], kc == 0, kc == 7, [slot.r, nT.r], [pr])
                yield
                P.copy("act", rv_tm.t[0:tn, tb, :], pt[0:tn, :], [pr], [rv_tm.r])
                PFREE(pr)
                yield

            def job_tr(src, dst, scl, tb, t0, tn, hg=hg):
                if src is rk_tm:
                    P.tt("dve", rkd_tm.t[0:tn, tb, :].rearrange("p (h d) -> p h d", h=4),
                         rk_tm.t[0:tn, tb, :].rearrange("p (h d) -> p h d", h=4),
                         cst.t[0:tn, zs_c + hg * 4:zs_c + hg * 4 + 4].unsqueeze(2).broadcast_to([tn, 4, 128]), ALU.mult,
                         [rk_tm.r, cst.r], [rkd_tm.r])
                pt, pr = yield from gPSB()
                for i in range(4):
                    P.tr(pt[:, i * 128:i * 128 + tn], src.t[0:tn, tb, i * 128:(i + 1) * 128], identb.t[0:tn, 0:tn], [src.r, identb.r], [pr])
                yield
                pv = pt[:, 0:512].rearrange("p (h t) -> p h t", h=4)[:, :, 0:tn]
                if scl is None:
                    P.copy("dve", dst.t[:, :, t0:t0 + tn], pv, [pr], [dst.r])
                else:
                    P.act(dst.t[:, :, t0:t0 + tn], pv, AF.Identity, [pr], [dst.r], scale=scl)
                PFREE(pr)
                yield

            cnt = 0
            if hg == 0:
                pj.append(("ba", ba_proj, []))
            for qi, (nm, dst) in enumerate((("q", qT), ("k", kT), ("v", vT))):
                for i in range(4):
                    deps = ["conv%d" % (cnt - NCS)] if cnt >= NCS else []
                    pj.append(("conv%d" % cnt, (lambda nm=nm, dst=dst, qi=qi, i=i, cs=cnt % NCS: job_conv(nm, dst, qi, i, cs)), deps))
                    cnt += 1
            for nm, dst in (("z", szT), ("rg", srgT)):
                for i in range(4):
                    pj.append(("silu_%s%d" % (nm, i), (lambda nm=nm, dst=dst, i=i: job_silu(nm, dst, i)), []))
            rcnt = 0
            for nm, dst in (("rq", rq_tm), ("rk", rk_tm)):
                for tb, (t0, tn) in enumerate(tbs):
                    deps = ["rot%d" % (rcnt - 2)] if rcnt >= 2 else []
                    pj.append(("rot%d" % rcnt, (lambda nm=nm, dst=dst, tb=tb, t0=t0, tn=tn, rs=rcnt % 2: job_rot(nm, dst, tb, t0, tn, rs)), deps))
                    rcnt += 1
            for tb, (t0, tn) in enumerate(tbs):
                pj.append(("rv%d" % tb, (lambda tb=tb, t0=t0, tn=tn: job_rv(tb, t0, tn)), []))
            ntb_ = len(tbs)
            for si, (src, dst, scl) in enumerate(((rq_tm, rqT, None), (rk_tm, rkT, 128.0 ** -0.5))):
                for tb, (t0, tn) in enumerate(tbs):
                    pj.append(("tr%d_%d" % (si, tb), (lambda src=src, dst=dst, scl=scl, tb=tb, t0=t0, tn=tn: job_tr(src, dst, scl, tb, t0, tn)),
                               ["rot%d" % (si * ntb_ + tb)]))
            run_jobs(pj, 5 if ntb_ > 1 else 1, True)
            for i in range(4):
                h = hg * 4 + i
                P.tt("pool", rqdT.t[:, i, 0:T].rearrange("p (n c) -> p n c", c=C), rqT.t[:, i, 0:T].rearrange("p (n c) -> p n c", c=C),
                     cst.t[:, CC_XI + h * 128:CC_XI + h * 128 + C].unsqueeze(1).broadcast_to([128, NCH, C]), ALU.mult, [rqT.r, cst.r], [rqdT.r])

            hs = slice(hg * 4, hg * 4 + 4)

            def h3(ap, w):
                return ap.rearrange("p (h c) -> p h c", h=4)

            def gen_prep(n, sub, hg=hg):
                K = cksub[sub][n % 2]
                c0 = n * C
                li = (2 * sub, 2 * sub + 1)
                hcs = slice(hg * 4 + 2 * sub, hg * 4 + 2 * sub + 2)

                def bc2(ap2):
                    return ap2.unsqueeze(2).broadcast_to([C, 2, C])

                def bcd(ap2):
                    return ap2.unsqueeze(2).broadcast_to([C, 2, 128])

                def v3(b):
                    return b.t[0:C, :, 0:C]

                def g3(ap):
                    return ap.rearrange("p (h c) -> p h c", h=2)

                def tab(c0_):
                    return cst.t[0:C, c0_:c0_ + C].unsqueeze(1).broadcast_to([C, 2, C])

                Ub, MUI, MLS, MLO = tab(CC_U), tab(CC_MUI), tab(CC_MLS), tab(CC_MLO)
                Idb = identb.t[0:C, 0:C].unsqueeze(1).broadcast_to([C, 2, C])
                two = (C == 128)
                P.tt("dve", v3(K["GU"]), Ub, bc2(gtok.t[0:C, n, hcs]), ALU.mult, [cst.r, gtok.r], [K["GU"].r])
                (pG, pGr), (pS, pSr) = yield from gPSF(2)
                for ii in range(2):
                    P.mm(pG[:, ii * C:(ii + 1) * C], cst.t[0:C, CC_ONES:CC_ONES + 128], K["GU"].t[0:C, ii, 0:C], True, True, [cst.r, K["GU"].r], [pGr])
                P.mm(pS[0:C, 0:2], cst.t[0:C, CC_U:CC_U + C], gtok.t[0:C, n, hcs], True, True, [cst.r, gtok.r], [pSr])
                P.mm(pS[:, 2:4], cst.t[0:C, CC_ONES:CC_ONES + 128], gtok.t[0:C, n, hcs], True, True, [cst.r, gtok.r], [pSr])
                yield
                gcs = K["gcs"]
                P.copy("act", gcs.t[:, 2:4], pS[:, 2:4], [pSr], [gcs.r])
                P.copy("act", gcs.t[0:C, 0:2], pS[0:C, 0:2], [pSr], [gcs.r])
                PFREE(pSr)
                P.act(K["EGQ"].t[:, :, 0:C], g3(pG[:, 0:2 * C]), AF.Exp, [pGr], [K["EGQ"].r])
                yield
                P.tt("dve", v3(K["Dm"]), g3(pG[0:C, 0:2 * C]), bc2(gcs.t[0:C, 0:2]), ALU.subtract, [pGr, gcs.r], [K["Dm"].r])
                PFREE(pGr)
                P.act(K["eg"].t[0:C, :], gcs.t[0:C, 0:2], AF.Exp, [gcs.r], [K["eg"].r])
                P.tt("pool", K["ekd"].t[0:C, :], gcs.t[0:C, 2:4], gcs.t[0:C, 0:2], ALU.subtract, [gcs.r], [K["ekd"].r])
                P.act(K["egl"].t[:, :], gcs.t[:, 2:4], AF.Exp, [gcs.r], [K["egl"].r])
                P.tt("pool", K["qdT"].t[:, :, 0:C], qT.t[:, li[0]:li[1] + 1, c0:c0 + C], K["EGQ"].t[:, :, 0:C], ALU.mult, [qT.r, K["EGQ"].r], [K["qdT"].r])
                yield
                P.ts("dve", v3(K["El"]), v3(K["Dm"]), 0.0, None, ALU.max, None, [K["Dm"].r], [K["El"].r])
                P.ts("dve", v3(K["Eu"]), v3(K["Dm"]), 0.0, None, ALU.min, None, [K["Dm"].r], [K["Eu"].r])
                P.act(K["ekd"].t[0:C, :], K["ekd"].t[0:C, :], AF.Exp, [K["ekd"].r], [K["ekd"].r])
                P.tt("pool", K["beg"].t[0:C, :], K["eg"].t[0:C, :], btok.t[0:C, n, hcs], ALU.mult, [K["eg"].r, btok.r], [K["beg"].r])
                ptkv, ptkvr = yield from gPSB()
                for ii in range(2):
                    P.tr(ptkv[0:C, ii * 128:(ii + 1) * 128], kT.t[:, li[ii], c0:c0 + C], identb.t[:, :], [kT.r, identb.r], [ptkvr])
                for ii in range(2):
                    P.tr(ptkv[0:C, 256 + ii * 128:256 + (ii + 1) * 128], vT.t[:, li[ii], c0:c0 + C], identb.t[:, :], [vT.r, identb.r], [ptkvr])
                yield
                P.act(v3(K["El"]), v3(K["El"]), AF.Exp, [K["El"].r], [K["El"].r], scale=-1.0)
                P.act(v3(K["Eu"]), v3(K["Eu"]), AF.Exp, [K["Eu"].r], [K["Eu"].r])
                ktv = ptkv[0:C, 0:256].rearrange("p (h d) -> p h d", h=2)
                vtv = ptkv[0:C, 256:512].rearrange("p (h d) -> p h d", h=2)
                P.tt("dve", K["kd"].t[0:C, :, :], ktv, bcd(K["ekd"].t[0:C, :]), ALU.mult, [ptkvr, K["ekd"].r], [K["kd"].r])
                P.tt("dve", K["kbg"].t[0:C, :, :], ktv, bcd(K["beg"].t[0:C, :]), ALU.mult, [ptkvr, K["beg"].r], [K["kbg"].r])
                P.tt("dve", K["bv"].t[0:C, :, :], vtv, bcd(btok.t[0:C, n, hcs]), ALU.mult, [ptkvr, btok.r], [K["bv"].r])
                PFREE(ptkvr)
                (pK, pKr), (pQ, pQr) = yield from gPSF(2)
                for ii in range(2):
                    P.mm(pQ[0:C, ii * C:(ii + 1) * C], kT.t[:, li[ii], c0:c0 + C], qT.t[:, li[ii], c0:c0 + C], True, True, [kT.r, qT.r], [pQr])
                for ii in range(2):
                    P.mm(pK[0:C, ii * C:(ii + 1) * C], kT.t[:, li[ii], c0:c0 + C], kT.t[:, li[ii], c0:c0 + C], True, True, [kT.r], [pKr])
                yield
                P.tt("pool", v3(K["EQ"]), v3(K["Eu"]), MUI, ALU.mult, [K["Eu"].r, cst.r], [K["EQ"].r])
                P.tt("pool", v3(K["El"]), v3(K["El"]), bc2(btok.t[0:C, n, hcs]), ALU.mult, [K["El"].r, btok.r], [K["El"].r])
                yield
                P.tt("dve", v3(K["PT"]), g3(pQ[0:C, 0:2 * C]), v3(K["EQ"]), ALU.mult, [pQr, K["EQ"].r], [K["PT"].r])
                PFREE(pQr)
                if two:
                    P.tt("pool", v3(K["EAo"]), v3(K["El"]), MLO, ALU.mult, [K["El"].r, cst.r], [K["EAo"].r])
                P.tt("pool", v3(K["EA"]), v3(K["El"]), MLS, ALU.mult, [K["El"].r, cst.r], [K["EA"].r])
                yield
                P.tt("dve", v3(K["Ain"]), g3(pK[0:C, 0:2 * C]), v3(K["EA"]), ALU.mult, [pKr, K["EA"].r], [K["Ain"].r])
                if two:
                    P.tt("dve", v3(K["Ao"]), g3(pK[0:C, 0:2 * C]), v3(K["EAo"]), ALU.mult, [pKr, K["EAo"].r], [K["Ao"].r])
                PFREE(pKr)
                yield

            def gen_prepB(n, sub, hg=hg):
                K = cksub[sub][n % 2]

                def v3(b):
                    return b.t[0:C, :, 0:C]

                def g3(ap):
                    return ap.rearrange("p (h c) -> p h c", h=2)

                Idb = identb.t[0:C, 0:C].unsqueeze(1).broadcast_to([C, 2, C])
                two = (C == 128)
                ptb, ptbr = yield from gPSB()
                for ii in range(2):
                    P.tr(ptb[0:C, ii * C:(ii + 1) * C], K["Ain"].t[0:C, ii, 0:C], identb.t[0:C, 0:C], [K["Ain"].r, identb.r], [ptbr])
                yield
                def bxv(buf, j):
                    return buf.t[0:C, :, j * 128:j * 128 + C]

                def pbv(ps_, j):
                    return ps_[0:C, 0:512].rearrange("p (h j c) -> p h j c", h=2, j=2)[:, :, j, 0:C]

                P.copy("act", bxv(K["BX0"], 0), g3(ptb[0:C, 0:2 * C]), [ptbr], [K["BX0"].r])
                P.tt("dve", bxv(K["BX1"], 1), Idb, g3(ptb[0:C, 0:2 * C]), ALU.subtract, [identb.r, ptbr], [K["BX1"].r])
                PFREE(ptbr)
                yield
                for k in range(nlev + 1):
                    Ak = K["Ain"] if k == 0 else K["A%d" % (k % 2)]
                    An = K["A%d" % ((k + 1) % 2)]
                    cur, nxb = K["BX%d" % (k % 2)], K["BX%d" % ((k + 1) % 2)]
                    need_A, need_B, need_X = (k < nlev), (k < nlev - 1), (k >= 1)
                    need = (1 if need_A else 0) + 1
                    got = yield from gPSF(need)
                    if need == 1:
                        got = [got]
                    got = list(got)
                    pA = None
                    if need_A:
                        pA, pAr = got.pop(0)
                        for ii in range(2):
                            P.mm(pA[0:C, ii * C:(ii + 1) * C], cur.t[0:C, ii, 0:C], Ak.t[0:C, ii, 0:C], True, True, [cur.r, Ak.r], [pAr])
                    pBX, pBXr = got.pop(0)
                    for ii in range(2):
                        if need_B and need_X and C == 128:
                            P.mm(pBX[0:C, ii * 256:(ii + 1) * 256], Ak.t[0:C, ii, 0:C], cur.t[0:C, ii, 0:256], True, True, [cur.r, Ak.r], [pBXr])
                        else:
                            if need_B:
                                P.mm(pBX[0:C, ii * 256:ii * 256 + C], Ak.t[0:C, ii, 0:C], cur.t[0:C, ii, 0:C], True, True, [cur.r, Ak.r], [pBXr])
                            if need_X:
                                P.mm(pBX[0:C, ii * 256 + 128:ii * 256 + 128 + C], Ak.t[0:C, ii, 0:C], cur.t[0:C, ii, 128:128 + C], True, True,
                                     [cur.r, Ak.r], [pBXr])
                    yield
                    if pA is not None:
                        P.copy("act", v3(An), g3(pA[0:C, 0:2 * C]), [pAr], [An.r])
                        PFREE(pAr)
                    if need_B:
                        P.copy("act", bxv(nxb, 0), pbv(pBX, 0), [pBXr], [nxb.r])
                    if need_X:
                        P.tt("dve", bxv(nxb, 1), pbv(pBX, 1), bxv(cur, 1), ALU.add, [pBXr, cur.r], [nxb.r])
                    PFREE(pBXr)
                    yield
                TTb = K["BX%d" % ((nlev + 1) % 2)]

                class _TT:
                    r = TTb.r

                    class t:
                        pass
                TT = _TT()
                if two:
                    (pY, pYr), (pZ, pZr) = yield from gPSF(2)
                    for ii in range(2):
                        P.mm(pY[0:C, ii * C:(ii + 1) * C], K["Ao"].t[0:C, ii, 0:C], TTb.t[0:C, ii, 128:128 + C], True, True, [K["Ao"].r, TT.r], [pYr])
                    for ii in range(2):
                        P.mm(pZ[0:C, ii * 128:(ii + 1) * 128], TTb.t[0:C, ii, 128:128 + C], K["bv"].t[0:C, ii, :], True, True, [TT.r, K["bv"].r], [pZr])
                    for ii in range(2):
                        P.mm(pZ[0:C, 256 + ii * 128:256 + (ii + 1) * 128], TTb.t[0:C, ii, 128:128 + C], K["kbg"].t[0:C, ii, :], True, True, [TT.r, K["kbg"].r], [pZr])
                    yield
                    P.tt("dve", v3(K["Y"]), Idb, g3(pY[0:C, 0:2 * C]), ALU.subtract, [identb.r, pYr], [K["Y"].r])
                    PFREE(pYr)
                    P.copy("act", K["sz"].t[0:C, :, :], pZ[0:C, 0:512].rearrange("p (h d) -> p h d", h=4), [pZr], [K["sz"].r])
                    PFREE(pZr)
                    yield
                    (pU, pUr), (pW, pWr) = yield from gPSF(2)
                    for ii in range(2):
                        P.mm(pU[0:C, ii * 128:(ii + 1) * 128], K["Y"].t[0:C, ii, 0:C], K["sz"].t[0:C, ii, :], True, True, [K["Y"].r, K["sz"].r], [pUr])
                    for ii in range(2):
                        P.mm(pW[:, ii * C:(ii + 1) * C], K["sz"].t[0:C, 2 + ii, :], K["Y"].t[0:C, ii, 0:C], True, True, [K["Y"].r, K["sz"].r], [pWr])
                else:
                    (pU, pUr), (pW, pWr) = yield from gPSF(2)
                    for ii in range(2):
                        P.mm(pU[0:C, ii * 128:(ii + 1) * 128], TTb.t[0:C, ii, 128:128 + C], K["bv"].t[0:C, ii, :], True, True, [TT.r, K["bv"].r], [pUr])
                    for ii in range(2):
                        P.mm(pW[:, ii * C:(ii + 1) * C], K["kbg"].t[0:C, ii, :], TTb.t[0:C, ii, 128:128 + C], True, True, [TT.r, K["kbg"].r], [pWr])
                yield
                P.copy("act", K["u"].t[0:C, :, :], pU[0:C, 0:256].rearrange("p (h d) -> p h d", h=2), [pUr], [K["u"].r])
                PFREE(pUr)
                P.copy("dve", K["wT"].t[:, :, 0:C], g3(pW[:, 0:2 * C]), [pWr], [K["wT"].r])
                PFREE(pWr)
                yield

            def gen_scan(n, sub, hg=hg):
                K = cksub[sub][n % 2]
                c0 = n * C
                li = (2 * sub, 2 * sub + 1)
                hgl = (hg * 4 + 2 * sub, hg * 4 + 2 * sub + 1)

                def bcd(ap2):
                    return ap2.unsqueeze(2).broadcast_to([C, 2, 128])

                def g3(ap):
                    return ap.rearrange("p (h c) -> p h c", h=2)

                p1, p1r = yield from gPSF()
                for ii in range(2):
                    h = hgl[ii]
                    P.mm(p1[0:C, ii * 128:(ii + 1) * 128], K["wT"].t[:, ii, 0:C], Sgb[h].t[:, :], True, True, [K["wT"].r, Sgb[h].r], [p1r])
                yield
                P.tt("dve", K["vnew"].t[0:C, :, :], K["u"].t[0:C, :, :], p1[0:C, 0:256].rearrange("p (h d) -> p h d", h=2), ALU.subtract,
                     [K["u"].r, p1r], [K["vnew"].r])
                PFREE(p1r)
                yield
                (pO, pOr), (pSS, pSSr) = yield from gPSF(2)
                for ii in range(2):
                    P.mm(pSS[:, ii * 128:(ii + 1) * 128], K["kd"].t[0:C, ii, :], K["vnew"].t[0:C, ii, :], True, True, [K["kd"].r, K["vnew"].r], [pSSr])
                for ii in range(2):
                    h = hgl[ii]
                    P.mm(pO[0:C, ii * 128:(ii + 1) * 128], K["qdT"].t[:, ii, 0:C], Sgb[h].t[:, :], True, False, [K["qdT"].r, Sgb[h].r], [pOr])
                    P.mm(pO[0:C, ii * 128:(ii + 1) * 128], K["PT"].t[0:C, ii, 0:C], K["vnew"].t[0:C, ii, :], False, True, [K["PT"].r, K["vnew"].r], [pOr])
                yield
                for ii in range(2):
                    h = hgl[ii]
                    P.stt("dve", Sg[h].t[:, :], Sg[h].t[:, :], K["egl"].t[:, ii:ii + 1], pSS[:, ii * 128:(ii + 1) * 128], ALU.mult, ALU.add,
                          [Sg[h].r, K["egl"].r, pSSr], [Sg[h].r])
                PFREE(pSSr)
                P.copy("act", K["osb"].t[0:C, :, :], pO[0:C, 0:256].rearrange("p (h d) -> p h d", h=2), [pOr], [K["osb"].r])
                PFREE(pOr)
                yield
                for ii in range(2):
                    h = hgl[ii]
                    P.copy("act" if ii == 0 else "pool", Sgb[h].t[:, :], Sg[h].t[:, :], [Sg[h].r], [Sgb[h].r])
                P.tt("pool", K["osq"].t[0:C, :, :], K["osb"].t[0:C, :, :], K["osb"].t[0:C, :, :], ALU.mult, [K["osb"].r], [K["osq"].r])
                yield
                P.red("dve", K["ost"].t[0:C, 0:2], K["osq"].t[0:C, :, :], [K["osq"].r], [K["ost"].r])
                yield
                P.act(K["ost"].t[0:C, 2:4], K["ost"].t[0:C, 0:2], AF.Ln, [K["ost"].r], [K["ost"].r], scale=1.0 / 128, bias=eps_ap(C))
                P.act(K["ost"].t[0:C, 2:4], K["ost"].t[0:C, 2:4], AF.Exp, [K["ost"].r], [K["ost"].r], scale=-0.5)
                yield
                P.tt("pool", K["on"].t[0:C, :, :], K["osb"].t[0:C, :, :], bcd(K["ost"].t[0:C, 2:4]), ALU.mult, [K["osb"].r, K["ost"].r], [K["on"].r])
                yield
                pt, pr = yield from gPSB()
                for ii in range(2):
                    P.tr(pt[:, ii * C:(ii + 1) * C], K["on"].t[0:C, ii, :], identb.t[0:C, 0:C], [K["on"].r, identb.r], [pr])
                yield
                P.stt("dve", yaT.t[:, hgl[0]:hgl[1] + 1, c0:c0 + C], g3(pt[:, 0:2 * C]), pc(PC_GDNN), szT.t[:, li[0]:li[1] + 1, c0:c0 + C],
                      ALU.mult, ALU.mult, [pr, prm.r, szT.r], [yaT.r])
                PFREE(pr)
                yield

            def gen_ret(n, sub, hg=hg):
                K = cksub[sub][n % 2]
                c0 = n * C
                tb = c0 // 128
                li = (2 * sub, 2 * sub + 1)
                hgl = (hg * 4 + 2 * sub, hg * 4 + 2 * sub + 1)

                def bcd(ap2):
                    return ap2.unsqueeze(2).broadcast_to([C, 2, 128])

                def g3(ap):
                    return ap.rearrange("p (h c) -> p h c", h=2)

                pSc, pScr = yield from gPSF()
                for ii in range(2):
                    i = li[ii]
                    P.mm(pSc[0:C, ii * C:(ii + 1) * C], rkT.t[:, i, c0:c0 + C], rqT.t[:, i, c0:c0 + C], True, True, [rkT.r, rqT.r], [pScr])
                yield
                P.tt("dve", K["scT"].t[0:C, :, 0:C], g3(pSc[0:C, 0:2 * C]),
                     cst.t[0:C, CC_DRT + hgl[0] * 128:CC_DRT + hgl[0] * 128 + 256].rearrange("p (h c) -> p h c", h=2)[:, :, 0:C], ALU.mult,
                     [pScr, cst.r], [K["scT"].r])
                PFREE(pScr)
                yield
                (pOr2, pOr2r), (pSR, pSRr) = yield from gPSF(2)
                for ii in range(2):
                    i = li[ii]
                    h = hgl[ii]
                    P.mm(pOr2[0:C, ii * 128:(ii + 1) * 128], rqdT.t[:, i, c0:c0 + C], Srb[h].t[:, :], True, False, [rqdT.r, Srb[h].r], [pOr2r])
                    P.mm(pOr2[0:C, ii * 128:(ii + 1) * 128], K["scT"].t[0:C, ii, 0:C], rv_tm.t[0:C, tb, i * 128:(i + 1) * 128], False, True,
                         [K["scT"].r, rv_tm.r], [pOr2r])
                for ii in range(2):
                    i = li[ii]
                    P.mm(pSR[:, ii * 128:(ii + 1) * 128], rkd_tm.t[0:C, tb, i * 128:(i + 1) * 128], rv_tm.t[0:C, tb, i * 128:(i + 1) * 128], True, True,
                         [rkd_tm.r, rv_tm.r], [pSRr])
                yield
                for ii in range(2):
                    h = hgl[ii]
                    P.stt("dve", Sr[h].t[:, :], Sr[h].t[:, :], cst.t[:, cd_c + h:cd_c + h + 1], pSR[:, ii * 128:(ii + 1) * 128], ALU.mult, ALU.add,
                          [Sr[h].r, cst.r, pSRr], [Sr[h].r])
                PFREE(pSRr)
                P.copy("act", K["orb"].t[0:C, :, :], pOr2[0:C, 0:256].rearrange("p (h d) -> p h d", h=2), [pOr2r], [K["orb"].r])
                PFREE(pOr2r)
                yield
                for ii in range(2):
                    h = hgl[ii]
                    P.copy("act" if ii == 1 else "pool", Srb[h].t[:, :], Sr[h].t[:, :], [Sr[h].r], [Srb[h].r])
                P.red("dve", K["ort"].t[0:C, 0:2], K["orb"].t[0:C, :, :], [K["orb"].r], [K["ort"].r])
                yield
                P.ts("dve", K["ort"].t[0:C, 0:2], K["ort"].t[0:C, 0:2], 1.0 / 128, None, ALU.mult, None, [K["ort"].r], [K["ort"].r])
                yield
                P.tt("pool", K["orc"].t[0:C, :, :], K["orb"].t[0:C, :, :], bcd(K["ort"].t[0:C, 0:2]), ALU.subtract, [K["orb"].r, K["ort"].r], [K["orc"].r])
                yield
                P.tt("pool", K["orq"].t[0:C, :, :], K["orc"].t[0:C, :, :], K["orc"].t[0:C, :, :], ALU.mult, [K["orc"].r], [K["orq"].r])
                yield
                P.red("dve", K["ort"].t[0:C, 2:4], K["orq"].t[0:C, :, :], [K["orq"].r], [K["ort"].r])
                yield
                P.act(K["ort"].t[0:C, 2:4], K["ort"].t[0:C, 2:4], AF.Ln, [K["ort"].r], [K["ort"].r], scale=1.0 / 128, bias=eps_ap(C))
                P.act(K["ort"].t[0:C, 2:4], K["ort"].t[0:C, 2:4], AF.Exp, [K["ort"].r], [K["ort"].r], scale=-0.5)
                yield
                P.tt("pool", K["orn"].t[0:C, :, :], K["orc"].t[0:C, :, :], bcd(K["ort"].t[0:C, 2:4]), ALU.mult, [K["orc"].r, K["ort"].r], [K["orn"].r])
                yield
                pt, pr = yield from gPSB()
                for ii in range(2):
                    P.tr(pt[:, ii * C:(ii + 1) * C], K["orn"].t[0:C, ii, :], identb.t[0:C, 0:C], [K["orn"].r, identb.r], [pr])
                yield
                P.tt("dve", K["ybt"].t[:, :, 0:C], g3(pt[:, 0:2 * C]),
                     prm.t[:, PC_RETN + hgl[0]:PC_RETN + hgl[0] + 2].unsqueeze(2).broadcast_to([128, 2, C]), ALU.mult, [pr, prm.r], [K["ybt"].r])
                PFREE(pr)
                yield
                P.tt("pool", ybT.t[:, hgl[0]:hgl[1] + 1, c0:c0 + C], K["ybt"].t[:, :, 0:C], srgT.t[:, li[0]:li[1] + 1, c0:c0 + C], ALU.mult,
                     [K["ybt"].r, srgT.r], [ybT.r])
                yield

            cj = []
            for n in range(NCH):
                for sub in range(2):
                    d = []
                    if n >= 2:
                        d.append("prep%d_%d" % (n - 2, sub))
                        d.append("scan%d_%d" % (n - 2, sub))
                        d.append("prepB%d_%d" % (n - 2, sub))
                    cj.append(("prep%d_%d" % (n, sub), (lambda n=n, sub=sub: gen_prep(n, sub)), d))
                for sub in range(2):
                    d = ["prep%d_%d" % (n, sub)]
                    if n >= 1:
                        d.append("prepB%d_%d" % (n - 1, sub))
                    if n >= 2:
                        d.append("scan%d_%d" % (n - 2, sub))
                    cj.append(("prepB%d_%d" % (n, sub), (lambda n=n, sub=sub: gen_prepB(n, sub)), d))
                for sub in range(2):
                    d = ["ret%d_%d" % (n - 1, sub), "prepB%d_%d" % (n - 1, sub)] if n >= 1 else []
                    cj.append(("ret%d_%d" % (n, sub), (lambda n=n, sub=sub: gen_ret(n, sub)), d))
                for sub in range(2):
                    d = ["prepB%d_%d" % (n, sub)]
                    if n >= 1:
                        d.append("scan%d_%d" % (n - 1, sub))
                    cj.append(("scan%d_%d" % (n, sub), (lambda n=n, sub=sub: gen_scan(n, sub)), d))
            run_jobs(cj, 12, False)

        for p in range(2):
            slot = next_piece("ga")
            for i in range(4):
                pt, pr = fm_group(slot, i, T, nT, nT.r)
                P.act(sgT.t[:, i, 0:T], pt[:, 0:T], AF.Sigmoid, [pr], [sgT.r])
            slot = next_piece("gb")
            for i in range(4):
                pt, pr = fm_group(slot, i, T, nT, nT.r)
                P.act(sbT.t[:, i, 0:T], pt[:, 0:T], AF.Sigmoid, [pr], [sbT.r])
            slot = next_piece("brg")
            for i in range(4):
                pt, pr = fm_group(slot, i, T, yaT, yaT.r)
                P.tt("dve", mtmp[i].t[:, 0:T], pt[:, 0:T], sgT.t[:, i, 0:T], ALU.mult, [pr, sgT.r], [mtmp[i].r])
            slot = next_piece("brr")
            for i in range(4):
                pt, pr = fm_group(slot, i, T, ybT, ybT.r)
                m2 = mtmp2[i % 2]
                P.tt("dve", m2.t[:, 0:T], pt[:, 0:T], sbT.t[:, i, 0:T], ALU.mult, [pr, sbT.r], [m2.r])
                P.tt("pool", mergedT.t[:, 4 * p + i, 0:T], m2.t[:, 0:T], mtmp[i].t[:, 0:T], ALU.add, [m2.r, mtmp[i].r], [mergedT.r])
        for ch in range(2):
            slot = next_piece("wo")
            for tb, (t0, tn) in enumerate(tbs):
                pt, pr = tm_group(slot, mergedT, mergedT.r, t0, tn)
                hh = H[tb].t[0:tn, ch * 512:(ch + 1) * 512]
                P.tt("dve", hh, pt[0:tn, :], hh, ALU.add, [pr, H[tb].r], [H[tb].r])

    def final_out(t):
        tbs = [(tb * 128, 128) for tb in range(4)]
        fins = []
        P.dma(fnw.t[:], fnw_d[:, :], [], [fnw.r])
        for tb in range(4):
            P.act(junk.t[:, :], H[tb].t[:, :], AF.Square, [H[tb].r], [junk.r, ss.r], accum_out=ss.t[:, tb:tb + 1])
        P.act(rstd.t[:, 0:4], ss.t[:, 0:4], AF.Ln, [ss.r], [rstd.r], scale=1.0 / D, bias=eps_ap())
        P.act(rstd.t[:, 0:4], rstd.t[:, 0:4], AF.Exp, [rstd.r], [rstd.r], scale=-0.5)
        for tb in range(4):
            P.stt("dve", OUTB[tb].t[:, :], H[tb].t[:, :], rstd.t[:, tb:tb + 1], fnw.t[:, :], ALU.mult, ALU.mult,
                  [H[tb].r, rstd.r, fnw.r], [OUTB[tb].r])
            r0 = t * 512 + tb * 128
            fins.append(P.dma(out_d[r0:r0 + 128, :], OUTB[tb].t[:, :], [OUTB[tb].r], []))
        return fins

    fin_ops = []
    tbs_m = [(0, NMETA)]
    tbs = [(tb * 128, 128) for tb in range(4)]

    def prefetch_hooks(tn_):
        def pre():
            for tb in range(4):
                r0 = tn_ * 512 + tb * 128
                P.dma(XN[tb].t[:, :], x_d[r0:r0 + 128, :], [], [XN[tb].r])

        def mid():
            norm_A(tbs, XN, nbx, junk2, ss2, rstd2)

        def post():
            norm_B(tbs, nbx)
        return pre, mid, post

    P.dma(H[0].t[0:NMETA, :], meta_d[:, :], [], [H[0].r])
    ffn(1, NMETA, tbs_m)
    mixer(NMETA, tbs_m, 16, 0)
    pre, mid, post = prefetch_hooks(0)
    ffn(2, NMETA, tbs_m, pre=pre, mid=mid, post=post)
    for t in range(nt):
        for tb in range(4):
            P.copy("pool", H[tb].t[:, :], XN[tb].t[:, :], [XN[tb].r], [H[tb].r])
        ffn(1, 512, tbs, skip_norm=True)
        mixer(512, tbs, 128, NMETA + t * 512)
        if t + 1 < nt:
            pre, mid, post = prefetch_hooks(t + 1)
            ffn(2, 512, tbs, pre=pre, mid=mid, post=post)
        else:
            ffn(2, 512, tbs)
        fin_ops += final_out(t)
    P.emit(fin_ops)
    if debug:
        print("ops", len(P.ops), {e: sum(1 for o in P.ops if o.eng == e) for e in ENGS})
    return nc


WNAMES = ("ffn1_w_in", "ffn1_w_out", "w_in", "w_branch_gdn", "w_branch_ret", "w_out", "ffn2_w_in", "ffn2_w_out")


def make_in_maps(inputs, nb, nt):
    W = {k: np.asarray(inputs[k], np.float32)[0] for k in WNAMES}
    wst = host_pack_weights(W)
    prm = host_pack_params({k: np.asarray(inputs[k], np.float32)[0] for k in
                            ("ffn1_norm", "mix_norm", "ffn2_norm", "ret_out_norm", "gdn_out_norm", "gdn_conv_w",
                             "gdn_a_log", "gdn_dt_bias", "w_in")})
    fnw = np.ascontiguousarray(np.broadcast_to(np.asarray(inputs["final_norm"], np.float32)[None, :], (128, D)))
    cst = host_consts()
    rope = host_rope(NMETA + nt * 512)
    meta = np.ascontiguousarray(np.asarray(inputs["meta_tokens"], np.float32))
    x = np.asarray(inputs["x"], np.float32)
    return [{"x": np.ascontiguousarray(x[b, :nt * 512]), "meta": meta, "wst": wst, "prm": prm, "fnw": fnw, "cst": cst, "rope": rope}
            for b in range(nb)]


def kernel(**inputs):
    nt = SEQ // 512
    nc = build(nt)
    in_maps = make_in_maps(inputs, 8, nt)
    res = run_bass_kernel_spmd(nc, in_maps, core_ids=list(range(8)))
    return np.stack([np.asarray(r["out"], np.float32) for r in res.results], axis=0)
```
